# Optimizing a Trainium2 kernel written in Bass

```python
import numpy as np
import jax
import jax.numpy as jnp
from jax import lax

D_MODEL = 1024
BATCH = 8
SEQ = 8192
DEPTH = 2
DEC_BATCH = 8
DEC_SEQ = 32
PAST_LEN = 4096

CHUNK = 64
Q_BLOCK = 128
N_EVEN = (DEPTH + 1) // 2
N_ODD = DEPTH // 2
RET_H = 4
RET_D = 128
RET_W = RET_H * RET_D
ROPE_BASE = 10000.0
FOX_H = 8
FOX_D = 64
FOX_W = FOX_H * FOX_D
AB_IN = 4 * RET_W + 3 * FOX_W + FOX_H
AB_OUT = RET_W + FOX_W
S5_GROUP = 16
S5_G = D_MODEL // S5_GROUP
S5_P = 64
DT_MIN = 1e-3
DT_MAX = 1e-1
MEM_LEN = 256
X_H = 4
X_D = D_MODEL // X_H
D_FF = 4 * D_MODEL
EPS = 1e-6
NEG_INF = -1e30
F32 = jnp.float32

kernel_name = 'hybrid_retention_fox_s5_stream_step'


def _rmsnorm(x, w):
    xf = x.astype(F32)
    y = xf * lax.rsqrt(jnp.mean(xf * xf, axis=-1, keepdims=True) + EPS)
    return y * w.astype(F32)


def _rotary(x, pos):
    half = x.shape[-1] // 2
    inv = ROPE_BASE ** (-jnp.arange(half, dtype=F32) / half)
    ang = pos.astype(F32)[:, None] * inv[None, :]
    cos = jnp.cos(ang)[None, :, None, :]
    sin = jnp.sin(ang)[None, :, None, :]
    xf = x.astype(F32)
    x1, x2 = xf[..., :half], xf[..., half:]
    return jnp.concatenate([x1 * cos - x2 * sin, x2 * cos + x1 * sin], axis=-1)


def _head_groupnorm(o, w):
    mu = jnp.mean(o, axis=-1, keepdims=True)
    var = jnp.mean(jnp.square(o - mu), axis=-1, keepdims=True)
    return (o - mu) * lax.rsqrt(var + EPS) * w.astype(F32)


def _retention(q, k, v, s0, chunk):
    B, T, H, d = q.shape
    n = T // chunk
    qc = q.reshape(B, n, chunk, H, d)
    kc = k.reshape(B, n, chunk, H, d)
    vc = v.reshape(B, n, chunk, H, d)
    lg = jnp.log(1.0 - 2.0 ** (-5.0 - jnp.arange(H, dtype=F32)))
    idx = jnp.arange(chunk, dtype=F32)
    dec_in = jnp.exp(lg[:, None, None] * jnp.abs(idx[:, None] - idx[None, :]))
    w_q = jnp.exp(lg[None, :] * (idx[:, None] + 1.0))
    w_k = jnp.exp(lg[None, :] * (chunk - 1.0 - idx[:, None]))
    scores = jnp.einsum('bnihd,bnjhd->bnhij', qc, kc) * dec_in
    intra = jnp.einsum('bnhij,bnjhe->bnihe', scores, vc)
    kv = jnp.einsum('bnjhd,bnjhe->nbhde', kc * w_k[:, :, None], vc)
    decay_chunk = jnp.exp(lg * chunk)[None, :, None, None]

    def step(s, kv_c):
        return s * decay_chunk + kv_c, s

    s_fin, s_before = lax.scan(step, s0, kv)
    inter = jnp.einsum('bnihd,nbhde->bnihe', qc * w_q[:, :, None], s_before)
    return (intra + inter).reshape(B, T, H, d), s_fin


def _fox_attend(q, k, v, cq, ck, qpos, kpos):
    s = jnp.einsum('bqhd,bkhd->bhqk', q, k).astype(F32) * (FOX_D ** -0.5)
    s = s + jnp.swapaxes(cq, 1, 2)[..., :, None] - jnp.swapaxes(ck, 1, 2)[..., None, :]
    s = jnp.where(kpos[None, :] <= qpos[:, None], s, NEG_INF)
    p = jax.nn.softmax(s, axis=-1)
    return jnp.einsum('bhqk,bkhd->bqhd', p.astype(v.dtype), v)


def _fox_prompt(q, k, v, logf):
    B, T, H, Dh = q.shape
    c = jnp.cumsum(logf, axis=1)
    kpos = jnp.arange(T)

    def block(i):
        start = i * Q_BLOCK
        qb = lax.dynamic_slice_in_dim(q, start, Q_BLOCK, axis=1)
        cb = lax.dynamic_slice_in_dim(c, start, Q_BLOCK, axis=1)
        return _fox_attend(qb, k, v, cb, c, start + jnp.arange(Q_BLOCK), kpos)

    out = lax.map(block, jnp.arange(T // Q_BLOCK))
    return jnp.moveaxis(out, 0, 1).reshape(B, T, H, Dh)


def _fox_sample(q, k, v, logf, k_past, v_past, logf_past):
    L = q.shape[1]
    P = k_past.shape[1]
    k_all = jnp.concatenate([k_past.astype(k.dtype), k], axis=1)
    v_all = jnp.concatenate([v_past.astype(v.dtype), v], axis=1)
    c = jnp.cumsum(jnp.concatenate([logf_past.astype(F32), logf], axis=1), axis=1)
    return _fox_attend(q, k_all, v_all, c[:, P:], c, P + jnp.arange(L), jnp.arange(P + L))


def _ab_mixer(h, pos, p, e, st):
    B, T, _ = h.shape
    z = h @ p['w_in_ab'][e]
    splits = [RET_W, 2 * RET_W, 3 * RET_W, 4 * RET_W,
              4 * RET_W + FOX_W, 4 * RET_W + 2 * FOX_W, 4 * RET_W + 3 * FOX_W]
    rq, rk, rv, rg, fq, fk, fv, fl = jnp.split(z, splits, axis=-1)
    rq = _rotary(rq.reshape(B, T, RET_H, RET_D), pos)
    rk = _rotary(rk.reshape(B, T, RET_H, RET_D), pos) * (RET_D ** -0.5)
    rv = rv.reshape(B, T, RET_H, RET_D).astype(F32)
    if st is None:
        s0 = jnp.zeros((B, RET_H, RET_D, RET_D), F32)
        blk = CHUNK
    else:
        s0 = st['ret'][e].astype(F32)
        blk = T
    ro, s_new = _retention(rq, rk, rv, s0, blk)
    ro = _head_groupnorm(ro, p['ret_gn_w'][e]).reshape(B, T, RET_W) * jax.nn.silu(rg.astype(F32))
    fq = fq.reshape(B, T, FOX_H, FOX_D)
    fk = fk.reshape(B, T, FOX_H, FOX_D)
    fv = fv.reshape(B, T, FOX_H, FOX_D)
    logf = jax.nn.log_sigmoid(fl.astype(F32) + p['b_fox_f'][e].astype(F32))
    if st is None:
        fo = _fox_prompt(fq, fk, fv, logf)
    else:
        fo = _fox_sample(fq, fk, fv, logf, st['fox_k'][e], st['fox_v'][e], st['fox_logf'][e])
    mixed = jnp.concatenate([ro.astype(h.dtype), fo.reshape(B, T, FOX_W).astype(h.dtype)], axis=-1)
    return mixed @ p['w_out_ab'][e], (s_new, fk, fv, logf)


def _s5_discretize(lam_re, lam_im, log_dt, b_re, b_im):
    lam_re = lam_re.astype(F32)
    lam_im = lam_im.astype(F32)
    dt = jnp.exp(log_dt.astype(F32))[:, None]
    mag = jnp.exp(lam_re * dt)
    a_re = mag * jnp.cos(lam_im * dt)
    a_im = mag * jnp.sin(lam_im * dt)
    den = lam_re * lam_re + lam_im * lam_im
    num_re = a_re - 1.0
    f_re = (num_re * lam_re + a_im * lam_im) / den
    f_im = (a_im * lam_re - num_re * lam_im) / den
    b_re = b_re.astype(F32)
    b_im = b_im.astype(F32)
    bb_re = f_re[..., None] * b_re - f_im[..., None] * b_im
    bb_im = f_re[..., None] * b_im + f_im[..., None] * b_re
    return a_re, a_im, bb_re, bb_im


def _s5_combine(e1, e2):
    a1r, a1i, b1r, b1i = e1
    a2r, a2i, b2r, b2i = e2
    return (a2r * a1r - a2i * a1i, a2r * a1i + a2i * a1r,
            a2r * b1r - a2i * b1i + b2r, a2r * b1i + a2i * b1r + b2i)


def _s5_mixer(h, p, o, st):
    B, T, _ = h.shape
    u = (h @ p['w_in_c'][o]).astype(F32).reshape(B, T, S5_G, S5_GROUP)
    a_re, a_im, bb_re, bb_im = _s5_discretize(p['s5_lambda_re'][o], p['s5_lambda_im'][o],
                                              p['s5_log_dt'][o], p['s5_b_re'][o], p['s5_b_im'][o])
    bu_re = jnp.einsum('gpc,btgc->btgp', bb_re, u)
    bu_im = jnp.einsum('gpc,btgc->btgp', bb_im, u)
    if st is not None:
        x0_re = st['s5_re'][o].astype(F32)
        x0_im = st['s5_im'][o].astype(F32)
        bu_re = bu_re.at[:, 0].add(a_re * x0_re - a_im * x0_im)
        bu_im = bu_im.at[:, 0].add(a_re * x0_im + a_im * x0_re)
    A_re = jnp.broadcast_to(a_re, bu_re.shape)
    A_im = jnp.broadcast_to(a_im, bu_im.shape)
    _, _, xr, xi = lax.associative_scan(_s5_combine, (A_re, A_im, bu_re, bu_im), axis=1)
    c_re = p['s5_c_re'][o].astype(F32)
    c_im = p['s5_c_im'][o].astype(F32)
    y = (jnp.einsum('gcp,btgp->btgc', c_re, xr) - jnp.einsum('gcp,btgp->btgc', c_im, xi)
         + p['s5_d'][o].astype(F32) * u)
    y = y.reshape(B, T, D_MODEL).astype(h.dtype)
    g = y @ p['w_glu'][o]
    out = g[..., :D_MODEL] * jax.nn.sigmoid(g[..., D_MODEL:])
    return out, (xr[:, -1], xi[:, -1])


def _memory_kv(mem, norm_w, w_k, w_v):
    B, M, _ = mem.shape
    m = _rmsnorm(mem, norm_w).astype(mem.dtype)
    return (m @ w_k).reshape(B, M, X_H, X_D), (m @ w_v).reshape(B, M, X_H, X_D)


def _cross_attn(h, mk, mv, w_q, w_o):
    B, T, _ = h.shape
    q = (h @ w_q).reshape(B, T, X_H, X_D)
    s = jnp.einsum('bthd,bmhd->bhtm', q, mk.astype(q.dtype)).astype(F32) * (X_D ** -0.5)
    pr = jax.nn.softmax(s, axis=-1)
    o = jnp.einsum('bhtm,bmhd->bthd', pr.astype(h.dtype), mv.astype(h.dtype))
    return o.reshape(B, T, D_MODEL) @ w_o


def _sq_relu_mlp(h, w_up, w_down):
    return jnp.square(jax.nn.relu(h @ w_up)) @ w_down


def _run_group(x, pos, p, mem=None, st=None):
    dt = x.dtype
    new = {'ret': [], 'fox_k': [], 'fox_v': [], 'fox_logf': [], 's5_re': [], 's5_im': [],
           'mem_k': [], 'mem_v': []}
    for layer in range(DEPTH):
        g = p['norm_w'][layer]
        h = _rmsnorm(x, g[0]).astype(dt)
        if layer % 2 == 0:
            mix, (s_ret, fk, fv, flf) = _ab_mixer(h, pos, p, layer // 2, st)
            new['ret'].append(s_ret)
            new['fox_k'].append(fk)
            new['fox_v'].append(fv)
            new['fox_logf'].append(flf)
        else:
            mix, (sr, si) = _s5_mixer(h, p, layer // 2, st)
            new['s5_re'].append(sr)
            new['s5_im'].append(si)
        x = x + _rmsnorm(mix, g[1]).astype(dt)
        h = _rmsnorm(x, g[2]).astype(dt)
        if st is None:
            mk, mv = _memory_kv(mem, p['mem_norm_w'][layer], p['w_xk'][layer], p['w_xv'][layer])
            new['mem_k'].append(mk)
            new['mem_v'].append(mv)
        else:
            mk, mv = st['mem_k'][layer], st['mem_v'][layer]
        x = x + _rmsnorm(_cross_attn(h, mk, mv, p['w_xq'][layer], p['w_xo'][layer]), g[3]).astype(dt)
        h = _rmsnorm(x, g[4]).astype(dt)
        x = x + _rmsnorm(_sq_relu_mlp(h, p['w_up'][layer], p['w_down'][layer]), g[5]).astype(dt)
    return x, new


def setup_inputs(seed: int = 0) -> dict:
    key = jax.random.key(seed)
    ks = jax.random.split(key, 40)

    def nrm(i, shape, scale=1.0):
        return jax.random.normal(ks[i], shape, F32) * scale

    n = jnp.arange(S5_P, dtype=F32)
    return {
        'x_prompt': nrm(0, (BATCH, SEQ, D_MODEL)),
        'x_sample': nrm(1, (DEC_BATCH, DEC_SEQ, D_MODEL)),
        'cache_ret_state': nrm(2, (N_EVEN, DEC_BATCH, RET_H, RET_D, RET_D), 0.5),
        'cache_fox_k': nrm(3, (N_EVEN, DEC_BATCH, PAST_LEN, FOX_H, FOX_D)),
        'cache_fox_v': nrm(4, (N_EVEN, DEC_BATCH, PAST_LEN, FOX_H, FOX_D)),
        'cache_fox_logf': jax.nn.log_sigmoid(2.5 + nrm(5, (N_EVEN, DEC_BATCH, PAST_LEN, FOX_H))),
        'state_s5_re': nrm(6, (N_ODD, DEC_BATCH, S5_G, S5_P), 0.1),
        'state_s5_im': nrm(7, (N_ODD, DEC_BATCH, S5_G, S5_P), 0.1),
        'cache_mem_k': nrm(8, (DEPTH, DEC_BATCH, MEM_LEN, X_H, X_D)),
        'cache_mem_v': nrm(9, (DEPTH, DEC_BATCH, MEM_LEN, X_H, X_D)),
        'mem_prompt': nrm(10, (BATCH, MEM_LEN, D_MODEL)),
        'norm_w': 1.0 + nrm(11, (DEPTH, 6, D_MODEL), 0.05),
        'w_in_ab': nrm(12, (N_EVEN, D_MODEL, AB_IN), D_MODEL ** -0.5),
        'b_fox_f': jax.random.uniform(ks[13], (N_EVEN, FOX_H), F32, 1.0, 4.0),
        'ret_gn_w': 1.0 + nrm(14, (N_EVEN, RET_H, RET_D), 0.05),
        'w_out_ab': nrm(15, (N_EVEN, AB_OUT, D_MODEL), AB_OUT ** -0.5),
        'w_in_c': nrm(16, (N_ODD, D_MODEL, D_MODEL), D_MODEL ** -0.5),
        's5_lambda_re': -0.5 + nrm(17, (N_ODD, S5_G, S5_P), 0.01),
        's5_lambda_im': jnp.pi * n + nrm(18, (N_ODD, S5_G, S5_P), 0.01),
        's5_log_dt': jax.random.uniform(ks[19], (N_ODD, S5_G), F32, float(np.log(DT_MIN)), float(np.log(DT_MAX))),
        's5_b_re': nrm(20, (N_ODD, S5_G, S5_P, S5_GROUP), (2.0 * S5_GROUP) ** -0.5),
        's5_b_im': nrm(21, (N_ODD, S5_G, S5_P, S5_GROUP), (2.0 * S5_GROUP) ** -0.5),
        's5_c_re': nrm(22, (N_ODD, S5_G, S5_GROUP, S5_P), (2.0 * S5_P) ** -0.5),
        's5_c_im': nrm(23, (N_ODD, S5_G, S5_GROUP, S5_P), (2.0 * S5_P) ** -0.5),
        's5_d': nrm(24, (N_ODD, S5_G, S5_GROUP)),
        'w_glu': nrm(25, (N_ODD, D_MODEL, 2 * D_MODEL), D_MODEL ** -0.5),
        'mem_norm_w': 1.0 + nrm(26, (DEPTH, D_MODEL), 0.05),
        'w_xq': nrm(27, (DEPTH, D_MODEL, D_MODEL), D_MODEL ** -0.5),
        'w_xk': nrm(28, (DEPTH, D_MODEL, D_MODEL), D_MODEL ** -0.5),
        'w_xv': nrm(29, (DEPTH, D_MODEL, D_MODEL), D_MODEL ** -0.5),
        'w_xo': nrm(30, (DEPTH, D_MODEL, D_MODEL), D_MODEL ** -0.5),
        'w_up': nrm(31, (DEPTH, D_MODEL, D_FF), D_MODEL ** -0.5),
        'w_down': nrm(32, (DEPTH, D_FF, D_MODEL), D_FF ** -0.5),
    }


def reference(x_prompt, x_sample, cache_ret_state, cache_fox_k, cache_fox_v, cache_fox_logf,
              state_s5_re, state_s5_im, cache_mem_k, cache_mem_v, mem_prompt,
              norm_w, w_in_ab, b_fox_f, ret_gn_w, w_out_ab, w_in_c,
              s5_lambda_re, s5_lambda_im, s5_log_dt, s5_b_re, s5_b_im, s5_c_re, s5_c_im, s5_d, w_glu,
              mem_norm_w, w_xq, w_xk, w_xv, w_xo, w_up, w_down):
    p = {'norm_w': norm_w, 'w_in_ab': w_in_ab, 'b_fox_f': b_fox_f, 'ret_gn_w': ret_gn_w,
         'w_out_ab': w_out_ab, 'w_in_c': w_in_c, 's5_lambda_re': s5_lambda_re,
         's5_lambda_im': s5_lambda_im, 's5_log_dt': s5_log_dt, 's5_b_re': s5_b_re, 's5_b_im': s5_b_im,
         's5_c_re': s5_c_re, 's5_c_im': s5_c_im, 's5_d': s5_d, 'w_glu': w_glu,
         'mem_norm_w': mem_norm_w, 'w_xq': w_xq, 'w_xk': w_xk, 'w_xv': w_xv, 'w_xo': w_xo,
         'w_up': w_up, 'w_down': w_down}
    st = {'ret': cache_ret_state, 'fox_k': cache_fox_k, 'fox_v': cache_fox_v,
          'fox_logf': cache_fox_logf, 's5_re': state_s5_re, 's5_im': state_s5_im,
          'mem_k': cache_mem_k, 'mem_v': cache_mem_v}
    pos_prompt = jnp.arange(x_prompt.shape[1])
    pos_sample = cache_fox_k.shape[2] + jnp.arange(x_sample.shape[1])
    y_prompt, np_ = _run_group(x_prompt, pos_prompt, p, mem=mem_prompt)
    y_sample, ns_ = _run_group(x_sample, pos_sample, p, st=st)
    return (y_prompt, y_sample,
            jnp.stack(np_['ret']), jnp.stack(ns_['ret']),
            jnp.stack(np_['fox_k']), jnp.stack(np_['fox_v']), jnp.stack(np_['fox_logf']),
            jnp.stack(ns_['fox_k']), jnp.stack(ns_['fox_v']), jnp.stack(ns_['fox_logf']),
            jnp.stack(np_['s5_re']), jnp.stack(np_['s5_im']),
            jnp.stack(ns_['s5_re']), jnp.stack(ns_['s5_im']),
            jnp.stack(np_['mem_k']), jnp.stack(np_['mem_v']))
```

```python
import contextlib
import numpy as np
import ml_dtypes
import concourse.bass as bass
import concourse.mybir as mybir
from concourse.bass_utils import run_bass_kernel_spmd

F32 = mybir.dt.float32
BF16 = mybir.dt.bfloat16
AF = mybir.ActivationFunctionType
ALU = mybir.AluOpType
AX = mybir.AxisListType

D = 1024
T = 8192
TS = 32
PAST = 4096
TALL = T + 128
TT = 512
EPS = 1e-6
AB_IN = 3592
NCORES = 8


class Buf:
    __slots__ = ("name", "w", "r", "pw")

    def __init__(self, name=""):
        self.name = name
        self.w = {}
        self.r = {}
        self.pw = {}


class Op:
    __slots__ = ("eng", "fn", "deps", "marked", "sem", "val", "is_dma", "slotwait", "phase")

    def __init__(self, eng, fn, is_dma, phase):
        self.eng = eng
        self.fn = fn
        self.deps = []
        self.marked = False
        self.sem = None
        self.val = 0
        self.is_dma = is_dma
        self.slotwait = None
        self.phase = phase


class Prog:
    ENGS = ("pe", "act", "dve", "pool", "sp")
    NSLOT = 8

    def __init__(self, nc, stack):
        self.nc = nc
        self.ops = []
        self.phase = 0
        self.csem = {e: stack.enter_context(nc.semaphore("c_" + e)) for e in ("pe", "act", "dve", "pool")}
        self.dsem = {e: [stack.enter_context(nc.semaphore("d_%s%d" % (e, i))) for i in range(self.NSLOT)]
                     for e in ("sp", "act", "pool")}
        self.ccount = {e: 0 for e in self.csem}
        self.dcount = {e: 0 for e in self.dsem}
        self.waited = {e: {} for e in self.ENGS}
        self.dma_final = {}
        self.n_emitted = 0

    def _add(self, op, reads, writes):
        deps = {}
        for b in reads:
            for o in b.w.values():
                deps[id(o)] = o
        key = id(op) if op.is_dma else op.eng
        for b in writes:
            merge = op.is_dma and b.w and not b.r and all(o.is_dma and o.phase == self.phase for o in b.w.values())
            if merge:
                for o in getattr(b, "pw", {}).values():
                    deps[id(o)] = o
                b.w[key] = op
                continue
            prev = {}
            for o in b.w.values():
                deps[id(o)] = o
                prev[id(o)] = o
            for o in b.r.values():
                deps[id(o)] = o
                prev[id(o)] = o
            b.pw = prev
            b.w = {key: op}
            b.r = {}
        for b in reads:
            b.r[key] = op
        for o in deps.values():
            if o is op or o.phase != self.phase:
                continue
            if (not o.is_dma) and (not op.is_dma) and o.eng == "pe" and op.eng == "pe":
                continue
            op.deps.append(o)
            o.marked = True
        self.ops.append(op)
        return op

    def op(self, eng, name, reads=(), writes=(), **kw):
        def fn(e):
            return getattr(e, name)(**kw)
        return self._add(Op(eng, fn, False, self.phase), reads, writes)

    def dma(self, eng, out, in_, reads=(), writes=(), **kw):
        def fn(e):
            return e.dma_start(out=out, in_=in_, **kw)
        o = Op(eng, fn, True, self.phase)
        o.marked = True
        return self._add(o, reads, writes)

    def maybe_flush(self, limit=2500):
        if len(self.ops) >= limit:
            self.flush()

    def flush(self):
        if not self.ops:
            return
        nc = self.nc
        per = {e: [] for e in self.ENGS}
        last_c = {}
        for o in self.ops:
            per[o.eng].append(o)
            if not o.is_dma:
                last_c[o.eng] = o
        for o in last_c.values():
            o.marked = True
        for o in self.ops:
            if o.is_dma:
                i = self.dcount[o.eng]
                self.dcount[o.eng] += 1
                o.sem = self.dsem[o.eng][i % self.NSLOT]
                o.val = 16 * (i // self.NSLOT + 1)
                if i >= self.NSLOT:
                    o.slotwait = (o.sem, o.val - 16)
                self.dma_final[id(o.sem)] = (o.sem, o.val)
            elif o.marked:
                self.ccount[o.eng] += 1
                o.sem = self.csem[o.eng]
                o.val = self.ccount[o.eng]
        finals = list(self.dma_final.values()) + [(self.csem[e], self.ccount[e]) for e in self.csem]
        self.n_emitted += len(self.ops)

        def make(ename, ops):
            waited = self.waited[ename]

            def body(e):
                def w(sem, val):
                    if val > 0 and waited.get(id(sem), 0) < val:
                        e.wait_ge(sem, val)
                        waited[id(sem)] = val
                for o in ops:
                    if o.slotwait is not None:
                        w(*o.slotwait)
                    for d in o.deps:
                        w(d.sem, d.val)
                    ins = o.fn(e)
                    if o.marked:
                        ins.then_inc(o.sem, 16 if o.is_dma else 1)
                for sem, val in finals:
                    w(sem, val)
            return body

        with nc.allow_low_precision(reason="bf16 matmul operands by design"), nc.Block() as blk:
            blk.tensor(make("pe", per["pe"]))
            blk.scalar(make("act", per["act"]))
            blk.vector(make("dve", per["dve"]))
            blk.gpsimd(make("pool", per["pool"]))
            blk.sync(make("sp", per["sp"]))
        self.ops = []
        self.phase += 1


def _consts():
    c = {}
    c["ident_f"] = np.eye(128, dtype=np.float32)
    c["ident_b"] = np.eye(128, dtype=np.float32).astype(ml_dtypes.bfloat16)
    j = np.arange(128)
    c["tri_f"] = (j[:, None] <= j[None, :]).astype(np.float32)
    c["ones_f"] = np.ones((128, 128), np.float32)
    c["ones_b"] = np.ones((128, 128), np.float32).astype(ml_dtypes.bfloat16)
    perm = np.zeros((128, 128), np.float32)
    perm[(j + 64) % 128, j] = 1.0
    c["perm_b"] = perm.astype(ml_dtypes.bfloat16)
    half = 64
    inv = (10000.0 ** (-np.arange(half, dtype=np.float32) / half)).astype(np.float32)
    pos = np.concatenate([np.arange(T), PAST + np.arange(128)]).astype(np.float32)
    ang = pos[None, :] * inv[:, None]
    cos = np.cos(ang).astype(np.float32)
    sin = np.sin(ang).astype(np.float32)
    c["ropec"] = np.concatenate([cos, cos], 0)
    c["ropes"] = np.concatenate([-sin, sin], 0)
    lg = np.log(1.0 - 2.0 ** (-5.0 - np.arange(4, dtype=np.float64)))
    il = np.arange(512)[None, :]
    jl = np.arange(128)[:, None]
    rmask = np.zeros((4, 128, 512), np.float64)
    for h in range(4):
        m = np.exp(lg[h] * np.abs(il - jl))
        ok = (jl // 64) <= (il // 64)
        rmask[h] = np.where(ok, m, 0.0)
    c["rmask"] = rmask.transpose(1, 0, 2).astype(np.float32).copy()
    ii = np.arange(512).reshape(4, 128)
    wq = np.exp(lg[None, None, :] * (ii.T[:, :, None] + 1.0))
    c["wq"] = np.repeat(wq[:, :, :, None], 128, 3).astype(np.float32)
    wk = np.exp(lg[None, None, :] * (511.0 - ii.T[:, :, None]))
    c["wk"] = np.repeat(wk[:, :, :, None], 128, 3).astype(np.float32)
    wks = np.exp(lg[None, :] * (31.0 - np.arange(128)[:, None]))
    wks[32:] = 0.0
    c["wks"] = np.repeat(wks[:, :, None], 128, 2).astype(np.float32)
    c["rdec"] = np.stack([np.exp(lg * 512.0), np.exp(lg * 32.0)]).astype(np.float32)
    qs = np.ones((128, 1), np.float32)
    qs[:64] = 0.125
    c["qscale"] = qs
    selc = np.zeros((8, 8, 3, 3), np.float32)
    for h in range(8):
        for p_ in range(3):
            selc[h, h, p_, p_] = 1.0
    c["selc"] = selc.astype(ml_dtypes.bfloat16)
    c["caus_b"] = (j[:, None] <= j[None, :]).astype(np.float32).astype(ml_dtypes.bfloat16)
    kv = np.concatenate([-(np.arange(8) + 1.0), np.arange(8) + 1.0]).astype(np.float32)
    c["kvec"] = np.repeat(kv[None, :], 128, 0).copy()
    c["nvec"] = np.repeat((8.0 * (np.arange(64) + 1.0)).astype(np.float32)[None, :], 128, 0).copy()
    sel = np.zeros((128, 8, 8, 128), np.float32)
    selT = np.zeros((128, 8, 8, 128), np.float32)
    for g_ in range(8):
        for s_ in range(8):
            for c_ in range(16):
                sel[g_ * 16 + c_, g_, s_, s_ * 16 + c_] = 1.0
                selT[s_ * 16 + c_, g_, s_, g_ * 16 + c_] = 1.0
    c["sel_b"] = sel.reshape(128, 64, 128).astype(ml_dtypes.bfloat16)
    c["selT_b"] = selT.reshape(128, 64, 128).astype(ml_dtypes.bfloat16)
    sg = np.arange(128) // 16
    tm_ = (sg[None, :] >= sg[:, None]).astype(np.float32)
    c["tmask"] = np.tile(tm_, (1, 4)).copy()
    return c


CONST = None


def get_consts():
    global CONST
    if CONST is None:
        CONST = _consts()
    return CONST


IN_SHAPES = {
    "x_prompt": [T, D], "x_sample": [TS, D], "cache_ret_state": [4, 128, 128],
    "cache_fox_k": [PAST, 512], "cache_fox_v": [PAST, 512], "cache_fox_logf": [PAST, 8],
    "state_s5_re": [64, 64], "state_s5_im": [64, 64],
    "cache_mem_k": [2, 256, 1024], "cache_mem_v": [2, 256, 1024], "mem_prompt": [256, 1024],
    "norm_w": [2, 6, D], "w_in_ab": [D, AB_IN], "b_fox_f": [1, 8], "ret_gn_w": [1, 512],
    "w_out_ab": [D, D], "w_in_c": [D, D], "s5_lambda_re": [64, 64], "s5_lambda_im": [64, 64],
    "s5_log_dt": [1, 64], "s5_b_re": [64, 64, 16], "s5_b_im": [64, 64, 16],
    "s5_c_re": [64, 16, 64], "s5_c_im": [64, 16, 64], "s5_d": [1, D], "w_glu": [D, 2 * D],
    "mem_norm_w": [2, D], "w_xq": [2, D, D], "w_xk": [2, D, D], "w_xv": [2, D, D], "w_xo": [2, D, D],
    "w_up": [2, D, 4 * D], "w_down": [2, 4 * D, D],
}
OUT_SHAPES = {
    "y_prompt": [T, D], "y_sample": [TS, D], "ret_state_prompt": [4, 128, 128], "ret_state_sample": [4, 128, 128],
    "fox_k_prompt": [T, 512], "fox_v_prompt": [T, 512], "fox_logf_prompt": [T, 8],
    "fox_k_sample": [TS, 512], "fox_v_sample": [TS, 512], "fox_logf_sample": [TS, 8],
    "s5_re_prompt": [64, 64], "s5_im_prompt": [64, 64], "s5_re_sample": [64, 64], "s5_im_sample": [64, 64],
    "mem_k_prompt": [2, 256, 1024], "mem_v_prompt": [2, 256, 1024],
}
OUT_ORDER = list(OUT_SHAPES.keys())


class K:
    def __init__(self):
        self.nc = bass.Bass("TRN2", target_bir_lowering=False)
        nc = self.nc
        self.I = {k: nc.dram_tensor(k, s, F32, kind="ExternalInput").ap() for k, s in IN_SHAPES.items()}
        self.O = {k: nc.dram_tensor(k, s, F32, kind="ExternalOutput").ap() for k, s in OUT_SHAPES.items()}
        self.C = {}
        for k, v in get_consts().items():
            dt = BF16 if v.dtype == ml_dtypes.bfloat16 else F32
            self.C[k] = nc.dram_tensor("c_" + k, list(v.shape), dt, kind="ExternalInput").ap()
        self.S = {}
        self.SB = {}

    def scratch(self, name, shape, dt):
        self.S[name] = self.nc.dram_tensor("s_" + name, shape, dt, kind="Internal").ap()
        self.SB[name] = {}
        return self.S[name]

    def sbuf(self, st, name, shape, dt):
        return st.enter_context(self.nc.sbuf_tensor(name, shape, dt))


def tiles():
    out = [(t * TT, TT, False) for t in range(T // TT)]
    out.append((T, TS, True))
    return out


def xsrc(k, which, t0, n, sample):
    return k.I["x_sample"] if sample else k.I["x_prompt"][t0:t0 + n, :]


def blocks(n):
    return [(b * 128, min(128, n - b * 128)) for b in range((n + 127) // 128)]


def build(phases=("W", "A1")):
    k = K()
    nc = k.nc
    with contextlib.ExitStack() as top:
        P = Prog(nc, top)
        ps = top.enter_context(nc.psum_tensor("ps", [128, 4096], F32))
        PB = [Buf("ps%d" % i) for i in range(8)]

        def bank(i):
            return ps[:, i * 512:(i + 1) * 512]

        identf = k.sbuf(top, "identf", [128, 128], F32)
        identb = k.sbuf(top, "identb", [128, 128], BF16)
        onesf = k.sbuf(top, "onesf", [128, 128], F32)
        onesb = k.sbuf(top, "onesb", [128, 128], BF16)
        CB = Buf("consts")
        P.dma("sp", identf[:], k.C["ident_f"], writes=[CB])
        P.dma("sp", identb[:], k.C["ident_b"], writes=[CB])
        P.dma("sp", onesf[:], k.C["ones_f"], writes=[CB])
        P.dma("sp", onesb[:], k.C["ones_b"], writes=[CB])
        P.flush()

        WNAMES = [("w_in_ab", None, [D, AB_IN]), ("w_out_ab", None, [D, D]), ("w_in_c", None, [D, D]),
                  ("w_glu", None, [D, 2 * D])]
        for l in range(2):
            for nm in ("w_xq", "w_xk", "w_xv", "w_xo"):
                WNAMES.append((nm, l, [D, D]))
            WNAMES.append(("w_up", l, [D, 4 * D]))
            WNAMES.append(("w_down", l, [4 * D, D]))
        WB = {}
        for nm, l, shp in WNAMES:
            key = nm if l is None else "%s%d" % (nm, l)
            WB[key] = k.scratch("wb_" + key, shp, BF16)
        if "W" in phases:
            wbuf = Buf("wcast")
            for nm, l, shp in WNAMES:
                key = nm if l is None else "%s%d" % (nm, l)
                src = k.I[nm] if l is None else k.I[nm][l]
                rows = shp[0]
                step = 256
                for r0 in range(0, rows, step):
                    P.dma("pool", WB[key][r0:r0 + step, :], src[r0:r0 + step, :], writes=[Buf()])
            P.flush()

        k.scratch("x", [TALL, D], F32)
        k.scratch("rqT", [4, 128, TALL], BF16)
        k.scratch("rkT", [4, 128, TALL], BF16)
        k.scratch("rkk", [TALL, 512], BF16)
        k.scratch("rv", [TALL, 512], BF16)
        k.scratch("rg", [TALL, 512], BF16)
        k.scratch("fqa", [17, 4, 67, 2, TT], BF16)
        k.scratch("fka", [25, 4, 67, 2, TT], BF16)
        k.scratch("fva", [25, 4, 128, 4, 130], BF16)
        k.scratch("negc", [25, 128, 4, 8], F32)

        k.scratch("foT", [8, 64, TALL], BF16)
        k.scratch("uT", [8, 128, TALL], BF16)
        k.scratch("yT", [8, 128, TALL], BF16)
        env = dict(k=k, P=P, ps=ps, PB=PB, bank=bank, WB=WB, identf=identf, identb=identb, onesf=onesf, onesb=onesb)
        if "A1" in phases:
            phase_A1(k, P, ps, PB, bank, WB, identf, identb, onesf, onesb)
        if "A3" in phases:
            phase_A3(**env)
        if "A2" in phases:
            phase_A2(**env)
        for l in range(2):
            if "S" in phases and l == 1:
                phase_S1(**env)
                phase_S2(**env)
                phase_S3(**env)
            if ("C%d" % l) in phases:
                phase_C(l=l, **env)
            if ("F%d" % l) in phases:
                phase_F(l=l, final=(l == 1), **env)
        if "dumpx" in DBG:
            P.dma("sp", k.O["y_prompt"], k.S["x"][0:T, :], writes=[Buf()])
            P.dma("sp", k.O["y_sample"], k.S["x"][T:T + TS, :], writes=[Buf()])
            P.flush()
        if "dumpfo" in DBG:
            P.dma("pool", k.O["y_prompt"][0:512, :].rearrange("(h r) t -> h r t", h=8),
                  k.S["foT"][:, :, DBG["dumpfo"]:DBG["dumpfo"] + 1024], writes=[Buf()])
            P.flush()
    return k


CB_GLOBAL = Buf("cglobal")
DBG = {}


def phase_A1(k, P, ps, PB, bank, WB, identf, identb, onesf, onesb):
    with contextlib.ExitStack() as st:
        def sb(name, shape, dt):
            return k.sbuf(st, "a1_" + name, shape, dt)
        win = sb("win", [128, 8, AB_IN], BF16)
        g0 = sb("g0", [128, D], F32)
        bfox = sb("bfox", [128, 8], F32)
        perm = sb("perm", [128, 128], BF16)
        trif = sb("trif", [128, 128], F32)
        qscale = sb("qscale", [128, 1], F32)
        selc = sb("selc", [8, 72], BF16)
        wk = sb("wk", [128, 4, 512], F32)
        wks = sb("wks", [128, 512], F32)
        WBUF = Buf("a1w")
        wsrc = WB["w_in_ab"].rearrange("(kc p) n -> p kc n", p=128)
        for kc in range(8):
            P.dma("sp" if kc % 2 == 0 else "act", win[:, kc, :], wsrc[:, kc, :], writes=[WBUF])
        P.dma("sp", g0[:], k.I["norm_w"][0, 0:1, :].broadcast_to([128, D]), writes=[WBUF])
        P.dma("sp", bfox[:], k.I["b_fox_f"][0:1, :].broadcast_to([128, 8]), writes=[WBUF])
        P.dma("sp", perm[:], k.C["perm_b"], writes=[WBUF])
        P.dma("sp", trif[:], k.C["tri_f"], writes=[WBUF])
        P.dma("sp", qscale[:], k.C["qscale"], writes=[WBUF])
        P.dma("sp", selc[:], k.C["selc"].rearrange("k h p m -> k (h p m)"), writes=[WBUF])
        P.dma("sp", wk[:], k.C["wk"].rearrange("p b h e -> p b (h e)"), writes=[WBUF])
        P.dma("sp", wks[:], k.C["wks"].rearrange("p h e -> p (h e)"), writes=[WBUF])

        xt0 = sb("xt0", [128, 4, D], F32)
        xt = [xt0, xt0]
        XT0 = Buf("xt0")
        XT = [XT0, XT0]
        rc0 = sb("rc0", [128, TT], F32)
        rs0 = sb("rs0", [128, TT], F32)
        rc = [rc0, rc0]
        rs = [rs0, rs0]
        RT0 = Buf("rt0")
        RT = [RT0, RT0]
        junk = sb("junk", [128, D], BF16)
        JB = Buf("junk")
        ss = sb("ss", [128, 4], F32)
        rstd = sb("rstd", [128, 4], F32)
        SSB = Buf("ss")
        h = sb("h", [128, 4, D], BF16)
        HB = Buf("h")
        hT = sb("hT", [128, 8, TT], BF16)
        HTB = Buf("hT")
        qbs = [sb("qb%d" % i, [128, TT], BF16) for i in range(2)]
        QBs = [Buf("qb%d" % i) for i in range(2)]
        t1s = [sb("t1%d" % i, [128, TT], F32) for i in range(2)]
        t2s = [sb("t2%d" % i, [128, TT], F32) for i in range(2)]
        T1Bs = [Buf("t1%d" % i) for i in range(2)]
        T2Bs = [Buf("t2%d" % i) for i in range(2)]
        rqT = sb("rqT", [128, 8, TT], BF16)
        RQB = [Buf("rq%d" % i) for i in range(8)]
        rkk = sb("rkk", [128, 4, 512], BF16)
        RKKB = Buf("rkk")
        rv = sb("rv", [128, 4, 512], BF16)
        RVB = Buf("rv")
        rg = sb("rg", [128, 4, 512], BF16)
        RGB = Buf("rg")
        fkf = sb("fkf", [128, 4, 512], F32)
        FKFB = Buf("fkf")
        fkb = h
        FKBB = HB
        fvf = sb("fvf", [128, 4, 512], F32)
        FVFB = Buf("fvf")
        fva = sb("fva", [128, 4, 8 * 65], BF16)
        FVAB = Buf("fva")
        fka = sb("fka", [67, 8, TT], BF16)
        FKAB = Buf("fka")
        fqa = sb("fqa", [67, 8, TT], BF16)
        FQAB = Buf("fqa")
        lf = sb("lf", [128, 4, 8], F32)
        LFB = Buf("lf")
        xb = sb("xb", [128, 4, 8], F32)
        XBB = Buf("xb")
        cS = sb("cS", [128, 4, 8], F32)
        CSB = Buf("cS")
        ngc = sb("ngc", [128, 4, 8], F32)
        NGB = Buf("ngc")
        carry = sb("carry", [1, 8], F32)
        CARB = Buf("carry")
        cT = sb("cT", [8, TT], F32)
        c3 = sb("c3", [8, 3, TT], BF16)
        r1 = sb("r1", [8, TT], F32)
        r2 = cT
        CTB, C3B, R1B = Buf("cT"), Buf("c3"), Buf("r1")
        R2B = CTB

        epsc = sb("epsc", [128, 1], F32)
        P.op("pool", "memset", writes=[SSB], ap=epsc[:], constant=EPS)
        P.op("pool", "memset", writes=[FVAB], ap=fva[:], constant=1.0)
        P.op("pool", "memset", writes=[FKAB], ap=fka[64:67, :, :], constant=1.0)
        P.op("pool", "memset", writes=[CARB], ap=carry[:], constant=0.0)

        def MM(out, lhsT, rhs, start, stop, R, Wb):
            P.op("pe", "matmul", reads=R, writes=[Wb], out=out, lhsT=lhsT, rhs=rhs, start=start, stop=stop)

        seqs = [("prm", t * TT, TT) for t in range(T // TT)]
        seqs += [("past", pt * TT, TT) for pt in range(PAST // TT)]
        seqs += [("smp", T, TS)]
        if "a1_seqs" in DBG:
            seqs = [seqs[i] for i in DBG["a1_seqs"]]
        for ti, (kind, t0, n) in enumerate(seqs):
            s = ti % 2
            blks = blocks(n)
            nb = len(blks)
            pp = min(128, n)
            kcol0 = t0 if kind == "past" else PAST + t0

            def tm(ap2d):
                return ap2d.rearrange("(b p) d -> p b d", p=pp)
            if kind == "past" and t0 == 0:
                P.op("pool", "memset", writes=[CARB], ap=carry[:], constant=0.0)
            if kind == "past":
                P.dma("sp", fkf[:, :, :], tm(k.I["cache_fox_k"][t0:t0 + n, :]), writes=[FKFB])
                P.dma("act", fvf[:, :, :], tm(k.I["cache_fox_v"][t0:t0 + n, :]), writes=[FVFB])
                P.dma("sp", lf[:, :, :], tm(k.I["cache_fox_logf"][t0:t0 + n, :]), writes=[LFB])
            else:
                xs = k.I["x_sample"] if kind == "smp" else k.I["x_prompt"][t0:t0 + n, :]
                P.dma("sp", xt[s][0:pp, 0:nb, :], tm(xs), writes=[XT[s]])
                P.dma("act", rc[s][:, 0:n], k.C["ropec"][:, t0:t0 + n], writes=[RT[s]])
                P.dma("act", rs[s][:, 0:n], k.C["ropes"][:, t0:t0 + n], writes=[RT[s]])
                for b, (o, bs) in enumerate(blks):
                    P.op("act", "activation", reads=[XT[s]], writes=[JB, SSB], out=junk[0:bs, :], in_=xt[s][0:bs, b, :],
                         func=AF.Square, accum_out=ss[0:bs, b:b + 1])
                P.op("act", "activation", reads=[SSB], writes=[SSB], out=rstd[:, 0:nb], in_=ss[:, 0:nb], func=AF.Sqrt,
                     scale=1.0 / D, bias=epsc[:, 0:1])
                P.op("dve", "reciprocal", reads=[SSB], writes=[SSB], out=rstd[:, 0:nb], in_=rstd[:, 0:nb])
                for b, (o, bs) in enumerate(blks):
                    P.op("dve", "scalar_tensor_tensor", reads=[XT[s], SSB, WBUF], writes=[HB],
                         out=h[0:bs, b, :], in0=xt[s][0:bs, b, :], scalar=rstd[0:bs, b:b + 1], in1=g0[0:bs, :],
                         op0=ALU.mult, op1=ALU.mult)
                for kc in range(8):
                    bk = kc % 4
                    for b, (o, bs) in enumerate(blks):
                        MM(bank(bk)[:, o:o + bs], h[0:bs, b, kc * 128:(kc + 1) * 128], identb[0:bs, 0:bs], True, True,
                           [HB], PB[bk])
                    if kc % 2 == 0:
                        P.op("act", "copy", reads=[PB[bk]], writes=[HTB], out=hT[:, kc, 0:n], in_=bank(bk)[:, 0:n])
                    else:
                        P.op("dve", "tensor_copy", reads=[PB[bk]], writes=[HTB], out=hT[:, kc, 0:n], in_=bank(bk)[:, 0:n])

                def proj_fm(bk, c0, m):
                    for kc in range(8):
                        MM(bank(bk)[0:m, 0:n], win[:, kc, c0:c0 + m], hT[:, kc, 0:n], kc == 0, kc == 7,
                           [HTB, WBUF], PB[bk])

                def proj_tm(bk, o, bs, c0, ncols):
                    for kc in range(8):
                        MM(bank(bk)[0:bs, 0:ncols], hT[:, kc, o:o + bs], win[:, kc, c0:c0 + ncols], kc == 0, kc == 7,
                           [HTB, WBUF], PB[bk])

                for i8 in range(8):
                    bk = 4 + (i8 % 2)
                    qb, QB = qbs[i8 % 2], QBs[i8 % 2]
                    t1, t2, T1B, T2B = t1s[i8 % 2], t2s[i8 % 2], T1Bs[i8 % 2], T2Bs[i8 % 2]
                    pbk = 6 if i8 % 2 == 0 else 7
                    sc = 1.0 if i8 < 4 else 128.0 ** -0.5
                    proj_fm(bk, i8 * 128, 128)
                    P.op("act", "activation", reads=[PB[bk]], writes=[QB], out=qb[:, 0:n], in_=bank(bk)[:, 0:n],
                         func=AF.Copy, scale=sc)
                    MM(bank(pbk)[:, 0:n], perm[:, :], qb[:, 0:n], True, True, [QB, WBUF], PB[pbk])
                    P.op("pool", "tensor_tensor", reads=[QB, RT[s]], writes=[T1B], out=t1[:, 0:n], in0=qb[:, 0:n],
                         in1=rc[s][:, 0:n], op=ALU.mult)
                    P.op("dve", "tensor_tensor", reads=[PB[pbk], RT[s]], writes=[T2B], out=t2[:, 0:n], in0=bank(pbk)[:, 0:n],
                         in1=rs[s][:, 0:n], op=ALU.mult)
                    P.op("dve", "tensor_tensor", reads=[T1B, T2B], writes=[RQB[i8]], out=rqT[:, i8, 0:n], in0=t1[:, 0:n],
                         in1=t2[:, 0:n], op=ALU.add)
                    dst = k.S["rqT"][i8, :, t0:t0 + n] if i8 < 4 else k.S["rkT"][i8 - 4, :, t0:t0 + n]
                    P.dma("sp", dst, rqT[:, i8, 0:n], reads=[RQB[i8]], writes=[Buf()])
                for b, (o, bs) in enumerate(blks):
                    bk = b % 2
                    for hh in range(4):
                        MM(bank(bk)[0:bs, hh * 128:(hh + 1) * 128], rqT[:, 4 + hh, o:o + bs], identb[:, :], True, True,
                           [RQB[4 + hh]], PB[bk])
                    wtab = wks[0:bs, :] if kind == "smp" else wk[0:bs, b, :]
                    P.op("dve", "tensor_tensor", reads=[PB[bk], WBUF], writes=[RKKB], out=rkk[0:bs, b, :],
                         in0=bank(bk)[0:bs, :], in1=wtab, op=ALU.mult)
                P.dma("act", tm(k.S["rkk"][t0:t0 + n, :]), rkk[0:pp, 0:nb, :], reads=[RKKB], writes=[Buf()])
                for b, (o, bs) in enumerate(blks):
                    proj_tm(2, o, bs, 1024, 512)
                    P.op("act", "copy", reads=[PB[2]], writes=[RVB], out=rv[0:bs, b, :], in_=bank(2)[0:bs, :])
                    proj_tm(3, o, bs, 1536, 512)
                    P.op("act", "activation", reads=[PB[3]], writes=[RGB], out=rg[0:bs, b, :], in_=bank(3)[0:bs, :],
                         func=AF.Silu)
                    proj_tm(2, o, bs, 2560, 512)
                    P.op("act", "copy", reads=[PB[2]], writes=[FKFB], out=fkf[0:bs, b, :], in_=bank(2)[0:bs, :])
                    proj_tm(3, o, bs, 3072, 512)
                    P.op("dve", "tensor_copy", reads=[PB[3]], writes=[FVFB], out=fvf[0:bs, b, :], in_=bank(3)[0:bs, :])
                    proj_tm(7, o, bs, 3584, 8)
                    P.op("dve", "tensor_tensor", reads=[PB[7], WBUF], writes=[XBB], out=xb[0:bs, b, :],
                         in0=bank(7)[0:bs, 0:8], in1=bfox[0:bs, :], op=ALU.add)
                P.dma("act", tm(k.S["rv"][t0:t0 + n, :]), rv[0:pp, 0:nb, :], reads=[RVB], writes=[Buf()])
                P.dma("act", tm(k.S["rg"][t0:t0 + n, :]), rg[0:pp, 0:nb, :], reads=[RGB], writes=[Buf()])
                P.op("act", "activation", reads=[XBB], writes=[XBB], out=xb[0:pp, 0:nb, :], in_=xb[0:pp, 0:nb, :],
                     func=AF.Exp, scale=-1.0)
                P.op("act", "activation", reads=[XBB], writes=[XBB], out=xb[0:pp, 0:nb, :], in_=xb[0:pp, 0:nb, :],
                     func=AF.Ln, bias=1.0)
                P.op("dve", "tensor_scalar", reads=[XBB], writes=[LFB], out=lf[0:pp, 0:nb, :], in0=xb[0:pp, 0:nb, :],
                     scalar1=-1.0, scalar2=None, op0=ALU.mult)
                ko = k.O["fox_k_sample"] if kind == "smp" else k.O["fox_k_prompt"][t0:t0 + n, :]
                vo = k.O["fox_v_sample"] if kind == "smp" else k.O["fox_v_prompt"][t0:t0 + n, :]
                lo = k.O["fox_logf_sample"] if kind == "smp" else k.O["fox_logf_prompt"][t0:t0 + n, :]
                P.dma("sp", tm(ko), fkf[0:pp, 0:nb, :], reads=[FKFB], writes=[Buf()])
                P.dma("sp", tm(vo), fvf[0:pp, 0:nb, :], reads=[FVFB], writes=[Buf()])
                P.dma("sp", tm(lo), lf[0:pp, 0:nb, :], reads=[LFB], writes=[Buf()])
            P.op("pool", "tensor_copy", reads=[FKFB], writes=[FKBB], out=fkb[0:pp, 0:nb, 0:512], in_=fkf[0:pp, 0:nb, :])
            P.op("pool", "tensor_copy", reads=[FVFB], writes=[FVAB],
                 out=fva[0:pp, 0:nb, :].rearrange("p b (h e) -> p b h e", e=65)[:, :, :, 0:64],
                 in_=fvf[0:pp, 0:nb, :].rearrange("p b (h e) -> p b h e", e=64))
            ktile = (t0 // TT) if kind == "past" else (8 + t0 // TT)
            for hp in range(4):
                P.dma("act", k.S["fva"][ktile, hp, 0:pp, 0:nb, :], fva[0:pp, 0:nb, 130 * hp:130 * hp + 130], reads=[FVAB],
                      writes=[Buf()])
            for hh in range(8):
                bk = hh % 2
                for b, (o, bs) in enumerate(blks):
                    MM(bank(bk)[0:64, o:o + bs], fkb[0:bs, b, hh * 64:(hh + 1) * 64], identb[0:bs, 0:bs], True, True,
                       [FKBB], PB[bk])
                if hh % 2 == 0:
                    P.op("act", "copy", reads=[PB[bk]], writes=[FKAB], out=fka[0:64, hh, 0:n], in_=bank(bk)[0:64, 0:n])
                else:
                    P.op("dve", "tensor_copy", reads=[PB[bk]], writes=[FKAB], out=fka[0:64, hh, 0:n],
                         in_=bank(bk)[0:64, 0:n])
            for hp in range(4):
                P.dma("sp", k.S["fka"][ktile, hp, :, :, 0:n], fka[:, 2 * hp:2 * hp + 2, 0:n], reads=[FKAB], writes=[Buf()])
            for b, (o, bs) in enumerate(blks):
                mm = [(trif[0:bs, 0:bs], lf[0:bs, b, :])]
                for b2 in range(b):
                    mm.append((onesf[0:128, 0:bs], lf[0:128, b2, :]))
                mm.append((onesf[0:1, 0:bs], carry[0:1, :]))
                for i, (l_, r_) in enumerate(mm):
                    MM(bank(7)[0:bs, 8 + b * 8:16 + b * 8], l_, r_, i == 0, i == len(mm) - 1, [LFB, CARB, WBUF], PB[7])
            cview = bank(7)[0:pp, 8:8 + nb * 8].rearrange("p (b h) -> p b h", h=8)
            P.op("dve", "tensor_copy", reads=[PB[7]], writes=[CSB], out=cS[0:pp, 0:nb, :], in_=cview)
            P.op("act", "activation", reads=[PB[7]], writes=[NGB], out=ngc[0:pp, 0:nb, :], in_=cview, func=AF.Copy,
                 scale=-1.0)
            P.dma("act", k.S["negc"][ktile, 0:pp, 0:nb, :], ngc[0:pp, 0:nb, :], reads=[NGB], writes=[Buf()])
            for b, (o, bs) in enumerate(blks):
                MM(bank(7)[0:1, 48:56], onesf[0:bs, 0:1], lf[0:bs, b, :], b == 0, b == nb - 1, [LFB], PB[7])
            P.op("dve", "tensor_tensor", reads=[PB[7], CARB], writes=[CARB], out=carry[:], in0=carry[:],
                 in1=bank(7)[0:1, 48:56], op=ALU.add)
            if kind == "past":
                continue
            for b, (o, bs) in enumerate(blks):
                MM(bank(6)[0:8, o:o + bs], cS[0:bs, b, :], identf[0:bs, 0:bs], True, True, [CSB], PB[6])
            P.op("dve", "tensor_copy", reads=[PB[6]], writes=[CTB], out=cT[:, 0:n], in_=bank(6)[0:8, 0:n])
            P.op("dve", "tensor_copy", reads=[CTB], writes=[C3B], out=c3[:, 0, 0:n], in_=cT[:, 0:n])
            P.op("dve", "tensor_tensor", reads=[CTB, C3B], writes=[R1B], out=r1[:, 0:n], in0=cT[:, 0:n],
                 in1=c3[:, 0, 0:n], op=ALU.subtract)
            P.op("dve", "tensor_copy", reads=[R1B], writes=[C3B], out=c3[:, 1, 0:n], in_=r1[:, 0:n])
            P.op("dve", "tensor_tensor", reads=[R1B, C3B], writes=[R2B], out=r2[:, 0:n], in0=r1[:, 0:n],
                 in1=c3[:, 1, 0:n], op=ALU.subtract)
            P.op("dve", "tensor_copy", reads=[R2B], writes=[C3B], out=c3[:, 2, 0:n], in_=r2[:, 0:n])
            for hh in range(8):
                bk = 4 + (hh % 2)
                proj_fm(bk, 2048 + hh * 64, 64)
                for p_ in range(3):
                    i0 = (hh * 3 + p_) * 3
                    MM(bank(bk)[64:67, 0:n], selc[0:8, i0:i0 + 3], c3[0:8, p_, 0:n], p_ == 0, p_ == 2, [C3B, WBUF], PB[bk])
                P.op("act", "activation", reads=[PB[bk], WBUF], writes=[FQAB], out=fqa[0:67, hh, 0:n],
                     in_=bank(bk)[0:67, 0:n], func=AF.Identity, scale=qscale[0:67, 0:1])
            for hp in range(4):
                P.dma("sp", k.S["fqa"][t0 // TT, hp, :, :, 0:n], fqa[:, 2 * hp:2 * hp + 2, 0:n], reads=[FQAB], writes=[Buf()])
            P.maybe_flush()
        P.flush()


ALL_PHASES = ("W", "A1", "A3", "A2", "C0", "F0", "S", "C1", "F1")


def core_inputs(inp, c):
    f = np.ascontiguousarray
    m = {
        "x_prompt": inp["x_prompt"][c], "x_sample": inp["x_sample"][c],
        "cache_ret_state": inp["cache_ret_state"][0, c],
        "cache_fox_k": inp["cache_fox_k"][0, c].reshape(PAST, 512),
        "cache_fox_v": inp["cache_fox_v"][0, c].reshape(PAST, 512),
        "cache_fox_logf": inp["cache_fox_logf"][0, c],
        "state_s5_re": inp["state_s5_re"][0, c], "state_s5_im": inp["state_s5_im"][0, c],
        "cache_mem_k": inp["cache_mem_k"][:, c].reshape(2, 256, 1024),
        "cache_mem_v": inp["cache_mem_v"][:, c].reshape(2, 256, 1024),
        "mem_prompt": inp["mem_prompt"][c],
        "norm_w": inp["norm_w"], "w_in_ab": inp["w_in_ab"][0], "b_fox_f": inp["b_fox_f"].reshape(1, 8),
        "ret_gn_w": inp["ret_gn_w"].reshape(1, 512), "w_out_ab": inp["w_out_ab"][0], "w_in_c": inp["w_in_c"][0],
        "s5_lambda_re": inp["s5_lambda_re"][0], "s5_lambda_im": inp["s5_lambda_im"][0],
        "s5_log_dt": inp["s5_log_dt"].reshape(1, 64), "s5_b_re": inp["s5_b_re"][0], "s5_b_im": inp["s5_b_im"][0],
        "s5_c_re": inp["s5_c_re"][0], "s5_c_im": inp["s5_c_im"][0], "s5_d": inp["s5_d"].reshape(1, D),
        "w_glu": inp["w_glu"][0], "mem_norm_w": inp["mem_norm_w"],
        "w_xq": inp["w_xq"], "w_xk": inp["w_xk"], "w_xv": inp["w_xv"], "w_xo": inp["w_xo"],
        "w_up": inp["w_up"], "w_down": inp["w_down"],
    }
    m = {k_: f(np.asarray(v, dtype=np.float32)) for k_, v in m.items()}
    for k_, v in get_consts().items():
        m["c_" + k_] = v
    return m


def run(inputs, phases=ALL_PHASES, cores=NCORES):
    inp = {k_: np.asarray(v) for k_, v in inputs.items()}
    kk = build(phases)
    in_maps = [core_inputs(inp, c) for c in range(cores)]
    res = run_bass_kernel_spmd(kk.nc, in_maps, core_ids=list(range(cores)))
    return res.results


def kernel(**inputs):
    r = run(inputs)
    B = NCORES

    def st(name, shape):
        return np.stack([np.asarray(r[c][name], dtype=np.float32).reshape(shape) for c in range(B)])
    y_prompt = st("y_prompt", (T, D))
    y_sample = st("y_sample", (TS, D))
    ret_p = st("ret_state_prompt", (4, 128, 128))[None]
    ret_s = st("ret_state_sample", (4, 128, 128))[None]
    fkp = st("fox_k_prompt", (T, 8, 64))[None]
    fvp = st("fox_v_prompt", (T, 8, 64))[None]
    flp = st("fox_logf_prompt", (T, 8))[None]
    fks = st("fox_k_sample", (TS, 8, 64))[None]
    fvs = st("fox_v_sample", (TS, 8, 64))[None]
    fls = st("fox_logf_sample", (TS, 8))[None]
    s5rp = st("s5_re_prompt", (64, 64))[None]
    s5ip = st("s5_im_prompt", (64, 64))[None]
    s5rs = st("s5_re_sample", (64, 64))[None]
    s5is = st("s5_im_sample", (64, 64))[None]
    mkp = np.stack([np.asarray(r[c]["mem_k_prompt"], dtype=np.float32).reshape(2, 256, 4, 256) for c in range(B)], 1)
    mvp = np.stack([np.asarray(r[c]["mem_v_prompt"], dtype=np.float32).reshape(2, 256, 4, 256) for c in range(B)], 1)
    return (y_prompt, y_sample, ret_p, ret_s, fkp, fvp, flp, fks, fvs, fls, s5rp, s5ip, s5rs, s5is, mkp, mvp)


def phase_A3(k, P, ps, PB, bank, WB, identf, identb, onesf, onesb):
    with contextlib.ExitStack() as st:
        def sb(name, shape, dt):
            return k.sbuf(st, "a3_" + name, shape, dt)
        caus = sb("caus", [128, 128], BF16)
        CW = Buf("a3c")
        P.dma("sp", caus[:], k.C["caus_b"], writes=[CW])
        qas = [sb("qa%d" % i, [67, 2, 2 * TT], BF16) for i in range(2)]
        QABs = [Buf("qa%d" % i) for i in range(2)]
        NKV = 4
        ka = [sb("ka%d" % i, [67, 2, TT], BF16) for i in range(2 * NKV)]
        va = [sb("va%d" % i, [128, 4, 2 * 65], BF16) for i in range(2 * NKV)]
        ng = [sb("ng%d" % i, [128, 4, 8], F32) for i in range(2 * NKV)]
        KVB = [Buf("kk%d" % i) for i in range(2 * NKV)]
        VVB = [Buf("kvv%d" % i) for i in range(2 * NKV)]
        NNB = [Buf("kn%d" % i) for i in range(2 * NKV)]
        pT = [sb("pT%d" % i, [128, TT], BF16) for i in range(4)]
        PTB = [Buf("pT%d" % i) for i in range(4)]
        osbs = [sb("osb%d" % i, [65, TT], F32) for i in range(4)]
        OSBs = [Buf("osb%d" % i) for i in range(4)]
        rrows = [sb("rrow%d" % i, [65, TT], BF16) for i in range(4)]
        RRBs = [Buf("rrow%d" % i) for i in range(4)]
        foT = sb("foT", [64, 4, TT], BF16)
        FOB = Buf("foT")

        def MM(out, lhsT, rhs, start, stop, R, Wb):
            P.op("pe", "matmul", reads=R, writes=[Wb], out=out, lhsT=lhsT, rhs=rhs, start=start, stop=stop)

        supers = []
        for Q in range(T // (2 * TT)):
            subs = [(2 * Q + s_, (2 * Q + s_) * TT, TT) for s_ in range(2)]
            ktl = [(8 + kt, TT, kt) for kt in range(2 * Q + 2)]
            supers.append(("prm", subs, ktl))
        supers.append(("smp", [(16, T, TS)], [(pt, TT, -1) for pt in range(PAST // TT)] + [(24, TS, 16)]))
        if "a3_q" in DBG:
            supers = [supers[i] for i in DBG["a3_q"]]
        groups = [(kind, subs, ktl, hp) for (kind, subs, ktl) in supers for hp in range(4)]
        pti = 0

        def load_kv(gi, kti):
            kind, subs, ktl, hp = groups[gi]
            ktile, nk, kpos = ktl[kti]
            sl = (gi % 2) * NKV + kti % NKV
            pk = min(128, nk)
            nkb = (nk + 127) // 128
            P.dma("sp", ka[sl][:, :, 0:nk], k.S["fka"][ktile, hp, :, :, 0:nk], writes=[KVB[sl]])
            P.dma("pool", va[sl][0:pk, 0:nkb, :], k.S["fva"][ktile, hp, 0:pk, 0:nkb, :], writes=[VVB[sl]])
            P.dma("pool", ng[sl][0:pk, 0:nkb, :], k.S["negc"][ktile, 0:pk, 0:nkb, :], writes=[NNB[sl]])

        def load_q(gi):
            kind, subs, ktl, hp = groups[gi]
            for s_, (qidx, t0, n) in enumerate(subs):
                P.dma("sp", qas[gi % 2][:, :, s_ * TT:s_ * TT + n], k.S["fqa"][qidx, hp, :, :, 0:n], writes=[QABs[gi % 2]])

        def prefetch(gi):
            if gi < len(groups):
                load_q(gi)
                for kti in range(min(NKV, len(groups[gi][2]))):
                    load_kv(gi, kti)
        prefetch(0)
        for gi, (kind, subs, ktl, hp) in enumerate(groups):
            qa, QAB = qas[gi % 2], QABs[gi % 2]
            prefetch(gi + 1)
            steps = []
            for kti, (ktile, nk, kpos) in enumerate(ktl):
                kblks = blocks(nk)
                for hh in range(2):
                    for jb, (jo, js) in enumerate(kblks):
                        for s_, (qidx, t0, n) in enumerate(subs):
                            if kpos > qidx:
                                continue
                            steps.append((kti, hh, jb, jo, js, kpos == qidx, s_, n))
            firsts, lasts = {}, {}
            for i, stp in enumerate(steps):
                a_ = stp[1] * 2 + stp[6]
                firsts.setdefault(a_, i)
                lasts[a_] = i
            LOOK = 2
            info = {}
            for i in range(len(steps) + LOOK):
                if i < len(steps):
                    kti, hh, jb, jo, js, diag, s_, n = steps[i]
                    sl = (gi % 2) * NKV + kti % NKV
                    h8 = 2 * hp + hh
                    c0 = jo if diag else 0
                    sbk = 4 + (pti % 4)
                    pb = pti % 4
                    pti += 1
                    info[i] = (sl, pb, c0)
                    MM(bank(sbk)[0:js, c0:n], ka[sl][0:67, hh, jo:jo + js], qa[0:67, hh, s_ * TT + c0:s_ * TT + n], True, True,
                       [KVB[sl], QAB], PB[sbk])
                    P.op("act", "activation", reads=[PB[sbk], NNB[sl]], writes=[PTB[pb]], out=pT[pb][0:js, c0:n],
                         in_=bank(sbk)[0:js, c0:n], func=AF.Exp, bias=ng[sl][0:js, jb, h8:h8 + 1], scale=1.0)
                    if diag:
                        P.op("pool", "tensor_tensor", reads=[PTB[pb], CW], writes=[PTB[pb]],
                             out=pT[pb][0:js, c0:c0 + js], in0=pT[pb][0:js, c0:c0 + js], in1=caus[0:js, 0:js], op=ALU.mult)
                j = i - LOOK
                if j >= 0:
                    kti, hh, jb, jo, js, diag, s_, n = steps[j]
                    sl, pb, c0 = info[j]
                    a_ = hh * 2 + s_
                    MM(bank(a_)[0:65, c0:n], va[sl][0:js, jb, hh * 65:(hh + 1) * 65], pT[pb][0:js, c0:n], firsts[a_] == j,
                       lasts[a_] == j, [VVB[sl], PTB[pb]], PB[a_])
                    if (j + 1 == len(steps) or steps[j + 1][0] != kti) and kti + NKV < len(ktl):
                        load_kv(gi, kti + NKV)
            accs = [(hh, s_, subs[s_][1], subs[s_][2]) for hh in range(2) for s_ in range(len(subs))]
            for (hh, s_, t0, n) in accs:
                a_ = hh * 2 + s_
                P.op("act", "copy", reads=[PB[a_]], writes=[OSBs[a_]], out=osbs[a_][0:65, 0:n], in_=bank(a_)[0:65, 0:n])
            for (hh, s_, t0, n) in accs:
                a_ = hh * 2 + s_
                P.op("dve", "reciprocal", reads=[OSBs[a_]], writes=[RRBs[a_]], out=rrows[a_][64:65, 0:n],
                     in_=osbs[a_][64:65, 0:n])
            for (hh, s_, t0, n) in accs:
                a_ = hh * 2 + s_
                MM(bank(4 + a_)[0:64, 0:n], onesb[64:65, 0:64], rrows[a_][64:65, 0:n], True, True, [RRBs[a_]], PB[4 + a_])
            for (hh, s_, t0, n) in accs:
                a_ = hh * 2 + s_
                P.op("dve", "tensor_tensor", reads=[PB[4 + a_], OSBs[a_]], writes=[FOB], out=foT[0:64, a_, 0:n],
                     in0=osbs[a_][0:64, 0:n], in1=bank(4 + a_)[0:64, 0:n], op=ALU.mult)
            for (hh, s_, t0, n) in accs:
                a_ = hh * 2 + s_
                P.dma("sp", k.S["foT"][2 * hp + hh, :, t0:t0 + n], foT[:, a_, 0:n], reads=[FOB], writes=[Buf()])
            P.maybe_flush()
        P.flush()


def rstd_ops(P, out, in_, scale, epsc, R, W):
    P.op("act", "activation", reads=R, writes=W, out=out, in_=in_, func=AF.Sqrt, scale=scale, bias=epsc)
    P.op("dve", "reciprocal", reads=W, writes=W, out=out, in_=out)


def post_norm_residual(P, pso, PSB, xt_blk, XTB, g_bc, GB, junk, JB, ssv, SSB, epsc, tmp, TMPB, bs):
    P.op("act", "activation", reads=PSB, writes=[JB, SSB], out=junk[0:bs, :], in_=pso, func=AF.Square,
         accum_out=ssv[0:bs, 0:1])
    rstd_ops(P, ssv[0:bs, 1:2], ssv[0:bs, 0:1], 1.0 / D, epsc[0:bs, 0:1], [SSB], [SSB])
    P.op("dve", "scalar_tensor_tensor", reads=PSB + [SSB, GB], writes=[TMPB], out=tmp[0:bs, :], in0=pso,
         scalar=ssv[0:bs, 1:2], in1=g_bc[0:bs, :], op0=ALU.mult, op1=ALU.mult)
    P.op("pool", "tensor_tensor", reads=[TMPB, XTB], writes=[XTB], out=xt_blk, in0=xt_blk, in1=tmp[0:bs, :], op=ALU.add)


def pre_norm_elem(P, xt, XTB, g_bc, GB, junk, JB, ssv, SSB, epsc, h, HB, blks):
    for b, (o, bs) in enumerate(blks):
        P.op("act", "activation", reads=[XTB], writes=[JB, SSB], out=junk[0:bs, :], in_=xt[0:bs, b, :], func=AF.Square,
             accum_out=ssv[0:bs, 2 * b:2 * b + 1])
        rstd_ops(P, ssv[0:bs, 2 * b + 1:2 * b + 2], ssv[0:bs, 2 * b:2 * b + 1], 1.0 / D, epsc[0:bs, 0:1], [SSB], [SSB])
        P.op("dve", "scalar_tensor_tensor", reads=[XTB, SSB, GB], writes=[HB], out=h[0:bs, b, :], in0=xt[0:bs, b, :],
             scalar=ssv[0:bs, 2 * b + 1:2 * b + 2], in1=g_bc[0:bs, :], op0=ALU.mult, op1=ALU.mult)


def pre_norm_T(P, bank, PB, banks, h, HB, hT, HTB, identb, blks, n):
    for kc in range(8):
        bk = banks[kc % len(banks)]
        for b, (o, bs) in enumerate(blks):
            P.op("pe", "matmul", reads=[HB], writes=[PB[bk]], out=bank(bk)[:, o:o + bs],
                 lhsT=h[0:bs, b, kc * 128:(kc + 1) * 128], rhs=identb[0:bs, 0:bs], start=True, stop=True)
        if kc % 2 == 0:
            P.op("act", "copy", reads=[PB[bk]], writes=[HTB], out=hT[:, kc, 0:n], in_=bank(bk)[:, 0:n])
        else:
            P.op("dve", "tensor_copy", reads=[PB[bk]], writes=[HTB], out=hT[:, kc, 0:n], in_=bank(bk)[:, 0:n])


def pre_norm_hT(P, bank, PB, banks, xt, XTB, g_bc, GB, junk, JB, ssv, SSB, epsc, h, HB, hT, HTB, identb, blks, n):
    pre_norm_elem(P, xt, XTB, g_bc, GB, junk, JB, ssv, SSB, epsc, h, HB, blks)
    pre_norm_T(P, bank, PB, banks, h, HB, hT, HTB, identb, blks, n)


def x_src(k, first_layer_input, kind, t0, n):
    if first_layer_input:
        return k.I["x_sample"] if kind == "smp" else k.I["x_prompt"][t0:t0 + n, :]
    return k.S["x"][t0:t0 + n, :]


def phase_A2(k, P, ps, PB, bank, WB, identf, identb, onesf, onesb):
    lg = np.log(1.0 - 2.0 ** (-5.0 - np.arange(4, dtype=np.float64)))
    with contextlib.ExitStack() as st:
        def sb(name, shape, dt):
            return k.sbuf(st, "a2_" + name, shape, dt)
        rmask = sb("rmask", [128, 4, 512], F32)
        wq = sb("wq", [128, 4, 512], F32)
        gnw = sb("gnw", [128, 512], F32)
        g1 = sb("g1", [128, D], F32)
        wor = sb("wor", [128, 4, D], BF16)
        wof = sb("wof", [64, 8, D], BF16)
        epsc = sb("epsc", [128, 1], F32)
        CW = Buf("a2c")
        P.dma("sp", rmask[:], k.C["rmask"], writes=[CW])
        P.dma("sp", wq[:], k.C["wq"].rearrange("p b h e -> p b (h e)"), writes=[CW])
        P.dma("sp", gnw[:], k.I["ret_gn_w"][0:1, :].broadcast_to([128, 512]), writes=[CW])
        P.dma("sp", g1[:], k.I["norm_w"][0, 1:2, :].broadcast_to([128, D]), writes=[CW])
        P.dma("act", wor[:], WB["w_out_ab"][0:512, :].rearrange("(kc p) n -> p kc n", p=128), writes=[CW])
        P.dma("act", wof[:], WB["w_out_ab"][512:1024, :].rearrange("(h p) n -> p h n", p=64), writes=[CW])
        P.op("pool", "memset", writes=[CW], ap=epsc[:], constant=EPS)
        Sf = sb("Sf", [128, 4, 128], F32)
        Sb = sb("Sb", [128, 4, 128], BF16)
        SFB, SBB = Buf("Sf"), Buf("Sb")
        qTs = [sb("qT%d" % i, [128, 4, TT], BF16) for i in range(2)]
        kTs = [sb("kT%d" % i, [128, 4, TT], BF16) for i in range(2)]
        rkks = [sb("rkk%d" % i, [128, 4, 512], BF16) for i in range(2)]
        rvs = [sb("rv%d" % i, [128, 4, 512], BF16) for i in range(2)]
        rgs = [sb("rg%d" % i, [128, 4, 512], BF16) for i in range(2)]
        foTs = [sb("foT%d" % i, [64, 8, TT], BF16) for i in range(2)]
        xts = [sb("xt%d" % i, [128, 4, D], F32) for i in range(2)]
        INBs = [Buf("a2in%d" % i) for i in range(2)]
        XTBs = [Buf("a2x%d" % i) for i in range(2)]
        PT = sb("PT", [128, 4, 4, TT], BF16)
        PTB = [Buf("PT%d" % i) for i in range(4)]
        ro = sb("ro", [128, 512], F32)
        t1 = sb("t1", [128, 512], F32)
        sq = sb("sq", [128, 512], F32)
        ROB, T1B, SQB = Buf("ro"), Buf("t1"), Buf("sq")
        st8 = sb("st8", [128, 24], F32)
        STB = Buf("st8")
        mixr = sb("mixr", [128, 512], BF16)
        MXB = Buf("mixr")
        roT = sb("roT", [128, 4, TT], BF16)
        ROTB = Buf("roT")
        junk = sb("junk", [128, D], BF16)
        JB = Buf("junk")
        ssv = sb("ssv", [128, 8], F32)
        SSB = Buf("ssv")
        tmp = sb("tmp", [128, D], F32)
        TMPB = Buf("tmp")

        def MM(out, lhsT, rhs, start, stop, R, Wb):
            P.op("pe", "matmul", reads=R, writes=[Wb], out=out, lhsT=lhsT, rhs=rhs, start=start, stop=stop)

        seqs = [("prm", t * TT, TT) for t in range(T // TT)] + [("smp", T, TS)]
        if "a2_seqs" in DBG:
            seqs = [seqs[i] for i in DBG["a2_seqs"]]
        P.op("pool", "memset", writes=[SFB], ap=Sf[:], constant=0.0)
        P.op("pool", "memset", writes=[SBB], ap=Sb[:], constant=0.0)
        def load_in(ti):
            kind, t0, n = seqs[ti]
            pp = min(128, n)
            nb = len(blocks(n))
            j = ti % 2

            def tm_(ap2d):
                return ap2d.rearrange("(b p) d -> p b d", p=pp)
            P.dma("sp", qTs[j][:, :, 0:n], k.S["rqT"][:, :, t0:t0 + n].rearrange("h d t -> d h t"), writes=[INBs[j]])
            P.dma("act", kTs[j][:, :, 0:n], k.S["rkT"][:, :, t0:t0 + n].rearrange("h d t -> d h t"), writes=[INBs[j]])
            P.dma("sp", rkks[j][0:pp, 0:nb, :], tm_(k.S["rkk"][t0:t0 + n, :]), writes=[INBs[j]])
            P.dma("act", rvs[j][0:pp, 0:nb, :], tm_(k.S["rv"][t0:t0 + n, :]), writes=[INBs[j]])
            P.dma("sp", rgs[j][0:pp, 0:nb, :], tm_(k.S["rg"][t0:t0 + n, :]), writes=[INBs[j]])
            P.dma("act", foTs[j][:, :, 0:n], k.S["foT"][:, :, t0:t0 + n].rearrange("h r t -> r h t"), writes=[INBs[j]])
            P.dma("sp", xts[j][0:pp, 0:nb, :], tm_(x_src(k, True, kind, t0, n)), writes=[XTBs[j]])
        load_in(0)
        for ti, (kind, t0, n) in enumerate(seqs):
            blks = blocks(n)
            nb = len(blks)
            pp = min(128, n)
            dec = np.exp(lg * (512.0 if kind == "prm" else 32.0))
            j_ = ti % 2
            qT, kT, rkk, rv, rg, foT, xt = qTs[j_], kTs[j_], rkks[j_], rvs[j_], rgs[j_], foTs[j_], xts[j_]
            INB, XTB = INBs[j_], XTBs[j_]
            if ti + 1 < len(seqs):
                load_in(ti + 1)
            if kind == "smp":
                P.dma("sp", Sf[:], k.I["cache_ret_state"].rearrange("h d e -> d h e"), writes=[SFB])
                P.op("act", "copy", reads=[SFB], writes=[SBB], out=Sb[:], in_=Sf[:])

            def tm(ap2d):
                return ap2d.rearrange("(b p) d -> p b d", p=pp)
            for hh in range(4):
                for jb, (jo, js) in enumerate(blks):
                    bk = (hh * 4 + jb) % 2
                    MM(bank(bk)[0:js, jo:n], kT[:, hh, jo:jo + js], qT[:, hh, jo:n], True, True, [INB], PB[bk])
                    P.op("dve", "tensor_tensor", reads=[PB[bk], CW], writes=[PTB[hh]], out=PT[0:js, hh, jb, jo:n],
                         in0=bank(bk)[0:js, jo:n], in1=rmask[0:js, hh, 0:n - jo], op=ALU.mult)
            for ib, (io, is_) in enumerate(blks):
                for hh in range(4):
                    for jb in range(ib + 1):
                        jo, js = blks[jb]
                        MM(bank(2)[0:is_, hh * 128:(hh + 1) * 128], PT[0:js, hh, jb, io:io + is_],
                           rv[0:js, jb, hh * 128:(hh + 1) * 128], jb == 0, jb == ib, [PTB[hh], INB], PB[2])
                    MM(bank(3)[0:is_, hh * 128:(hh + 1) * 128], qT[:, hh, io:io + is_], Sb[:, hh, :], True, True,
                       [INB, SBB], PB[3])
                P.op("dve", "tensor_tensor", reads=[PB[3], CW], writes=[T1B], out=t1[0:is_, :], in0=bank(3)[0:is_, :],
                     in1=wq[0:is_, ib, :], op=ALU.mult)
                P.op("dve", "tensor_tensor", reads=[PB[2], T1B], writes=[ROB], out=ro[0:is_, :], in0=bank(2)[0:is_, :],
                     in1=t1[0:is_, :], op=ALU.add)
                P.op("act", "activation", reads=[ROB], writes=[SQB], out=sq[0:is_, :], in_=ro[0:is_, :], func=AF.Square)
                ro3 = ro[0:is_, :].rearrange("p (h e) -> p h e", e=128)
                sq3 = sq[0:is_, :].rearrange("p (h e) -> p h e", e=128)
                P.op("dve", "tensor_reduce", reads=[ROB], writes=[STB], out=st8[0:is_, 0:4], in_=ro3, axis=AX.X, op=ALU.add)
                P.op("dve", "tensor_reduce", reads=[SQB], writes=[STB], out=st8[0:is_, 4:8], in_=sq3, axis=AX.X, op=ALU.add)
                P.op("dve", "tensor_scalar", reads=[STB], writes=[STB], out=st8[0:is_, 8:12], in0=st8[0:is_, 0:4],
                     scalar1=1.0 / 128, scalar2=None, op0=ALU.mult)
                P.op("dve", "tensor_tensor", reads=[STB], writes=[STB], out=st8[0:is_, 12:16], in0=st8[0:is_, 8:12],
                     in1=st8[0:is_, 8:12], op=ALU.mult)
                P.op("dve", "scalar_tensor_tensor", reads=[STB], writes=[STB], out=st8[0:is_, 16:20], in0=st8[0:is_, 4:8],
                     scalar=1.0 / 128, in1=st8[0:is_, 12:16], op0=ALU.mult, op1=ALU.subtract)
                rstd_ops(P, st8[0:is_, 20:24], st8[0:is_, 16:20], 1.0, epsc[0:is_, 0:1], [STB], [STB])
                t13 = t1[0:is_, :].rearrange("p (h e) -> p h e", e=128)
                P.op("dve", "tensor_tensor", reads=[ROB, STB], writes=[T1B], out=t13, in0=ro3,
                     in1=st8[0:is_, 8:12].unsqueeze(2).broadcast_to([is_, 4, 128]), op=ALU.subtract)
                P.op("dve", "tensor_tensor", reads=[T1B, STB], writes=[T1B], out=t13, in0=t13,
                     in1=st8[0:is_, 20:24].unsqueeze(2).broadcast_to([is_, 4, 128]), op=ALU.mult)
                P.op("pool", "tensor_tensor", reads=[T1B, CW], writes=[T1B], out=t1[0:is_, :], in0=t1[0:is_, :],
                     in1=gnw[0:is_, :], op=ALU.mult)
                P.op("pool", "tensor_tensor", reads=[T1B, INB], writes=[MXB], out=mixr[0:is_, :], in0=t1[0:is_, :],
                     in1=rg[0:is_, ib, :], op=ALU.mult)
                for hh in range(4):
                    MM(bank(4 + hh)[:, io:io + is_], mixr[0:is_, hh * 128:(hh + 1) * 128], identb[0:is_, 0:is_], True, True,
                       [MXB], PB[4 + hh])
            for hh in range(4):
                if hh % 2 == 0:
                    P.op("act", "copy", reads=[PB[4 + hh]], writes=[ROTB], out=roT[:, hh, 0:n], in_=bank(4 + hh)[:, 0:n])
                else:
                    P.op("dve", "tensor_copy", reads=[PB[4 + hh]], writes=[ROTB], out=roT[:, hh, 0:n],
                         in_=bank(4 + hh)[:, 0:n])
            for hh in range(4):
                for jb, (jo, js) in enumerate(blks):
                    MM(bank(2)[:, hh * 128:(hh + 1) * 128], rkk[0:js, jb, hh * 128:(hh + 1) * 128],
                       rv[0:js, jb, hh * 128:(hh + 1) * 128], jb == 0, jb == nb - 1, [INB], PB[2])
            for hh in range(4):
                P.op("dve", "scalar_tensor_tensor", reads=[PB[2], SFB], writes=[SFB], out=Sf[:, hh, :], in0=Sf[:, hh, :],
                     scalar=float(dec[hh]), in1=bank(2)[:, hh * 128:(hh + 1) * 128], op0=ALU.mult, op1=ALU.add)
            P.op("act", "copy", reads=[SFB], writes=[SBB], out=Sb[:], in_=Sf[:])
            last_of_kind = (kind == "smp") or (t0 + n == T)
            if last_of_kind:
                dst = k.O["ret_state_sample"] if kind == "smp" else k.O["ret_state_prompt"]
                P.dma("sp", dst.rearrange("h d e -> d h e"), Sf[:], reads=[SFB], writes=[Buf()])
            for ib, (io, is_) in enumerate(blks):
                for half in range(2):
                    bk = 2 * (ib % 2) + half
                    for i in range(12):
                        if i < 4:
                            l_, r_ = roT[:, i, io:io + is_], wor[:, i, half * 512:(half + 1) * 512]
                        else:
                            l_, r_ = foT[0:64, i - 4, io:io + is_], wof[0:64, i - 4, half * 512:(half + 1) * 512]
                        MM(bank(bk)[0:is_, :], l_, r_, i == 0, i == 11, [ROTB, INB, CW], PB[bk])
                b0 = 2 * (ib % 2)
                pso = ps[0:is_, b0 * 512:b0 * 512 + 1024]
                post_norm_residual(P, pso, [PB[b0], PB[b0 + 1]], xt[0:is_, ib, :], XTB, g1, CW, junk, JB, ssv, SSB, epsc,
                                   tmp, TMPB, is_)
            P.dma("sp", tm(k.S["x"][t0:t0 + n, :]), xt[0:pp, 0:nb, :], reads=[XTB], writes=[Buf()])
            P.maybe_flush()
        P.flush()


def phase_C(k, P, ps, PB, bank, WB, identf, identb, onesf, onesb, l):
    with contextlib.ExitStack() as st:
        def sb(name, shape, dt):
            return k.sbuf(st, "c%d_" % l + name, shape, dt)
        wxq = sb("wxq", [128, 8, D], BF16)
        wxo = sb("wxo", [128, 8, D], BF16)
        wkv = sb("wkv", [128, 8, D], BF16)
        g2 = sb("g2", [128, D], F32)
        g3 = sb("g3", [128, D], F32)
        gm = sb("gm", [128, D], F32)
        epsc = sb("epsc", [128, 1], F32)
        CW = Buf("cw")
        WKVB = Buf("wkv")
        P.dma("sp", wxq[:], WB["w_xq%d" % l].rearrange("(kc p) n -> p kc n", p=128), writes=[CW])
        P.dma("act", wxo[:], WB["w_xo%d" % l].rearrange("(kc p) n -> p kc n", p=128), writes=[CW])
        P.dma("sp", g2[:], k.I["norm_w"][l, 2:3, :].broadcast_to([128, D]), writes=[CW])
        P.dma("sp", g3[:], k.I["norm_w"][l, 3:4, :].broadcast_to([128, D]), writes=[CW])
        P.dma("sp", gm[:], k.I["mem_norm_w"][l:l + 1, :].broadcast_to([128, D]), writes=[CW])
        P.op("pool", "memset", writes=[CW], ap=epsc[:], constant=EPS)
        xts = [sb("xt%d" % i, [128, 4, D], F32) for i in range(2)]
        XTBs = [Buf("xt%d" % i) for i in range(2)]
        xt, XTB = xts[0], XTBs[0]
        junk2 = sb("junk2", [128, D], BF16)
        JB2 = Buf("junk2")
        ssv2 = sb("ssv2", [128, 8], F32)
        SSB2 = Buf("ssv2")
        h = sb("h", [128, 4, D], BF16)
        HB = Buf("h")
        hT = sb("hT", [128, 8, TT], BF16)
        HTB = Buf("hT")
        junk = sb("junk", [128, D], BF16)
        JB = Buf("junk")
        ssv = sb("ssv", [128, 8], F32)
        SSB = Buf("ssv")
        tmp = sb("tmp", [128, D], F32)
        TMPB = Buf("tmp")
        mkf = sb("mkf", [128, 2, D], F32)
        MKFB = Buf("mkf")
        mkb = sb("mkb", [128, 2, D], BF16)
        MKBB = Buf("mkb")
        mkT = sb("mkT", [128, 8, 256], BF16)
        MKTB = Buf("mkT")
        mvb = sb("mvb", [128, 2, D], BF16)
        MVBB = Buf("mvb")
        qT = sb("qT", [128, 8, TT], BF16)
        QTB = Buf("qT")
        oT = sb("oT", [128, 8, TT], BF16)
        OTB = Buf("oT")
        pT = [sb("pT%d" % i, [128, TT], BF16) for i in range(2)]
        PTB = [Buf("pT%d" % i) for i in range(2)]
        rec = sb("rec", [128, TT], F32)
        RECB = Buf("rec")

        def MM(out, lhsT, rhs, start, stop, R, Wb):
            P.op("pe", "matmul", reads=R, writes=[Wb], out=out, lhsT=lhsT, rhs=rhs, start=start, stop=stop)

        mblks = [(0, 128), (128, 128)]

        def kv_finish():
            P.op("pool", "tensor_copy", reads=[MKFB], writes=[MKBB], out=mkb[:], in_=mkf[:])
            for c8 in range(8):
                bk = c8 % 2
                for mb in range(2):
                    MM(bank(bk)[:, mb * 128:(mb + 1) * 128], mkb[:, mb, c8 * 128:(c8 + 1) * 128], identb[:, :], True, True,
                       [MKBB], PB[bk])
                if c8 % 2 == 0:
                    P.op("act", "copy", reads=[PB[bk]], writes=[MKTB], out=mkT[:, c8, :], in_=bank(bk)[:, 0:256])
                else:
                    P.op("dve", "tensor_copy", reads=[PB[bk]], writes=[MKTB], out=mkT[:, c8, :], in_=bank(bk)[:, 0:256])

        def setup_prompt_mem():
            P.dma("sp", xt[:, 0:2, :], k.I["mem_prompt"].rearrange("(b p) d -> p b d", p=128), writes=[XTB])
            pre_norm_hT(P, bank, PB, [0, 1, 2, 3], xt, XTB, gm, CW, junk, JB, ssv, SSB, epsc, h, HB, hT, HTB, identb,
                        mblks, 256)
            for which in ("k", "v"):
                P.dma("sp", wkv[:], WB["w_x%s%d" % (which, l)].rearrange("(kc p) n -> p kc n", p=128), writes=[WKVB])
                for mb in range(2):
                    for half in range(2):
                        bk = 4 + half
                        for kc in range(8):
                            MM(bank(bk)[:, :], hT[:, kc, mb * 128:(mb + 1) * 128], wkv[:, kc, half * 512:(half + 1) * 512],
                               kc == 0, kc == 7, [HTB, WKVB], PB[bk])
                        P.op("act", "copy", reads=[PB[bk]], writes=[MKFB], out=mkf[:, mb, half * 512:(half + 1) * 512],
                             in_=bank(bk)[:, :])
                dst = k.O["mem_k_prompt" if which == "k" else "mem_v_prompt"][l]
                P.dma("sp", dst.rearrange("(b p) d -> p b d", p=128), mkf[:], reads=[MKFB], writes=[Buf()])
                if which == "k":
                    kv_finish()
                else:
                    P.op("pool", "tensor_copy", reads=[MKFB], writes=[MVBB], out=mvb[:], in_=mkf[:])

        def setup_sample_mem():
            P.dma("sp", mkf[:], k.I["cache_mem_k"][l].rearrange("(b p) d -> p b d", p=128), writes=[MKFB])
            kv_finish()
            P.dma("sp", mkf[:], k.I["cache_mem_v"][l].rearrange("(b p) d -> p b d", p=128), writes=[MKFB])
            P.op("pool", "tensor_copy", reads=[MKFB], writes=[MVBB], out=mvb[:], in_=mkf[:])

        seqs = [("prm", t * TT, TT) for t in range(T // TT)] + [("smp", T, TS)]
        if "c_seqs" in DBG:
            seqs = [seqs[i] for i in DBG["c_seqs"]]
        def tile_geom(ti):
            kind, t0, n = seqs[ti]
            return kind, t0, n, blocks(n), min(128, n)

        def load_x(ti):
            kind, t0, n, blks, pp = tile_geom(ti)
            P.dma("sp", xts[ti % 2][0:pp, 0:len(blks), :], k.S["x"][t0:t0 + n, :].rearrange("(b p) d -> p b d", p=pp),
                  writes=[XTBs[ti % 2]])

        def prenorm_e(ti):
            kind, t0, n, blks, pp = tile_geom(ti)
            pre_norm_elem(P, xts[ti % 2], XTBs[ti % 2], g2, CW, junk2, JB2, ssv2, SSB2, epsc, h, HB, blks)

        def prenorm_t(ti):
            kind, t0, n, blks, pp = tile_geom(ti)
            pre_norm_T(P, bank, PB, [0, 1, 2, 3], h, HB, hT, HTB, identb, blks, n)
        if seqs[0][0] == "prm":
            setup_prompt_mem()
        load_x(0)
        prenorm_e(0)
        prenorm_t(0)
        for ti, (kind, t0, n) in enumerate(seqs):
            blks = blocks(n)
            nb = len(blks)
            pp = min(128, n)
            xt, XTB = xts[ti % 2], XTBs[ti % 2]
            if ti + 1 < len(seqs):
                load_x(ti + 1)
            if kind == "smp":
                setup_sample_mem()

            def tm(ap2d):
                return ap2d.rearrange("(b p) d -> p b d", p=pp)
            for mc in range(8):
                bk = 4 + mc % 2
                for kc in range(8):
                    MM(bank(bk)[:, 0:n], wxq[:, kc, mc * 128:(mc + 1) * 128], hT[:, kc, 0:n], kc == 0, kc == 7, [HTB, CW], PB[bk])
                if mc % 2 == 0:
                    P.op("act", "copy", reads=[PB[bk]], writes=[QTB], out=qT[:, mc, 0:n], in_=bank(bk)[:, 0:n])
                else:
                    P.op("dve", "tensor_copy", reads=[PB[bk]], writes=[QTB], out=qT[:, mc, 0:n], in_=bank(bk)[:, 0:n])
            for hh in range(4):
                for mb in range(2):
                    for dc in range(2):
                        MM(bank(mb)[:, 0:n], mkT[:, hh * 2 + dc, mb * 128:(mb + 1) * 128], qT[:, hh * 2 + dc, 0:n],
                           dc == 0, dc == 1, [MKTB, QTB], PB[mb])
                    P.op("act", "activation", reads=[PB[mb]], writes=[PTB[mb]], out=pT[mb][:, 0:n], in_=bank(mb)[:, 0:n],
                         func=AF.Exp, scale=1.0 / 16.0)
                for mb in range(2):
                    MM(bank(2)[:, 0:n], onesb[:, :], pT[mb][:, 0:n], mb == 0, mb == 1, [PTB[mb]], PB[2])
                P.op("dve", "reciprocal", reads=[PB[2]], writes=[RECB], out=rec[:, 0:n], in_=bank(2)[:, 0:n])
                for dc in range(2):
                    bk = 3 if dc == 0 else 6
                    for mb in range(2):
                        MM(bank(bk)[:, 0:n], mvb[:, mb, hh * 256 + dc * 128:hh * 256 + dc * 128 + 128], pT[mb][:, 0:n],
                           mb == 0, mb == 1, [MVBB, PTB[mb]], PB[bk])
                    P.op("dve", "tensor_tensor", reads=[PB[bk], RECB], writes=[OTB], out=oT[:, hh * 2 + dc, 0:n],
                         in0=bank(bk)[:, 0:n], in1=rec[:, 0:n], op=ALU.mult)
            for ib, (io, is_) in enumerate(blks):
                b0 = 4 if ib % 2 == 0 else 6
                if b0 == 6:
                    b0 = 0
                for half in range(2):
                    bk = b0 + half
                    for kc in range(8):
                        MM(bank(bk)[0:is_, :], oT[:, kc, io:io + is_], wxo[:, kc, half * 512:(half + 1) * 512], kc == 0, kc == 7,
                           [OTB, CW], PB[bk])
                if ib == 0 and ti + 1 < len(seqs):
                    prenorm_e(ti + 1)
                pso = ps[0:is_, b0 * 512:b0 * 512 + 1024]
                post_norm_residual(P, pso, [PB[b0], PB[b0 + 1]], xt[0:is_, ib, :], XTB, g3, CW, junk, JB, ssv, SSB, epsc,
                                   tmp, TMPB, is_)
            if ti + 1 < len(seqs):
                prenorm_t(ti + 1)
            P.dma("sp", tm(k.S["x"][t0:t0 + n, :]), xt[0:pp, 0:nb, :], reads=[XTB], writes=[Buf()])
            P.maybe_flush()
        P.flush()


def phase_F(k, P, ps, PB, bank, WB, identf, identb, onesf, onesb, l, final):
    with contextlib.ExitStack() as st:
        def sb(name, shape, dt):
            return k.sbuf(st, "f%d_" % l + name, shape, dt)
        g4 = sb("g4", [128, D], F32)
        g5 = sb("g5", [128, D], F32)
        epsc = sb("epsc", [128, 1], F32)
        CW = Buf("cw")
        P.dma("sp", g4[:], k.I["norm_w"][l, 4:5, :].broadcast_to([128, D]), writes=[CW])
        P.dma("sp", g5[:], k.I["norm_w"][l, 5:6, :].broadcast_to([128, D]), writes=[CW])
        P.op("pool", "memset", writes=[CW], ap=epsc[:], constant=EPS)
        up = [sb("up%d" % i, [128, 8, 1024], BF16) for i in range(2)]
        dn = [sb("dn%d" % i, [128, 8, 1024], BF16) for i in range(2)]
        UPB = [Buf("up%d" % i) for i in range(2)]
        DNB = [Buf("dn%d" % i) for i in range(2)]
        xts = [sb("xt%d" % i, [128, 4, D], F32) for i in range(2)]
        XTBs = [Buf("xt%d" % i) for i in range(2)]
        junk2 = sb("junk2", [128, D], BF16)
        JB2 = Buf("junk2")
        ssv2 = sb("ssv2", [128, 8], F32)
        SSB2 = Buf("ssv2")
        h = sb("h", [128, 4, D], BF16)
        HB = Buf("h")
        hT = sb("hT", [128, 8, TT], BF16)
        HTB = Buf("hT")
        junk = sb("junk", [128, D], BF16)
        JB = Buf("junk")
        ssv = sb("ssv", [128, 8], F32)
        SSB = Buf("ssv")
        tmp = sb("tmp", [128, D], F32)
        TMPB = Buf("tmp")
        aT = sb("aT", [128, 32, TT], BF16)
        ATB = Buf("aT")
        sqv = [sb("sqv%d" % i, [128, TT], F32) for i in range(2)]
        SQB = [Buf("sqv%d" % i) for i in range(2)]
        wup = WB["w_up%d" % l].rearrange("(kc p) n -> p kc n", p=128)
        wdn = WB["w_down%d" % l].rearrange("(fc p) n -> p fc n", p=128)

        def MM(out, lhsT, rhs, start, stop, R, Wb):
            P.op("pe", "matmul", reads=R, writes=[Wb], out=out, lhsT=lhsT, rhs=rhs, start=start, stop=stop)

        seqs = [("prm", t * TT, TT) for t in range(T // TT)] + [("smp", T, TS)]
        if "f_seqs" in DBG:
            seqs = [seqs[i] for i in DBG["f_seqs"]]
        ui = 0
        di = 0
        fi = 0

        def tile_geom(ti):
            kind, t0, n = seqs[ti]
            return kind, t0, n, blocks(n), min(128, n)

        def load_x(ti):
            kind, t0, n, blks, pp = tile_geom(ti)
            P.dma("sp", xts[ti % 2][0:pp, 0:len(blks), :], k.S["x"][t0:t0 + n, :].rearrange("(b p) d -> p b d", p=pp),
                  writes=[XTBs[ti % 2]])

        def prenorm_e(ti):
            kind, t0, n, blks, pp = tile_geom(ti)
            pre_norm_elem(P, xts[ti % 2], XTBs[ti % 2], g4, CW, junk2, JB2, ssv2, SSB2, epsc, h, HB, blks)

        def prenorm_t(ti):
            kind, t0, n, blks, pp = tile_geom(ti)
            pre_norm_T(P, bank, PB, [0, 1, 2, 3], h, HB, hT, HTB, identb, blks, n)
        load_x(0)
        prenorm_e(0)
        prenorm_t(0)
        for ti, (kind, t0, n) in enumerate(seqs):
            blks = blocks(n)
            nb = len(blks)
            pp = min(128, n)
            xt, XTB = xts[ti % 2], XTBs[ti % 2]
            if ti + 1 < len(seqs):
                load_x(ti + 1)

            def tm(ap2d):
                return ap2d.rearrange("(b p) d -> p b d", p=pp)
            for q in range(4):
                sl = ui % 2
                ui += 1
                P.dma("sp", up[sl][:, 0:4, :], wup[:, 0:4, q * 1024:(q + 1) * 1024], writes=[UPB[sl]])
                P.dma("pool", up[sl][:, 4:8, :], wup[:, 4:8, q * 1024:(q + 1) * 1024], writes=[UPB[sl]])
                for fcl in range(8):
                    fc = q * 8 + fcl
                    bk = 4 + fi % 4
                    sq_ = fi % 2
                    fi += 1
                    for kc in range(8):
                        MM(bank(bk)[:, 0:n], up[sl][:, kc, fcl * 128:(fcl + 1) * 128], hT[:, kc, 0:n], kc == 0, kc == 7,
                           [UPB[sl], HTB], PB[bk])
                    P.op("act", "activation", reads=[PB[bk]], writes=[SQB[sq_]], out=sqv[sq_][:, 0:n], in_=bank(bk)[:, 0:n],
                         func=AF.Square)
                    P.op("dve", "scalar_tensor_tensor", reads=[PB[bk], SQB[sq_]], writes=[ATB], out=aT[:, fc, 0:n],
                         in0=bank(bk)[:, 0:n], scalar=0.0, in1=sqv[sq_][:, 0:n], op0=ALU.is_gt, op1=ALU.mult)
            for q in range(4):
                sl = di % 2
                di += 1
                P.dma("sp", dn[sl][:, 0:4, :], wdn[:, q * 8:q * 8 + 4, :], writes=[DNB[sl]])
                P.dma("pool", dn[sl][:, 4:8, :], wdn[:, q * 8 + 4:q * 8 + 8, :], writes=[DNB[sl]])
                for ib, (io, is_) in enumerate(blks):
                    for half in range(2):
                        bk = 2 * ib + half
                        for fcl in range(8):
                            fc = q * 8 + fcl
                            MM(bank(bk)[0:is_, :], aT[:, fc, io:io + is_], dn[sl][:, fcl, half * 512:(half + 1) * 512],
                               q == 0 and fcl == 0, q == 3 and fcl == 7, [ATB, DNB[sl]], PB[bk])
            if ti + 1 < len(seqs):
                prenorm_e(ti + 1)
            for ib, (io, is_) in enumerate(blks):
                pso = ps[0:is_, 2 * ib * 512:2 * ib * 512 + 1024]
                post_norm_residual(P, pso, [PB[2 * ib], PB[2 * ib + 1]], xt[0:is_, ib, :], XTB, g5, CW, junk, JB, ssv, SSB,
                                   epsc, tmp, TMPB, is_)
            if ti + 1 < len(seqs):
                prenorm_t(ti + 1)
            if final:
                dst = k.O["y_sample"] if kind == "smp" else k.O["y_prompt"][t0:t0 + n, :]
            else:
                dst = k.S["x"][t0:t0 + n, :]
            P.dma("sp", tm(dst), xt[0:pp, 0:nb, :], reads=[XTB], writes=[Buf()])
            P.maybe_flush()
        P.flush()


def phase_S1(k, P, ps, PB, bank, WB, identf, identb, onesf, onesb):
    with contextlib.ExitStack() as st:
        def sb(name, shape, dt):
            return k.sbuf(st, "s1_" + name, shape, dt)
        winc = sb("winc", [128, 8, D], BF16)
        g0 = sb("g0", [128, D], F32)
        epsc = sb("epsc", [128, 1], F32)
        CW = Buf("cw")
        P.dma("sp", winc[:], WB["w_in_c"].rearrange("(kc p) n -> p kc n", p=128), writes=[CW])
        P.dma("sp", g0[:], k.I["norm_w"][1, 0:1, :].broadcast_to([128, D]), writes=[CW])
        P.op("pool", "memset", writes=[CW], ap=epsc[:], constant=EPS)
        xts = [sb("xt%d" % i, [128, 4, D], F32) for i in range(2)]
        XTBs = [Buf("xt%d" % i) for i in range(2)]
        h = sb("h", [128, 4, D], BF16)
        HB = Buf("h")
        hT = sb("hT", [128, 8, TT], BF16)
        HTB = Buf("hT")
        junk = sb("junk", [128, D], BF16)
        JB = Buf("junk")
        ssv = sb("ssv", [128, 8], F32)
        SSB = Buf("ssv")
        uTs = [sb("uT%d" % i, [128, 8, TT], BF16) for i in range(2)]
        UTBs = [Buf("uT%d" % i) for i in range(2)]
        seqs = [("prm", t * TT, TT) for t in range(T // TT)] + [("smp", T, TS)]
        def load_x(ti):
            kind, t0, n = seqs[ti]
            pp = min(128, n)
            P.dma("sp", xts[ti % 2][0:pp, 0:len(blocks(n)), :], k.S["x"][t0:t0 + n, :].rearrange("(b p) d -> p b d", p=pp),
                  writes=[XTBs[ti % 2]])
        load_x(0)
        for ti, (kind, t0, n) in enumerate(seqs):
            blks = blocks(n)
            nb = len(blks)
            pp = min(128, n)
            xt, XTB = xts[ti % 2], XTBs[ti % 2]
            uT, UTB = uTs[ti % 2], UTBs[ti % 2]
            if ti + 1 < len(seqs):
                load_x(ti + 1)
            pre_norm_hT(P, bank, PB, [0, 1, 2, 3], xt, XTB, g0, CW, junk, JB, ssv, SSB, epsc, h, HB, hT, HTB, identb, blks, n)
            for mc in range(8):
                bk = 4 + mc % 4
                for kc in range(8):
                    P.op("pe", "matmul", reads=[HTB, CW], writes=[PB[bk]], out=bank(bk)[:, 0:n],
                         lhsT=winc[:, kc, mc * 128:(mc + 1) * 128], rhs=hT[:, kc, 0:n], start=(kc == 0), stop=(kc == 7))
                if mc % 2 == 0:
                    P.op("act", "copy", reads=[PB[bk]], writes=[UTB], out=uT[:, mc, 0:n], in_=bank(bk)[:, 0:n])
                else:
                    P.op("dve", "tensor_copy", reads=[PB[bk]], writes=[UTB], out=uT[:, mc, 0:n], in_=bank(bk)[:, 0:n])
            P.dma("act", k.S["uT"][:, :, t0:t0 + n].rearrange("c p t -> p c t"), uT[:, :, 0:n], reads=[UTB], writes=[Buf()])
            P.maybe_flush()
        P.flush()


def phase_S3(k, P, ps, PB, bank, WB, identf, identb, onesf, onesb):
    with contextlib.ExitStack() as st:
        def sb(name, shape, dt):
            return k.sbuf(st, "s3_" + name, shape, dt)
        wg = sb("wg", [128, 8, 2 * D], BF16)
        g1 = sb("g1", [128, D], F32)
        epsc = sb("epsc", [128, 1], F32)
        CW = Buf("cw")
        P.dma("sp", wg[:], WB["w_glu"].rearrange("(kc p) n -> p kc n", p=128), writes=[CW])
        P.dma("sp", g1[:], k.I["norm_w"][1, 1:2, :].broadcast_to([128, D]), writes=[CW])
        P.op("pool", "memset", writes=[CW], ap=epsc[:], constant=EPS)
        xts = [sb("xt%d" % i, [128, 4, D], F32) for i in range(2)]
        XTBs = [Buf("xt%d" % i) for i in range(2)]
        yTs = [sb("yT%d" % i, [128, 8, TT], BF16) for i in range(2)]
        YTBs = [Buf("yT%d" % i) for i in range(2)]
        junk = sb("junk", [128, D], BF16)
        JB = Buf("junk")
        ssv = sb("ssv", [128, 8], F32)
        SSB = Buf("ssv")
        tmp = sb("tmp", [128, D], F32)
        TMPB = Buf("tmp")
        sig = sb("sig", [128, D], F32)
        SIGB = Buf("sig")
        gl = sb("gl", [128, D], F32)
        GLB = Buf("gl")
        seqs = [("prm", t * TT, TT) for t in range(T // TT)] + [("smp", T, TS)]
        def load_in(ti):
            kind, t0, n = seqs[ti]
            pp = min(128, n)
            P.dma("sp", xts[ti % 2][0:pp, 0:len(blocks(n)), :], k.S["x"][t0:t0 + n, :].rearrange("(b p) d -> p b d", p=pp),
                  writes=[XTBs[ti % 2]])
            P.dma("sp", yTs[ti % 2][:, :, 0:n], k.S["yT"][:, :, t0:t0 + n].rearrange("c p t -> p c t"), writes=[YTBs[ti % 2]])
        load_in(0)
        for ti, (kind, t0, n) in enumerate(seqs):
            blks = blocks(n)
            nb = len(blks)
            pp = min(128, n)
            xt, XTB = xts[ti % 2], XTBs[ti % 2]
            yT, YTB = yTs[ti % 2], YTBs[ti % 2]
            if ti + 1 < len(seqs):
                load_in(ti + 1)
            for ib, (io, is_) in enumerate(blks):
                b0 = 4 * (ib % 2)
                for q4 in range(4):
                    bk = b0 + q4
                    for kc in range(8):
                        P.op("pe", "matmul", reads=[YTB, CW], writes=[PB[bk]], out=bank(bk)[0:is_, :],
                             lhsT=yT[:, kc, io:io + is_], rhs=wg[:, kc, q4 * 512:(q4 + 1) * 512], start=(kc == 0), stop=(kc == 7))
                P.op("act", "activation", reads=[PB[b0 + 2], PB[b0 + 3]], writes=[SIGB], out=sig[0:is_, :],
                     in_=ps[0:is_, (b0 + 2) * 512:(b0 + 4) * 512], func=AF.Sigmoid)
                P.op("dve", "tensor_tensor", reads=[PB[b0], PB[b0 + 1], SIGB], writes=[GLB], out=gl[0:is_, :],
                     in0=ps[0:is_, b0 * 512:(b0 + 2) * 512], in1=sig[0:is_, :], op=ALU.mult)
                post_norm_residual(P, gl[0:is_, :], [GLB], xt[0:is_, ib, :], XTB, g1, CW, junk, JB, ssv, SSB, epsc, tmp, TMPB, is_)
            P.dma("sp", k.S["x"][t0:t0 + n, :].rearrange("(b p) d -> p b d", p=pp), xt[0:pp, 0:nb, :], reads=[XTB], writes=[Buf()])
            P.maybe_flush()
        P.flush()


PI = float(np.pi)


def phase_S2(k, P, ps, PB, bank, WB, identf, identb, onesf, onesb):
    with contextlib.ExitStack() as st:
        def sb(name, shape, dt):
            return k.sbuf(st, "s2_" + name, shape, dt)
        Tall = sb("Tall", [128, 64, 128], BF16)
        Vall = sb("Vall", [128, 64, 128], BF16)
        WR = sb("WR", [128, 32, 128], BF16)
        WI = sb("WI", [128, 32, 128], BF16)
        Ec = sb("Ec", [128, 32, 64], F32)
        Es = sb("Es", [128, 32, 64], F32)
        Rt = sb("Rt", [128, 32, 64], F32)
        r8s = sb("r8s", [128, 32], F32)
        dfm = sb("dfm", [128, 8], F32)
        XRc = sb("XRc", [128, 32], F32)
        XIc = sb("XIc", [128, 32], F32)
        TB = Buf("s2tab")
        CARB = Buf("s2carry")
        P.dma("sp", dfm[:], k.I["s5_d"].rearrange("o (c p) -> p (o c)", p=128), writes=[TB], allow_slow_non_contiguous=True)

        def MM(out, lhsT, rhs, start, stop, R, Wb):
            P.op("pe", "matmul", reads=R, writes=[Wb], out=out, lhsT=lhsT, rhs=rhs, start=start, stop=stop)

        with contextlib.ExitStack() as st2:
            def sb2(name, shape, dt):
                return k.sbuf(st2, "s2t_" + name, shape, dt)
            lamr = sb2("lamr", [128, 32], F32)
            lami = sb2("lami", [128, 32], F32)
            dtv = sb2("dtv", [128, 32], F32)
            lr = sb2("lr", [128, 32], F32)
            li = sb2("li", [128, 32], F32)
            kvec = sb2("kvec", [128, 16], F32)
            nvec = sb2("nvec", [128, 64], F32)
            tmask = sb2("tmask", [128, 512], F32)
            Bre = sb2("Bre", [128, 32, 16], F32)
            Bim = sb2("Bim", [128, 32, 16], F32)
            Cnat = sb2("Cnat", [128, 4, 128], F32)
            Cre = sb2("Cre", [128, 32, 16], F32)
            Cim = sb2("Cim", [128, 32, 16], F32)
            arg = sb2("arg", [128, 32, 16], F32)
            mag = sb2("mag", [128, 32, 16], F32)
            Are = sb2("Are", [128, 32, 16], F32)
            Aim = sb2("Aim", [128, 32, 16], F32)
            s1 = sb2("s1", [128, 32], F32)
            s2_ = sb2("s2", [128, 32], F32)
            s3 = sb2("s3", [128, 32], F32)
            fre = sb2("fre", [128, 32], F32)
            fim = sb2("fim", [128, 32], F32)
            bbr = sb2("bbr", [128, 32, 16], F32)
            bbi = sb2("bbi", [128, 32, 16], F32)
            t16 = sb2("t16", [128, 32, 16], F32)
            Gre = sb2("Gre", [128, 32, 128], F32)
            Gim = sb2("Gim", [128, 32, 128], F32)
            Hre = sb2("Hre", [128, 32, 128], F32)
            Him = sb2("Him", [128, 32, 128], F32)
            big = sb2("big", [128, 32, 128], F32)
            eb = sb2("eb", [128, 32, 64], F32)
            SB_ = Buf("setup")
            for gh in range(2):
                psl = slice(gh * 64, gh * 64 + 64)
                gs = slice(gh * 32, gh * 32 + 32)
                P.dma("sp", lamr[psl, :], k.I["s5_lambda_re"][gs, :].rearrange("g p -> p g"), writes=[SB_],
                      allow_slow_non_contiguous=True)
                P.dma("act", lami[psl, :], k.I["s5_lambda_im"][gs, :].rearrange("g p -> p g"), writes=[SB_],
                      allow_slow_non_contiguous=True)
                P.dma("sp", dtv[psl, :], k.I["s5_log_dt"][0:1, gs].broadcast_to([64, 32]), writes=[SB_])
                P.dma("sp", Bre[psl, :, :], k.I["s5_b_re"][gs].rearrange("g p c -> p g c"), writes=[SB_])
                P.dma("act", Bim[psl, :, :], k.I["s5_b_im"][gs].rearrange("g p c -> p g c"), writes=[SB_])
            P.dma("sp", kvec[:], k.C["kvec"], writes=[SB_])
            P.dma("sp", nvec[:], k.C["nvec"], writes=[SB_])
            P.dma("sp", tmask[:], k.C["tmask"], writes=[SB_])
            for (src, dstC) in ((k.I["s5_c_re"], Cre), (k.I["s5_c_im"], Cim)):
                for gh in range(2):
                    P.dma("sp", Cnat[:, :, gh * 64:gh * 64 + 64],
                          src[gh * 32:gh * 32 + 32].rearrange("(o g) c p -> (g c) o p", o=4), writes=[SB_])
                for o4 in range(4):
                    MM(bank(0)[:, 0:128], Cnat[:, o4, :], identf[:, :], True, True, [SB_], PB[0])
                    gl0 = o4 * 8
                    P.op("dve", "tensor_copy", reads=[PB[0]], writes=[SB_],
                         out=dstC[:, gl0:gl0 + 8, :], in_=bank(0)[:, 0:128].rearrange("p (g c) -> p g c", c=16))

            def V(eng, name, **kw):
                P.op(eng, name, reads=[SB_], writes=[SB_], **kw)
            V("act", "activation", out=dtv[:], in_=dtv[:], func=AF.Exp)
            V("dve", "tensor_tensor", out=lr[:], in0=lamr[:], in1=dtv[:], op=ALU.mult)
            V("dve", "tensor_tensor", out=li[:], in0=lami[:], in1=dtv[:], op=ALU.mult)
            kb = kvec[:].unsqueeze(1).broadcast_to([128, 32, 16])
            V("dve", "tensor_tensor", out=arg[:], in0=lr[:].unsqueeze(2).broadcast_to([128, 32, 16]), in1=kb, op=ALU.mult)
            V("act", "activation", out=mag[:], in_=arg[:], func=AF.Exp)
            V("dve", "tensor_tensor", out=arg[:], in0=li[:].unsqueeze(2).broadcast_to([128, 32, 16]), in1=kb, op=ALU.mult)

            isc = sb2("isc", [128, 32, 64], mybir.dt.int32)

            def sincos(out_s, out_c, ang, f1, f2, i1):
                for (dst, off) in ((out_s, 0.0), (out_c, 0.25)):
                    V("dve", "tensor_scalar", out=f1, in0=ang, scalar1=1.0 / (2 * PI), scalar2=off, op0=ALU.mult, op1=ALU.add)
                    V("dve", "tensor_copy", out=i1, in_=f1)
                    V("dve", "tensor_copy", out=f2, in_=i1)
                    V("dve", "tensor_tensor", out=f1, in0=f1, in1=f2, op=ALU.subtract)
                    V("act", "activation", out=dst, in_=f1, func=AF.Sin, scale=2 * PI)
            sincos(Aim[:], Are[:], arg[:], t16[:], bbr[:], isc[:, :, 0:16])
            V("dve", "tensor_tensor", out=Are[:], in0=Are[:], in1=mag[:], op=ALU.mult)
            V("dve", "tensor_tensor", out=Aim[:], in0=Aim[:], in1=mag[:], op=ALU.mult)
            a_re, a_im = Are[:, :, 8], Aim[:, :, 8]
            V("dve", "tensor_tensor", out=s1[:], in0=lamr[:], in1=lamr[:], op=ALU.mult)
            V("dve", "tensor_tensor", out=s2_[:], in0=lami[:], in1=lami[:], op=ALU.mult)
            V("dve", "tensor_tensor", out=s1[:], in0=s1[:], in1=s2_[:], op=ALU.add)
            V("dve", "reciprocal", out=s1[:], in_=s1[:])
            V("dve", "tensor_scalar", out=s2_[:], in0=a_re, scalar1=-1.0, scalar2=None, op0=ALU.add)
            V("dve", "tensor_tensor", out=fre[:], in0=s2_[:], in1=lamr[:], op=ALU.mult)
            V("dve", "tensor_tensor", out=s3[:], in0=a_im, in1=lami[:], op=ALU.mult)
            V("dve", "tensor_tensor", out=fre[:], in0=fre[:], in1=s3[:], op=ALU.add)
            V("dve", "tensor_tensor", out=fre[:], in0=fre[:], in1=s1[:], op=ALU.mult)
            V("dve", "tensor_tensor", out=fim[:], in0=a_im, in1=lamr[:], op=ALU.mult)
            V("dve", "tensor_tensor", out=s3[:], in0=s2_[:], in1=lami[:], op=ALU.mult)
            V("dve", "tensor_tensor", out=fim[:], in0=fim[:], in1=s3[:], op=ALU.subtract)
            V("dve", "tensor_tensor", out=fim[:], in0=fim[:], in1=s1[:], op=ALU.mult)
            frb = fre[:].unsqueeze(2).broadcast_to([128, 32, 16])
            fib = fim[:].unsqueeze(2).broadcast_to([128, 32, 16])
            V("dve", "tensor_tensor", out=bbr[:], in0=Bre[:], in1=frb, op=ALU.mult)
            V("dve", "tensor_tensor", out=t16[:], in0=Bim[:], in1=fib, op=ALU.mult)
            V("dve", "tensor_tensor", out=bbr[:], in0=bbr[:], in1=t16[:], op=ALU.subtract)
            V("dve", "tensor_tensor", out=bbi[:], in0=Bim[:], in1=frb, op=ALU.mult)
            V("dve", "tensor_tensor", out=t16[:], in0=Bre[:], in1=fib, op=ALU.mult)
            V("dve", "tensor_tensor", out=bbi[:], in0=bbi[:], in1=t16[:], op=ALU.add)

            def v4(t):
                return t[:].rearrange("p g (s c) -> p g s c", c=16)

            def outer(are, aim, xre, xim, ore, oim, neg_im=False):
                ab = are.unsqueeze(3).broadcast_to([128, 32, 8, 16])
                ai = aim.unsqueeze(3).broadcast_to([128, 32, 8, 16])
                xr = xre.unsqueeze(2).broadcast_to([128, 32, 8, 16])
                xi = xim.unsqueeze(2).broadcast_to([128, 32, 8, 16])
                V("dve", "tensor_tensor", out=v4(ore), in0=ab, in1=xr, op=ALU.mult)
                V("dve", "tensor_tensor", out=v4(big), in0=ai, in1=xi, op=ALU.mult)
                V("dve", "tensor_tensor", out=ore[:], in0=ore[:], in1=big[:], op=ALU.subtract)
                V("dve", "tensor_tensor", out=v4(oim), in0=ab, in1=xi, op=ALU.mult)
                V("dve", "tensor_tensor", out=v4(big), in0=ai, in1=xr, op=ALU.mult)
                V("dve", "tensor_tensor", out=oim[:], in0=oim[:], in1=big[:], op=ALU.add)
            outer(Are[:, :, 0:8], Aim[:, :, 0:8], bbr[:], bbi[:], Gre, Gim)
            outer(Are[:, :, 8:16], Aim[:, :, 8:16], Cre[:], Cim[:], Hre, Him)
            V("dve", "tensor_scalar", out=Him[:], in0=Him[:], scalar1=-1.0, scalar2=None, op0=ALU.mult)
            for g in range(64):
                gh, gl_ = g // 32, g % 32
                psl = slice(gh * 64, gh * 64 + 64)
                bk = (g // 4) % 2
                col = (g % 4) * 128
                MM(bank(bk)[:, col:col + 128], Gre[psl, gl_, :], Hre[psl, gl_, :], True, False, [SB_], PB[bk])
                MM(bank(bk)[:, col:col + 128], Gim[psl, gl_, :], Him[psl, gl_, :], False, True, [SB_], PB[bk])
                if g % 4 == 3:
                    g0 = g - 3
                    P.op("dve", "tensor_tensor", reads=[PB[bk], SB_], writes=[TB],
                         out=Tall[:, g0:g0 + 4, :], in0=bank(bk)[:, :].rearrange("p (g m) -> p g m", m=128),
                         in1=tmask[:].rearrange("p (g m) -> p g m", m=128), op=ALU.mult)
            V("act", "copy", out=WR[:], in_=Hre[:])
            V("act", "copy", out=WI[:], in_=Him[:])
            P.op("act", "copy", reads=[SB_], writes=[TB], out=WR[:], in_=Hre[:])
            a8r = Are[:, :, 15:16].broadcast_to([128, 32, 128])
            a8i = Aim[:, :, 15:16].broadcast_to([128, 32, 128])
            V("dve", "tensor_tensor", out=Hre[:], in0=Gre[:], in1=a8r, op=ALU.mult)
            V("dve", "tensor_tensor", out=big[:], in0=Gim[:], in1=a8i, op=ALU.mult)
            V("dve", "tensor_tensor", out=Hre[:], in0=Hre[:], in1=big[:], op=ALU.subtract)
            V("dve", "tensor_tensor", out=Him[:], in0=Gim[:], in1=a8r, op=ALU.mult)
            V("dve", "tensor_tensor", out=big[:], in0=Gre[:], in1=a8i, op=ALU.mult)
            V("dve", "tensor_tensor", out=Him[:], in0=Him[:], in1=big[:], op=ALU.add)
            for g in range(64):
                gh, gl_ = g // 32, g % 32
                psl = slice(gh * 64, gh * 64 + 64)
                bk = 2 + (g // 4) % 2
                col = (g % 4) * 128
                MM(bank(bk)[:, col:col + 64], Hre[psl, gl_, :], identf[psl, psl], True, True, [SB_], PB[bk])
                MM(bank(bk)[:, col + 64:col + 128], Him[psl, gl_, :], identf[psl, psl], True, True, [SB_], PB[bk])
                if g % 4 == 3:
                    g0 = g - 3
                    P.op("act", "copy", reads=[PB[bk]], writes=[TB], out=Vall[:, g0:g0 + 4, :],
                         in_=bank(bk)[:, :].rearrange("p (g m) -> p g m", m=128))
            nb_ = nvec[:].unsqueeze(1).broadcast_to([128, 32, 64])
            V("dve", "tensor_tensor", out=eb[:], in0=li[:].unsqueeze(2).broadcast_to([128, 32, 64]), in1=nb_, op=ALU.mult)
            sincos(Es[:], Ec[:], eb[:], Rt[:], big[:, :, 0:64], isc[:])
            V("act", "activation", out=r8s[:], in_=lr[:], func=AF.Exp, scale=8.0)
            P.op("dve", "tensor_copy", reads=[SB_], writes=[TB], out=Rt[:], in_=r8s[:].unsqueeze(2).broadcast_to([128, 32, 64]))
            P.op("pool", "memset", reads=[SB_], writes=[TB], ap=Rt[:, :, 0:1], constant=0.0)
            P.flush()

        sel = sb("sel", [128, 64, 128], BF16)
        selT = sb("selT", [128, 64, 128], BF16)
        P.dma("sp", sel[:], k.C["sel_b"], writes=[TB])
        P.dma("act", selT[:], k.C["selT_b"], writes=[TB])
        uTs = [sb("uT%d" % i, [128, 8, TT], BF16) for i in range(2)]
        UTBs = [Buf("uT%d" % i) for i in range(2)]
        Ugs = [sb("Ug%d" % i, [128, 64, 64], BF16) for i in range(2)]
        UGBs = [Buf("Ug%d" % i) for i in range(2)]
        XsR = sb("XsR", [128, 32, 64], BF16)
        XsI = sb("XsI", [128, 32, 64], BF16)
        XSB = Buf("Xs")
        Yg = sb("Yg", [128, 64, 64], BF16)
        YGB = Buf("Yg")
        yT = sb("yT", [128, 8, TT], BF16)
        YTB = Buf("yT")
        ta = sb("ta", [128, 16, 64], F32)
        tb = sb("tb", [128, 16, 64], F32)
        vr = sb("vr", [128, 16, 64], F32)
        vi = sb("vi", [128, 16, 64], F32)
        zr = sb("zr", [128, 16, 64], F32)
        zi = sb("zi", [128, 16, 64], F32)
        tc_ = sb("tc", [128, 16], F32)
        WKB = Buf("s2work")
        for t_ in (ta, tb, vr, vi, zr, zi):
            P.op("pool", "memset", writes=[WKB], ap=t_[:], constant=0.0)
        P.op("pool", "memset", writes=[CARB], ap=XRc[:], constant=0.0)
        P.op("pool", "memset", writes=[CARB], ap=XIc[:], constant=0.0)
        P.op("pool", "memset", writes=[XSB], ap=XsR[:], constant=0.0)
        P.op("pool", "memset", writes=[XSB], ap=XsI[:], constant=0.0)

        def f2(t):
            return t[:].rearrange("p g n -> p (g n)")
        seqs = [("prm", t * TT, TT) for t in range(T // TT)] + [("smp", T, TS)]
        if "s2_seqs" in DBG:
            seqs = [seqs[i] for i in DBG["s2_seqs"]]
        def load_u(ti):
            kind, t0, n = seqs[ti]
            P.dma("sp", uTs[ti % 2][:, :, 0:n], k.S["uT"][:, :, t0:t0 + n].rearrange("c p t -> p c t"), writes=[UTBs[ti % 2]])
        load_u(0)
        def tile_vars(ti):
            kind, t0, n = seqs[ti]
            return kind, t0, n, n // 8, uTs[ti % 2], UTBs[ti % 2], Ugs[ti % 2], UGBs[ti % 2]

        def stageA(ti):
            kind, t0, n, nch, uT, UTB, Ug, UGB = tile_vars(ti)
            for g in range(64):
                kc, g8 = g // 8, g % 8
                bk = (g // 8) % 2
                col = g8 * 64
                for s_ in range(8):
                    rhs = uT[:, kc, 0:n].rearrange("p (n s) -> p n s", s=8)[:, :, s_]
                    MM(bank(bk)[:, col:col + nch], sel[:, g8 * 8 + s_, :], rhs, s_ == 0, s_ == 7, [UTB, TB], PB[bk])
                if g8 == 7:
                    g0 = g - 7
                    src = bank(bk)[:, :].rearrange("p (g n) -> p g n", n=64)[:, :, 0:nch]
                    if (g // 8) % 2 == 0:
                        P.op("act", "copy", reads=[PB[bk]], writes=[UGB], out=Ug[:, g0:g0 + 8, 0:nch], in_=src)
                    else:
                        P.op("dve", "tensor_copy", reads=[PB[bk]], writes=[UGB], out=Ug[:, g0:g0 + 8, 0:nch], in_=src)

        def stageB(ti):
            kind, t0, n, nch, uT, UTB, Ug, UGB = tile_vars(ti)
            if kind == "smp":
                for gh in range(2):
                    psl = slice(gh * 64, gh * 64 + 64)
                    gs = slice(gh * 32, gh * 32 + 32)
                    P.dma("sp", XRc[psl, :], k.I["state_s5_re"][gs, :].rearrange("g p -> p g"), writes=[CARB],
                          allow_slow_non_contiguous=True)
                    P.dma("act", XIc[psl, :], k.I["state_s5_im"][gs, :].rearrange("g p -> p g"), writes=[CARB],
                          allow_slow_non_contiguous=True)
            for hf in range(2):
                glo = hf * 16
                for gh in range(2):
                    psl = slice(gh * 64, gh * 64 + 64)
                    for gq in range(16):
                        g = gh * 32 + glo + gq
                        bR = 2 + gq // 8
                        bI = 4 + gq // 8
                        col = (gq % 8) * 64
                        MM(bank(bR)[psl, col:col + nch], Vall[:, g, 0:64], Ug[:, g, 0:nch], True, True, [UGB, TB], PB[bR])
                        MM(bank(bI)[psl, col:col + nch], Vall[:, g, 64:128], Ug[:, g, 0:nch], True, True, [UGB, TB], PB[bI])
                pR = ps[:, 2 * 512:4 * 512].rearrange("p (g n) -> p g n", n=64)[:, :, 0:nch]
                pI = ps[:, 4 * 512:6 * 512].rearrange("p (g n) -> p g n", n=64)[:, :, 0:nch]
                ec = Ec[:, glo:glo + 16, 0:nch]
                es = Es[:, glo:glo + 16, 0:nch]
                RD = [PB[2], PB[3], PB[4], PB[5], TB, WKB]
                WW = [WKB]

                def W3(t):
                    return t[:, :, 0:nch]
                P.op("dve", "tensor_tensor", reads=RD, writes=WW, out=W3(ta), in0=pR, in1=ec, op=ALU.mult)
                P.op("dve", "tensor_tensor", reads=RD, writes=WW, out=W3(tb), in0=pI, in1=es, op=ALU.mult)
                P.op("pool", "tensor_tensor", reads=RD, writes=WW, out=W3(vr), in0=W3(ta), in1=W3(tb), op=ALU.add)
                P.op("dve", "tensor_tensor", reads=RD, writes=WW, out=W3(ta), in0=pI, in1=ec, op=ALU.mult)
                P.op("dve", "tensor_tensor", reads=RD, writes=WW, out=W3(tb), in0=pR, in1=es, op=ALU.mult)
                P.op("pool", "tensor_tensor", reads=RD, writes=WW, out=W3(vi), in0=W3(ta), in1=W3(tb), op=ALU.subtract)
                for (vv, cc) in ((vr, XRc), (vi, XIc)):
                    P.op("dve", "tensor_tensor", reads=[CARB, TB, WKB], writes=WW, out=tc_[:], in0=cc[:, glo:glo + 16],
                         in1=r8s[:, glo:glo + 16], op=ALU.mult)
                    P.op("dve", "tensor_tensor", reads=[WKB], writes=WW, out=vv[:, :, 0], in0=vv[:, :, 0], in1=tc_[:],
                         op=ALU.add)
                rt2 = Rt[:, glo:glo + 16, :].rearrange("p g n -> p (g n)")
                P.op("dve", "tensor_tensor_scan", reads=[WKB, TB], writes=WW, out=f2(zr), data0=rt2, data1=f2(vr), initial=0.0,
                     op0=ALU.mult, op1=ALU.add)
                P.op("dve", "tensor_tensor_scan", reads=[WKB, TB], writes=WW, out=f2(zi), data0=rt2, data1=f2(vi), initial=0.0,
                     op0=ALU.mult, op1=ALU.add)
                P.op("dve", "tensor_tensor", reads=[WKB, TB], writes=WW, out=W3(ta), in0=W3(zr), in1=ec, op=ALU.mult)
                P.op("pool", "tensor_tensor", reads=[WKB, TB], writes=WW, out=W3(tb), in0=W3(zi), in1=es, op=ALU.mult)
                P.op("dve", "tensor_tensor", reads=[WKB], writes=WW, out=W3(vr), in0=W3(ta), in1=W3(tb), op=ALU.subtract)
                P.op("dve", "tensor_tensor", reads=[WKB, TB], writes=WW, out=W3(ta), in0=W3(zi), in1=ec, op=ALU.mult)
                P.op("pool", "tensor_tensor", reads=[WKB, TB], writes=WW, out=W3(tb), in0=W3(zr), in1=es, op=ALU.mult)
                P.op("dve", "tensor_tensor", reads=[WKB], writes=WW, out=W3(vi), in0=W3(ta), in1=W3(tb), op=ALU.add)
                P.op("act", "copy", reads=[CARB], writes=[XSB], out=XsR[:, glo:glo + 16, 0], in_=XRc[:, glo:glo + 16])
                P.op("act", "copy", reads=[CARB], writes=[XSB], out=XsI[:, glo:glo + 16, 0], in_=XIc[:, glo:glo + 16])
                if nch > 1:
                    P.op("act", "copy", reads=[WKB], writes=[XSB], out=XsR[:, glo:glo + 16, 1:nch], in_=vr[:, :, 0:nch - 1])
                    P.op("act", "copy", reads=[WKB], writes=[XSB], out=XsI[:, glo:glo + 16, 1:nch], in_=vi[:, :, 0:nch - 1])
                P.op("dve", "tensor_copy", reads=[WKB, XSB], writes=[CARB], out=XRc[:, glo:glo + 16], in_=vr[:, :, nch - 1])
                P.op("dve", "tensor_copy", reads=[WKB, XSB], writes=[CARB], out=XIc[:, glo:glo + 16], in_=vi[:, :, nch - 1])
            last_of_kind = (kind == "smp") or (t0 + n == T)
            if last_of_kind:
                for gh in range(2):
                    psl = slice(gh * 64, gh * 64 + 64)
                    gs = slice(gh * 32, gh * 32 + 32)
                    dr = k.O["s5_re_sample" if kind == "smp" else "s5_re_prompt"]
                    di = k.O["s5_im_sample" if kind == "smp" else "s5_im_prompt"]
                    P.dma("sp", dr[gs, :].rearrange("g p -> p g"), XRc[psl, :], reads=[CARB], writes=[Buf()],
                          allow_slow_non_contiguous=True)
                    P.dma("act", di[gs, :].rearrange("g p -> p g"), XIc[psl, :], reads=[CARB], writes=[Buf()],
                          allow_slow_non_contiguous=True)

        def stageC(ti):
            kind, t0, n, nch, uT, UTB, Ug, UGB = tile_vars(ti)
            for g in range(64):
                gh, gl_ = g // 32, g % 32
                psl = slice(gh * 64, gh * 64 + 64)
                bk = 6 + (g // 8) % 2
                col = (g % 8) * 64
                MM(bank(bk)[:, col:col + nch], Tall[:, g, :], Ug[:, g, 0:nch], True, False, [UGB, TB], PB[bk])
                MM(bank(bk)[:, col:col + nch], WR[psl, gl_, :], XsR[psl, gl_, 0:nch], False, False, [XSB, TB], PB[bk])
                MM(bank(bk)[:, col:col + nch], WI[psl, gl_, :], XsI[psl, gl_, 0:nch], False, True, [XSB, TB], PB[bk])
                if g % 8 == 7:
                    g0 = g - 7
                    src = bank(bk)[:, :].rearrange("p (g n) -> p g n", n=64)[:, :, 0:nch]
                    if (g // 8) % 2 == 0:
                        P.op("act", "copy", reads=[PB[bk]], writes=[YGB], out=Yg[:, g0:g0 + 8, 0:nch], in_=src)
                    else:
                        P.op("dve", "tensor_copy", reads=[PB[bk]], writes=[YGB], out=Yg[:, g0:g0 + 8, 0:nch], in_=src)
            for kc in range(8):
                bk = kc % 2
                for tau in range(8):
                    for g8 in range(8):
                        MM(bank(bk)[:, tau * 64:tau * 64 + nch], selT[:, g8 * 8 + tau, :], Yg[:, kc * 8 + g8, 0:nch], g8 == 0,
                           g8 == 7, [YGB, TB], PB[bk])
                yv = yT[:, kc, 0:n].rearrange("p (n s) -> p s n", s=8)
                uv = uT[:, kc, 0:n].rearrange("p (n s) -> p s n", s=8)
                pv = bank(bk)[:, :].rearrange("p (s n) -> p s n", n=64)[:, :, 0:nch]
                P.op("dve", "scalar_tensor_tensor", reads=[PB[bk], UTB, TB], writes=[YTB], out=yv, in0=uv, scalar=dfm[:, kc:kc + 1],
                     in1=pv, op0=ALU.mult, op1=ALU.add)
            P.dma("act", k.S["yT"][:, :, t0:t0 + n].rearrange("c p t -> p c t"), yT[:, :, 0:n], reads=[YTB], writes=[Buf()])

        stageA(0)
        for ti in range(len(seqs)):
            if ti + 1 < len(seqs):
                load_u(ti + 1)
            stageB(ti)
            if ti + 1 < len(seqs):
                stageA(ti + 1)
            stageC(ti)
            P.maybe_flush()
        P.flush()
```

```python
import contextlib
import numpy as np
import ml_dtypes
import concourse.bass as bass
import concourse.mybir as mybir
from concourse.bass_utils import run_bass_kernel_spmd

F32 = mybir.dt.float32
BF16 = mybir.dt.bfloat16
AF = mybir.ActivationFunctionType
ALU = mybir.AluOpType
AX = mybir.AxisListType

D = 1024
T = 8192
TS = 32
PAST = 4096
TALL = T + 128
TT = 512
EPS = 1e-6
AB_IN = 3592
NCORES = 8


class Buf:
    __slots__ = ("name", "w", "r", "pw")

    def __init__(self, name=""):
        self.name = name
        self.w = {}
        self.r = {}
        self.pw = {}


class Op:
    __slots__ = ("eng", "fn", "deps", "marked", "sem", "val", "is_dma", "slotwait", "phase")

    def __init__(self, eng, fn, is_dma, phase):
        self.eng = eng
        self.fn = fn
        self.deps = []
        self.marked = False
        self.sem = None
        self.val = 0
        self.is_dma = is_dma
        self.slotwait = None
        self.phase = phase


class Prog:
    ENGS = ("pe", "act", "dve", "pool", "sp")
    NSLOT = 8

    def __init__(self, nc, stack):
        self.nc = nc
        self.ops = []
        self.phase = 0
        self.csem = {e: stack.enter_context(nc.semaphore("c_" + e)) for e in ("pe", "act", "dve", "pool")}
        self.dsem = {e: [stack.enter_context(nc.semaphore("d_%s%d" % (e, i))) for i in range(self.NSLOT)]
                     for e in ("sp", "act", "pool")}
        self.ccount = {e: 0 for e in self.csem}
        self.dcount = {e: 0 for e in self.dsem}
        self.waited = {e: {} for e in self.ENGS}
        self.dma_final = {}
        self.n_emitted = 0

    def _add(self, op, reads, writes):
        deps = {}
        for b in reads:
            for o in b.w.values():
                deps[id(o)] = o
        key = id(op) if op.is_dma else op.eng
        for b in writes:
            merge = op.is_dma and b.w and not b.r and all(o.is_dma and o.phase == self.phase for o in b.w.values())
            if merge:
                for o in getattr(b, "pw", {}).values():
                    deps[id(o)] = o
                b.w[key] = op
                continue
            prev = {}
            for o in b.w.values():
                deps[id(o)] = o
                prev[id(o)] = o
            for o in b.r.values():
                deps[id(o)] = o
                prev[id(o)] = o
            b.pw = prev
            b.w = {key: op}
            b.r = {}
        for b in reads:
            b.r[key] = op
        for o in deps.values():
            if o is op or o.phase != self.phase:
                continue
            if (not o.is_dma) and (not op.is_dma) and o.eng == "pe" and op.eng == "pe":
                continue
            op.deps.append(o)
            o.marked = True
        self.ops.append(op)
        return op

    def op(self, eng, name, reads=(), writes=(), **kw):
        def fn(e):
            return getattr(e, name)(**kw)
        return self._add(Op(eng, fn, False, self.phase), reads, writes)

    def dma(self, eng, out, in_, reads=(), writes=(), **kw):
        def fn(e):
            return e.dma_start(out=out, in_=in_, **kw)
        o = Op(eng, fn, True, self.phase)
        o.marked = True
        return self._add(o, reads, writes)

    def maybe_flush(self, limit=2500):
        if len(self.ops) >= limit:
            self.flush()

    def flush(self):
        if not self.ops:
            return
        nc = self.nc
        per = {e: [] for e in self.ENGS}
        last_c = {}
        for o in self.ops:
            per[o.eng].append(o)
            if not o.is_dma:
                last_c[o.eng] = o
        for o in last_c.values():
            o.marked = True
        for o in self.ops:
            if o.is_dma:
                i = self.dcount[o.eng]
                self.dcount[o.eng] += 1
                o.sem = self.dsem[o.eng][i % self.NSLOT]
                o.val = 16 * (i // self.NSLOT + 1)
                if i >= self.NSLOT:
                    o.slotwait = (o.sem, o.val - 16)
                self.dma_final[id(o.sem)] = (o.sem, o.val)
            elif o.marked:
                self.ccount[o.eng] += 1
                o.sem = self.csem[o.eng]
                o.val = self.ccount[o.eng]
        finals = list(self.dma_final.values()) + [(self.csem[e], self.ccount[e]) for e in self.csem]
        self.n_emitted += len(self.ops)

        def make(ename, ops):
            waited = self.waited[ename]

            def body(e):
                def w(sem, val):
                    if val > 0 and waited.get(id(sem), 0) < val:
                        e.wait_ge(sem, val)
                        waited[id(sem)] = val
                for o in ops:
                    if o.slotwait is not None:
                        w(*o.slotwait)
                    for d in o.deps:
                        w(d.sem, d.val)
                    ins = o.fn(e)
                    if o.marked:
                        ins.then_inc(o.sem, 16 if o.is_dma else 1)
                for sem, val in finals:
                    w(sem, val)
            return body

        with nc.allow_low_precision(reason="bf16 matmul operands by design"), nc.Block() as blk:
            blk.tensor(make("pe", per["pe"]))
            blk.scalar(make("act", per["act"]))
            blk.vector(make("dve", per["dve"]))
            blk.gpsimd(make("pool", per["pool"]))
            blk.sync(make("sp", per["sp"]))
        self.ops = []
        self.phase += 1


def _consts():
    c = {}
    c["ident_f"] = np.eye(128, dtype=np.float32)
    c["ident_b"] = np.eye(128, dtype=np.float32).astype(ml_dtypes.bfloat16)
    j = np.arange(128)
    c["tri_f"] = (j[:, None] <= j[None, :]).astype(np.float32)
    c["ones_f"] = np.ones((128, 128), np.float32)
    c["ones_b"] = np.ones((128, 128), np.float32).astype(ml_dtypes.bfloat16)
    perm = np.zeros((128, 128), np.float32)
    perm[(j + 64) % 128, j] = 1.0
    c["perm_b"] = perm.astype(ml_dtypes.bfloat16)
    half = 64
    inv = (10000.0 ** (-np.arange(half, dtype=np.float32) / half)).astype(np.float32)
    pos = np.concatenate([np.arange(T), PAST + np.arange(128)]).astype(np.float32)
    ang = pos[None, :] * inv[:, None]
    cos = np.cos(ang).astype(np.float32)
    sin = np.sin(ang).astype(np.float32)
    c["ropec"] = np.concatenate([cos, cos], 0)
    c["ropes"] = np.concatenate([-sin, sin], 0)
    lg = np.log(1.0 - 2.0 ** (-5.0 - np.arange(4, dtype=np.float64)))
    il = np.arange(512)[None, :]
    jl = np.arange(128)[:, None]
    rmask = np.zeros((4, 128, 512), np.float64)
    for h in range(4):
        m = np.exp(lg[h] * np.abs(il - jl))
        ok = (jl // 64) <= (il // 64)
        rmask[h] = np.where(ok, m, 0.0)
    c["rmask"] = rmask.transpose(1, 0, 2).astype(np.float32).copy()
    ii = np.arange(512).reshape(4, 128)
    wq = np.exp(lg[None, None, :] * (ii.T[:, :, None] + 1.0))
    c["wq"] = np.repeat(wq[:, :, :, None], 128, 3).astype(np.float32)
    wk = np.exp(lg[None, None, :] * (511.0 - ii.T[:, :, None]))
    c["wk"] = np.repeat(wk[:, :, :, None], 128, 3).astype(np.float32)
    wks = np.exp(lg[None, :] * (31.0 - np.arange(128)[:, None]))
    wks[32:] = 0.0
    c["wks"] = np.repeat(wks[:, :, None], 128, 2).astype(np.float32)
    c["rdec"] = np.stack([np.exp(lg * 512.0), np.exp(lg * 32.0)]).astype(np.float32)
    qs = np.ones((128, 1), np.float32)
    qs[:64] = 0.125
    c["qscale"] = qs
    selc = np.zeros((8, 8, 3, 3), np.float32)
    for h in range(8):
        for p_ in range(3):
            selc[h, h, p_, p_] = 1.0
    c["selc"] = selc.astype(ml_dtypes.bfloat16)
    c["caus_b"] = (j[:, None] <= j[None, :]).astype(np.float32).astype(ml_dtypes.bfloat16)
    kv = np.concatenate([-(np.arange(8) + 1.0), np.arange(8) + 1.0]).astype(np.float32)
    c["kvec"] = np.repeat(kv[None, :], 128, 0).copy()
    c["nvec"] = np.repeat((8.0 * (np.arange(64) + 1.0)).astype(np.float32)[None, :], 128, 0).copy()
    sel = np.zeros((128, 8, 8, 128), np.float32)
    selT = np.zeros((128, 8, 8, 128), np.float32)
    for g_ in range(8):
        for s_ in range(8):
            for c_ in range(16):
                sel[g_ * 16 + c_, g_, s_, s_ * 16 + c_] = 1.0
                selT[s_ * 16 + c_, g_, s_, g_ * 16 + c_] = 1.0
    c["sel_b"] = sel.reshape(128, 64, 128).astype(ml_dtypes.bfloat16)
    c["selT_b"] = selT.reshape(128, 64, 128).astype(ml_dtypes.bfloat16)
    sg = np.arange(128) // 16
    tm_ = (sg[None, :] >= sg[:, None]).astype(np.float32)
    c["tmask"] = np.tile(tm_, (1, 4)).copy()
    return c


CONST = None


def get_consts():
    global CONST
    if CONST is None:
        CONST = _consts()
    return CONST


IN_SHAPES = {
    "x_prompt": [T, D], "x_sample": [TS, D], "cache_ret_state": [4, 128, 128],
    "cache_fox_k": [PAST, 512], "cache_fox_v": [PAST, 512], "cache_fox_logf": [PAST, 8],
    "state_s5_re": [64, 64], "state_s5_im": [64, 64],
    "cache_mem_k": [2, 256, 1024], "cache_mem_v": [2, 256, 1024], "mem_prompt": [256, 1024],
    "norm_w": [2, 6, D], "w_in_ab": [D, AB_IN], "b_fox_f": [1, 8], "ret_gn_w": [1, 512],
    "w_out_ab": [D, D], "w_in_c": [D, D], "s5_lambda_re": [64, 64], "s5_lambda_im": [64, 64],
    "s5_log_dt": [1, 64], "s5_b_re": [64, 64, 16], "s5_b_im": [64, 64, 16],
    "s5_c_re": [64, 16, 64], "s5_c_im": [64, 16, 64], "s5_d": [1, D], "w_glu": [D, 2 * D],
    "mem_norm_w": [2, D], "w_xq": [2, D, D], "w_xk": [2, D, D], "w_xv": [2, D, D], "w_xo": [2, D, D],
    "w_up": [2, D, 4 * D], "w_down": [2, 4 * D, D],
}
OUT_SHAPES = {
    "y_prompt": [T, D], "y_sample": [TS, D], "ret_state_prompt": [4, 128, 128], "ret_state_sample": [4, 128, 128],
    "fox_k_prompt": [T, 512], "fox_v_prompt": [T, 512], "fox_logf_prompt": [T, 8],
    "fox_k_sample": [TS, 512], "fox_v_sample": [TS, 512], "fox_logf_sample": [TS, 8],
    "s5_re_prompt": [64, 64], "s5_im_prompt": [64, 64], "s5_re_sample": [64, 64], "s5_im_sample": [64, 64],
    "mem_k_prompt": [2, 256, 1024], "mem_v_prompt": [2, 256, 1024],
}
OUT_ORDER = list(OUT_SHAPES.keys())


class K:
    def __init__(self):
        self.nc = bass.Bass("TRN2", target_bir_lowering=False)
        nc = self.nc
        self.I = {k: nc.dram_tensor(k, s, F32, kind="ExternalInput").ap() for k, s in IN_SHAPES.items()}
        self.O = {k: nc.dram_tensor(k, s, F32, kind="ExternalOutput").ap() for k, s in OUT_SHAPES.items()}
        self.C = {}
        for k, v in get_consts().items():
            dt = BF16 if v.dtype == ml_dtypes.bfloat16 else F32
            self.C[k] = nc.dram_tensor("c_" + k, list(v.shape), dt, kind="ExternalInput").ap()
        self.S = {}
        self.SB = {}

    def scratch(self, name, shape, dt):
        self.S[name] = self.nc.dram_tensor("s_" + name, shape, dt, kind="Internal").ap()
        self.SB[name] = {}
        return self.S[name]

    def sbuf(self, st, name, shape, dt):
        return st.enter_context(self.nc.sbuf_tensor(name, shape, dt))


def tiles():
    out = [(t * TT, TT, False) for t in range(T // TT)]
    out.append((T, TS, True))
    return out


def xsrc(k, which, t0, n, sample):
    return k.I["x_sample"] if sample else k.I["x_prompt"][t0:t0 + n, :]


def blocks(n):
    return [(b * 128, min(128, n - b * 128)) for b in range((n + 127) // 128)]


def build(phases=("W", "A1")):
    k = K()
    nc = k.nc
    with contextlib.ExitStack() as top:
        P = Prog(nc, top)
        ps = top.enter_context(nc.psum_tensor("ps", [128, 4096], F32))
        PB = [Buf("ps%d" % i) for i in range(8)]

        def bank(i):
            return ps[:, i * 512:(i + 1) * 512]

        identf = k.sbuf(top, "identf", [128, 128], F32)
        identb = k.sbuf(top, "identb", [128, 128], BF16)
        onesf = k.sbuf(top, "onesf", [128, 128], F32)
        onesb = k.sbuf(top, "onesb", [128, 128], BF16)
        CB = Buf("consts")
        P.dma("sp", identf[:], k.C["ident_f"], writes=[CB])
        P.dma("sp", identb[:], k.C["ident_b"], writes=[CB])
        P.dma("sp", onesf[:], k.C["ones_f"], writes=[CB])
        P.dma("sp", onesb[:], k.C["ones_b"], writes=[CB])
        P.flush()

        WNAMES = [("w_in_ab", None, [D, AB_IN]), ("w_out_ab", None, [D, D]), ("w_in_c", None, [D, D]),
                  ("w_glu", None, [D, 2 * D])]
        for l in range(2):
            for nm in ("w_xq", "w_xk", "w_xv", "w_xo"):
                WNAMES.append((nm, l, [D, D]))
            WNAMES.append(("w_up", l, [D, 4 * D]))
            WNAMES.append(("w_down", l, [4 * D, D]))
        WB = {}
        for nm, l, shp in WNAMES:
            key = nm if l is None else "%s%d" % (nm, l)
            WB[key] = k.scratch("wb_" + key, shp, BF16)
        if "W" in phases:
            wbuf = Buf("wcast")
            for nm, l, shp in WNAMES:
                key = nm if l is None else "%s%d" % (nm, l)
                src = k.I[nm] if l is None else k.I[nm][l]
                rows = shp[0]
                step = 256
                for r0 in range(0, rows, step):
                    P.dma("pool", WB[key][r0:r0 + step, :], src[r0:r0 + step, :], writes=[Buf()])
            P.flush()

        k.scratch("x", [TALL, D], F32)
        k.scratch("rqT", [4, 128, TALL], BF16)
        k.scratch("rkT", [4, 128, TALL], BF16)
        k.scratch("rkk", [TALL, 512], BF16)
        k.scratch("rv", [TALL, 512], BF16)
        k.scratch("rg", [TALL, 512], BF16)
        k.scratch("fqa", [17, 4, 67, 2, TT], BF16)
        k.scratch("fka", [25, 4, 67, 2, TT], BF16)
        k.scratch("fva", [25, 4, 128, 4, 130], BF16)
        k.scratch("negc", [25, 128, 4, 8], F32)

        k.scratch("foT", [8, 64, TALL], BF16)
        k.scratch("uT", [8, 128, TALL], BF16)
        k.scratch("yT", [8, 128, TALL], BF16)
        env = dict(k=k, P=P, ps=ps, PB=PB, bank=bank, WB=WB, identf=identf, identb=identb, onesf=onesf, onesb=onesb)
        if "A1" in phases:
            phase_A1(k, P, ps, PB, bank, WB, identf, identb, onesf, onesb)
        if "A3" in phases:
            phase_A3(**env)
        if "A2" in phases:
            phase_A2(**env)
        for l in range(2):
            if "S" in phases and l == 1:
                phase_S1(**env)
                phase_S2(**env)
                phase_S3(**env)
            if ("C%d" % l) in phases:
                phase_C(l=l, **env)
            if ("F%d" % l) in phases:
                phase_F(l=l, final=(l == 1), **env)
        if "dumpx" in DBG:
            P.dma("sp", k.O["y_prompt"], k.S["x"][0:T, :], writes=[Buf()])
            P.dma("sp", k.O["y_sample"], k.S["x"][T:T + TS, :], writes=[Buf()])
            P.flush()
        if "dumpfo" in DBG:
            P.dma("pool", k.O["y_prompt"][0:512, :].rearrange("(h r) t -> h r t", h=8),
                  k.S["foT"][:, :, DBG["dumpfo"]:DBG["dumpfo"] + 1024], writes=[Buf()])
            P.flush()
    return k


CB_GLOBAL = Buf("cglobal")
DBG = {}


def phase_A1(k, P, ps, PB, bank, WB, identf, identb, onesf, onesb):
    with contextlib.ExitStack() as st:
        def sb(name, shape, dt):
            return k.sbuf(st, "a1_" + name, shape, dt)
        win = sb("win", [128, 8, AB_IN], BF16)
        g0 = sb("g0", [128, D], F32)
        bfox = sb("bfox", [128, 8], F32)
        perm = sb("perm", [128, 128], BF16)
        trif = sb("trif", [128, 128], F32)
        qscale = sb("qscale", [128, 1], F32)
        selc = sb("selc", [8, 72], BF16)
        wk = sb("wk", [128, 4, 512], F32)
        wks = sb("wks", [128, 512], F32)
        WBUF = Buf("a1w")
        wsrc = WB["w_in_ab"].rearrange("(kc p) n -> p kc n", p=128)
        for kc in range(8):
            P.dma("sp" if kc % 2 == 0 else "act", win[:, kc, :], wsrc[:, kc, :], writes=[WBUF])
        P.dma("sp", g0[:], k.I["norm_w"][0, 0:1, :].broadcast_to([128, D]), writes=[WBUF])
        P.dma("sp", bfox[:], k.I["b_fox_f"][0:1, :].broadcast_to([128, 8]), writes=[WBUF])
        P.dma("sp", perm[:], k.C["perm_b"], writes=[WBUF])
        P.dma("sp", trif[:], k.C["tri_f"], writes=[WBUF])
        P.dma("sp", qscale[:], k.C["qscale"], writes=[WBUF])
        P.dma("sp", selc[:], k.C["selc"].rearrange("k h p m -> k (h p m)"), writes=[WBUF])
        P.dma("sp", wk[:], k.C["wk"].rearrange("p b h e -> p b (h e)"), writes=[WBUF])
        P.dma("sp", wks[:], k.C["wks"].rearrange("p h e -> p (h e)"), writes=[WBUF])

        xt0 = sb("xt0", [128, 4, D], F32)
        xt = [xt0, xt0]
        XT0 = Buf("xt0")
        XT = [XT0, XT0]
        rc0 = sb("rc0", [128, TT], F32)
        rs0 = sb("rs0", [128, TT], F32)
        rc = [rc0, rc0]
        rs = [rs0, rs0]
        RT0 = Buf("rt0")
        RT = [RT0, RT0]
        junk = sb("junk", [128, D], BF16)
        JB = Buf("junk")
        ss = sb("ss", [128, 4], F32)
        rstd = sb("rstd", [128, 4], F32)
        SSB = Buf("ss")
        h = sb("h", [128, 4, D], BF16)
        HB = Buf("h")
        hT = sb("hT", [128, 8, TT], BF16)
        HTB = Buf("hT")
        qbs = [sb("qb%d" % i, [128, TT], BF16) for i in range(2)]
        QBs = [Buf("qb%d" % i) for i in range(2)]
        t1s = [sb("t1%d" % i, [128, TT], F32) for i in range(2)]
        t2s = [sb("t2%d" % i, [128, TT], F32) for i in range(2)]
        T1Bs = [Buf("t1%d" % i) for i in range(2)]
        T2Bs = [Buf("t2%d" % i) for i in range(2)]
        rqT = sb("rqT", [128, 8, TT], BF16)
        RQB = [Buf("rq%d" % i) for i in range(8)]
        rkk = sb("rkk", [128, 4, 512], BF16)
        RKKB = Buf("rkk")
        rv = sb("rv", [128, 4, 512], BF16)
        RVB = Buf("rv")
        rg = sb("rg", [128, 4, 512], BF16)
        RGB = Buf("rg")
        fkf = sb("fkf", [128, 4, 512], F32)
        FKFB = Buf("fkf")
        fkb = h
        FKBB = HB
        fvf = sb("fvf", [128, 4, 512], F32)
        FVFB = Buf("fvf")
        fva = sb("fva", [128, 4, 8 * 65], BF16)
        FVAB = Buf("fva")
        fka = sb("fka", [67, 8, TT], BF16)
        FKAB = Buf("fka")
        fqa = sb("fqa", [67, 8, TT], BF16)
        FQAB = Buf("fqa")
        lf = sb("lf", [128, 4, 8], F32)
        LFB = Buf("lf")
        xb = sb("xb", [128, 4, 8], F32)
        XBB = Buf("xb")
        cS = sb("cS", [128, 4, 8], F32)
        CSB = Buf("cS")
        ngc = sb("ngc", [128, 4, 8], F32)
        NGB = Buf("ngc")
        carry = sb("carry", [1, 8], F32)
        CARB = Buf("carry")
        cT = sb("cT", [8, TT], F32)
        c3 = sb("c3", [8, 3, TT], BF16)
        r1 = sb("r1", [8, TT], F32)
        r2 = cT
        CTB, C3B, R1B = Buf("cT"), Buf("c3"), Buf("r1")
        R2B = CTB

        epsc = sb("epsc", [128, 1], F32)
        P.op("pool", "memset", writes=[SSB], ap=epsc[:], constant=EPS)
        P.op("pool", "memset", writes=[FVAB], ap=fva[:], constant=1.0)
        P.op("pool", "memset", writes=[FKAB], ap=fka[64:67, :, :], constant=1.0)
        P.op("pool", "memset", writes=[CARB], ap=carry[:], constant=0.0)

        def MM(out, lhsT, rhs, start, stop, R, Wb):
            P.op("pe", "matmul", reads=R, writes=[Wb], out=out, lhsT=lhsT, rhs=rhs, start=start, stop=stop)

        seqs = [("prm", t * TT, TT) for t in range(T // TT)]
        seqs += [("past", pt * TT, TT) for pt in range(PAST // TT)]
        seqs += [("smp", T, TS)]
        if "a1_seqs" in DBG:
            seqs = [seqs[i] for i in DBG["a1_seqs"]]
        for ti, (kind, t0, n) in enumerate(seqs):
            s = ti % 2
            blks = blocks(n)
            nb = len(blks)
            pp = min(128, n)
            kcol0 = t0 if kind == "past" else PAST + t0

            def tm(ap2d):
                return ap2d.rearrange("(b p) d -> p b d", p=pp)
            if kind == "past" and t0 == 0:
                P.op("pool", "memset", writes=[CARB], ap=carry[:], constant=0.0)
            if kind == "past":
                P.dma("sp", fkf[:, :, :], tm(k.I["cache_fox_k"][t0:t0 + n, :]), writes=[FKFB])
                P.dma("act", fvf[:, :, :], tm(k.I["cache_fox_v"][t0:t0 + n, :]), writes=[FVFB])
                P.dma("sp", lf[:, :, :], tm(k.I["cache_fox_logf"][t0:t0 + n, :]), writes=[LFB])
            else:
                xs = k.I["x_sample"] if kind == "smp" else k.I["x_prompt"][t0:t0 + n, :]
                P.dma("sp", xt[s][0:pp, 0:nb, :], tm(xs), writes=[XT[s]])
                P.dma("act", rc[s][:, 0:n], k.C["ropec"][:, t0:t0 + n], writes=[RT[s]])
                P.dma("act", rs[s][:, 0:n], k.C["ropes"][:, t0:t0 + n], writes=[RT[s]])
                for b, (o, bs) in enumerate(blks):
                    P.op("act", "activation", reads=[XT[s]], writes=[JB, SSB], out=junk[0:bs, :], in_=xt[s][0:bs, b, :],
                         func=AF.Square, accum_out=ss[0:bs, b:b + 1])
                P.op("act", "activation", reads=[SSB], writes=[SSB], out=rstd[:, 0:nb], in_=ss[:, 0:nb], func=AF.Sqrt,
                     scale=1.0 / D, bias=epsc[:, 0:1])
                P.op("dve", "reciprocal", reads=[SSB], writes=[SSB], out=rstd[:, 0:nb], in_=rstd[:, 0:nb])
                for b, (o, bs) in enumerate(blks):
                    P.op("dve", "scalar_tensor_tensor", reads=[XT[s], SSB, WBUF], writes=[HB],
                         out=h[0:bs, b, :], in0=xt[s][0:bs, b, :], scalar=rstd[0:bs, b:b + 1], in1=g0[0:bs, :],
                         op0=ALU.mult, op1=ALU.mult)
                for kc in range(8):
                    bk = kc % 4
                    for b, (o, bs) in enumerate(blks):
                        MM(bank(bk)[:, o:o + bs], h[0:bs, b, kc * 128:(kc + 1) * 128], identb[0:bs, 0:bs], True, True,
                           [HB], PB[bk])
                    if kc % 2 == 0:
                        P.op("act", "copy", reads=[PB[bk]], writes=[HTB], out=hT[:, kc, 0:n], in_=bank(bk)[:, 0:n])
                    else:
                        P.op("dve", "tensor_copy", reads=[PB[bk]], writes=[HTB], out=hT[:, kc, 0:n], in_=bank(bk)[:, 0:n])

                def proj_fm(bk, c0, m):
                    for kc in range(8):
                        MM(bank(bk)[0:m, 0:n], win[:, kc, c0:c0 + m], hT[:, kc, 0:n], kc == 0, kc == 7,
                           [HTB, WBUF], PB[bk])

                def proj_tm(bk, o, bs, c0, ncols):
                    for kc in range(8):
                        MM(bank(bk)[0:bs, 0:ncols], hT[:, kc, o:o + bs], win[:, kc, c0:c0 + ncols], kc == 0, kc == 7,
                           [HTB, WBUF], PB[bk])

                for i8 in range(8):
                    bk = 4 + (i8 % 2)
                    qb, QB = qbs[i8 % 2], QBs[i8 % 2]
                    t1, t2, T1B, T2B = t1s[i8 % 2], t2s[i8 % 2], T1Bs[i8 % 2], T2Bs[i8 % 2]
                    pbk = 6 if i8 % 2 == 0 else 7
                    sc = 1.0 if i8 < 4 else 128.0 ** -0.5
                    proj_fm(bk, i8 * 128, 128)
                    P.op("act", "activation", reads=[PB[bk]], writes=[QB], out=qb[:, 0:n], in_=bank(bk)[:, 0:n],
                         func=AF.Copy, scale=sc)
                    MM(bank(pbk)[:, 0:n], perm[:, :], qb[:, 0:n], True, True, [QB, WBUF], PB[pbk])
                    P.op("pool", "tensor_tensor", reads=[QB, RT[s]], writes=[T1B], out=t1[:, 0:n], in0=qb[:, 0:n],
                         in1=rc[s][:, 0:n], op=ALU.mult)
                    P.op("dve", "tensor_tensor", reads=[PB[pbk], RT[s]], writes=[T2B], out=t2[:, 0:n], in0=bank(pbk)[:, 0:n],
                         in1=rs[s][:, 0:n], op=ALU.mult)
                    P.op("dve", "tensor_tensor", reads=[T1B, T2B], writes=[RQB[i8]], out=rqT[:, i8, 0:n], in0=t1[:, 0:n],
                         in1=t2[:, 0:n], op=ALU.add)
                    dst = k.S["rqT"][i8, :, t0:t0 + n] if i8 < 4 else k.S["rkT"][i8 - 4, :, t0:t0 + n]
                    P.dma("sp", dst, rqT[:, i8, 0:n], reads=[RQB[i8]], writes=[Buf()])
                for b, (o, bs) in enumerate(blks):
                    bk = b % 2
                    for hh in range(4):
                        MM(bank(bk)[0:bs, hh * 128:(hh + 1) * 128], rqT[:, 4 + hh, o:o + bs], identb[:, :], True, True,
                           [RQB[4 + hh]], PB[bk])
                    wtab = wks[0:bs, :] if kind == "smp" else wk[0:bs, b, :]
                    P.op("dve", "tensor_tensor", reads=[PB[bk], WBUF], writes=[RKKB], out=rkk[0:bs, b, :],
                         in0=bank(bk)[0:bs, :], in1=wtab, op=ALU.mult)
                P.dma("act", tm(k.S["rkk"][t0:t0 + n, :]), rkk[0:pp, 0:nb, :], reads=[RKKB], writes=[Buf()])
                for b, (o, bs) in enumerate(blks):
                    proj_tm(2, o, bs, 1024, 512)
                    P.op("act", "copy", reads=[PB[2]], writes=[RVB], out=rv[0:bs, b, :], in_=bank(2)[0:bs, :])
                    proj_tm(3, o, bs, 1536, 512)
                    P.op("act", "activation", reads=[PB[3]], writes=[RGB], out=rg[0:bs, b, :], in_=bank(3)[0:bs, :],
                         func=AF.Silu)
                    proj_tm(2, o, bs, 2560, 512)
                    P.op("act", "copy", reads=[PB[2]], writes=[FKFB], out=fkf[0:bs, b, :], in_=bank(2)[0:bs, :])
                    proj_tm(3, o, bs, 3072, 512)
                    P.op("dve", "tensor_copy", reads=[PB[3]], writes=[FVFB], out=fvf[0:bs, b, :], in_=bank(3)[0:bs, :])
                    proj_tm(7, o, bs, 3584, 8)
                    P.op("dve", "tensor_tensor", reads=[PB[7], WBUF], writes=[XBB], out=xb[0:bs, b, :],
                         in0=bank(7)[0:bs, 0:8], in1=bfox[0:bs, :], op=ALU.add)
                P.dma("act", tm(k.S["rv"][t0:t0 + n, :]), rv[0:pp, 0:nb, :], reads=[RVB], writes=[Buf()])
                P.dma("act", tm(k.S["rg"][t0:t0 + n, :]), rg[0:pp, 0:nb, :], reads=[RGB], writes=[Buf()])
                P.op("act", "activation", reads=[XBB], writes=[XBB], out=xb[0:pp, 0:nb, :], in_=xb[0:pp, 0:nb, :],
                     func=AF.Exp, scale=-1.0)
                P.op("act", "activation", reads=[XBB], writes=[XBB], out=xb[0:pp, 0:nb, :], in_=xb[0:pp, 0:nb, :],
                     func=AF.Ln, bias=1.0)
                P.op("dve", "tensor_scalar", reads=[XBB], writes=[LFB], out=lf[0:pp, 0:nb, :], in0=xb[0:pp, 0:nb, :],
                     scalar1=-1.0, scalar2=None, op0=ALU.mult)
                ko = k.O["fox_k_sample"] if kind == "smp" else k.O["fox_k_prompt"][t0:t0 + n, :]
                vo = k.O["fox_v_sample"] if kind == "smp" else k.O["fox_v_prompt"][t0:t0 + n, :]
                lo = k.O["fox_logf_sample"] if kind == "smp" else k.O["fox_logf_prompt"][t0:t0 + n, :]
                P.dma("sp", tm(ko), fkf[0:pp, 0:nb, :], reads=[FKFB], writes=[Buf()])
                P.dma("sp", tm(vo), fvf[0:pp, 0:nb, :], reads=[FVFB], writes=[Buf()])
                P.dma("sp", tm(lo), lf[0:pp, 0:nb, :], reads=[LFB], writes=[Buf()])
            P.op("pool", "tensor_copy", reads=[FKFB], writes=[FKBB], out=fkb[0:pp, 0:nb, 0:512], in_=fkf[0:pp, 0:nb, :])
            P.op("pool", "tensor_copy", reads=[FVFB], writes=[FVAB],
                 out=fva[0:pp, 0:nb, :].rearrange("p b (h e) -> p b h e", e=65)[:, :, :, 0:64],
                 in_=fvf[0:pp, 0:nb, :].rearrange("p b (h e) -> p b h e", e=64))
            ktile = (t0 // TT) if kind == "past" else (8 + t0 // TT)
            for hp in range(4):
                P.dma("act", k.S["fva"][ktile, hp, 0:pp, 0:nb, :], fva[0:pp, 0:nb, 130 * hp:130 * hp + 130], reads=[FVAB],
                      writes=[Buf()])
            for hh in range(8):
                bk = hh % 2
                for b, (o, bs) in enumerate(blks):
                    MM(bank(bk)[0:64, o:o + bs], fkb[0:bs, b, hh * 64:(hh + 1) * 64], identb[0:bs, 0:bs], True, True,
                       [FKBB], PB[bk])
                if hh % 2 == 0:
                    P.op("act", "copy", reads=[PB[bk]], writes=[FKAB], out=fka[0:64, hh, 0:n], in_=bank(bk)[0:64, 0:n])
                else:
                    P.op("dve", "tensor_copy", reads=[PB[bk]], writes=[FKAB], out=fka[0:64, hh, 0:n],
                         in_=bank(bk)[0:64, 0:n])
            for hp in range(4):
                P.dma("sp", k.S["fka"][ktile, hp, :, :, 0:n], fka[:, 2 * hp:2 * hp + 2, 0:n], reads=[FKAB], writes=[Buf()])
            for b, (o, bs) in enumerate(blks):
                mm = [(trif[0:bs, 0:bs], lf[0:bs, b, :])]
                for b2 in range(b):
                    mm.append((onesf[0:128, 0:bs], lf[0:128, b2, :]))
                mm.append((onesf[0:1, 0:bs], carry[0:1, :]))
                for i, (l_, r_) in enumerate(mm):
                    MM(bank(7)[0:bs, 8 + b * 8:16 + b * 8], l_, r_, i == 0, i == len(mm) - 1, [LFB, CARB, WBUF], PB[7])
            cview = bank(7)[0:pp, 8:8 + nb * 8].rearrange("p (b h) -> p b h", h=8)
            P.op("dve", "tensor_copy", reads=[PB[7]], writes=[CSB], out=cS[0:pp, 0:nb, :], in_=cview)
            P.op("act", "activation", reads=[PB[7]], writes=[NGB], out=ngc[0:pp, 0:nb, :], in_=cview, func=AF.Copy,
                 scale=-1.0)
            P.dma("act", k.S["negc"][ktile, 0:pp, 0:nb, :], ngc[0:pp, 0:nb, :], reads=[NGB], writes=[Buf()])
            for b, (o, bs) in enumerate(blks):
                MM(bank(7)[0:1, 48:56], onesf[0:bs, 0:1], lf[0:bs, b, :], b == 0, b == nb - 1, [LFB], PB[7])
            P.op("dve", "tensor_tensor", reads=[PB[7], CARB], writes=[CARB], out=carry[:], in0=carry[:],
                 in1=bank(7)[0:1, 48:56], op=ALU.add)
            if kind == "past":
                continue
            for b, (o, bs) in enumerate(blks):
                MM(bank(6)[0:8, o:o + bs], cS[0:bs, b, :], identf[0:bs, 0:bs], True, True, [CSB], PB[6])
            P.op("dve", "tensor_copy", reads=[PB[6]], writes=[CTB], out=cT[:, 0:n], in_=bank(6)[0:8, 0:n])
            P.op("dve", "tensor_copy", reads=[CTB], writes=[C3B], out=c3[:, 0, 0:n], in_=cT[:, 0:n])
            P.op("dve", "tensor_tensor", reads=[CTB, C3B], writes=[R1B], out=r1[:, 0:n], in0=cT[:, 0:n],
                 in1=c3[:, 0, 0:n], op=ALU.subtract)
            P.op("dve", "tensor_copy", reads=[R1B], writes=[C3B], out=c3[:, 1, 0:n], in_=r1[:, 0:n])
            P.op("dve", "tensor_tensor", reads=[R1B, C3B], writes=[R2B], out=r2[:, 0:n], in0=r1[:, 0:n],
                 in1=c3[:, 1, 0:n], op=ALU.subtract)
            P.op("dve", "tensor_copy", reads=[R2B], writes=[C3B], out=c3[:, 2, 0:n], in_=r2[:, 0:n])
            for hh in range(8):
                bk = 4 + (hh % 2)
                proj_fm(bk, 2048 + hh * 64, 64)
                for p_ in range(3):
                    i0 = (hh * 3 + p_) * 3
                    MM(bank(bk)[64:67, 0:n], selc[0:8, i0:i0 + 3], c3[0:8, p_, 0:n], p_ == 0, p_ == 2, [C3B, WBUF], PB[bk])
                P.op("act", "activation", reads=[PB[bk], WBUF], writes=[FQAB], out=fqa[0:67, hh, 0:n],
                     in_=bank(bk)[0:67, 0:n], func=AF.Identity, scale=qscale[0:67, 0:1])
            for hp in range(4):
                P.dma("sp", k.S["fqa"][t0 // TT, hp, :, :, 0:n], fqa[:, 2 * hp:2 * hp + 2, 0:n], reads=[FQAB], writes=[Buf()])
            P.maybe_flush()
        P.flush()


ALL_PHASES = ("W", "A1", "A3", "A2", "C0", "F0", "S", "C1", "F1")


def core_inputs(inp, c):
    f = np.ascontiguousarray
    m = {
        "x_prompt": inp["x_prompt"][c], "x_sample": inp["x_sample"][c],
        "cache_ret_state": inp["cache_ret_state"][0, c],
        "cache_fox_k": inp["cache_fox_k"][0, c].reshape(PAST, 512),
        "cache_fox_v": inp["cache_fox_v"][0, c].reshape(PAST, 512),
        "cache_fox_logf": inp["cache_fox_logf"][0, c],
        "state_s5_re": inp["state_s5_re"][0, c], "state_s5_im": inp["state_s5_im"][0, c],
        "cache_mem_k": inp["cache_mem_k"][:, c].reshape(2, 256, 1024),
        "cache_mem_v": inp["cache_mem_v"][:, c].reshape(2, 256, 1024),
        "mem_prompt": inp["mem_prompt"][c],
        "norm_w": inp["norm_w"], "w_in_ab": inp["w_in_ab"][0], "b_fox_f": inp["b_fox_f"].reshape(1, 8),
        "ret_gn_w": inp["ret_gn_w"].reshape(1, 512), "w_out_ab": inp["w_out_ab"][0], "w_in_c": inp["w_in_c"][0],
        "s5_lambda_re": inp["s5_lambda_re"][0], "s5_lambda_im": inp["s5_lambda_im"][0],
        "s5_log_dt": inp["s5_log_dt"].reshape(1, 64), "s5_b_re": inp["s5_b_re"][0], "s5_b_im": inp["s5_b_im"][0],
        "s5_c_re": inp["s5_c_re"][0], "s5_c_im": inp["s5_c_im"][0], "s5_d": inp["s5_d"].reshape(1, D),
        "w_glu": inp["w_glu"][0], "mem_norm_w": inp["mem_norm_w"],
        "w_xq": inp["w_xq"], "w_xk": inp["w_xk"], "w_xv": inp["w_xv"], "w_xo": inp["w_xo"],
        "w_up": inp["w_up"], "w_down": inp["w_down"],
    }
    m = {k_: f(np.asarray(v, dtype=np.float32)) for k_, v in m.items()}
    for k_, v in get_consts().items():
        m["c_" + k_] = v
    return m


def run(inputs, phases=ALL_PHASES, cores=NCORES):
    inp = {k_: np.asarray(v) for k_, v in inputs.items()}
    kk = build(phases)
    in_maps = [core_inputs(inp, c) for c in range(cores)]
    res = run_bass_kernel_spmd(kk.nc, in_maps, core_ids=list(range(cores)))
    return res.results


def kernel(**inputs):
    r = run(inputs)
    B = NCORES

    def st(name, shape):
        return np.stack([np.asarray(r[c][name], dtype=np.float32).reshape(shape) for c in range(B)])
    y_prompt = st("y_prompt", (T, D))
    y_sample = st("y_sample", (TS, D))
    ret_p = st("ret_state_prompt", (4, 128, 128))[None]
    ret_s = st("ret_state_sample", (4, 128, 128))[None]
    fkp = st("fox_k_prompt", (T, 8, 64))[None]
    fvp = st("fox_v_prompt", (T, 8, 64))[None]
    flp = st("fox_logf_prompt", (T, 8))[None]
    fks = st("fox_k_sample", (TS, 8, 64))[None]
    fvs = st("fox_v_sample", (TS, 8, 64))[None]
    fls = st("fox_logf_sample", (TS, 8))[None]
    s5rp = st("s5_re_prompt", (64, 64))[None]
    s5ip = st("s5_im_prompt", (64, 64))[None]
    s5rs = st("s5_re_sample", (64, 64))[None]
    s5is = st("s5_im_sample", (64, 64))[None]
    mkp = np.stack([np.asarray(r[c]["mem_k_prompt"], dtype=np.float32).reshape(2, 256, 4, 256) for c in range(B)], 1)
    mvp = np.stack([np.asarray(r[c]["mem_v_prompt"], dtype=np.float32).reshape(2, 256, 4, 256) for c in range(B)], 1)
    return (y_prompt, y_sample, ret_p, ret_s, fkp, fvp, flp, fks, fvs, fls, s5rp, s5ip, s5rs, s5is, mkp, mvp)


def phase_A3(k, P, ps, PB, bank, WB, identf, identb, onesf, onesb):
    with contextlib.ExitStack() as st:
        def sb(name, shape, dt):
            return k.sbuf(st, "a3_" + name, shape, dt)
        caus = sb("caus", [128, 128], BF16)
        CW = Buf("a3c")
        P.dma("sp", caus[:], k.C["caus_b"], writes=[CW])
        qas = [sb("qa%d" % i, [67, 2, 2 * TT], BF16) for i in range(2)]
        QABs = [Buf("qa%d" % i) for i in range(2)]
        NKV = 4
        ka = [sb("ka%d" % i, [67, 2, TT], BF16) for i in range(2 * NKV)]
        va = [sb("va%d" % i, [128, 4, 2 * 65], BF16) for i in range(2 * NKV)]
        ng = [sb("ng%d" % i, [128, 4, 8], F32) for i in range(2 * NKV)]
        KVB = [Buf("kk%d" % i) for i in range(2 * NKV)]
        VVB = [Buf("kvv%d" % i) for i in range(2 * NKV)]
        NNB = [Buf("kn%d" % i) for i in range(2 * NKV)]
        pT = [sb("pT%d" % i, [128, TT], BF16) for i in range(4)]
        PTB = [Buf("pT%d" % i) for i in range(4)]
        osbs = [sb("osb%d" % i, [65, TT], F32) for i in range(4)]
        OSBs = [Buf("osb%d" % i) for i in range(4)]
        rrows = [sb("rrow%d" % i, [65, TT], BF16) for i in range(4)]
        RRBs = [Buf("rrow%d" % i) for i in range(4)]
        foT = sb("foT", [64, 4, TT], BF16)
        FOB = Buf("foT")

        def MM(out, lhsT, rhs, start, stop, R, Wb):
            P.op("pe", "matmul", reads=R, writes=[Wb], out=out, lhsT=lhsT, rhs=rhs, start=start, stop=stop)

        supers = []
        for Q in range(T // (2 * TT)):
            subs = [(2 * Q + s_, (2 * Q + s_) * TT, TT) for s_ in range(2)]
            ktl = [(8 + kt, TT, kt) for kt in range(2 * Q + 2)]
            supers.append(("prm", subs, ktl))
        supers.append(("smp", [(16, T, TS)], [(pt, TT, -1) for pt in range(PAST // TT)] + [(24, TS, 16)]))
        if "a3_q" in DBG:
            supers = [supers[i] for i in DBG["a3_q"]]
        groups = [(kind, subs, ktl, hp) for (kind, subs, ktl) in supers for hp in range(4)]
        pti = 0

        def load_kv(gi, kti):
            kind, subs, ktl, hp = groups[gi]
            ktile, nk, kpos = ktl[kti]
            sl = (gi % 2) * NKV + kti % NKV
            pk = min(128, nk)
            nkb = (nk + 127) // 128
            P.dma("sp", ka[sl][:, :, 0:nk], k.S["fka"][ktile, hp, :, :, 0:nk], writes=[KVB[sl]])
            P.dma("pool", va[sl][0:pk, 0:nkb, :], k.S["fva"][ktile, hp, 0:pk, 0:nkb, :], writes=[VVB[sl]])
            P.dma("pool", ng[sl][0:pk, 0:nkb, :], k.S["negc"][ktile, 0:pk, 0:nkb, :], writes=[NNB[sl]])

        def load_q(gi):
            kind, subs, ktl, hp = groups[gi]
            for s_, (qidx, t0, n) in enumerate(subs):
                P.dma("sp", qas[gi % 2][:, :, s_ * TT:s_ * TT + n], k.S["fqa"][qidx, hp, :, :, 0:n], writes=[QABs[gi % 2]])

        def prefetch(gi):
            if gi < len(groups):
                load_q(gi)
                for kti in range(min(NKV, len(groups[gi][2]))):
                    load_kv(gi, kti)
        prefetch(0)
        for gi, (kind, subs, ktl, hp) in enumerate(groups):
            qa, QAB = qas[gi % 2], QABs[gi % 2]
            prefetch(gi + 1)
            steps = []
            for kti, (ktile, nk, kpos) in enumerate(ktl):
                kblks = blocks(nk)
                for hh in range(2):
                    for jb, (jo, js) in enumerate(kblks):
                        for s_, (qidx, t0, n) in enumerate(subs):
                            if kpos > qidx:
                                continue
                            steps.append((kti, hh, jb, jo, js, kpos == qidx, s_, n))
            firsts, lasts = {}, {}
            for i, stp in enumerate(steps):
                a_ = stp[1] * 2 + stp[6]
                firsts.setdefault(a_, i)
                lasts[a_] = i
            LOOK = 2
            info = {}
            for i in range(len(steps) + LOOK):
                if i < len(steps):
                    kti, hh, jb, jo, js, diag, s_, n = steps[i]
                    sl = (gi % 2) * NKV + kti % NKV
                    h8 = 2 * hp + hh
                    c0 = jo if diag else 0
                    sbk = 4 + (pti % 4)
                    pb = pti % 4
                    pti += 1
                    info[i] = (sl, pb, c0)
                    MM(bank(sbk)[0:js, c0:n], ka[sl][0:67, hh, jo:jo + js], qa[0:67, hh, s_ * TT + c0:s_ * TT + n], True, True,
                       [KVB[sl], QAB], PB[sbk])
                    P.op("act", "activation", reads=[PB[sbk], NNB[sl]], writes=[PTB[pb]], out=pT[pb][0:js, c0:n],
                         in_=bank(sbk)[0:js, c0:n], func=AF.Exp, bias=ng[sl][0:js, jb, h8:h8 + 1], scale=1.0)
                    if diag:
                        P.op("pool", "tensor_tensor", reads=[PTB[pb], CW], writes=[PTB[pb]],
                             out=pT[pb][0:js, c0:c0 + js], in0=pT[pb][0:js, c0:c0 + js], in1=caus[0:js, 0:js], op=ALU.mult)
                j = i - LOOK
                if j >= 0:
                    kti, hh, jb, jo, js, diag, s_, n = steps[j]
                    sl, pb, c0 = info[j]
                    a_ = hh * 2 + s_
                    MM(bank(a_)[0:65, c0:n], va[sl][0:js, jb, hh * 65:(hh + 1) * 65], pT[pb][0:js, c0:n], firsts[a_] == j,
                       lasts[a_] == j, [VVB[sl], PTB[pb]], PB[a_])
                    if (j + 1 == len(steps) or steps[j + 1][0] != kti) and kti + NKV < len(ktl):
                        load_kv(gi, kti + NKV)
            accs = [(hh, s_, subs[s_][1], subs[s_][2]) for hh in range(2) for s_ in range(len(subs))]
            for (hh, s_, t0, n) in accs:
                a_ = hh * 2 + s_
                P.op("act", "copy", reads=[PB[a_]], writes=[OSBs[a_]], out=osbs[a_][0:65, 0:n], in_=bank(a_)[0:65, 0:n])
            for (hh, s_, t0, n) in accs:
                a_ = hh * 2 + s_
                P.op("dve", "reciprocal", reads=[OSBs[a_]], writes=[RRBs[a_]], out=rrows[a_][64:65, 0:n],
                     in_=osbs[a_][64:65, 0:n])
            for (hh, s_, t0, n) in accs:
                a_ = hh * 2 + s_
                MM(bank(4 + a_)[0:64, 0:n], onesb[64:65, 0:64], rrows[a_][64:65, 0:n], True, True, [RRBs[a_]], PB[4 + a_])
            for (hh, s_, t0, n) in accs:
                a_ = hh * 2 + s_
                P.op("dve", "tensor_tensor", reads=[PB[4 + a_], OSBs[a_]], writes=[FOB], out=foT[0:64, a_, 0:n],
                     in0=osbs[a_][0:64, 0:n], in1=bank(4 + a_)[0:64, 0:n], op=ALU.mult)
            for (hh, s_, t0, n) in accs:
                a_ = hh * 2 + s_
                P.dma("sp", k.S["foT"][2 * hp + hh, :, t0:t0 + n], foT[:, a_, 0:n], reads=[FOB], writes=[Buf()])
            P.maybe_flush()
        P.flush()


def rstd_ops(P, out, in_, scale, epsc, R, W):
    P.op("act", "activation", reads=R, writes=W, out=out, in_=in_, func=AF.Sqrt, scale=scale, bias=epsc)
    P.op("dve", "reciprocal", reads=W, writes=W, out=out, in_=out)


def post_norm_residual(P, pso, PSB, xt_blk, XTB, g_bc, GB, junk, JB, ssv, SSB, epsc, tmp, TMPB, bs):
    P.op("act", "activation", reads=PSB, writes=[JB, SSB], out=junk[0:bs, :], in_=pso, func=AF.Square,
         accum_out=ssv[0:bs, 0:1])
    rstd_ops(P, ssv[0:bs, 1:2], ssv[0:bs, 0:1], 1.0 / D, epsc[0:bs, 0:1], [SSB], [SSB])
    P.op("dve", "scalar_tensor_tensor", reads=PSB + [SSB, GB], writes=[TMPB], out=tmp[0:bs, :], in0=pso,
         scalar=ssv[0:bs, 1:2], in1=g_bc[0:bs, :], op0=ALU.mult, op1=ALU.mult)
    P.op("pool", "tensor_tensor", reads=[TMPB, XTB], writes=[XTB], out=xt_blk, in0=xt_blk, in1=tmp[0:bs, :], op=ALU.add)


def pre_norm_elem(P, xt, XTB, g_bc, GB, junk, JB, ssv, SSB, epsc, h, HB, blks):
    for b, (o, bs) in enumerate(blks):
        P.op("act", "activation", reads=[XTB], writes=[JB, SSB], out=junk[0:bs, :], in_=xt[0:bs, b, :], func=AF.Square,
             accum_out=ssv[0:bs, 2 * b:2 * b + 1])
        rstd_ops(P, ssv[0:bs, 2 * b + 1:2 * b + 2], ssv[0:bs, 2 * b:2 * b + 1], 1.0 / D, epsc[0:bs, 0:1], [SSB], [SSB])
        P.op("dve", "scalar_tensor_tensor", reads=[XTB, SSB, GB], writes=[HB], out=h[0:bs, b, :], in0=xt[0:bs, b, :],
             scalar=ssv[0:bs, 2 * b + 1:2 * b + 2], in1=g_bc[0:bs, :], op0=ALU.mult, op1=ALU.mult)


def pre_norm_T(P, bank, PB, banks, h, HB, hT, HTB, identb, blks, n):
    for kc in range(8):
        bk = banks[kc % len(banks)]
        for b, (o, bs) in enumerate(blks):
            P.op("pe", "matmul", reads=[HB], writes=[PB[bk]], out=bank(bk)[:, o:o + bs],
                 lhsT=h[0:bs, b, kc * 128:(kc + 1) * 128], rhs=identb[0:bs, 0:bs], start=True, stop=True)
        if kc % 2 == 0:
            P.op("act", "copy", reads=[PB[bk]], writes=[HTB], out=hT[:, kc, 0:n], in_=bank(bk)[:, 0:n])
        else:
            P.op("dve", "tensor_copy", reads=[PB[bk]], writes=[HTB], out=hT[:, kc, 0:n], in_=bank(bk)[:, 0:n])


def pre_norm_hT(P, bank, PB, banks, xt, XTB, g_bc, GB, junk, JB, ssv, SSB, epsc, h, HB, hT, HTB, identb, blks, n):
    pre_norm_elem(P, xt, XTB, g_bc, GB, junk, JB, ssv, SSB, epsc, h, HB, blks)
    pre_norm_T(P, bank, PB, banks, h, HB, hT, HTB, identb, blks, n)


def x_src(k, first_layer_input, kind, t0, n):
    if first_layer_input:
        return k.I["x_sample"] if kind == "smp" else k.I["x_prompt"][t0:t0 + n, :]
    return k.S["x"][t0:t0 + n, :]


def phase_A2(k, P, ps, PB, bank, WB, identf, identb, onesf, onesb):
    lg = np.log(1.0 - 2.0 ** (-5.0 - np.arange(4, dtype=np.float64)))
    with contextlib.ExitStack() as st:
        def sb(name, shape, dt):
            return k.sbuf(st, "a2_" + name, shape, dt)
        rmask = sb("rmask", [128, 4, 512], F32)
        wq = sb("wq", [128, 4, 512], F32)
        gnw = sb("gnw", [128, 512], F32)
        g1 = sb("g1", [128, D], F32)
        wor = sb("wor", [128, 4, D], BF16)
        wof = sb("wof", [64, 8, D], BF16)
        epsc = sb("epsc", [128, 1], F32)
        CW = Buf("a2c")
        P.dma("sp", rmask[:], k.C["rmask"], writes=[CW])
        P.dma("sp", wq[:], k.C["wq"].rearrange("p b h e -> p b (h e)"), writes=[CW])
        P.dma("sp", gnw[:], k.I["ret_gn_w"][0:1, :].broadcast_to([128, 512]), writes=[CW])
        P.dma("sp", g1[:], k.I["norm_w"][0, 1:2, :].broadcast_to([128, D]), writes=[CW])
        P.dma("act", wor[:], WB["w_out_ab"][0:512, :].rearrange("(kc p) n -> p kc n", p=128), writes=[CW])
        P.dma("act", wof[:], WB["w_out_ab"][512:1024, :].rearrange("(h p) n -> p h n", p=64), writes=[CW])
        P.op("pool", "memset", writes=[CW], ap=epsc[:], constant=EPS)
        Sf = sb("Sf", [128, 4, 128], F32)
        Sb = sb("Sb", [128, 4, 128], BF16)
        SFB, SBB = Buf("Sf"), Buf("Sb")
        qTs = [sb("qT%d" % i, [128, 4, TT], BF16) for i in range(2)]
        kTs = [sb("kT%d" % i, [128, 4, TT], BF16) for i in range(2)]
        rkks = [sb("rkk%d" % i, [128, 4, 512], BF16) for i in range(2)]
        rvs = [sb("rv%d" % i, [128, 4, 512], BF16) for i in range(2)]
        rgs = [sb("rg%d" % i, [128, 4, 512], BF16) for i in range(2)]
        foTs = [sb("foT%d" % i, [64, 8, TT], BF16) for i in range(2)]
        xts = [sb("xt%d" % i, [128, 4, D], F32) for i in range(2)]
        INBs = [Buf("a2in%d" % i) for i in range(2)]
        XTBs = [Buf("a2x%d" % i) for i in range(2)]
        PT = sb("PT", [128, 4, 4, TT], BF16)
        PTB = [Buf("PT%d" % i) for i in range(4)]
        ro = sb("ro", [128, 512], F32)
        t1 = sb("t1", [128, 512], F32)
        sq = sb("sq", [128, 512], F32)
        ROB, T1B, SQB = Buf("ro"), Buf("t1"), Buf("sq")
        st8 = sb("st8", [128, 24], F32)
        STB = Buf("st8")
        mixr = sb("mixr", [128, 512], BF16)
        MXB = Buf("mixr")
        roT = sb("roT", [128, 4, TT], BF16)
        ROTB = Buf("roT")
        junk = sb("junk", [128, D], BF16)
        JB = Buf("junk")
        ssv = sb("ssv", [128, 8], F32)
        SSB = Buf("ssv")
        tmp = sb("tmp", [128, D], F32)
        TMPB = Buf("tmp")

        def MM(out, lhsT, rhs, start, stop, R, Wb):
            P.op("pe", "matmul", reads=R, writes=[Wb], out=out, lhsT=lhsT, rhs=rhs, start=start, stop=stop)

        seqs = [("prm", t * TT, TT) for t in range(T // TT)] + [("smp", T, TS)]
        if "a2_seqs" in DBG:
            seqs = [seqs[i] for i in DBG["a2_seqs"]]
        P.op("pool", "memset", writes=[SFB], ap=Sf[:], constant=0.0)
        P.op("pool", "memset", writes=[SBB], ap=Sb[:], constant=0.0)
        def load_in(ti):
            kind, t0, n = seqs[ti]
            pp = min(128, n)
            nb = len(blocks(n))
            j = ti % 2

            def tm_(ap2d):
                return ap2d.rearrange("(b p) d -> p b d", p=pp)
            P.dma("sp", qTs[j][:, :, 0:n], k.S["rqT"][:, :, t0:t0 + n].rearrange("h d t -> d h t"), writes=[INBs[j]])
            P.dma("act", kTs[j][:, :, 0:n], k.S["rkT"][:, :, t0:t0 + n].rearrange("h d t -> d h t"), writes=[INBs[j]])
            P.dma("sp", rkks[j][0:pp, 0:nb, :], tm_(k.S["rkk"][t0:t0 + n, :]), writes=[INBs[j]])
            P.dma("act", rvs[j][0:pp, 0:nb, :], tm_(k.S["rv"][t0:t0 + n, :]), writes=[INBs[j]])
            P.dma("sp", rgs[j][0:pp, 0:nb, :], tm_(k.S["rg"][t0:t0 + n, :]), writes=[INBs[j]])
            P.dma("act", foTs[j][:, :, 0:n], k.S["foT"][:, :, t0:t0 + n].rearrange("h r t -> r h t"), writes=[INBs[j]])
            P.dma("sp", xts[j][0:pp, 0:nb, :], tm_(x_src(k, True, kind, t0, n)), writes=[XTBs[j]])
        load_in(0)
        for ti, (kind, t0, n) in enumerate(seqs):
            blks = blocks(n)
            nb = len(blks)
            pp = min(128, n)
            dec = np.exp(lg * (512.0 if kind == "prm" else 32.0))
            j_ = ti % 2
            qT, kT, rkk, rv, rg, foT, xt = qTs[j_], kTs[j_], rkks[j_], rvs[j_], rgs[j_], foTs[j_], xts[j_]
            INB, XTB = INBs[j_], XTBs[j_]
            if ti + 1 < len(seqs):
                load_in(ti + 1)
            if kind == "smp":
                P.dma("sp", Sf[:], k.I["cache_ret_state"].rearrange("h d e -> d h e"), writes=[SFB])
                P.op("act", "copy", reads=[SFB], writes=[SBB], out=Sb[:], in_=Sf[:])

            def tm(ap2d):
                return ap2d.rearrange("(b p) d -> p b d", p=pp)
            for hh in range(4):
                for jb, (jo, js) in enumerate(blks):
                    bk = (hh * 4 + jb) % 2
                    MM(bank(bk)[0:js, jo:n], kT[:, hh, jo:jo + js], qT[:, hh, jo:n], True, True, [INB], PB[bk])
                    P.op("dve", "tensor_tensor", reads=[PB[bk], CW], writes=[PTB[hh]], out=PT[0:js, hh, jb, jo:n],
                         in0=bank(bk)[0:js, jo:n], in1=rmask[0:js, hh, 0:n - jo], op=ALU.mult)
            for ib, (io, is_) in enumerate(blks):
                for hh in range(4):
                    for jb in range(ib + 1):
                        jo, js = blks[jb]
                        MM(bank(2)[0:is_, hh * 128:(hh + 1) * 128], PT[0:js, hh, jb, io:io + is_],
                           rv[0:js, jb, hh * 128:(hh + 1) * 128], jb == 0, jb == ib, [PTB[hh], INB], PB[2])
                    MM(bank(3)[0:is_, hh * 128:(hh + 1) * 128], qT[:, hh, io:io + is_], Sb[:, hh, :], True, True,
                       [INB, SBB], PB[3])
                P.op("dve", "tensor_tensor", reads=[PB[3], CW], writes=[T1B], out=t1[0:is_, :], in0=bank(3)[0:is_, :],
                     in1=wq[0:is_, ib, :], op=ALU.mult)
                P.op("dve", "tensor_tensor", reads=[PB[2], T1B], writes=[ROB], out=ro[0:is_, :], in0=bank(2)[0:is_, :],
                     in1=t1[0:is_, :], op=ALU.add)
                P.op("act", "activation", reads=[ROB], writes=[SQB], out=sq[0:is_, :], in_=ro[0:is_, :], func=AF.Square)
                ro3 = ro[0:is_, :].rearrange("p (h e) -> p h e", e=128)
                sq3 = sq[0:is_, :].rearrange("p (h e) -> p h e", e=128)
                P.op("dve", "tensor_reduce", reads=[ROB], writes=[STB], out=st8[0:is_, 0:4], in_=ro3, axis=AX.X, op=ALU.add)
                P.op("dve", "tensor_reduce", reads=[SQB], writes=[STB], out=st8[0:is_, 4:8], in_=sq3, axis=AX.X, op=ALU.add)
                P.op("dve", "tensor_scalar", reads=[STB], writes=[STB], out=st8[0:is_, 8:12], in0=st8[0:is_, 0:4],
                     scalar1=1.0 / 128, scalar2=None, op0=ALU.mult)
                P.op("dve", "tensor_tensor", reads=[STB], writes=[STB], out=st8[0:is_, 12:16], in0=st8[0:is_, 8:12],
                     in1=st8[0:is_, 8:12], op=ALU.mult)
                P.op("dve", "scalar_tensor_tensor", reads=[STB], writes=[STB], out=st8[0:is_, 16:20], in0=st8[0:is_, 4:8],
                     scalar=1.0 / 128, in1=st8[0:is_, 12:16], op0=ALU.mult, op1=ALU.subtract)
                rstd_ops(P, st8[0:is_, 20:24], st8[0:is_, 16:20], 1.0, epsc[0:is_, 0:1], [STB], [STB])
                t13 = t1[0:is_, :].rearrange("p (h e) -> p h e", e=128)
                P.op("dve", "tensor_tensor", reads=[ROB, STB], writes=[T1B], out=t13, in0=ro3,
                     in1=st8[0:is_, 8:12].unsqueeze(2).broadcast_to([is_, 4, 128]), op=ALU.subtract)
                P.op("dve", "tensor_tensor", reads=[T1B, STB], writes=[T1B], out=t13, in0=t13,
                     in1=st8[0:is_, 20:24].unsqueeze(2).broadcast_to([is_, 4, 128]), op=ALU.mult)
                P.op("pool", "tensor_tensor", reads=[T1B, CW], writes=[T1B], out=t1[0:is_, :], in0=t1[0:is_, :],
                     in1=gnw[0:is_, :], op=ALU.mult)
                P.op("pool", "tensor_tensor", reads=[T1B, INB], writes=[MXB], out=mixr[0:is_, :], in0=t1[0:is_, :],
                     in1=rg[0:is_, ib, :], op=ALU.mult)
                for hh in range(4):
                    MM(bank(4 + hh)[:, io:io + is_], mixr[0:is_, hh * 128:(hh + 1) * 128], identb[0:is_, 0:is_], True, True,
                       [MXB], PB[4 + hh])
            for hh in range(4):
                if hh % 2 == 0:
                    P.op("act", "copy", reads=[PB[4 + hh]], writes=[ROTB], out=roT[:, hh, 0:n], in_=bank(4 + hh)[:, 0:n])
                else:
                    P.op("dve", "tensor_copy", reads=[PB[4 + hh]], writes=[ROTB], out=roT[:, hh, 0:n],
                         in_=bank(4 + hh)[:, 0:n])
            for hh in range(4):
                for jb, (jo, js) in enumerate(blks):
                    MM(bank(2)[:, hh * 128:(hh + 1) * 128], rkk[0:js, jb, hh * 128:(hh + 1) * 128],
                       rv[0:js, jb, hh * 128:(hh + 1) * 128], jb == 0, jb == nb - 1, [INB], PB[2])
            for hh in range(4):
                P.op("dve", "scalar_tensor_tensor", reads=[PB[2], SFB], writes=[SFB], out=Sf[:, hh, :], in0=Sf[:, hh, :],
                     scalar=float(dec[hh]), in1=bank(2)[:, hh * 128:(hh + 1) * 128], op0=ALU.mult, op1=ALU.add)
            P.op("act", "copy", reads=[SFB], writes=[SBB], out=Sb[:], in_=Sf[:])
            last_of_kind = (kind == "smp") or (t0 + n == T)
            if last_of_kind:
                dst = k.O["ret_state_sample"] if kind == "smp" else k.O["ret_state_prompt"]
                P.dma("sp", dst.rearrange("h d e -> d h e"), Sf[:], reads=[SFB], writes=[Buf()])
            for ib, (io, is_) in enumerate(blks):
                for half in range(2):
                    bk = 2 * (ib % 2) + half
                    for i in range(12):
                        if i < 4:
                            l_, r_ = roT[:, i, io:io + is_], wor[:, i, half * 512:(half + 1) * 512]
                        else:
                            l_, r_ = foT[0:64, i - 4, io:io + is_], wof[0:64, i - 4, half * 512:(half + 1) * 512]
                        MM(bank(bk)[0:is_, :], l_, r_, i == 0, i == 11, [ROTB, INB, CW], PB[bk])
                b0 = 2 * (ib % 2)
                pso = ps[0:is_, b0 * 512:b0 * 512 + 1024]
                post_norm_residual(P, pso, [PB[b0], PB[b0 + 1]], xt[0:is_, ib, :], XTB, g1, CW, junk, JB, ssv, SSB, epsc,
                                   tmp, TMPB, is_)
            P.dma("sp", tm(k.S["x"][t0:t0 + n, :]), xt[0:pp, 0:nb, :], reads=[XTB], writes=[Buf()])
            P.maybe_flush()
        P.flush()


def phase_C(k, P, ps, PB, bank, WB, identf, identb, onesf, onesb, l):
    with contextlib.ExitStack() as st:
        def sb(name, shape, dt):
            return k.sbuf(st, "c%d_" % l + name, shape, dt)
        wxq = sb("wxq", [128, 8, D], BF16)
        wxo = sb("wxo", [128, 8, D], BF16)
        wkv = sb("wkv", [128, 8, D], BF16)
        g2 = sb("g2", [128, D], F32)
        g3 = sb("g3", [128, D], F32)
        gm = sb("gm", [128, D], F32)
        epsc = sb("epsc", [128, 1], F32)
        CW = Buf("cw")
        WKVB = Buf("wkv")
        P.dma("sp", wxq[:], WB["w_xq%d" % l].rearrange("(kc p) n -> p kc n", p=128), writes=[CW])
        P.dma("act", wxo[:], WB["w_xo%d" % l].rearrange("(kc p) n -> p kc n", p=128), writes=[CW])
        P.dma("sp", g2[:], k.I["norm_w"][l, 2:3, :].broadcast_to([128, D]), writes=[CW])
        P.dma("sp", g3[:], k.I["norm_w"][l, 3:4, :].broadcast_to([128, D]), writes=[CW])
        P.dma("sp", gm[:], k.I["mem_norm_w"][l:l + 1, :].broadcast_to([128, D]), writes=[CW])
        P.op("pool", "memset", writes=[CW], ap=epsc[:], constant=EPS)
        xts = [sb("xt%d" % i, [128, 4, D], F32) for i in range(2)]
        XTBs = [Buf("xt%d" % i) for i in range(2)]
        xt, XTB = xts[0], XTBs[0]
        junk2 = sb("junk2", [128, D], BF16)
        JB2 = Buf("junk2")
        ssv2 = sb("ssv2", [128, 8], F32)
        SSB2 = Buf("ssv2")
        h = sb("h", [128, 4, D], BF16)
        HB = Buf("h")
        hT = sb("hT", [128, 8, TT], BF16)
        HTB = Buf("hT")
        junk = sb("junk", [128, D], BF16)
        JB = Buf("junk")
        ssv = sb("ssv", [128, 8], F32)
        SSB = Buf("ssv")
        tmp = sb("tmp", [128, D], F32)
        TMPB = Buf("tmp")
        mkf = sb("mkf", [128, 2, D], F32)
        MKFB = Buf("mkf")
        mkb = sb("mkb", [128, 2, D], BF16)
        MKBB = Buf("mkb")
        mkT = sb("mkT", [128, 8, 256], BF16)
        MKTB = Buf("mkT")
        mvb = sb("mvb", [128, 2, D], BF16)
        MVBB = Buf("mvb")
        qT = sb("qT", [128, 8, TT], BF16)
        QTB = Buf("qT")
        oT = sb("oT", [128, 8, TT], BF16)
        OTB = Buf("oT")
        pT = [sb("pT%d" % i, [128, TT], BF16) for i in range(2)]
        PTB = [Buf("pT%d" % i) for i in range(2)]
        rec = sb("rec", [128, TT], F32)
        RECB = Buf("rec")

        def MM(out, lhsT, rhs, start, stop, R, Wb):
            P.op("pe", "matmul", reads=R, writes=[Wb], out=out, lhsT=lhsT, rhs=rhs, start=start, stop=stop)

        mblks = [(0, 128), (128, 128)]

        def kv_finish():
            P.op("pool", "tensor_copy", reads=[MKFB], writes=[MKBB], out=mkb[:], in_=mkf[:])
            for c8 in range(8):
                bk = c8 % 2
                for mb in range(2):
                    MM(bank(bk)[:, mb * 128:(mb + 1) * 128], mkb[:, mb, c8 * 128:(c8 + 1) * 128], identb[:, :], True, True,
                       [MKBB], PB[bk])
                if c8 % 2 == 0:
                    P.op("act", "copy", reads=[PB[bk]], writes=[MKTB], out=mkT[:, c8, :], in_=bank(bk)[:, 0:256])
                else:
                    P.op("dve", "tensor_copy", reads=[PB[bk]], writes=[MKTB], out=mkT[:, c8, :], in_=bank(bk)[:, 0:256])

        def setup_prompt_mem():
            P.dma("sp", xt[:, 0:2, :], k.I["mem_prompt"].rearrange("(b p) d -> p b d", p=128), writes=[XTB])
            pre_norm_hT(P, bank, PB, [0, 1, 2, 3], xt, XTB, gm, CW, junk, JB, ssv, SSB, epsc, h, HB, hT, HTB, identb,
                        mblks, 256)
            for which in ("k", "v"):
                P.dma("sp", wkv[:], WB["w_x%s%d" % (which, l)].rearrange("(kc p) n -> p kc n", p=128), writes=[WKVB])
                for mb in range(2):
                    for half in range(2):
                        bk = 4 + half
                        for kc in range(8):
                            MM(bank(bk)[:, :], hT[:, kc, mb * 128:(mb + 1) * 128], wkv[:, kc, half * 512:(half + 1) * 512],
                               kc == 0, kc == 7, [HTB, WKVB], PB[bk])
                        P.op("act", "copy", reads=[PB[bk]], writes=[MKFB], out=mkf[:, mb, half * 512:(half + 1) * 512],
                             in_=bank(bk)[:, :])
                dst = k.O["mem_k_prompt" if which == "k" else "mem_v_prompt"][l]
                P.dma("sp", dst.rearrange("(b p) d -> p b d", p=128), mkf[:], reads=[MKFB], writes=[Buf()])
                if which == "k":
                    kv_finish()
                else:
                    P.op("pool", "tensor_copy", reads=[MKFB], writes=[MVBB], out=mvb[:], in_=mkf[:])

        def setup_sample_mem():
            P.dma("sp", mkf[:], k.I["cache_mem_k"][l].rearrange("(b p) d -> p b d", p=128), writes=[MKFB])
            kv_finish()
            P.dma("sp", mkf[:], k.I["cache_mem_v"][l].rearrange("(b p) d -> p b d", p=128), writes=[MKFB])
            P.op("pool", "tensor_copy", reads=[MKFB], writes=[MVBB], out=mvb[:], in_=mkf[:])

        seqs = [("prm", t * TT, TT) for t in range(T // TT)] + [("smp", T, TS)]
        if "c_seqs" in DBG:
            seqs = [seqs[i] for i in DBG["c_seqs"]]
        def tile_geom(ti):
            kind, t0, n = seqs[ti]
            return kind, t0, n, blocks(n), min(128, n)

        def load_x(ti):
            kind, t0, n, blks, pp = tile_geom(ti)
            P.dma("sp", xts[ti % 2][0:pp, 0:len(blks), :], k.S["x"][t0:t0 + n, :].rearrange("(b p) d -> p b d", p=pp),
                  writes=[XTBs[ti % 2]])

        def prenorm_e(ti):
            kind, t0, n, blks, pp = tile_geom(ti)
            pre_norm_elem(P, xts[ti % 2], XTBs[ti % 2], g2, CW, junk2, JB2, ssv2, SSB2, epsc, h, HB, blks)

        def prenorm_t(ti):
            kind, t0, n, blks, pp = tile_geom(ti)
            pre_norm_T(P, bank, PB, [0, 1, 2, 3], h, HB, hT, HTB, identb, blks, n)
        if seqs[0][0] == "prm":
            setup_prompt_mem()
        load_x(0)
        prenorm_e(0)
        prenorm_t(0)
        for ti, (kind, t0, n) in enumerate(seqs):
            blks = blocks(n)
            nb = len(blks)
            pp = min(128, n)
            xt, XTB = xts[ti % 2], XTBs[ti % 2]
            if ti + 1 < len(seqs):
                load_x(ti + 1)
            if kind == "smp":
                setup_sample_mem()

            def tm(ap2d):
                return ap2d.rearrange("(b p) d -> p b d", p=pp)
            for mc in range(8):
                bk = 4 + mc % 2
                for kc in range(8):
                    MM(bank(bk)[:, 0:n], wxq[:, kc, mc * 128:(mc + 1) * 128], hT[:, kc, 0:n], kc == 0, kc == 7, [HTB, CW], PB[bk])
                if mc % 2 == 0:
                    P.op("act", "copy", reads=[PB[bk]], writes=[QTB], out=qT[:, mc, 0:n], in_=bank(bk)[:, 0:n])
                else:
                    P.op("dve", "tensor_copy", reads=[PB[bk]], writes=[QTB], out=qT[:, mc, 0:n], in_=bank(bk)[:, 0:n])
            for hh in range(4):
                for mb in range(2):
                    for dc in range(2):
                        MM(bank(mb)[:, 0:n], mkT[:, hh * 2 + dc, mb * 128:(mb + 1) * 128], qT[:, hh * 2 + dc, 0:n],
                           dc == 0, dc == 1, [MKTB, QTB], PB[mb])
                    P.op("act", "activation", reads=[PB[mb]], writes=[PTB[mb]], out=pT[mb][:, 0:n], in_=bank(mb)[:, 0:n],
                         func=AF.Exp, scale=1.0 / 16.0)
                for mb in range(2):
                    MM(bank(2)[:, 0:n], onesb[:, :], pT[mb][:, 0:n], mb == 0, mb == 1, [PTB[mb]], PB[2])
                P.op("dve", "reciprocal", reads=[PB[2]], writes=[RECB], out=rec[:, 0:n], in_=bank(2)[:, 0:n])
                for dc in range(2):
                    bk = 3 if dc == 0 else 6
                    for mb in range(2):
                        MM(bank(bk)[:, 0:n], mvb[:, mb, hh * 256 + dc * 128:hh * 256 + dc * 128 + 128], pT[mb][:, 0:n],
                           mb == 0, mb == 1, [MVBB, PTB[mb]], PB[bk])
                    P.op("dve", "tensor_tensor", reads=[PB[bk], RECB], writes=[OTB], out=oT[:, hh * 2 + dc, 0:n],
                         in0=bank(bk)[:, 0:n], in1=rec[:, 0:n], op=ALU.mult)
            for ib, (io, is_) in enumerate(blks):
                b0 = 4 if ib % 2 == 0 else 6
                if b0 == 6:
                    b0 = 0
                for half in range(2):
                    bk = b0 + half
                    for kc in range(8):
                        MM(bank(bk)[0:is_, :], oT[:, kc, io:io + is_], wxo[:, kc, half * 512:(half + 1) * 512], kc == 0, kc == 7,
                           [OTB, CW], PB[bk])
                if ib == 0 and ti + 1 < len(seqs):
                    prenorm_e(ti + 1)
                pso = ps[0:is_, b0 * 512:b0 * 512 + 1024]
                post_norm_residual(P, pso, [PB[b0], PB[b0 + 1]], xt[0:is_, ib, :], XTB, g3, CW, junk, JB, ssv, SSB, epsc,
                                   tmp, TMPB, is_)
            if ti + 1 < len(seqs):
                prenorm_t(ti + 1)
            P.dma("sp", tm(k.S["x"][t0:t0 + n, :]), xt[0:pp, 0:nb, :], reads=[XTB], writes=[Buf()])
            P.maybe_flush()
        P.flush()


def phase_F(k, P, ps, PB, bank, WB, identf, identb, onesf, onesb, l, final):
    with contextlib.ExitStack() as st:
        def sb(name, shape, dt):
            return k.sbuf(st, "f%d_" % l + name, shape, dt)
        g4 = sb("g4", [128, D], F32)
        g5 = sb("g5", [128, D], F32)
        epsc = sb("epsc", [128, 1], F32)
        CW = Buf("cw")
        P.dma("sp", g4[:], k.I["norm_w"][l, 4:5, :].broadcast_to([128, D]), writes=[CW])
        P.dma("sp", g5[:], k.I["norm_w"][l, 5:6, :].broadcast_to([128, D]), writes=[CW])
        P.op("pool", "memset", writes=[CW], ap=epsc[:], constant=EPS)
        up = [sb("up%d" % i, [128, 8, 1024], BF16) for i in range(2)]
        dn = [sb("dn%d" % i, [128, 8, 1024], BF16) for i in range(2)]
        UPB = [Buf("up%d" % i) for i in range(2)]
        DNB = [Buf("dn%d" % i) for i in range(2)]
        xts = [sb("xt%d" % i, [128, 4, D], F32) for i in range(2)]
        XTBs = [Buf("xt%d" % i) for i in range(2)]
        junk2 = sb("junk2", [128, D], BF16)
        JB2 = Buf("junk2")
        ssv2 = sb("ssv2", [128, 8], F32)
        SSB2 = Buf("ssv2")
        h = sb("h", [128, 4, D], BF16)
        HB = Buf("h")
        hT = sb("hT", [128, 8, TT], BF16)
        HTB = Buf("hT")
        junk = sb("junk", [128, D], BF16)
        JB = Buf("junk")
        ssv = sb("ssv", [128, 8], F32)
        SSB = Buf("ssv")
        tmp = sb("tmp", [128, D], F32)
        TMPB = Buf("tmp")
        aT = sb("aT", [128, 32, TT], BF16)
        ATB = Buf("aT")
        sqv = [sb("sqv%d" % i, [128, TT], F32) for i in range(2)]
        SQB = [Buf("sqv%d" % i) for i in range(2)]
        wup = WB["w_up%d" % l].rearrange("(kc p) n -> p kc n", p=128)
        wdn = WB["w_down%d" % l].rearrange("(fc p) n -> p fc n", p=128)

        def MM(out, lhsT, rhs, start, stop, R, Wb):
            P.op("pe", "matmul", reads=R, writes=[Wb], out=out, lhsT=lhsT, rhs=rhs, start=start, stop=stop)

        seqs = [("prm", t * TT, TT) for t in range(T // TT)] + [("smp", T, TS)]
        if "f_seqs" in DBG:
            seqs = [seqs[i] for i in DBG["f_seqs"]]
        ui = 0
        di = 0
        fi = 0

        def tile_geom(ti):
            kind, t0, n = seqs[ti]
            return kind, t0, n, blocks(n), min(128, n)

        def load_x(ti):
            kind, t0, n, blks, pp = tile_geom(ti)
            P.dma("sp", xts[ti % 2][0:pp, 0:len(blks), :], k.S["x"][t0:t0 + n, :].rearrange("(b p) d -> p b d", p=pp),
                  writes=[XTBs[ti % 2]])

        def prenorm_e(ti):
            kind, t0, n, blks, pp = tile_geom(ti)
            pre_norm_elem(P, xts[ti % 2], XTBs[ti % 2], g4, CW, junk2, JB2, ssv2, SSB2, epsc, h, HB, blks)

        def prenorm_t(ti):
            kind, t0, n, blks, pp = tile_geom(ti)
            pre_norm_T(P, bank, PB, [0, 1, 2, 3], h, HB, hT, HTB, identb, blks, n)
        load_x(0)
        prenorm_e(0)
        prenorm_t(0)
        for ti, (kind, t0, n) in enumerate(seqs):
            blks = blocks(n)
            nb = len(blks)
            pp = min(128, n)
            xt, XTB = xts[ti % 2], XTBs[ti % 2]
            if ti + 1 < len(seqs):
                load_x(ti + 1)

            def tm(ap2d):
                return ap2d.rearrange("(b p) d -> p b d", p=pp)
            for q in range(4):
                sl = ui % 2
                ui += 1
                P.dma("sp", up[sl][:, 0:4, :], wup[:, 0:4, q * 1024:(q + 1) * 1024], writes=[UPB[sl]])
                P.dma("pool", up[sl][:, 4:8, :], wup[:, 4:8, q * 1024:(q + 1) * 1024], writes=[UPB[sl]])
                for fcl in range(8):
                    fc = q * 8 + fcl
                    bk = 4 + fi % 4
                    sq_ = fi % 2
                    fi += 1
                    for kc in range(8):
                        MM(bank(bk)[:, 0:n], up[sl][:, kc, fcl * 128:(fcl + 1) * 128], hT[:, kc, 0:n], kc == 0, kc == 7,
                           [UPB[sl], HTB], PB[bk])
                    P.op("act", "activation", reads=[PB[bk]], writes=[SQB[sq_]], out=sqv[sq_][:, 0:n], in_=bank(bk)[:, 0:n],
                         func=AF.Square)
                    P.op("dve", "scalar_tensor_tensor", reads=[PB[bk], SQB[sq_]], writes=[ATB], out=aT[:, fc, 0:n],
                         in0=bank(bk)[:, 0:n], scalar=0.0, in1=sqv[sq_][:, 0:n], op0=ALU.is_gt, op1=ALU.mult)
            for q in range(4):
                sl = di % 2
                di += 1
                P.dma("sp", dn[sl][:, 0:4, :], wdn[:, q * 8:q * 8 + 4, :], writes=[DNB[sl]])
                P.dma("pool", dn[sl][:, 4:8, :], wdn[:, q * 8 + 4:q * 8 + 8, :], writes=[DNB[sl]])
                for ib, (io, is_) in enumerate(blks):
                    for half in range(2):
                        bk = 2 * ib + half
                        for fcl in range(8):
                            fc = q * 8 + fcl
                            MM(bank(bk)[0:is_, :], aT[:, fc, io:io + is_], dn[sl][:, fcl, half * 512:(half + 1) * 512],
                               q == 0 and fcl == 0, q == 3 and fcl == 7, [ATB, DNB[sl]], PB[bk])
            if ti + 1 < len(seqs):
                prenorm_e(ti + 1)
            for ib, (io, is_) in enumerate(blks):
                pso = ps[0:is_, 2 * ib * 512:2 * ib * 512 + 1024]
                post_norm_residual(P, pso, [PB[2 * ib], PB[2 * ib + 1]], xt[0:is_, ib, :], XTB, g5, CW, junk, JB, ssv, SSB,
                                   epsc, tmp, TMPB, is_)
            if ti + 1 < len(seqs):
                prenorm_t(ti + 1)
            if final:
                dst = k.O["y_sample"] if kind == "smp" else k.O["y_prompt"][t0:t0 + n, :]
            else:
                dst = k.S["x"][t0:t0 + n, :]
            P.dma("sp", tm(dst), xt[0:pp, 0:nb, :], reads=[XTB], writes=[Buf()])
            P.maybe_flush()
        P.flush()


def phase_S1(k, P, ps, PB, bank, WB, identf, identb, onesf, onesb):
    with contextlib.ExitStack() as st:
        def sb(name, shape, dt):
            return k.sbuf(st, "s1_" + name, shape, dt)
        winc = sb("winc", [128, 8, D], BF16)
        g0 = sb("g0", [128, D], F32)
        epsc = sb("epsc", [128, 1], F32)
        CW = Buf("cw")
        P.dma("sp", winc[:], WB["w_in_c"].rearrange("(kc p) n -> p kc n", p=128), writes=[CW])
        P.dma("sp", g0[:], k.I["norm_w"][1, 0:1, :].broadcast_to([128, D]), writes=[CW])
        P.op("pool", "memset", writes=[CW], ap=epsc[:], constant=EPS)
        xts = [sb("xt%d" % i, [128, 4, D], F32) for i in range(2)]
        XTBs = [Buf("xt%d" % i) for i in range(2)]
        h = sb("h", [128, 4, D], BF16)
        HB = Buf("h")
        hT = sb("hT", [128, 8, TT], BF16)
        HTB = Buf("hT")
        junk = sb("junk", [128, D], BF16)
        JB = Buf("junk")
        ssv = sb("ssv", [128, 8], F32)
        SSB = Buf("ssv")
        uTs = [sb("uT%d" % i, [128, 8, TT], BF16) for i in range(2)]
        UTBs = [Buf("uT%d" % i) for i in range(2)]
        seqs = [("prm", t * TT, TT) for t in range(T // TT)] + [("smp", T, TS)]
        def load_x(ti):
            kind, t0, n = seqs[ti]
            pp = min(128, n)
            P.dma("sp", xts[ti % 2][0:pp, 0:len(blocks(n)), :], k.S["x"][t0:t0 + n, :].rearrange("(b p) d -> p b d", p=pp),
                  writes=[XTBs[ti % 2]])
        load_x(0)
        for ti, (kind, t0, n) in enumerate(seqs):
            blks = blocks(n)
            nb = len(blks)
            pp = min(128, n)
            xt, XTB = xts[ti % 2], XTBs[ti % 2]
            uT, UTB = uTs[ti % 2], UTBs[ti % 2]
            if ti + 1 < len(seqs):
                load_x(ti + 1)
            pre_norm_hT(P, bank, PB, [0, 1, 2, 3], xt, XTB, g0, CW, junk, JB, ssv, SSB, epsc, h, HB, hT, HTB, identb, blks, n)
            for mc in range(8):
                bk = 4 + mc % 4
                for kc in range(8):
                    P.op("pe", "matmul", reads=[HTB, CW], writes=[PB[bk]], out=bank(bk)[:, 0:n],
                         lhsT=winc[:, kc, mc * 128:(mc + 1) * 128], rhs=hT[:, kc, 0:n], start=(kc == 0), stop=(kc == 7))
                if mc % 2 == 0:
                    P.op("act", "copy", reads=[PB[bk]], writes=[UTB], out=uT[:, mc, 0:n], in_=bank(bk)[:, 0:n])
                else:
                    P.op("dve", "tensor_copy", reads=[PB[bk]], writes=[UTB], out=uT[:, mc, 0:n], in_=bank(bk)[:, 0:n])
            P.dma("act", k.S["uT"][:, :, t0:t0 + n].rearrange("c p t -> p c t"), uT[:, :, 0:n], reads=[UTB], writes=[Buf()])
            P.maybe_flush()
        P.flush()


def phase_S3(k, P, ps, PB, bank, WB, identf, identb, onesf, onesb):
    with contextlib.ExitStack() as st:
        def sb(name, shape, dt):
            return k.sbuf(st, "s3_" + name, shape, dt)
        wg = sb("wg", [128, 8, 2 * D], BF16)
        g1 = sb("g1", [128, D], F32)
        epsc = sb("epsc", [128, 1], F32)
        CW = Buf("cw")
        P.dma("sp", wg[:], WB["w_glu"].rearrange("(kc p) n -> p kc n", p=128), writes=[CW])
        P.dma("sp", g1[:], k.I["norm_w"][1, 1:2, :].broadcast_to([128, D]), writes=[CW])
        P.op("pool", "memset", writes=[CW], ap=epsc[:], constant=EPS)
        xts = [sb("xt%d" % i, [128, 4, D], F32) for i in range(2)]
        XTBs = [Buf("xt%d" % i) for i in range(2)]
        yTs = [sb("yT%d" % i, [128, 8, TT], BF16) for i in range(2)]
        YTBs = [Buf("yT%d" % i) for i in range(2)]
        junk = sb("junk", [128, D], BF16)
        JB = Buf("junk")
        ssv = sb("ssv", [128, 8], F32)
        SSB = Buf("ssv")
        tmp = sb("tmp", [128, D], F32)
        TMPB = Buf("tmp")
        sig = sb("sig", [128, D], F32)
        SIGB = Buf("sig")
        gl = sb("gl", [128, D], F32)
        GLB = Buf("gl")
        seqs = [("prm", t * TT, TT) for t in range(T // TT)] + [("smp", T, TS)]
        def load_in(ti):
            kind, t0, n = seqs[ti]
            pp = min(128, n)
            P.dma("sp", xts[ti % 2][0:pp, 0:len(blocks(n)), :], k.S["x"][t0:t0 + n, :].rearrange("(b p) d -> p b d", p=pp),
                  writes=[XTBs[ti % 2]])
            P.dma("sp", yTs[ti % 2][:, :, 0:n], k.S["yT"][:, :, t0:t0 + n].rearrange("c p t -> p c t"), writes=[YTBs[ti % 2]])
        load_in(0)
        for ti, (kind, t0, n) in enumerate(seqs):
            blks = blocks(n)
            nb = len(blks)
            pp = min(128, n)
            xt, XTB = xts[ti % 2], XTBs[ti % 2]
            yT, YTB = yTs[ti % 2], YTBs[ti % 2]
            if ti + 1 < len(seqs):
                load_in(ti + 1)
            for ib, (io, is_) in enumerate(blks):
                b0 = 4 * (ib % 2)
                for q4 in range(4):
                    bk = b0 + q4
                    for kc in range(8):
                        P.op("pe", "matmul", reads=[YTB, CW], writes=[PB[bk]], out=bank(bk)[0:is_, :],
                             lhsT=yT[:, kc, io:io + is_], rhs=wg[:, kc, q4 * 512:(q4 + 1) * 512], start=(kc == 0), stop=(kc == 7))
                P.op("act", "activation", reads=[PB[b0 + 2], PB[b0 + 3]], writes=[SIGB], out=sig[0:is_, :],
                     in_=ps[0:is_, (b0 + 2) * 512:(b0 + 4) * 512], func=AF.Sigmoid)
                P.op("dve", "tensor_tensor", reads=[PB[b0], PB[b0 + 1], SIGB], writes=[GLB], out=gl[0:is_, :],
                     in0=ps[0:is_, b0 * 512:(b0 + 2) * 512], in1=sig[0:is_, :], op=ALU.mult)
                post_norm_residual(P, gl[0:is_, :], [GLB], xt[0:is_, ib, :], XTB, g1, CW, junk, JB, ssv, SSB, epsc, tmp, TMPB, is_)
            P.dma("sp", k.S["x"][t0:t0 + n, :].rearrange("(b p) d -> p b d", p=pp), xt[0:pp, 0:nb, :], reads=[XTB], writes=[Buf()])
            P.maybe_flush()
        P.flush()


PI = float(np.pi)


def phase_S2(k, P, ps, PB, bank, WB, identf, identb, onesf, onesb):
    with contextlib.ExitStack() as st:
        def sb(name, shape, dt):
            return k.sbuf(st, "s2_" + name, shape, dt)
        Tall = sb("Tall", [128, 64, 128], BF16)
        Vall = sb("Vall", [128, 64, 128], BF16)
        WR = sb("WR", [128, 32, 128], BF16)
        WI = sb("WI", [128, 32, 128], BF16)
        Ec = sb("Ec", [128, 32, 64], F32)
        Es = sb("Es", [128, 32, 64], F32)
        Rt = sb("Rt", [128, 32, 64], F32)
        r8s = sb("r8s", [128, 32], F32)
        dfm = sb("dfm", [128, 8], F32)
        XRc = sb("XRc", [128, 32], F32)
        XIc = sb("XIc", [128, 32], F32)
        TB = Buf("s2tab")
        CARB = Buf("s2carry")
        P.dma("sp", dfm[:], k.I["s5_d"].rearrange("o (c p) -> p (o c)", p=128), writes=[TB], allow_slow_non_contiguous=True)

        def MM(out, lhsT, rhs, start, stop, R, Wb):
            P.op("pe", "matmul", reads=R, writes=[Wb], out=out, lhsT=lhsT, rhs=rhs, start=start, stop=stop)

        with contextlib.ExitStack() as st2:
            def sb2(name, shape, dt):
                return k.sbuf(st2, "s2t_" + name, shape, dt)
            lamr = sb2("lamr", [128, 32], F32)
            lami = sb2("lami", [128, 32], F32)
            dtv = sb2("dtv", [128, 32], F32)
            lr = sb2("lr", [128, 32], F32)
            li = sb2("li", [128, 32], F32)
            kvec = sb2("kvec", [128, 16], F32)
            nvec = sb2("nvec", [128, 64], F32)
            tmask = sb2("tmask", [128, 512], F32)
            Bre = sb2("Bre", [128, 32, 16], F32)
            Bim = sb2("Bim", [128, 32, 16], F32)
            Cnat = sb2("Cnat", [128, 4, 128], F32)
            Cre = sb2("Cre", [128, 32, 16], F32)
            Cim = sb2("Cim", [128, 32, 16], F32)
            arg = sb2("arg", [128, 32, 16], F32)
            mag = sb2("mag", [128, 32, 16], F32)
            Are = sb2("Are", [128, 32, 16], F32)
            Aim = sb2("Aim", [128, 32, 16], F32)
            s1 = sb2("s1", [128, 32], F32)
            s2_ = sb2("s2", [128, 32], F32)
            s3 = sb2("s3", [128, 32], F32)
            fre = sb2("fre", [128, 32], F32)
            fim = sb2("fim", [128, 32], F32)
            bbr = sb2("bbr", [128, 32, 16], F32)
            bbi = sb2("bbi", [128, 32, 16], F32)
            t16 = sb2("t16", [128, 32, 16], F32)
            Gre = sb2("Gre", [128, 32, 128], F32)
            Gim = sb2("Gim", [128, 32, 128], F32)
            Hre = sb2("Hre", [128, 32, 128], F32)
            Him = sb2("Him", [128, 32, 128], F32)
            big = sb2("big", [128, 32, 128], F32)
            eb = sb2("eb", [128, 32, 64], F32)
            SB_ = Buf("setup")
            for gh in range(2):
                psl = slice(gh * 64, gh * 64 + 64)
                gs = slice(gh * 32, gh * 32 + 32)
                P.dma("sp", lamr[psl, :], k.I["s5_lambda_re"][gs, :].rearrange("g p -> p g"), writes=[SB_],
                      allow_slow_non_contiguous=True)
                P.dma("act", lami[psl, :], k.I["s5_lambda_im"][gs, :].rearrange("g p -> p g"), writes=[SB_],
                      allow_slow_non_contiguous=True)
                P.dma("sp", dtv[psl, :], k.I["s5_log_dt"][0:1, gs].broadcast_to([64, 32]), writes=[SB_])
                P.dma("sp", Bre[psl, :, :], k.I["s5_b_re"][gs].rearrange("g p c -> p g c"), writes=[SB_])
                P.dma("act", Bim[psl, :, :], k.I["s5_b_im"][gs].rearrange("g p c -> p g c"), writes=[SB_])
            P.dma("sp", kvec[:], k.C["kvec"], writes=[SB_])
            P.dma("sp", nvec[:], k.C["nvec"], writes=[SB_])
            P.dma("sp", tmask[:], k.C["tmask"], writes=[SB_])
            for (src, dstC) in ((k.I["s5_c_re"], Cre), (k.I["s5_c_im"], Cim)):
                for gh in range(2):
                    P.dma("sp", Cnat[:, :, gh * 64:gh * 64 + 64],
                          src[gh * 32:gh * 32 + 32].rearrange("(o g) c p -> (g c) o p", o=4), writes=[SB_])
                for o4 in range(4):
                    MM(bank(0)[:, 0:128], Cnat[:, o4, :], identf[:, :], True, True, [SB_], PB[0])
                    gl0 = o4 * 8
                    P.op("dve", "tensor_copy", reads=[PB[0]], writes=[SB_],
                         out=dstC[:, gl0:gl0 + 8, :], in_=bank(0)[:, 0:128].rearrange("p (g c) -> p g c", c=16))

            def V(eng, name, **kw):
                P.op(eng, name, reads=[SB_], writes=[SB_], **kw)
            V("act", "activation", out=dtv[:], in_=dtv[:], func=AF.Exp)
            V("dve", "tensor_tensor", out=lr[:], in0=lamr[:], in1=dtv[:], op=ALU.mult)
            V("dve", "tensor_tensor", out=li[:], in0=lami[:], in1=dtv[:], op=ALU.mult)
            kb = kvec[:].unsqueeze(1).broadcast_to([128, 32, 16])
            V("dve", "tensor_tensor", out=arg[:], in0=lr[:].unsqueeze(2).broadcast_to([128, 32, 16]), in1=kb, op=ALU.mult)
            V("act", "activation", out=mag[:], in_=arg[:], func=AF.Exp)
            V("dve", "tensor_tensor", out=arg[:], in0=li[:].unsqueeze(2).broadcast_to([128, 32, 16]), in1=kb, op=ALU.mult)

            isc = sb2("isc", [128, 32, 64], mybir.dt.int32)

            def sincos(out_s, out_c, ang, f1, f2, i1):
                for (dst, off) in ((out_s, 0.0), (out_c, 0.25)):
                    V("dve", "tensor_scalar", out=f1, in0=ang, scalar1=1.0 / (2 * PI), scalar2=off, op0=ALU.mult, op1=ALU.add)
                    V("dve", "tensor_copy", out=i1, in_=f1)
                    V("dve", "tensor_copy", out=f2, in_=i1)
                    V("dve", "tensor_tensor", out=f1, in0=f1, in1=f2, op=ALU.subtract)
                    V("act", "activation", out=dst, in_=f1, func=AF.Sin, scale=2 * PI)
            sincos(Aim[:], Are[:], arg[:], t16[:], bbr[:], isc[:, :, 0:16])
            V("dve", "tensor_tensor", out=Are[:], in0=Are[:], in1=mag[:], op=ALU.mult)
            V("dve", "tensor_tensor", out=Aim[:], in0=Aim[:], in1=mag[:], op=ALU.mult)
            a_re, a_im = Are[:, :, 8], Aim[:, :, 8]
            V("dve", "tensor_tensor", out=s1[:], in0=lamr[:], in1=lamr[:], op=ALU.mult)
            V("dve", "tensor_tensor", out=s2_[:], in0=lami[:], in1=lami[:], op=ALU.mult)
            V("dve", "tensor_tensor", out=s1[:], in0=s1[:], in1=s2_[:], op=ALU.add)
            V("dve", "reciprocal", out=s1[:], in_=s1[:])
            V("dve", "tensor_scalar", out=s2_[:], in0=a_re, scalar1=-1.0, scalar2=None, op0=ALU.add)
            V("dve", "tensor_tensor", out=fre[:], in0=s2_[:], in1=lamr[:], op=ALU.mult)
            V("dve", "tensor_tensor", out=s3[:], in0=a_im, in1=lami[:], op=ALU.mult)
            V("dve", "tensor_tensor", out=fre[:], in0=fre[:], in1=s3[:], op=ALU.add)
            V("dve", "tensor_tensor", out=fre[:], in0=fre[:], in1=s1[:], op=ALU.mult)
            V("dve", "tensor_tensor", out=fim[:], in0=a_im, in1=lamr[:], op=ALU.mult)
            V("dve", "tensor_tensor", out=s3[:], in0=s2_[:], in1=lami[:], op=ALU.mult)
            V("dve", "tensor_tensor", out=fim[:], in0=fim[:], in1=s3[:], op=ALU.subtract)
            V("dve", "tensor_tensor", out=fim[:], in0=fim[:], in1=s1[:], op=ALU.mult)
            frb = fre[:].unsqueeze(2).broadcast_to([128, 32, 16])
            fib = fim[:].unsqueeze(2).broadcast_to([128, 32, 16])
            V("dve", "tensor_tensor", out=bbr[:], in0=Bre[:], in1=frb, op=ALU.mult)
            V("dve", "tensor_tensor", out=t16[:], in0=Bim[:], in1=fib, op=ALU.mult)
            V("dve", "tensor_tensor", out=bbr[:], in0=bbr[:], in1=t16[:], op=ALU.subtract)
            V("dve", "tensor_tensor", out=bbi[:], in0=Bim[:], in1=frb, op=ALU.mult)
            V("dve", "tensor_tensor", out=t16[:], in0=Bre[:], in1=fib, op=ALU.mult)
            V("dve", "tensor_tensor", out=bbi[:], in0=bbi[:], in1=t16[:], op=ALU.add)

            def v4(t):
                return t[:].rearrange("p g (s c) -> p g s c", c=16)

            def outer(are, aim, xre, xim, ore, oim, neg_im=False):
                ab = are.unsqueeze(3).broadcast_to([128, 32, 8, 16])
                ai = aim.unsqueeze(3).broadcast_to([128, 32, 8, 16])
                xr = xre.unsqueeze(2).broadcast_to([128, 32, 8, 16])
                xi = xim.unsqueeze(2).broadcast_to([128, 32, 8, 16])
                V("dve", "tensor_tensor", out=v4(ore), in0=ab, in1=xr, op=ALU.mult)
                V("dve", "tensor_tensor", out=v4(big), in0=ai, in1=xi, op=ALU.mult)
                V("dve", "tensor_tensor", out=ore[:], in0=ore[:], in1=big[:], op=ALU.subtract)
                V("dve", "tensor_tensor", out=v4(oim), in0=ab, in1=xi, op=ALU.mult)
                V("dve", "tensor_tensor", out=v4(big), in0=ai, in1=xr, op=ALU.mult)
                V("dve", "tensor_tensor", out=oim[:], in0=oim[:], in1=big[:], op=ALU.add)
            outer(Are[:, :, 0:8], Aim[:, :, 0:8], bbr[:], bbi[:], Gre, Gim)
            outer(Are[:, :, 8:16], Aim[:, :, 8:16], Cre[:], Cim[:], Hre, Him)
            V("dve", "tensor_scalar", out=Him[:], in0=Him[:], scalar1=-1.0, scalar2=None, op0=ALU.mult)
            for g in range(64):
                gh, gl_ = g // 32, g % 32
                psl = slice(gh * 64, gh * 64 + 64)
                bk = (g // 4) % 2
                col = (g % 4) * 128
                MM(bank(bk)[:, col:col + 128], Gre[psl, gl_, :], Hre[psl, gl_, :], True, False, [SB_], PB[bk])
                MM(bank(bk)[:, col:col + 128], Gim[psl, gl_, :], Him[psl, gl_, :], False, True, [SB_], PB[bk])
                if g % 4 == 3:
                    g0 = g - 3
                    P.op("dve", "tensor_tensor", reads=[PB[bk], SB_], writes=[TB],
                         out=Tall[:, g0:g0 + 4, :], in0=bank(bk)[:, :].rearrange("p (g m) -> p g m", m=128),
                         in1=tmask[:].rearrange("p (g m) -> p g m", m=128), op=ALU.mult)
            V("act", "copy", out=WR[:], in_=Hre[:])
            V("act", "copy", out=WI[:], in_=Him[:])
            P.op("act", "copy", reads=[SB_], writes=[TB], out=WR[:], in_=Hre[:])
            a8r = Are[:, :, 15:16].broadcast_to([128, 32, 128])
            a8i = Aim[:, :, 15:16].broadcast_to([128, 32, 128])
            V("dve", "tensor_tensor", out=Hre[:], in0=Gre[:], in1=a8r, op=ALU.mult)
            V("dve", "tensor_tensor", out=big[:], in0=Gim[:], in1=a8i, op=ALU.mult)
            V("dve", "tensor_tensor", out=Hre[:], in0=Hre[:], in1=big[:], op=ALU.subtract)
            V("dve", "tensor_tensor", out=Him[:], in0=Gim[:], in1=a8r, op=ALU.mult)
            V("dve", "tensor_tensor", out=big[:], in0=Gre[:], in1=a8i, op=ALU.mult)
            V("dve", "tensor_tensor", out=Him[:], in0=Him[:], in1=big[:], op=ALU.add)
            for g in range(64):
                gh, gl_ = g // 32, g % 32
                psl = slice(gh * 64, gh * 64 + 64)
                bk = 2 + (g // 4) % 2
                col = (g % 4) * 128
                MM(bank(bk)[:, col:col + 64], Hre[psl, gl_, :], identf[psl, psl], True, True, [SB_], PB[bk])
                MM(bank(bk)[:, col + 64:col + 128], Him[psl, gl_, :], identf[psl, psl], True, True, [SB_], PB[bk])
                if g % 4 == 3:
                    g0 = g - 3
                    P.op("act", "copy", reads=[PB[bk]], writes=[TB], out=Vall[:, g0:g0 + 4, :],
                         in_=bank(bk)[:, :].rearrange("p (g m) -> p g m", m=128))
            nb_ = nvec[:].unsqueeze(1).broadcast_to([128, 32, 64])
            V("dve", "tensor_tensor", out=eb[:], in0=li[:].unsqueeze(2).broadcast_to([128, 32, 64]), in1=nb_, op=ALU.mult)
            sincos(Es[:], Ec[:], eb[:], Rt[:], big[:, :, 0:64], isc[:])
            V("act", "activation", out=r8s[:], in_=lr[:], func=AF.Exp, scale=8.0)
            P.op("dve", "tensor_copy", reads=[SB_], writes=[TB], out=Rt[:], in_=r8s[:].unsqueeze(2).broadcast_to([128, 32, 64]))
            P.op("pool", "memset", reads=[SB_], writes=[TB], ap=Rt[:, :, 0:1], constant=0.0)
            P.flush()

        sel = sb("sel", [128, 64, 128], BF16)
        selT = sb("selT", [128, 64, 128], BF16)
        P.dma("sp", sel[:], k.C["sel_b"], writes=[TB])
        P.dma("act", selT[:], k.C["selT_b"], writes=[TB])
        uTs = [sb("uT%d" % i, [128, 8, TT], BF16) for i in range(2)]
        UTBs = [Buf("uT%d" % i) for i in range(2)]
        Ugs = [sb("Ug%d" % i, [128, 64, 64], BF16) for i in range(2)]
        UGBs = [Buf("Ug%d" % i) for i in range(2)]
        XsR = sb("XsR", [128, 32, 64], BF16)
        XsI = sb("XsI", [128, 32, 64], BF16)
        XSB = Buf("Xs")
        Yg = sb("Yg", [128, 64, 64], BF16)
        YGB = Buf("Yg")
        yT = sb("yT", [128, 8, TT], BF16)
        YTB = Buf("yT")
        ta = sb("ta", [128, 16, 64], F32)
        tb = sb("tb", [128, 16, 64], F32)
        vr = sb("vr", [128, 16, 64], F32)
        vi = sb("vi", [128, 16, 64], F32)
        zr = sb("zr", [128, 16, 64], F32)
        zi = sb("zi", [128, 16, 64], F32)
        tc_ = sb("tc", [128, 16], F32)
        WKB = Buf("s2work")
        for t_ in (ta, tb, vr, vi, zr, zi):
            P.op("pool", "memset", writes=[WKB], ap=t_[:], constant=0.0)
        P.op("pool", "memset", writes=[CARB], ap=XRc[:], constant=0.0)
        P.op("pool", "memset", writes=[CARB], ap=XIc[:], constant=0.0)
        P.op("pool", "memset", writes=[XSB], ap=XsR[:], constant=0.0)
        P.op("pool", "memset", writes=[XSB], ap=XsI[:], constant=0.0)

        def f2(t):
            return t[:].rearrange("p g n -> p (g n)")
        seqs = [("prm", t * TT, TT) for t in range(T // TT)] + [("smp", T, TS)]
        if "s2_seqs" in DBG:
            seqs = [seqs[i] for i in DBG["s2_seqs"]]
        def load_u(ti):
            kind, t0, n = seqs[ti]
            P.dma("sp", uTs[ti % 2][:, :, 0:n], k.S["uT"][:, :, t0:t0 + n].rearrange("c p t -> p c t"), writes=[UTBs[ti % 2]])
        load_u(0)
        def tile_vars(ti):
            kind, t0, n = seqs[ti]
            return kind, t0, n, n // 8, uTs[ti % 2], UTBs[ti % 2], Ugs[ti % 2], UGBs[ti % 2]

        def stageA(ti):
            kind, t0, n, nch, uT, UTB, Ug, UGB = tile_vars(ti)
            for g in range(64):
                kc, g8 = g // 8, g % 8
                bk = (g // 8) % 2
                col = g8 * 64
                for s_ in range(8):
                    rhs = uT[:, kc, 0:n].rearrange("p (n s) -> p n s", s=8)[:, :, s_]
                    MM(bank(bk)[:, col:col + nch], sel[:, g8 * 8 + s_, :], rhs, s_ == 0, s_ == 7, [UTB, TB], PB[bk])
                if g8 == 7:
                    g0 = g - 7
                    src = bank(bk)[:, :].rearrange("p (g n) -> p g n", n=64)[:, :, 0:nch]
                    if (g // 8) % 2 == 0:
                        P.op("act", "copy", reads=[PB[bk]], writes=[UGB], out=Ug[:, g0:g0 + 8, 0:nch], in_=src)
                    else:
                        P.op("dve", "tensor_copy", reads=[PB[bk]], writes=[UGB], out=Ug[:, g0:g0 + 8, 0:nch], in_=src)

        def stageB(ti):
            kind, t0, n, nch, uT, UTB, Ug, UGB = tile_vars(ti)
            if kind == "smp":
                for gh in range(2):
                    psl = slice(gh * 64, gh * 64 + 64)
                    gs = slice(gh * 32, gh * 32 + 32)
                    P.dma("sp", XRc[psl, :], k.I["state_s5_re"][gs, :].rearrange("g p -> p g"), writes=[CARB],
                          allow_slow_non_contiguous=True)
                    P.dma("act", XIc[psl, :], k.I["state_s5_im"][gs, :].rearrange("g p -> p g"), writes=[CARB],
                          allow_slow_non_contiguous=True)
            for hf in range(2):
                glo = hf * 16
                for gh in range(2):
                    psl = slice(gh * 64, gh * 64 + 64)
                    for gq in range(16):
                        g = gh * 32 + glo + gq
                        bR = 2 + gq // 8
                        bI = 4 + gq // 8
                        col = (gq % 8) * 64
                        MM(bank(bR)[psl, col:col + nch], Vall[:, g, 0:64], Ug[:, g, 0:nch], True, True, [UGB, TB], PB[bR])
                        MM(bank(bI)[psl, col:col + nch], Vall[:, g, 64:128], Ug[:, g, 0:nch], True, True, [UGB, TB], PB[bI])
                pR = ps[:, 2 * 512:4 * 512].rearrange("p (g n) -> p g n", n=64)[:, :, 0:nch]
                pI = ps[:, 4 * 512:6 * 512].rearrange("p (g n) -> p g n", n=64)[:, :, 0:nch]
                ec = Ec[:, glo:glo + 16, 0:nch]
                es = Es[:, glo:glo + 16, 0:nch]
                RD = [PB[2], PB[3], PB[4], PB[5], TB, WKB]
                WW = [WKB]

                def W3(t):
                    return t[:, :, 0:nch]
                P.op("dve", "tensor_tensor", reads=RD, writes=WW, out=W3(ta), in0=pR, in1=ec, op=ALU.mult)
                P.op("dve", "tensor_tensor", reads=RD, writes=WW, out=W3(tb), in0=pI, in1=es, op=ALU.mult)
                P.op("dve", "tensor_tensor", reads=RD, writes=WW, out=W3(vr), in0=W3(ta), in1=W3(tb), op=ALU.add)
                P.op("dve", "tensor_tensor", reads=RD, writes=WW, out=W3(ta), in0=pI, in1=ec, op=ALU.mult)
                P.op("dve", "tensor_tensor", reads=RD, writes=WW, out=W3(tb), in0=pR, in1=es, op=ALU.mult)
                P.op("dve", "tensor_tensor", reads=RD, writes=WW, out=W3(vi), in0=W3(ta), in1=W3(tb), op=ALU.subtract)
                for (vv, cc) in ((vr, XRc), (vi, XIc)):
                    P.op("dve", "tensor_tensor", reads=[CARB, TB, WKB], writes=WW, out=tc_[:], in0=cc[:, glo:glo + 16],
                         in1=r8s[:, glo:glo + 16], op=ALU.mult)
                    P.op("dve", "tensor_tensor", reads=[WKB], writes=WW, out=vv[:, :, 0], in0=vv[:, :, 0], in1=tc_[:],
                         op=ALU.add)
                rt2 = Rt[:, glo:glo + 16, :].rearrange("p g n -> p (g n)")
                P.op("dve", "tensor_tensor_scan", reads=[WKB, TB], writes=WW, out=f2(zr), data0=rt2, data1=f2(vr), initial=0.0,
                     op0=ALU.mult, op1=ALU.add)
                P.op("dve", "tensor_tensor_scan", reads=[WKB, TB], writes=WW, out=f2(zi), data0=rt2, data1=f2(vi), initial=0.0,
                     op0=ALU.mult, op1=ALU.add)
                P.op("dve", "tensor_tensor", reads=[WKB, TB], writes=WW, out=W3(ta), in0=W3(zr), in1=ec, op=ALU.mult)
                P.op("dve", "tensor_tensor", reads=[WKB, TB], writes=WW, out=W3(tb), in0=W3(zi), in1=es, op=ALU.mult)
                P.op("dve", "tensor_tensor", reads=[WKB], writes=WW, out=W3(vr), in0=W3(ta), in1=W3(tb), op=ALU.subtract)
                P.op("dve", "tensor_tensor", reads=[WKB, TB], writes=WW, out=W3(ta), in0=W3(zi), in1=ec, op=ALU.mult)
                P.op("dve", "tensor_tensor", reads=[WKB, TB], writes=WW, out=W3(tb), in0=W3(zr), in1=es, op=ALU.mult)
                P.op("dve", "tensor_tensor", reads=[WKB], writes=WW, out=W3(vi), in0=W3(ta), in1=W3(tb), op=ALU.add)
                P.op("act", "copy", reads=[CARB], writes=[XSB], out=XsR[:, glo:glo + 16, 0], in_=XRc[:, glo:glo + 16])
                P.op("act", "copy", reads=[CARB], writes=[XSB], out=XsI[:, glo:glo + 16, 0], in_=XIc[:, glo:glo + 16])
                if nch > 1:
                    P.op("act", "copy", reads=[WKB], writes=[XSB], out=XsR[:, glo:glo + 16, 1:nch], in_=vr[:, :, 0:nch - 1])
                    P.op("act", "copy", reads=[WKB], writes=[XSB], out=XsI[:, glo:glo + 16, 1:nch], in_=vi[:, :, 0:nch - 1])
                P.op("dve", "tensor_copy", reads=[WKB, XSB], writes=[CARB], out=XRc[:, glo:glo + 16], in_=vr[:, :, nch - 1])
                P.op("dve", "tensor_copy", reads=[WKB, XSB], writes=[CARB], out=XIc[:, glo:glo + 16], in_=vi[:, :, nch - 1])
            last_of_kind = (kind == "smp") or (t0 + n == T)
            if last_of_kind:
                for gh in range(2):
                    psl = slice(gh * 64, gh * 64 + 64)
                    gs = slice(gh * 32, gh * 32 + 32)
                    dr = k.O["s5_re_sample" if kind == "smp" else "s5_re_prompt"]
                    di = k.O["s5_im_sample" if kind == "smp" else "s5_im_prompt"]
                    P.dma("sp", dr[gs, :].rearrange("g p -> p g"), XRc[psl, :], reads=[CARB], writes=[Buf()],
                          allow_slow_non_contiguous=True)
                    P.dma("act", di[gs, :].rearrange("g p -> p g"), XIc[psl, :], reads=[CARB], writes=[Buf()],
                          allow_slow_non_contiguous=True)

        def stageC(ti):
            kind, t0, n, nch, uT, UTB, Ug, UGB = tile_vars(ti)
            for g in range(64):
                gh, gl_ = g // 32, g % 32
                psl = slice(gh * 64, gh * 64 + 64)
                bk = 6 + (g // 8) % 2
                col = (g % 8) * 64
                MM(bank(bk)[:, col:col + nch], Tall[:, g, :], Ug[:, g, 0:nch], True, False, [UGB, TB], PB[bk])
                MM(bank(bk)[:, col:col + nch], WR[psl, gl_, :], XsR[psl, gl_, 0:nch], False, False, [XSB, TB], PB[bk])
                MM(bank(bk)[:, col:col + nch], WI[psl, gl_, :], XsI[psl, gl_, 0:nch], False, True, [XSB, TB], PB[bk])
                if g % 8 == 7:
                    g0 = g - 7
                    src = bank(bk)[:, :].rearrange("p (g n) -> p g n", n=64)[:, :, 0:nch]
                    if (g // 8) % 2 == 0:
                        P.op("act", "copy", reads=[PB[bk]], writes=[YGB], out=Yg[:, g0:g0 + 8, 0:nch], in_=src)
                    else:
                        P.op("dve", "tensor_copy", reads=[PB[bk]], writes=[YGB], out=Yg[:, g0:g0 + 8, 0:nch], in_=src)
            for kc in range(8):
                bk = kc % 2
                for tau in range(8):
                    for g8 in range(8):
                        MM(bank(bk)[:, tau * 64:tau * 64 + nch], selT[:, g8 * 8 + tau, :], Yg[:, kc * 8 + g8, 0:nch], g8 == 0,
                           g8 == 7, [YGB, TB], PB[bk])
                yv = yT[:, kc, 0:n].rearrange("p (n s) -> p s n", s=8)
                uv = uT[:, kc, 0:n].rearrange("p (n s) -> p s n", s=8)
                pv = bank(bk)[:, :].rearrange("p (s n) -> p s n", n=64)[:, :, 0:nch]
                P.op("dve", "scalar_tensor_tensor", reads=[PB[bk], UTB, TB], writes=[YTB], out=yv, in0=uv, scalar=dfm[:, kc:kc + 1],
                     in1=pv, op0=ALU.mult, op1=ALU.add)
            P.dma("act", k.S["yT"][:, :, t0:t0 + n].rearrange("c p t -> p c t"), yT[:, :, 0:n], reads=[YTB], writes=[Buf()])

        stageA(0)
        for ti in range(len(seqs)):
            if ti + 1 < len(seqs):
                load_u(ti + 1)
            stageB(ti)
            if ti + 1 < len(seqs):
                stageA(ti + 1)
            stageC(ti)
            P.maybe_flush()
        P.flush()
```

```python
import contextlib
import numpy as np
import ml_dtypes
import concourse.bass as bass
import concourse.mybir as mybir
from concourse.bass_utils import run_bass_kernel_spmd

F32 = mybir.dt.float32
BF16 = mybir.dt.bfloat16
AF = mybir.ActivationFunctionType
ALU = mybir.AluOpType
AX = mybir.AxisListType

D = 1024
T = 8192
TS = 32
PAST = 4096
TALL = T + 128
TT = 512
EPS = 1e-6
AB_IN = 3592
NCORES = 8


class Buf:
    __slots__ = ("name", "w", "r", "pw")

    def __init__(self, name=""):
        self.name = name
        self.w = {}
        self.r = {}
        self.pw = {}


class Op:
    __slots__ = ("eng", "fn", "deps", "marked", "sem", "val", "is_dma", "slotwait", "phase")

    def __init__(self, eng, fn, is_dma, phase):
        self.eng = eng
        self.fn = fn
        self.deps = []
        self.marked = False
        self.sem = None
        self.val = 0
        self.is_dma = is_dma
        self.slotwait = None
        self.phase = phase


class Prog:
    ENGS = ("pe", "act", "dve", "pool", "sp")
    NSLOT = 8

    def __init__(self, nc, stack):
        self.nc = nc
        self.ops = []
        self.phase = 0
        self.csem = {e: stack.enter_context(nc.semaphore("c_" + e)) for e in ("pe", "act", "dve", "pool")}
        self.dsem = {e: [stack.enter_context(nc.semaphore("d_%s%d" % (e, i))) for i in range(self.NSLOT)]
                     for e in ("sp", "act", "pool")}
        self.ccount = {e: 0 for e in self.csem}
        self.dcount = {e: 0 for e in self.dsem}
        self.waited = {e: {} for e in self.ENGS}
        self.dma_final = {}
        self.n_emitted = 0

    def _add(self, op, reads, writes):
        deps = {}
        for b in reads:
            for o in b.w.values():
                deps[id(o)] = o
        key = id(op) if op.is_dma else op.eng
        for b in writes:
            merge = op.is_dma and b.w and not b.r and all(o.is_dma and o.phase == self.phase for o in b.w.values())
            if merge:
                for o in getattr(b, "pw", {}).values():
                    deps[id(o)] = o
                b.w[key] = op
                continue
            prev = {}
            for o in b.w.values():
                deps[id(o)] = o
                prev[id(o)] = o
            for o in b.r.values():
                deps[id(o)] = o
                prev[id(o)] = o
            b.pw = prev
            b.w = {key: op}
            b.r = {}
        for b in reads:
            b.r[key] = op
        for o in deps.values():
            if o is op or o.phase != self.phase:
                continue
            if (not o.is_dma) and (not op.is_dma) and o.eng == "pe" and op.eng == "pe":
                continue
            op.deps.append(o)
            o.marked = True
        self.ops.append(op)
        return op

    def op(self, eng, name, reads=(), writes=(), **kw):
        def fn(e):
            return getattr(e, name)(**kw)
        return self._add(Op(eng, fn, False, self.phase), reads, writes)

    def dma(self, eng, out, in_, reads=(), writes=(), **kw):
        def fn(e):
            return e.dma_start(out=out, in_=in_, **kw)
        o = Op(eng, fn, True, self.phase)
        o.marked = True
        return self._add(o, reads, writes)

    def maybe_flush(self, limit=2500):
        if len(self.ops) >= limit:
            self.flush()

    def flush(self):
        if not self.ops:
            return
        nc = self.nc
        per = {e: [] for e in self.ENGS}
        last_c = {}
        for o in self.ops:
            per[o.eng].append(o)
            if not o.is_dma:
                last_c[o.eng] = o
        for o in last_c.values():
            o.marked = True
        for o in self.ops:
            if o.is_dma:
                i = self.dcount[o.eng]
                self.dcount[o.eng] += 1
                o.sem = self.dsem[o.eng][i % self.NSLOT]
                o.val = 16 * (i // self.NSLOT + 1)
                if i >= self.NSLOT:
                    o.slotwait = (o.sem, o.val - 16)
                self.dma_final[id(o.sem)] = (o.sem, o.val)
            elif o.marked:
                self.ccount[o.eng] += 1
                o.sem = self.csem[o.eng]
                o.val = self.ccount[o.eng]
        finals = list(self.dma_final.values()) + [(self.csem[e], self.ccount[e]) for e in self.csem]
        self.n_emitted += len(self.ops)

        def make(ename, ops):
            waited = self.waited[ename]

            def body(e):
                def w(sem, val):
                    if val > 0 and waited.get(id(sem), 0) < val:
                        e.wait_ge(sem, val)
                        waited[id(sem)] = val
                for o in ops:
                    if o.slotwait is not None:
                        w(*o.slotwait)
                    for d in o.deps:
                        w(d.sem, d.val)
                    ins = o.fn(e)
                    if o.marked:
                        ins.then_inc(o.sem, 16 if o.is_dma else 1)
                for sem, val in finals:
                    w(sem, val)
            return body

        with nc.allow_low_precision(reason="bf16 matmul operands by design"), nc.Block() as blk:
            blk.tensor(make("pe", per["pe"]))
            blk.scalar(make("act", per["act"]))
            blk.vector(make("dve", per["dve"]))
            blk.gpsimd(make("pool", per["pool"]))
            blk.sync(make("sp", per["sp"]))
        self.ops = []
        self.phase += 1


def _consts():
    c = {}
    c["ident_f"] = np.eye(128, dtype=np.float32)
    c["ident_b"] = np.eye(128, dtype=np.float32).astype(ml_dtypes.bfloat16)
    j = np.arange(128)
    c["tri_f"] = (j[:, None] <= j[None, :]).astype(np.float32)
    c["ones_f"] = np.ones((128, 128), np.float32)
    c["ones_b"] = np.ones((128, 128), np.float32).astype(ml_dtypes.bfloat16)
    perm = np.zeros((128, 128), np.float32)
    perm[(j + 64) % 128, j] = 1.0
    c["perm_b"] = perm.astype(ml_dtypes.bfloat16)
    half = 64
    inv = (10000.0 ** (-np.arange(half, dtype=np.float32) / half)).astype(np.float32)
    pos = np.concatenate([np.arange(T), PAST + np.arange(128)]).astype(np.float32)
    ang = pos[None, :] * inv[:, None]
    cos = np.cos(ang).astype(np.float32)
    sin = np.sin(ang).astype(np.float32)
    c["ropec"] = np.concatenate([cos, cos], 0)
    c["ropes"] = np.concatenate([-sin, sin], 0)
    lg = np.log(1.0 - 2.0 ** (-5.0 - np.arange(4, dtype=np.float64)))
    il = np.arange(512)[None, :]
    jl = np.arange(128)[:, None]
    rmask = np.zeros((4, 128, 512), np.float64)
    for h in range(4):
        m = np.exp(lg[h] * np.abs(il - jl))
        ok = (jl // 64) <= (il // 64)
        rmask[h] = np.where(ok, m, 0.0)
    c["rmask"] = rmask.transpose(1, 0, 2).astype(np.float32).copy()
    ii = np.arange(512).reshape(4, 128)
    wq = np.exp(lg[None, None, :] * (ii.T[:, :, None] + 1.0))
    c["wq"] = np.repeat(wq[:, :, :, None], 128, 3).astype(np.float32)
    wk = np.exp(lg[None, None, :] * (511.0 - ii.T[:, :, None]))
    c["wk"] = np.repeat(wk[:, :, :, None], 128, 3).astype(np.float32)
    wks = np.exp(lg[None, :] * (31.0 - np.arange(128)[:, None]))
    wks[32:] = 0.0
    c["wks"] = np.repeat(wks[:, :, None], 128, 2).astype(np.float32)
    c["rdec"] = np.stack([np.exp(lg * 512.0), np.exp(lg * 32.0)]).astype(np.float32)
    qs = np.ones((128, 1), np.float32)
    qs[:64] = 0.125
    c["qscale"] = qs
    selc = np.zeros((8, 8, 3, 3), np.float32)
    for h in range(8):
        for p_ in range(3):
            selc[h, h, p_, p_] = 1.0
    c["selc"] = selc.astype(ml_dtypes.bfloat16)
    c["caus_b"] = (j[:, None] <= j[None, :]).astype(np.float32).astype(ml_dtypes.bfloat16)
    kv = np.concatenate([-(np.arange(8) + 1.0), np.arange(8) + 1.0]).astype(np.float32)
    c["kvec"] = np.repeat(kv[None, :], 128, 0).copy()
    c["nvec"] = np.repeat((8.0 * (np.arange(64) + 1.0)).astype(np.float32)[None, :], 128, 0).copy()
    sel = np.zeros((128, 8, 8, 128), np.float32)
    selT = np.zeros((128, 8, 8, 128), np.float32)
    for g_ in range(8):
        for s_ in range(8):
            for c_ in range(16):
                sel[g_ * 16 + c_, g_, s_, s_ * 16 + c_] = 1.0
                selT[s_ * 16 + c_, g_, s_, g_ * 16 + c_] = 1.0
    c["sel_b"] = sel.reshape(128, 64, 128).astype(ml_dtypes.bfloat16)
    c["selT_b"] = selT.reshape(128, 64, 128).astype(ml_dtypes.bfloat16)
    sg = np.arange(128) // 16
    tm_ = (sg[None, :] >= sg[:, None]).astype(np.float32)
    c["tmask"] = np.tile(tm_, (1, 4)).copy()
    return c


CONST = None


def get_consts():
    global CONST
    if CONST is None:
        CONST = _consts()
    return CONST


IN_SHAPES = {
    "x_prompt": [T, D], "x_sample": [TS, D], "cache_ret_state": [4, 128, 128],
    "cache_fox_k": [PAST, 512], "cache_fox_v": [PAST, 512], "cache_fox_logf": [PAST, 8],
    "state_s5_re": [64, 64], "state_s5_im": [64, 64],
    "cache_mem_k": [2, 256, 1024], "cache_mem_v": [2, 256, 1024], "mem_prompt": [256, 1024],
    "norm_w": [2, 6, D], "w_in_ab": [D, AB_IN], "b_fox_f": [1, 8], "ret_gn_w": [1, 512],
    "w_out_ab": [D, D], "w_in_c": [D, D], "s5_lambda_re": [64, 64], "s5_lambda_im": [64, 64],
    "s5_log_dt": [1, 64], "s5_b_re": [64, 64, 16], "s5_b_im": [64, 64, 16],
    "s5_c_re": [64, 16, 64], "s5_c_im": [64, 16, 64], "s5_d": [1, D], "w_glu": [D, 2 * D],
    "mem_norm_w": [2, D], "w_xq": [2, D, D], "w_xk": [2, D, D], "w_xv": [2, D, D], "w_xo": [2, D, D],
    "w_up": [2, D, 4 * D], "w_down": [2, 4 * D, D],
}
OUT_SHAPES = {
    "y_prompt": [T, D], "y_sample": [TS, D], "ret_state_prompt": [4, 128, 128], "ret_state_sample": [4, 128, 128],
    "fox_k_prompt": [T, 512], "fox_v_prompt": [T, 512], "fox_logf_prompt": [T, 8],
    "fox_k_sample": [TS, 512], "fox_v_sample": [TS, 512], "fox_logf_sample": [TS, 8],
    "s5_re_prompt": [64, 64], "s5_im_prompt": [64, 64], "s5_re_sample": [64, 64], "s5_im_sample": [64, 64],
    "mem_k_prompt": [2, 256, 1024], "mem_v_prompt": [2, 256, 1024],
}
OUT_ORDER = list(OUT_SHAPES.keys())


class K:
    def __init__(self):
        self.nc = bass.Bass("TRN2", target_bir_lowering=False)
        nc = self.nc
        self.I = {k: nc.dram_tensor(k, s, F32, kind="ExternalInput").ap() for k, s in IN_SHAPES.items()}
        self.O = {k: nc.dram_tensor(k, s, F32, kind="ExternalOutput").ap() for k, s in OUT_SHAPES.items()}
        self.C = {}
        for k, v in get_consts().items():
            dt = BF16 if v.dtype == ml_dtypes.bfloat16 else F32
            self.C[k] = nc.dram_tensor("c_" + k, list(v.shape), dt, kind="ExternalInput").ap()
        self.S = {}
        self.SB = {}

    def scratch(self, name, shape, dt):
        self.S[name] = self.nc.dram_tensor("s_" + name, shape, dt, kind="Internal").ap()
        self.SB[name] = {}
        return self.S[name]

    def sbuf(self, st, name, shape, dt):
        return st.enter_context(self.nc.sbuf_tensor(name, shape, dt))


def tiles():
    out = [(t * TT, TT, False) for t in range(T // TT)]
    out.append((T, TS, True))
    return out


def xsrc(k, which, t0, n, sample):
    return k.I["x_sample"] if sample else k.I["x_prompt"][t0:t0 + n, :]


def blocks(n):
    return [(b * 128, min(128, n - b * 128)) for b in range((n + 127) // 128)]


def build(phases=("W", "A1")):
    k = K()
    nc = k.nc
    with contextlib.ExitStack() as top:
        P = Prog(nc, top)
        ps = top.enter_context(nc.psum_tensor("ps", [128, 4096], F32))
        PB = [Buf("ps%d" % i) for i in range(8)]

        def bank(i):
            return ps[:, i * 512:(i + 1) * 512]

        identf = k.sbuf(top, "identf", [128, 128], F32)
        identb = k.sbuf(top, "identb", [128, 128], BF16)
        onesf = k.sbuf(top, "onesf", [128, 128], F32)
        onesb = k.sbuf(top, "onesb", [128, 128], BF16)
        CB = Buf("consts")
        P.dma("sp", identf[:], k.C["ident_f"], writes=[CB])
        P.dma("sp", identb[:], k.C["ident_b"], writes=[CB])
        P.dma("sp", onesf[:], k.C["ones_f"], writes=[CB])
        P.dma("sp", onesb[:], k.C["ones_b"], writes=[CB])
        P.flush()

        WNAMES = [("w_in_ab", None, [D, AB_IN]), ("w_out_ab", None, [D, D]), ("w_in_c", None, [D, D]),
                  ("w_glu", None, [D, 2 * D])]
        for l in range(2):
            for nm in ("w_xq", "w_xk", "w_xv", "w_xo"):
                WNAMES.append((nm, l, [D, D]))
            WNAMES.append(("w_up", l, [D, 4 * D]))
            WNAMES.append(("w_down", l, [4 * D, D]))
        WB = {}
        for nm, l, shp in WNAMES:
            key = nm if l is None else "%s%d" % (nm, l)
            WB[key] = k.scratch("wb_" + key, shp, BF16)
        if "W" in phases:
            wbuf = Buf("wcast")
            for nm, l, shp in WNAMES:
                key = nm if l is None else "%s%d" % (nm, l)
                src = k.I[nm] if l is None else k.I[nm][l]
                rows = shp[0]
                step = 256
                for r0 in range(0, rows, step):
                    P.dma("pool", WB[key][r0:r0 + step, :], src[r0:r0 + step, :], writes=[Buf()])
            P.flush()

        k.scratch("x", [TALL, D], F32)
        k.scratch("rqT", [4, 128, TALL], BF16)
        k.scratch("rkT", [4, 128, TALL], BF16)
        k.scratch("rkk", [TALL, 512], BF16)
        k.scratch("rv", [TALL, 512], BF16)
        k.scratch("rg", [TALL, 512], BF16)
        k.scratch("fqa", [17, 4, 67, 2, TT], BF16)
        k.scratch("fka", [25, 4, 67, 2, TT], BF16)
        k.scratch("fva", [25, 4, 128, 4, 130], BF16)
        k.scratch("negc", [25, 128, 4, 8], F32)

        k.scratch("foT", [8, 64, TALL], BF16)
        k.scratch("uT", [8, 128, TALL], BF16)
        k.scratch("yT", [8, 128, TALL], BF16)
        env = dict(k=k, P=P, ps=ps, PB=PB, bank=bank, WB=WB, identf=identf, identb=identb, onesf=onesf, onesb=onesb)
        if "A1" in phases:
            phase_A1(k, P, ps, PB, bank, WB, identf, identb, onesf, onesb)
        if "A3" in phases:
            phase_A3(**env)
        if "A2" in phases:
            phase_A2(**env)
        for l in range(2):
            if "S" in phases and l == 1:
                phase_S1(**env)
                phase_S2(**env)
                phase_S3(**env)
            if ("C%d" % l) in phases:
                phase_C(l=l, **env)
            if ("F%d" % l) in phases:
                phase_F(l=l, final=(l == 1), **env)
        if "dumpx" in DBG:
            P.dma("sp", k.O["y_prompt"], k.S["x"][0:T, :], writes=[Buf()])
            P.dma("sp", k.O["y_sample"], k.S["x"][T:T + TS, :], writes=[Buf()])
            P.flush()
        if "dumpfo" in DBG:
            P.dma("pool", k.O["y_prompt"][0:512, :].rearrange("(h r) t -> h r t", h=8),
                  k.S["foT"][:, :, DBG["dumpfo"]:DBG["dumpfo"] + 1024], writes=[Buf()])
            P.flush()
    return k


CB_GLOBAL = Buf("cglobal")
DBG = {}


def phase_A1(k, P, ps, PB, bank, WB, identf, identb, onesf, onesb):
    with contextlib.ExitStack() as st:
        def sb(name, shape, dt):
            return k.sbuf(st, "a1_" + name, shape, dt)
        win = sb("win", [128, 8, AB_IN], BF16)
        g0 = sb("g0", [128, D], F32)
        bfox = sb("bfox", [128, 8], F32)
        perm = sb("perm", [128, 128], BF16)
        trif = sb("trif", [128, 128], F32)
        qscale = sb("qscale", [128, 1], F32)
        selc = sb("selc", [8, 72], BF16)
        wk = sb("wk", [128, 4, 512], F32)
        wks = sb("wks", [128, 512], F32)
        WBUF = Buf("a1w")
        wsrc = WB["w_in_ab"].rearrange("(kc p) n -> p kc n", p=128)
        for kc in range(8):
            P.dma("sp" if kc % 2 == 0 else "act", win[:, kc, :], wsrc[:, kc, :], writes=[WBUF])
        P.dma("sp", g0[:], k.I["norm_w"][0, 0:1, :].broadcast_to([128, D]), writes=[WBUF])
        P.dma("sp", bfox[:], k.I["b_fox_f"][0:1, :].broadcast_to([128, 8]), writes=[WBUF])
        P.dma("sp", perm[:], k.C["perm_b"], writes=[WBUF])
        P.dma("sp", trif[:], k.C["tri_f"], writes=[WBUF])
        P.dma("sp", qscale[:], k.C["qscale"], writes=[WBUF])
        P.dma("sp", selc[:], k.C["selc"].rearrange("k h p m -> k (h p m)"), writes=[WBUF])
        P.dma("sp", wk[:], k.C["wk"].rearrange("p b h e -> p b (h e)"), writes=[WBUF])
        P.dma("sp", wks[:], k.C["wks"].rearrange("p h e -> p (h e)"), writes=[WBUF])

        xt0 = sb("xt0", [128, 4, D], F32)
        xt = [xt0, xt0]
        XT0 = Buf("xt0")
        XT = [XT0, XT0]
        rc0 = sb("rc0", [128, TT], F32)
        rs0 = sb("rs0", [128, TT], F32)
        rc = [rc0, rc0]
        rs = [rs0, rs0]
        RT0 = Buf("rt0")
        RT = [RT0, RT0]
        junk = sb("junk", [128, D], BF16)
        JB = Buf("junk")
        ss = sb("ss", [128, 4], F32)
        rstd = sb("rstd", [128, 4], F32)
        SSB = Buf("ss")
        h = sb("h", [128, 4, D], BF16)
        HB = Buf("h")
        hT = sb("hT", [128, 8, TT], BF16)
        HTB = Buf("hT")
        qbs = [sb("qb%d" % i, [128, TT], BF16) for i in range(2)]
        QBs = [Buf("qb%d" % i) for i in range(2)]
        t1s = [sb("t1%d" % i, [128, TT], F32) for i in range(2)]
        t2s = [sb("t2%d" % i, [128, TT], F32) for i in range(2)]
        T1Bs = [Buf("t1%d" % i) for i in range(2)]
        T2Bs = [Buf("t2%d" % i) for i in range(2)]
        rqT = sb("rqT", [128, 8, TT], BF16)
        RQB = [Buf("rq%d" % i) for i in range(8)]
        rkk = sb("rkk", [128, 4, 512], BF16)
        RKKB = Buf("rkk")
        rv = sb("rv", [128, 4, 512], BF16)
        RVB = Buf("rv")
        rg = sb("rg", [128, 4, 512], BF16)
        RGB = Buf("rg")
        fkf = sb("fkf", [128, 4, 512], F32)
        FKFB = Buf("fkf")
        fkb = h
        FKBB = HB
        fvf = sb("fvf", [128, 4, 512], F32)
        FVFB = Buf("fvf")
        fva = sb("fva", [128, 4, 8 * 65], BF16)
        FVAB = Buf("fva")
        fka = sb("fka", [67, 8, TT], BF16)
        FKAB = Buf("fka")
        fqa = sb("fqa", [67, 8, TT], BF16)
        FQAB = Buf("fqa")
        lf = sb("lf", [128, 4, 8], F32)
        LFB = Buf("lf")
        xb = sb("xb", [128, 4, 8], F32)
        XBB = Buf("xb")
        cS = sb("cS", [128, 4, 8], F32)
        CSB = Buf("cS")
        ngc = sb("ngc", [128, 4, 8], F32)
        NGB = Buf("ngc")
        carry = sb("carry", [1, 8], F32)
        CARB = Buf("carry")
        cT = sb("cT", [8, TT], F32)
        c3 = sb("c3", [8, 3, TT], BF16)
        r1 = sb("r1", [8, TT], F32)
        r2 = cT
        CTB, C3B, R1B = Buf("cT"), Buf("c3"), Buf("r1")
        R2B = CTB

        epsc = sb("epsc", [128, 1], F32)
        P.op("pool", "memset", writes=[SSB], ap=epsc[:], constant=EPS)
        P.op("pool", "memset", writes=[FVAB], ap=fva[:], constant=1.0)
        P.op("pool", "memset", writes=[FKAB], ap=fka[64:67, :, :], constant=1.0)
        P.op("pool", "memset", writes=[CARB], ap=carry[:], constant=0.0)

        def MM(out, lhsT, rhs, start, stop, R, Wb):
            P.op("pe", "matmul", reads=R, writes=[Wb], out=out, lhsT=lhsT, rhs=rhs, start=start, stop=stop)

        seqs = [("prm", t * TT, TT) for t in range(T // TT)]
        seqs += [("past", pt * TT, TT) for pt in range(PAST // TT)]
        seqs += [("smp", T, TS)]
        if "a1_seqs" in DBG:
            seqs = [seqs[i] for i in DBG["a1_seqs"]]
        for ti, (kind, t0, n) in enumerate(seqs):
            s = ti % 2
            blks = blocks(n)
            nb = len(blks)
            pp = min(128, n)
            kcol0 = t0 if kind == "past" else PAST + t0

            def tm(ap2d):
                return ap2d.rearrange("(b p) d -> p b d", p=pp)
            if kind == "past" and t0 == 0:
                P.op("pool", "memset", writes=[CARB], ap=carry[:], constant=0.0)
            if kind == "past":
                P.dma("sp", fkf[:, :, :], tm(k.I["cache_fox_k"][t0:t0 + n, :]), writes=[FKFB])
                P.dma("act", fvf[:, :, :], tm(k.I["cache_fox_v"][t0:t0 + n, :]), writes=[FVFB])
                P.dma("sp", lf[:, :, :], tm(k.I["cache_fox_logf"][t0:t0 + n, :]), writes=[LFB])
            else:
                xs = k.I["x_sample"] if kind == "smp" else k.I["x_prompt"][t0:t0 + n, :]
                P.dma("sp", xt[s][0:pp, 0:nb, :], tm(xs), writes=[XT[s]])
                P.dma("act", rc[s][:, 0:n], k.C["ropec"][:, t0:t0 + n], writes=[RT[s]])
                P.dma("act", rs[s][:, 0:n], k.C["ropes"][:, t0:t0 + n], writes=[RT[s]])
                for b, (o, bs) in enumerate(blks):
                    P.op("act", "activation", reads=[XT[s]], writes=[JB, SSB], out=junk[0:bs, :], in_=xt[s][0:bs, b, :],
                         func=AF.Square, accum_out=ss[0:bs, b:b + 1])
                P.op("act", "activation", reads=[SSB], writes=[SSB], out=rstd[:, 0:nb], in_=ss[:, 0:nb], func=AF.Sqrt,
                     scale=1.0 / D, bias=epsc[:, 0:1])
                P.op("dve", "reciprocal", reads=[SSB], writes=[SSB], out=rstd[:, 0:nb], in_=rstd[:, 0:nb])
                for b, (o, bs) in enumerate(blks):
                    P.op("dve", "scalar_tensor_tensor", reads=[XT[s], SSB, WBUF], writes=[HB],
                         out=h[0:bs, b, :], in0=xt[s][0:bs, b, :], scalar=rstd[0:bs, b:b + 1], in1=g0[0:bs, :],
                         op0=ALU.mult, op1=ALU.mult)
                for kc in range(8):
                    bk = kc % 4
                    for b, (o, bs) in enumerate(blks):
                        MM(bank(bk)[:, o:o + bs], h[0:bs, b, kc * 128:(kc + 1) * 128], identb[0:bs, 0:bs], True, True,
                           [HB], PB[bk])
                    if kc % 2 == 0:
                        P.op("act", "copy", reads=[PB[bk]], writes=[HTB], out=hT[:, kc, 0:n], in_=bank(bk)[:, 0:n])
                    else:
                        P.op("dve", "tensor_copy", reads=[PB[bk]], writes=[HTB], out=hT[:, kc, 0:n], in_=bank(bk)[:, 0:n])

                def proj_fm(bk, c0, m):
                    for kc in range(8):
                        MM(bank(bk)[0:m, 0:n], win[:, kc, c0:c0 + m], hT[:, kc, 0:n], kc == 0, kc == 7,
                           [HTB, WBUF], PB[bk])

                def proj_tm(bk, o, bs, c0, ncols):
                    for kc in range(8):
                        MM(bank(bk)[0:bs, 0:ncols], hT[:, kc, o:o + bs], win[:, kc, c0:c0 + ncols], kc == 0, kc == 7,
                           [HTB, WBUF], PB[bk])

                for i8 in range(8):
                    bk = 4 + (i8 % 2)
                    qb, QB = qbs[i8 % 2], QBs[i8 % 2]
                    t1, t2, T1B, T2B = t1s[i8 % 2], t2s[i8 % 2], T1Bs[i8 % 2], T2Bs[i8 % 2]
                    pbk = 6 if i8 % 2 == 0 else 7
                    sc = 1.0 if i8 < 4 else 128.0 ** -0.5
                    proj_fm(bk, i8 * 128, 128)
                    P.op("act", "activation", reads=[PB[bk]], writes=[QB], out=qb[:, 0:n], in_=bank(bk)[:, 0:n],
                         func=AF.Copy, scale=sc)
                    MM(bank(pbk)[:, 0:n], perm[:, :], qb[:, 0:n], True, True, [QB, WBUF], PB[pbk])
                    P.op("pool", "tensor_tensor", reads=[QB, RT[s]], writes=[T1B], out=t1[:, 0:n], in0=qb[:, 0:n],
                         in1=rc[s][:, 0:n], op=ALU.mult)
                    P.op("dve", "tensor_tensor", reads=[PB[pbk], RT[s]], writes=[T2B], out=t2[:, 0:n], in0=bank(pbk)[:, 0:n],
                         in1=rs[s][:, 0:n], op=ALU.mult)
                    P.op("dve", "tensor_tensor", reads=[T1B, T2B], writes=[RQB[i8]], out=rqT[:, i8, 0:n], in0=t1[:, 0:n],
                         in1=t2[:, 0:n], op=ALU.add)
                    dst = k.S["rqT"][i8, :, t0:t0 + n] if i8 < 4 else k.S["rkT"][i8 - 4, :, t0:t0 + n]
                    P.dma("sp", dst, rqT[:, i8, 0:n], reads=[RQB[i8]], writes=[Buf()])
                for b, (o, bs) in enumerate(blks):
                    bk = b % 2
                    for hh in range(4):
                        MM(bank(bk)[0:bs, hh * 128:(hh + 1) * 128], rqT[:, 4 + hh, o:o + bs], identb[:, :], True, True,
                           [RQB[4 + hh]], PB[bk])
                    wtab = wks[0:bs, :] if kind == "smp" else wk[0:bs, b, :]
                    P.op("dve", "tensor_tensor", reads=[PB[bk], WBUF], writes=[RKKB], out=rkk[0:bs, b, :],
                         in0=bank(bk)[0:bs, :], in1=wtab, op=ALU.mult)
                P.dma("act", tm(k.S["rkk"][t0:t0 + n, :]), rkk[0:pp, 0:nb, :], reads=[RKKB], writes=[Buf()])
                for b, (o, bs) in enumerate(blks):
                    proj_tm(2, o, bs, 1024, 512)
                    P.op("act", "copy", reads=[PB[2]], writes=[RVB], out=rv[0:bs, b, :], in_=bank(2)[0:bs, :])
                    proj_tm(3, o, bs, 1536, 512)
                    P.op("act", "activation", reads=[PB[3]], writes=[RGB], out=rg[0:bs, b, :], in_=bank(3)[0:bs, :],
                         func=AF.Silu)
                    proj_tm(2, o, bs, 2560, 512)
                    P.op("act", "copy", reads=[PB[2]], writes=[FKFB], out=fkf[0:bs, b, :], in_=bank(2)[0:bs, :])
                    proj_tm(3, o, bs, 3072, 512)
                    P.op("dve", "tensor_copy", reads=[PB[3]], writes=[FVFB], out=fvf[0:bs, b, :], in_=bank(3)[0:bs, :])
                    proj_tm(7, o, bs, 3584, 8)
                    P.op("dve", "tensor_tensor", reads=[PB[7], WBUF], writes=[XBB], out=xb[0:bs, b, :],
                         in0=bank(7)[0:bs, 0:8], in1=bfox[0:bs, :], op=ALU.add)
                P.dma("act", tm(k.S["rv"][t0:t0 + n, :]), rv[0:pp, 0:nb, :], reads=[RVB], writes=[Buf()])
                P.dma("act", tm(k.S["rg"][t0:t0 + n, :]), rg[0:pp, 0:nb, :], reads=[RGB], writes=[Buf()])
                P.op("act", "activation", reads=[XBB], writes=[XBB], out=xb[0:pp, 0:nb, :], in_=xb[0:pp, 0:nb, :],
                     func=AF.Exp, scale=-1.0)
                P.op("act", "activation", reads=[XBB], writes=[XBB], out=xb[0:pp, 0:nb, :], in_=xb[0:pp, 0:nb, :],
                     func=AF.Ln, bias=1.0)
                P.op("dve", "tensor_scalar", reads=[XBB], writes=[LFB], out=lf[0:pp, 0:nb, :], in0=xb[0:pp, 0:nb, :],
                     scalar1=-1.0, scalar2=None, op0=ALU.mult)
                ko = k.O["fox_k_sample"] if kind == "smp" else k.O["fox_k_prompt"][t0:t0 + n, :]
                vo = k.O["fox_v_sample"] if kind == "smp" else k.O["fox_v_prompt"][t0:t0 + n, :]
                lo = k.O["fox_logf_sample"] if kind == "smp" else k.O["fox_logf_prompt"][t0:t0 + n, :]
                P.dma("sp", tm(ko), fkf[0:pp, 0:nb, :], reads=[FKFB], writes=[Buf()])
                P.dma("sp", tm(vo), fvf[0:pp, 0:nb, :], reads=[FVFB], writes=[Buf()])
                P.dma("sp", tm(lo), lf[0:pp, 0:nb, :], reads=[LFB], writes=[Buf()])
            P.op("pool", "tensor_copy", reads=[FKFB], writes=[FKBB], out=fkb[0:pp, 0:nb, 0:512], in_=fkf[0:pp, 0:nb, :])
            P.op("pool", "tensor_copy", reads=[FVFB], writes=[FVAB],
                 out=fva[0:pp, 0:nb, :].rearrange("p b (h e) -> p b h e", e=65)[:, :, :, 0:64],
                 in_=fvf[0:pp, 0:nb, :].rearrange("p b (h e) -> p b h e", e=64))
            ktile = (t0 // TT) if kind == "past" else (8 + t0 // TT)
            for hp in range(4):
                P.dma("act", k.S["fva"][ktile, hp, 0:pp, 0:nb, :], fva[0:pp, 0:nb, 130 * hp:130 * hp + 130], reads=[FVAB],
                      writes=[Buf()])
            for hh in range(8):
                bk = hh % 2
                for b, (o, bs) in enumerate(blks):
                    MM(bank(bk)[0:64, o:o + bs], fkb[0:bs, b, hh * 64:(hh + 1) * 64], identb[0:bs, 0:bs], True, True,
                       [FKBB], PB[bk])
                if hh % 2 == 0:
                    P.op("act", "copy", reads=[PB[bk]], writes=[FKAB], out=fka[0:64, hh, 0:n], in_=bank(bk)[0:64, 0:n])
                else:
                    P.op("dve", "tensor_copy", reads=[PB[bk]], writes=[FKAB], out=fka[0:64, hh, 0:n],
                         in_=bank(bk)[0:64, 0:n])
            for hp in range(4):
                P.dma("sp", k.S["fka"][ktile, hp, :, :, 0:n], fka[:, 2 * hp:2 * hp + 2, 0:n], reads=[FKAB], writes=[Buf()])
            for b, (o, bs) in enumerate(blks):
                mm = [(trif[0:bs, 0:bs], lf[0:bs, b, :])]
                for b2 in range(b):
                    mm.append((onesf[0:128, 0:bs], lf[0:128, b2, :]))
                mm.append((onesf[0:1, 0:bs], carry[0:1, :]))
                for i, (l_, r_) in enumerate(mm):
                    MM(bank(7)[0:bs, 8 + b * 8:16 + b * 8], l_, r_, i == 0, i == len(mm) - 1, [LFB, CARB, WBUF], PB[7])
            cview = bank(7)[0:pp, 8:8 + nb * 8].rearrange("p (b h) -> p b h", h=8)
            P.op("dve", "tensor_copy", reads=[PB[7]], writes=[CSB], out=cS[0:pp, 0:nb, :], in_=cview)
            P.op("act", "activation", reads=[PB[7]], writes=[NGB], out=ngc[0:pp, 0:nb, :], in_=cview, func=AF.Copy,
                 scale=-1.0)
            P.dma("act", k.S["negc"][ktile, 0:pp, 0:nb, :], ngc[0:pp, 0:nb, :], reads=[NGB], writes=[Buf()])
            for b, (o, bs) in enumerate(blks):
                MM(bank(7)[0:1, 48:56], onesf[0:bs, 0:1], lf[0:bs, b, :], b == 0, b == nb - 1, [LFB], PB[7])
            P.op("dve", "tensor_tensor", reads=[PB[7], CARB], writes=[CARB], out=carry[:], in0=carry[:],
                 in1=bank(7)[0:1, 48:56], op=ALU.add)
            if kind == "past":
                continue
            for b, (o, bs) in enumerate(blks):
                MM(bank(6)[0:8, o:o + bs], cS[0:bs, b, :], identf[0:bs, 0:bs], True, True, [CSB], PB[6])
            P.op("dve", "tensor_copy", reads=[PB[6]], writes=[CTB], out=cT[:, 0:n], in_=bank(6)[0:8, 0:n])
            P.op("dve", "tensor_copy", reads=[CTB], writes=[C3B], out=c3[:, 0, 0:n], in_=cT[:, 0:n])
            P.op("dve", "tensor_tensor", reads=[CTB, C3B], writes=[R1B], out=r1[:, 0:n], in0=cT[:, 0:n],
                 in1=c3[:, 0, 0:n], op=ALU.subtract)
            P.op("dve", "tensor_copy", reads=[R1B], writes=[C3B], out=c3[:, 1, 0:n], in_=r1[:, 0:n])
            P.op("dve", "tensor_tensor", reads=[R1B, C3B], writes=[R2B], out=r2[:, 0:n], in0=r1[:, 0:n],
                 in1=c3[:, 1, 0:n], op=ALU.subtract)
            P.op("dve", "tensor_copy", reads=[R2B], writes=[C3B], out=c3[:, 2, 0:n], in_=r2[:, 0:n])
            for hh in range(8):
                bk = 4 + (hh % 2)
                proj_fm(bk, 2048 + hh * 64, 64)
                for p_ in range(3):
                    i0 = (hh * 3 + p_) * 3
                    MM(bank(bk)[64:67, 0:n], selc[0:8, i0:i0 + 3], c3[0:8, p_, 0:n], p_ == 0, p_ == 2, [C3B, WBUF], PB[bk])
                P.op("act", "activation", reads=[PB[bk], WBUF], writes=[FQAB], out=fqa[0:67, hh, 0:n],
                     in_=bank(bk)[0:67, 0:n], func=AF.Identity, scale=qscale[0:67, 0:1])
            for hp in range(4):
                P.dma("sp", k.S["fqa"][t0 // TT, hp, :, :, 0:n], fqa[:, 2 * hp:2 * hp + 2, 0:n], reads=[FQAB], writes=[Buf()])
            P.maybe_flush()
        P.flush()


ALL_PHASES = ("W", "A1", "A3", "A2", "C0", "F0", "S", "C1", "F1")


def core_inputs(inp, c):
    f = np.ascontiguousarray
    m = {
        "x_prompt": inp["x_prompt"][c], "x_sample": inp["x_sample"][c],
        "cache_ret_state": inp["cache_ret_state"][0, c],
        "cache_fox_k": inp["cache_fox_k"][0, c].reshape(PAST, 512),
        "cache_fox_v": inp["cache_fox_v"][0, c].reshape(PAST, 512),
        "cache_fox_logf": inp["cache_fox_logf"][0, c],
        "state_s5_re": inp["state_s5_re"][0, c], "state_s5_im": inp["state_s5_im"][0, c],
        "cache_mem_k": inp["cache_mem_k"][:, c].reshape(2, 256, 1024),
        "cache_mem_v": inp["cache_mem_v"][:, c].reshape(2, 256, 1024),
        "mem_prompt": inp["mem_prompt"][c],
        "norm_w": inp["norm_w"], "w_in_ab": inp["w_in_ab"][0], "b_fox_f": inp["b_fox_f"].reshape(1, 8),
        "ret_gn_w": inp["ret_gn_w"].reshape(1, 512), "w_out_ab": inp["w_out_ab"][0], "w_in_c": inp["w_in_c"][0],
        "s5_lambda_re": inp["s5_lambda_re"][0], "s5_lambda_im": inp["s5_lambda_im"][0],
        "s5_log_dt": inp["s5_log_dt"].reshape(1, 64), "s5_b_re": inp["s5_b_re"][0], "s5_b_im": inp["s5_b_im"][0],
        "s5_c_re": inp["s5_c_re"][0], "s5_c_im": inp["s5_c_im"][0], "s5_d": inp["s5_d"].reshape(1, D),
        "w_glu": inp["w_glu"][0], "mem_norm_w": inp["mem_norm_w"],
        "w_xq": inp["w_xq"], "w_xk": inp["w_xk"], "w_xv": inp["w_xv"], "w_xo": inp["w_xo"],
        "w_up": inp["w_up"], "w_down": inp["w_down"],
    }
    m = {k_: f(np.asarray(v, dtype=np.float32)) for k_, v in m.items()}
    for k_, v in get_consts().items():
        m["c_" + k_] = v
    return m


def run(inputs, phases=ALL_PHASES, cores=NCORES):
    inp = {k_: np.asarray(v) for k_, v in inputs.items()}
    kk = build(phases)
    in_maps = [core_inputs(inp, c) for c in range(cores)]
    res = run_bass_kernel_spmd(kk.nc, in_maps, core_ids=list(range(cores)))
    return res.results


def kernel(**inputs):
    r = run(inputs)
    B = NCORES

    def st(name, shape):
        return np.stack([np.asarray(r[c][name], dtype=np.float32).reshape(shape) for c in range(B)])
    y_prompt = st("y_prompt", (T, D))
    y_sample = st("y_sample", (TS, D))
    ret_p = st("ret_state_prompt", (4, 128, 128))[None]
    ret_s = st("ret_state_sample", (4, 128, 128))[None]
    fkp = st("fox_k_prompt", (T, 8, 64))[None]
    fvp = st("fox_v_prompt", (T, 8, 64))[None]
    flp = st("fox_logf_prompt", (T, 8))[None]
    fks = st("fox_k_sample", (TS, 8, 64))[None]
    fvs = st("fox_v_sample", (TS, 8, 64))[None]
    fls = st("fox_logf_sample", (TS, 8))[None]
    s5rp = st("s5_re_prompt", (64, 64))[None]
    s5ip = st("s5_im_prompt", (64, 64))[None]
    s5rs = st("s5_re_sample", (64, 64))[None]
    s5is = st("s5_im_sample", (64, 64))[None]
    mkp = np.stack([np.asarray(r[c]["mem_k_prompt"], dtype=np.float32).reshape(2, 256, 4, 256) for c in range(B)], 1)
    mvp = np.stack([np.asarray(r[c]["mem_v_prompt"], dtype=np.float32).reshape(2, 256, 4, 256) for c in range(B)], 1)
    return (y_prompt, y_sample, ret_p, ret_s, fkp, fvp, flp, fks, fvs, fls, s5rp, s5ip, s5rs, s5is, mkp, mvp)


def phase_A3(k, P, ps, PB, bank, WB, identf, identb, onesf, onesb):
    with contextlib.ExitStack() as st:
        def sb(name, shape, dt):
            return k.sbuf(st, "a3_" + name, shape, dt)
        caus = sb("caus", [128, 128], BF16)
        CW = Buf("a3c")
        P.dma("sp", caus[:], k.C["caus_b"], writes=[CW])
        qas = [sb("qa%d" % i, [67, 2, 2 * TT], BF16) for i in range(2)]
        QABs = [Buf("qa%d" % i) for i in range(2)]
        NKV = 4
        ka = [sb("ka%d" % i, [67, 2, TT], BF16) for i in range(2 * NKV)]
        va = [sb("va%d" % i, [128, 4, 2 * 65], BF16) for i in range(2 * NKV)]
        ng = [sb("ng%d" % i, [128, 4, 8], F32) for i in range(2 * NKV)]
        KVB = [Buf("kk%d" % i) for i in range(2 * NKV)]
        VVB = [Buf("kvv%d" % i) for i in range(2 * NKV)]
        NNB = [Buf("kn%d" % i) for i in range(2 * NKV)]
        pT = [sb("pT%d" % i, [128, TT], BF16) for i in range(4)]
        PTB = [Buf("pT%d" % i) for i in range(4)]
        osbs = [sb("osb%d" % i, [65, TT], F32) for i in range(4)]
        OSBs = [Buf("osb%d" % i) for i in range(4)]
        rrows = [sb("rrow%d" % i, [65, TT], BF16) for i in range(4)]
        RRBs = [Buf("rrow%d" % i) for i in range(4)]
        foT = sb("foT", [64, 4, TT], BF16)
        FOB = Buf("foT")

        def MM(out, lhsT, rhs, start, stop, R, Wb):
            P.op("pe", "matmul", reads=R, writes=[Wb], out=out, lhsT=lhsT, rhs=rhs, start=start, stop=stop)

        supers = []
        for Q in range(T // (2 * TT)):
            subs = [(2 * Q + s_, (2 * Q + s_) * TT, TT) for s_ in range(2)]
            ktl = [(8 + kt, TT, kt) for kt in range(2 * Q + 2)]
            supers.append(("prm", subs, ktl))
        supers.append(("smp", [(16, T, TS)], [(pt, TT, -1) for pt in range(PAST // TT)] + [(24, TS, 16)]))
        if "a3_q" in DBG:
            supers = [supers[i] for i in DBG["a3_q"]]
        groups = [(kind, subs, ktl, hp) for (kind, subs, ktl) in supers for hp in range(4)]
        pti = 0

        def load_kv(gi, kti):
            kind, subs, ktl, hp = groups[gi]
            ktile, nk, kpos = ktl[kti]
            sl = (gi % 2) * NKV + kti % NKV
            pk = min(128, nk)
            nkb = (nk + 127) // 128
            P.dma("sp", ka[sl][:, :, 0:nk], k.S["fka"][ktile, hp, :, :, 0:nk], writes=[KVB[sl]])
            P.dma("pool", va[sl][0:pk, 0:nkb, :], k.S["fva"][ktile, hp, 0:pk, 0:nkb, :], writes=[VVB[sl]])
            P.dma("pool", ng[sl][0:pk, 0:nkb, :], k.S["negc"][ktile, 0:pk, 0:nkb, :], writes=[NNB[sl]])

        def load_q(gi):
            kind, subs, ktl, hp = groups[gi]
            for s_, (qidx, t0, n) in enumerate(subs):
                P.dma("sp", qas[gi % 2][:, :, s_ * TT:s_ * TT + n], k.S["fqa"][qidx, hp, :, :, 0:n], writes=[QABs[gi % 2]])

        def prefetch(gi):
            if gi < len(groups):
                load_q(gi)
                for kti in range(min(NKV, len(groups[gi][2]))):
                    load_kv(gi, kti)
        prefetch(0)
        for gi, (kind, subs, ktl, hp) in enumerate(groups):
            qa, QAB = qas[gi % 2], QABs[gi % 2]
            prefetch(gi + 1)
            steps = []
            for kti, (ktile, nk, kpos) in enumerate(ktl):
                kblks = blocks(nk)
                for hh in range(2):
                    for jb, (jo, js) in enumerate(kblks):
                        for s_, (qidx, t0, n) in enumerate(subs):
                            if kpos > qidx:
                                continue
                            steps.append((kti, hh, jb, jo, js, kpos == qidx, s_, n))
            firsts, lasts = {}, {}
            for i, stp in enumerate(steps):
                a_ = stp[1] * 2 + stp[6]
                firsts.setdefault(a_, i)
                lasts[a_] = i
            LOOK = 2
            info = {}
            for i in range(len(steps) + LOOK):
                if i < len(steps):
                    kti, hh, jb, jo, js, diag, s_, n = steps[i]
                    sl = (gi % 2) * NKV + kti % NKV
                    h8 = 2 * hp + hh
                    c0 = jo if diag else 0
                    sbk = 4 + (pti % 4)
                    pb = pti % 4
                    pti += 1
                    info[i] = (sl, pb, c0)
                    MM(bank(sbk)[0:js, c0:n], ka[sl][0:67, hh, jo:jo + js], qa[0:67, hh, s_ * TT + c0:s_ * TT + n], True, True,
                       [KVB[sl], QAB], PB[sbk])
                    P.op("act", "activation", reads=[PB[sbk], NNB[sl]], writes=[PTB[pb]], out=pT[pb][0:js, c0:n],
                         in_=bank(sbk)[0:js, c0:n], func=AF.Exp, bias=ng[sl][0:js, jb, h8:h8 + 1], scale=1.0)
                    if diag:
                        P.op("pool", "tensor_tensor", reads=[PTB[pb], CW], writes=[PTB[pb]],
                             out=pT[pb][0:js, c0:c0 + js], in0=pT[pb][0:js, c0:c0 + js], in1=caus[0:js, 0:js], op=ALU.mult)
                j = i - LOOK
                if j >= 0:
                    kti, hh, jb, jo, js, diag, s_, n = steps[j]
                    sl, pb, c0 = info[j]
                    a_ = hh * 2 + s_
                    MM(bank(a_)[0:65, c0:n], va[sl][0:js, jb, hh * 65:(hh + 1) * 65], pT[pb][0:js, c0:n], firsts[a_] == j,
                       lasts[a_] == j, [VVB[sl], PTB[pb]], PB[a_])
                    if (j + 1 == len(steps) or steps[j + 1][0] != kti) and kti + NKV < len(ktl):
                        load_kv(gi, kti + NKV)
            accs = [(hh, s_, subs[s_][1], subs[s_][2]) for hh in range(2) for s_ in range(len(subs))]
            for (hh, s_, t0, n) in accs:
                a_ = hh * 2 + s_
                P.op("act", "copy", reads=[PB[a_]], writes=[OSBs[a_]], out=osbs[a_][0:65, 0:n], in_=bank(a_)[0:65, 0:n])
            for (hh, s_, t0, n) in accs:
                a_ = hh * 2 + s_
                P.op("dve", "reciprocal", reads=[OSBs[a_]], writes=[RRBs[a_]], out=rrows[a_][64:65, 0:n],
                     in_=osbs[a_][64:65, 0:n])
            for (hh, s_, t0, n) in accs:
                a_ = hh * 2 + s_
                MM(bank(4 + a_)[0:64, 0:n], onesb[64:65, 0:64], rrows[a_][64:65, 0:n], True, True, [RRBs[a_]], PB[4 + a_])
            for (hh, s_, t0, n) in accs:
                a_ = hh * 2 + s_
                P.op("dve", "tensor_tensor", reads=[PB[4 + a_], OSBs[a_]], writes=[FOB], out=foT[0:64, a_, 0:n],
                     in0=osbs[a_][0:64, 0:n], in1=bank(4 + a_)[0:64, 0:n], op=ALU.mult)
            for (hh, s_, t0, n) in accs:
                a_ = hh * 2 + s_
                P.dma("sp", k.S["foT"][2 * hp + hh, :, t0:t0 + n], foT[:, a_, 0:n], reads=[FOB], writes=[Buf()])
            P.maybe_flush()
        P.flush()


def rstd_ops(P, out, in_, scale, epsc, R, W):
    P.op("act", "activation", reads=R, writes=W, out=out, in_=in_, func=AF.Sqrt, scale=scale, bias=epsc)
    P.op("dve", "reciprocal", reads=W, writes=W, out=out, in_=out)


def post_norm_residual(P, pso, PSB, xt_blk, XTB, g_bc, GB, junk, JB, ssv, SSB, epsc, tmp, TMPB, bs):
    P.op("act", "activation", reads=PSB, writes=[JB, SSB], out=junk[0:bs, :], in_=pso, func=AF.Square,
         accum_out=ssv[0:bs, 0:1])
    rstd_ops(P, ssv[0:bs, 1:2], ssv[0:bs, 0:1], 1.0 / D, epsc[0:bs, 0:1], [SSB], [SSB])
    P.op("dve", "scalar_tensor_tensor", reads=PSB + [SSB, GB], writes=[TMPB], out=tmp[0:bs, :], in0=pso,
         scalar=ssv[0:bs, 1:2], in1=g_bc[0:bs, :], op0=ALU.mult, op1=ALU.mult)
    P.op("pool", "tensor_tensor", reads=[TMPB, XTB], writes=[XTB], out=xt_blk, in0=xt_blk, in1=tmp[0:bs, :], op=ALU.add)


def pre_norm_elem(P, xt, XTB, g_bc, GB, junk, JB, ssv, SSB, epsc, h, HB, blks):
    for b, (o, bs) in enumerate(blks):
        P.op("act", "activation", reads=[XTB], writes=[JB, SSB], out=junk[0:bs, :], in_=xt[0:bs, b, :], func=AF.Square,
             accum_out=ssv[0:bs, 2 * b:2 * b + 1])
        rstd_ops(P, ssv[0:bs, 2 * b + 1:2 * b + 2], ssv[0:bs, 2 * b:2 * b + 1], 1.0 / D, epsc[0:bs, 0:1], [SSB], [SSB])
        P.op("dve", "scalar_tensor_tensor", reads=[XTB, SSB, GB], writes=[HB], out=h[0:bs, b, :], in0=xt[0:bs, b, :],
             scalar=ssv[0:bs, 2 * b + 1:2 * b + 2], in1=g_bc[0:bs, :], op0=ALU.mult, op1=ALU.mult)


def pre_norm_T(P, bank, PB, banks, h, HB, hT, HTB, identb, blks, n):
    for kc in range(8):
        bk = banks[kc % len(banks)]
        for b, (o, bs) in enumerate(blks):
            P.op("pe", "matmul", reads=[HB], writes=[PB[bk]], out=bank(bk)[:, o:o + bs],
                 lhsT=h[0:bs, b, kc * 128:(kc + 1) * 128], rhs=identb[0:bs, 0:bs], start=True, stop=True)
        if kc % 2 == 0:
            P.op("act", "copy", reads=[PB[bk]], writes=[HTB], out=hT[:, kc, 0:n], in_=bank(bk)[:, 0:n])
        else:
            P.op("dve", "tensor_copy", reads=[PB[bk]], writes=[HTB], out=hT[:, kc, 0:n], in_=bank(bk)[:, 0:n])


def pre_norm_hT(P, bank, PB, banks, xt, XTB, g_bc, GB, junk, JB, ssv, SSB, epsc, h, HB, hT, HTB, identb, blks, n):
    pre_norm_elem(P, xt, XTB, g_bc, GB, junk, JB, ssv, SSB, epsc, h, HB, blks)
    pre_norm_T(P, bank, PB, banks, h, HB, hT, HTB, identb, blks, n)


def x_src(k, first_layer_input, kind, t0, n):
    if first_layer_input:
        return k.I["x_sample"] if kind == "smp" else k.I["x_prompt"][t0:t0 + n, :]
    return k.S["x"][t0:t0 + n, :]


def phase_A2(k, P, ps, PB, bank, WB, identf, identb, onesf, onesb):
    lg = np.log(1.0 - 2.0 ** (-5.0 - np.arange(4, dtype=np.float64)))
    with contextlib.ExitStack() as st:
        def sb(name, shape, dt):
            return k.sbuf(st, "a2_" + name, shape, dt)
        rmask = sb("rmask", [128, 4, 512], F32)
        wq = sb("wq", [128, 4, 512], F32)
        gnw = sb("gnw", [128, 512], F32)
        g1 = sb("g1", [128, D], F32)
        wor = sb("wor", [128, 4, D], BF16)
        wof = sb("wof", [64, 8, D], BF16)
        epsc = sb("epsc", [128, 1], F32)
        CW = Buf("a2c")
        P.dma("sp", rmask[:], k.C["rmask"], writes=[CW])
        P.dma("sp", wq[:], k.C["wq"].rearrange("p b h e -> p b (h e)"), writes=[CW])
        P.dma("sp", gnw[:], k.I["ret_gn_w"][0:1, :].broadcast_to([128, 512]), writes=[CW])
        P.dma("sp", g1[:], k.I["norm_w"][0, 1:2, :].broadcast_to([128, D]), writes=[CW])
        P.dma("act", wor[:], WB["w_out_ab"][0:512, :].rearrange("(kc p) n -> p kc n", p=128), writes=[CW])
        P.dma("act", wof[:], WB["w_out_ab"][512:1024, :].rearrange("(h p) n -> p h n", p=64), writes=[CW])
        P.op("pool", "memset", writes=[CW], ap=epsc[:], constant=EPS)
        Sf = sb("Sf", [128, 4, 128], F32)
        Sb = sb("Sb", [128, 4, 128], BF16)
        SFB, SBB = Buf("Sf"), Buf("Sb")
        qTs = [sb("qT%d" % i, [128, 4, TT], BF16) for i in range(2)]
        kTs = [sb("kT%d" % i, [128, 4, TT], BF16) for i in range(2)]
        rkks = [sb("rkk%d" % i, [128, 4, 512], BF16) for i in range(2)]
        rvs = [sb("rv%d" % i, [128, 4, 512], BF16) for i in range(2)]
        rgs = [sb("rg%d" % i, [128, 4, 512], BF16) for i in range(2)]
        foTs = [sb("foT%d" % i, [64, 8, TT], BF16) for i in range(2)]
        xts = [sb("xt%d" % i, [128, 4, D], F32) for i in range(2)]
        INBs = [Buf("a2in%d" % i) for i in range(2)]
        XTBs = [Buf("a2x%d" % i) for i in range(2)]
        PT = sb("PT", [128, 4, 4, TT], BF16)
        PTB = [Buf("PT%d" % i) for i in range(4)]
        ro = sb("ro", [128, 512], F32)
        t1 = sb("t1", [128, 512], F32)
        sq = sb("sq", [128, 512], F32)
        ROB, T1B, SQB = Buf("ro"), Buf("t1"), Buf("sq")
        st8 = sb("st8", [128, 24], F32)
        STB = Buf("st8")
        mixr = sb("mixr", [128, 512], BF16)
        MXB = Buf("mixr")
        roT = sb("roT", [128, 4, TT], BF16)
        ROTB = Buf("roT")
        junk = sb("junk", [128, D], BF16)
        JB = Buf("junk")
        ssv = sb("ssv", [128, 8], F32)
        SSB = Buf("ssv")
        tmp = sb("tmp", [128, D], F32)
        TMPB = Buf("tmp")

        def MM(out, lhsT, rhs, start, stop, R, Wb):
            P.op("pe", "matmul", reads=R, writes=[Wb], out=out, lhsT=lhsT, rhs=rhs, start=start, stop=stop)

        seqs = [("prm", t * TT, TT) for t in range(T // TT)] + [("smp", T, TS)]
        if "a2_seqs" in DBG:
            seqs = [seqs[i] for i in DBG["a2_seqs"]]
        P.op("pool", "memset", writes=[SFB], ap=Sf[:], constant=0.0)
        P.op("pool", "memset", writes=[SBB], ap=Sb[:], constant=0.0)
        def load_in(ti):
            kind, t0, n = seqs[ti]
            pp = min(128, n)
            nb = len(blocks(n))
            j = ti % 2

            def tm_(ap2d):
                return ap2d.rearrange("(b p) d -> p b d", p=pp)
            P.dma("sp", qTs[j][:, :, 0:n], k.S["rqT"][:, :, t0:t0 + n].rearrange("h d t -> d h t"), writes=[INBs[j]])
            P.dma("act", kTs[j][:, :, 0:n], k.S["rkT"][:, :, t0:t0 + n].rearrange("h d t -> d h t"), writes=[INBs[j]])
            P.dma("sp", rkks[j][0:pp, 0:nb, :], tm_(k.S["rkk"][t0:t0 + n, :]), writes=[INBs[j]])
            P.dma("act", rvs[j][0:pp, 0:nb, :], tm_(k.S["rv"][t0:t0 + n, :]), writes=[INBs[j]])
            P.dma("sp", rgs[j][0:pp, 0:nb, :], tm_(k.S["rg"][t0:t0 + n, :]), writes=[INBs[j]])
            P.dma("act", foTs[j][:, :, 0:n], k.S["foT"][:, :, t0:t0 + n].rearrange("h r t -> r h t"), writes=[INBs[j]])
            P.dma("sp", xts[j][0:pp, 0:nb, :], tm_(x_src(k, True, kind, t0, n)), writes=[XTBs[j]])
        load_in(0)
        for ti, (kind, t0, n) in enumerate(seqs):
            blks = blocks(n)
            nb = len(blks)
            pp = min(128, n)
            dec = np.exp(lg * (512.0 if kind == "prm" else 32.0))
            j_ = ti % 2
            qT, kT, rkk, rv, rg, foT, xt = qTs[j_], kTs[j_], rkks[j_], rvs[j_], rgs[j_], foTs[j_], xts[j_]
            INB, XTB = INBs[j_], XTBs[j_]
            if ti + 1 < len(seqs):
                load_in(ti + 1)
            if kind == "smp":
                P.dma("sp", Sf[:], k.I["cache_ret_state"].rearrange("h d e -> d h e"), writes=[SFB])
                P.op("act", "copy", reads=[SFB], writes=[SBB], out=Sb[:], in_=Sf[:])

            def tm(ap2d):
                return ap2d.rearrange("(b p) d -> p b d", p=pp)
            for hh in range(4):
                for jb, (jo, js) in enumerate(blks):
                    bk = (hh * 4 + jb) % 2
                    MM(bank(bk)[0:js, jo:n], kT[:, hh, jo:jo + js], qT[:, hh, jo:n], True, True, [INB], PB[bk])
                    P.op("dve", "tensor_tensor", reads=[PB[bk], CW], writes=[PTB[hh]], out=PT[0:js, hh, jb, jo:n],
                         in0=bank(bk)[0:js, jo:n], in1=rmask[0:js, hh, 0:n - jo], op=ALU.mult)
            for ib, (io, is_) in enumerate(blks):
                for hh in range(4):
                    for jb in range(ib + 1):
                        jo, js = blks[jb]
                        MM(bank(2)[0:is_, hh * 128:(hh + 1) * 128], PT[0:js, hh, jb, io:io + is_],
                           rv[0:js, jb, hh * 128:(hh + 1) * 128], jb == 0, jb == ib, [PTB[hh], INB], PB[2])
                    MM(bank(3)[0:is_, hh * 128:(hh + 1) * 128], qT[:, hh, io:io + is_], Sb[:, hh, :], True, True,
                       [INB, SBB], PB[3])
                P.op("dve", "tensor_tensor", reads=[PB[3], CW], writes=[T1B], out=t1[0:is_, :], in0=bank(3)[0:is_, :],
                     in1=wq[0:is_, ib, :], op=ALU.mult)
                P.op("dve", "tensor_tensor", reads=[PB[2], T1B], writes=[ROB], out=ro[0:is_, :], in0=bank(2)[0:is_, :],
                     in1=t1[0:is_, :], op=ALU.add)
                P.op("act", "activation", reads=[ROB], writes=[SQB], out=sq[0:is_, :], in_=ro[0:is_, :], func=AF.Square)
                ro3 = ro[0:is_, :].rearrange("p (h e) -> p h e", e=128)
                sq3 = sq[0:is_, :].rearrange("p (h e) -> p h e", e=128)
                P.op("dve", "tensor_reduce", reads=[ROB], writes=[STB], out=st8[0:is_, 0:4], in_=ro3, axis=AX.X, op=ALU.add)
                P.op("dve", "tensor_reduce", reads=[SQB], writes=[STB], out=st8[0:is_, 4:8], in_=sq3, axis=AX.X, op=ALU.add)
                P.op("dve", "tensor_scalar", reads=[STB], writes=[STB], out=st8[0:is_, 8:12], in0=st8[0:is_, 0:4],
                     scalar1=1.0 / 128, scalar2=None, op0=ALU.mult)
                P.op("dve", "tensor_tensor", reads=[STB], writes=[STB], out=st8[0:is_, 12:16], in0=st8[0:is_, 8:12],
                     in1=st8[0:is_, 8:12], op=ALU.mult)
                P.op("dve", "scalar_tensor_tensor", reads=[STB], writes=[STB], out=st8[0:is_, 16:20], in0=st8[0:is_, 4:8],
                     scalar=1.0 / 128, in1=st8[0:is_, 12:16], op0=ALU.mult, op1=ALU.subtract)
                rstd_ops(P, st8[0:is_, 20:24], st8[0:is_, 16:20], 1.0, epsc[0:is_, 0:1], [STB], [STB])
                t13 = t1[0:is_, :].rearrange("p (h e) -> p h e", e=128)
                P.op("dve", "tensor_tensor", reads=[ROB, STB], writes=[T1B], out=t13, in0=ro3,
                     in1=st8[0:is_, 8:12].unsqueeze(2).broadcast_to([is_, 4, 128]), op=ALU.subtract)
                P.op("dve", "tensor_tensor", reads=[T1B, STB], writes=[T1B], out=t13, in0=t13,
                     in1=st8[0:is_, 20:24].unsqueeze(2).broadcast_to([is_, 4, 128]), op=ALU.mult)
                P.op("pool", "tensor_tensor", reads=[T1B, CW], writes=[T1B], out=t1[0:is_, :], in0=t1[0:is_, :],
                     in1=gnw[0:is_, :], op=ALU.mult)
                P.op("pool", "tensor_tensor", reads=[T1B, INB], writes=[MXB], out=mixr[0:is_, :], in0=t1[0:is_, :],
                     in1=rg[0:is_, ib, :], op=ALU.mult)
                for hh in range(4):
                    MM(bank(4 + hh)[:, io:io + is_], mixr[0:is_, hh * 128:(hh + 1) * 128], identb[0:is_, 0:is_], True, True,
                       [MXB], PB[4 + hh])
            for hh in range(4):
                if hh % 2 == 0:
                    P.op("act", "copy", reads=[PB[4 + hh]], writes=[ROTB], out=roT[:, hh, 0:n], in_=bank(4 + hh)[:, 0:n])
                else:
                    P.op("dve", "tensor_copy", reads=[PB[4 + hh]], writes=[ROTB], out=roT[:, hh, 0:n],
                         in_=bank(4 + hh)[:, 0:n])
            for hh in range(4):
                for jb, (jo, js) in enumerate(blks):
                    MM(bank(2)[:, hh * 128:(hh + 1) * 128], rkk[0:js, jb, hh * 128:(hh + 1) * 128],
                       rv[0:js, jb, hh * 128:(hh + 1) * 128], jb == 0, jb == nb - 1, [INB], PB[2])
            for hh in range(4):
                P.op("dve", "scalar_tensor_tensor", reads=[PB[2], SFB], writes=[SFB], out=Sf[:, hh, :], in0=Sf[:, hh, :],
                     scalar=float(dec[hh]), in1=bank(2)[:, hh * 128:(hh + 1) * 128], op0=ALU.mult, op1=ALU.add)
            P.op("act", "copy", reads=[SFB], writes=[SBB], out=Sb[:], in_=Sf[:])
            last_of_kind = (kind == "smp") or (t0 + n == T)
            if last_of_kind:
                dst = k.O["ret_state_sample"] if kind == "smp" else k.O["ret_state_prompt"]
                P.dma("sp", dst.rearrange("h d e -> d h e"), Sf[:], reads=[SFB], writes=[Buf()])
            for ib, (io, is_) in enumerate(blks):
                for half in range(2):
                    bk = 2 * (ib % 2) + half
                    for i in range(12):
                        if i < 4:
                            l_, r_ = roT[:, i, io:io + is_], wor[:, i, half * 512:(half + 1) * 512]
                        else:
                            l_, r_ = foT[0:64, i - 4, io:io + is_], wof[0:64, i - 4, half * 512:(half + 1) * 512]
                        MM(bank(bk)[0:is_, :], l_, r_, i == 0, i == 11, [ROTB, INB, CW], PB[bk])
                b0 = 2 * (ib % 2)
                pso = ps[0:is_, b0 * 512:b0 * 512 + 1024]
                post_norm_residual(P, pso, [PB[b0], PB[b0 + 1]], xt[0:is_, ib, :], XTB, g1, CW, junk, JB, ssv, SSB, epsc,
                                   tmp, TMPB, is_)
            P.dma("sp", tm(k.S["x"][t0:t0 + n, :]), xt[0:pp, 0:nb, :], reads=[XTB], writes=[Buf()])
            P.maybe_flush()
        P.flush()


def phase_C(k, P, ps, PB, bank, WB, identf, identb, onesf, onesb, l):
    with contextlib.ExitStack() as st:
        def sb(name, shape, dt):
            return k.sbuf(st, "c%d_" % l + name, shape, dt)
        wxq = sb("wxq", [128, 8, D], BF16)
        wxo = sb("wxo", [128, 8, D], BF16)
        wkv = sb("wkv", [128, 8, D], BF16)
        g2 = sb("g2", [128, D], F32)
        g3 = sb("g3", [128, D], F32)
        gm = sb("gm", [128, D], F32)
        epsc = sb("epsc", [128, 1], F32)
        CW = Buf("cw")
        WKVB = Buf("wkv")
        P.dma("sp", wxq[:], WB["w_xq%d" % l].rearrange("(kc p) n -> p kc n", p=128), writes=[CW])
        P.dma("act", wxo[:], WB["w_xo%d" % l].rearrange("(kc p) n -> p kc n", p=128), writes=[CW])
        P.dma("sp", g2[:], k.I["norm_w"][l, 2:3, :].broadcast_to([128, D]), writes=[CW])
        P.dma("sp", g3[:], k.I["norm_w"][l, 3:4, :].broadcast_to([128, D]), writes=[CW])
        P.dma("sp", gm[:], k.I["mem_norm_w"][l:l + 1, :].broadcast_to([128, D]), writes=[CW])
        P.op("pool", "memset", writes=[CW], ap=epsc[:], constant=EPS)
        xts = [sb("xt%d" % i, [128, 4, D], F32) for i in range(2)]
        XTBs = [Buf("xt%d" % i) for i in range(2)]
        xt, XTB = xts[0], XTBs[0]
        junk2 = sb("junk2", [128, D], BF16)
        JB2 = Buf("junk2")
        ssv2 = sb("ssv2", [128, 8], F32)
        SSB2 = Buf("ssv2")
        h = sb("h", [128, 4, D], BF16)
        HB = Buf("h")
        hT = sb("hT", [128, 8, TT], BF16)
        HTB = Buf("hT")
        junk = sb("junk", [128, D], BF16)
        JB = Buf("junk")
        ssv = sb("ssv", [128, 8], F32)
        SSB = Buf("ssv")
        tmp = sb("tmp", [128, D], F32)
        TMPB = Buf("tmp")
        mkf = sb("mkf", [128, 2, D], F32)
        MKFB = Buf("mkf")
        mkb = sb("mkb", [128, 2, D], BF16)
        MKBB = Buf("mkb")
        mkT = sb("mkT", [128, 8, 256], BF16)
        MKTB = Buf("mkT")
        mvb = sb("mvb", [128, 2, D], BF16)
        MVBB = Buf("mvb")
        qT = sb("qT", [128, 8, TT], BF16)
        QTB = Buf("qT")
        oT = sb("oT", [128, 8, TT], BF16)
        OTB = Buf("oT")
        pT = [sb("pT%d" % i, [128, TT], BF16) for i in range(2)]
        PTB = [Buf("pT%d" % i) for i in range(2)]
        rec = sb("rec", [128, TT], F32)
        RECB = Buf("rec")

        def MM(out, lhsT, rhs, start, stop, R, Wb):
            P.op("pe", "matmul", reads=R, writes=[Wb], out=out, lhsT=lhsT, rhs=rhs, start=start, stop=stop)

        mblks = [(0, 128), (128, 128)]

        def kv_finish():
            P.op("pool", "tensor_copy", reads=[MKFB], writes=[MKBB], out=mkb[:], in_=mkf[:])
            for c8 in range(8):
                bk = c8 % 2
                for mb in range(2):
                    MM(bank(bk)[:, mb * 128:(mb + 1) * 128], mkb[:, mb, c8 * 128:(c8 + 1) * 128], identb[:, :], True, True,
                       [MKBB], PB[bk])
                if c8 % 2 == 0:
                    P.op("act", "copy", reads=[PB[bk]], writes=[MKTB], out=mkT[:, c8, :], in_=bank(bk)[:, 0:256])
                else:
                    P.op("dve", "tensor_copy", reads=[PB[bk]], writes=[MKTB], out=mkT[:, c8, :], in_=bank(bk)[:, 0:256])

        def setup_prompt_mem():
            P.dma("sp", xt[:, 0:2, :], k.I["mem_prompt"].rearrange("(b p) d -> p b d", p=128), writes=[XTB])
            pre_norm_hT(P, bank, PB, [0, 1, 2, 3], xt, XTB, gm, CW, junk, JB, ssv, SSB, epsc, h, HB, hT, HTB, identb,
                        mblks, 256)
            for which in ("k", "v"):
                P.dma("sp", wkv[:], WB["w_x%s%d" % (which, l)].rearrange("(kc p) n -> p kc n", p=128), writes=[WKVB])
                for mb in range(2):
                    for half in range(2):
                        bk = 4 + half
                        for kc in range(8):
                            MM(bank(bk)[:, :], hT[:, kc, mb * 128:(mb + 1) * 128], wkv[:, kc, half * 512:(half + 1) * 512],
                               kc == 0, kc == 7, [HTB, WKVB], PB[bk])
                        P.op("act", "copy", reads=[PB[bk]], writes=[MKFB], out=mkf[:, mb, half * 512:(half + 1) * 512],
                             in_=bank(bk)[:, :])
                dst = k.O["mem_k_prompt" if which == "k" else "mem_v_prompt"][l]
                P.dma("sp", dst.rearrange("(b p) d -> p b d", p=128), mkf[:], reads=[MKFB], writes=[Buf()])
                if which == "k":
                    kv_finish()
                else:
                    P.op("pool", "tensor_copy", reads=[MKFB], writes=[MVBB], out=mvb[:], in_=mkf[:])

        def setup_sample_mem():
            P.dma("sp", mkf[:], k.I["cache_mem_k"][l].rearrange("(b p) d -> p b d", p=128), writes=[MKFB])
            kv_finish()
            P.dma("sp", mkf[:], k.I["cache_mem_v"][l].rearrange("(b p) d -> p b d", p=128), writes=[MKFB])
            P.op("pool", "tensor_copy", reads=[MKFB], writes=[MVBB], out=mvb[:], in_=mkf[:])

        seqs = [("prm", t * TT, TT) for t in range(T // TT)] + [("smp", T, TS)]
        if "c_seqs" in DBG:
            seqs = [seqs[i] for i in DBG["c_seqs"]]
        def tile_geom(ti):
            kind, t0, n = seqs[ti]
            return kind, t0, n, blocks(n), min(128, n)

        def load_x(ti):
            kind, t0, n, blks, pp = tile_geom(ti)
            P.dma("sp", xts[ti % 2][0:pp, 0:len(blks), :], k.S["x"][t0:t0 + n, :].rearrange("(b p) d -> p b d", p=pp),
                  writes=[XTBs[ti % 2]])

        def prenorm_e(ti):
            kind, t0, n, blks, pp = tile_geom(ti)
            pre_norm_elem(P, xts[ti % 2], XTBs[ti % 2], g2, CW, junk2, JB2, ssv2, SSB2, epsc, h, HB, blks)

        def prenorm_t(ti):
            kind, t0, n, blks, pp = tile_geom(ti)
            pre_norm_T(P, bank, PB, [0, 1, 2, 3], h, HB, hT, HTB, identb, blks, n)
        if seqs[0][0] == "prm":
            setup_prompt_mem()
        load_x(0)
        prenorm_e(0)
        prenorm_t(0)
        for ti, (kind, t0, n) in enumerate(seqs):
            blks = blocks(n)
            nb = len(blks)
            pp = min(128, n)
            xt, XTB = xts[ti % 2], XTBs[ti % 2]
            if ti + 1 < len(seqs):
                load_x(ti + 1)
            if kind == "smp":
                setup_sample_mem()

            def tm(ap2d):
                return ap2d.rearrange("(b p) d -> p b d", p=pp)
            for mc in range(8):
                bk = 4 + mc % 2
                for kc in range(8):
                    MM(bank(bk)[:, 0:n], wxq[:, kc, mc * 128:(mc + 1) * 128], hT[:, kc, 0:n], kc == 0, kc == 7, [HTB, CW], PB[bk])
                if mc % 2 == 0:
                    P.op("act", "copy", reads=[PB[bk]], writes=[QTB], out=qT[:, mc, 0:n], in_=bank(bk)[:, 0:n])
                else:
                    P.op("dve", "tensor_copy", reads=[PB[bk]], writes=[QTB], out=qT[:, mc, 0:n], in_=bank(bk)[:, 0:n])
            for hh in range(4):
                for mb in range(2):
                    for dc in range(2):
                        MM(bank(mb)[:, 0:n], mkT[:, hh * 2 + dc, mb * 128:(mb + 1) * 128], qT[:, hh * 2 + dc, 0:n],
                           dc == 0, dc == 1, [MKTB, QTB], PB[mb])
                    P.op("act", "activation", reads=[PB[mb]], writes=[PTB[mb]], out=pT[mb][:, 0:n], in_=bank(mb)[:, 0:n],
                         func=AF.Exp, scale=1.0 / 16.0)
                for mb in range(2):
                    MM(bank(2)[:, 0:n], onesb[:, :], pT[mb][:, 0:n], mb == 0, mb == 1, [PTB[mb]], PB[2])
                P.op("dve", "reciprocal", reads=[PB[2]], writes=[RECB], out=rec[:, 0:n], in_=bank(2)[:, 0:n])
                for dc in range(2):
                    bk = 3 if dc == 0 else 6
                    for mb in range(2):
                        MM(bank(bk)[:, 0:n], mvb[:, mb, hh * 256 + dc * 128:hh * 256 + dc * 128 + 128], pT[mb][:, 0:n],
                           mb == 0, mb == 1, [MVBB, PTB[mb]], PB[bk])
                    P.op("dve", "tensor_tensor", reads=[PB[bk], RECB], writes=[OTB], out=oT[:, hh * 2 + dc, 0:n],
                         in0=bank(bk)[:, 0:n], in1=rec[:, 0:n], op=ALU.mult)
            for ib, (io, is_) in enumerate(blks):
                b0 = 4 if ib % 2 == 0 else 6
                if b0 == 6:
                    b0 = 0
                for half in range(2):
                    bk = b0 + half
                    for kc in range(8):
                        MM(bank(bk)[0:is_, :], oT[:, kc, io:io + is_], wxo[:, kc, half * 512:(half + 1) * 512], kc == 0, kc == 7,
                           [OTB, CW], PB[bk])
                if ib == 0 and ti + 1 < len(seqs):
                    prenorm_e(ti + 1)
                pso = ps[0:is_, b0 * 512:b0 * 512 + 1024]
                post_norm_residual(P, pso, [PB[b0], PB[b0 + 1]], xt[0:is_, ib, :], XTB, g3, CW, junk, JB, ssv, SSB, epsc,
                                   tmp, TMPB, is_)
            if ti + 1 < len(seqs):
                prenorm_t(ti + 1)
            P.dma("sp", tm(k.S["x"][t0:t0 + n, :]), xt[0:pp, 0:nb, :], reads=[XTB], writes=[Buf()])
            P.maybe_flush()
        P.flush()


def phase_F(k, P, ps, PB, bank, WB, identf, identb, onesf, onesb, l, final):
    with contextlib.ExitStack() as st:
        def sb(name, shape, dt):
            return k.sbuf(st, "f%d_" % l + name, shape, dt)
        g4 = sb("g4", [128, D], F32)
        g5 = sb("g5", [128, D], F32)
        epsc = sb("epsc", [128, 1], F32)
        CW = Buf("cw")
        P.dma("sp", g4[:], k.I["norm_w"][l, 4:5, :].broadcast_to([128, D]), writes=[CW])
        P.dma("sp", g5[:], k.I["norm_w"][l, 5:6, :].broadcast_to([128, D]), writes=[CW])
        P.op("pool", "memset", writes=[CW], ap=epsc[:], constant=EPS)
        up = [sb("up%d" % i, [128, 8, 1024], BF16) for i in range(2)]
        dn = [sb("dn%d" % i, [128, 8, 1024], BF16) for i in range(2)]
        UPB = [Buf("up%d" % i) for i in range(2)]
        DNB = [Buf("dn%d" % i) for i in range(2)]
        xts = [sb("xt%d" % i, [128, 4, D], F32) for i in range(2)]
        XTBs = [Buf("xt%d" % i) for i in range(2)]
        junk2 = sb("junk2", [128, D], BF16)
        JB2 = Buf("junk2")
        ssv2 = sb("ssv2", [128, 8], F32)
        SSB2 = Buf("ssv2")
        h = sb("h", [128, 4, D], BF16)
        HB = Buf("h")
        hT = sb("hT", [128, 8, TT], BF16)
        HTB = Buf("hT")
        junk = sb("junk", [128, D], BF16)
        JB = Buf("junk")
        ssv = sb("ssv", [128, 8], F32)
        SSB = Buf("ssv")
        tmp = sb("tmp", [128, D], F32)
        TMPB = Buf("tmp")
        aT = sb("aT", [128, 32, TT], BF16)
        ATB = Buf("aT")
        sqv = [sb("sqv%d" % i, [128, TT], F32) for i in range(2)]
        SQB = [Buf("sqv%d" % i) for i in range(2)]
        wup = WB["w_up%d" % l].rearrange("(kc p) n -> p kc n", p=128)
        wdn = WB["w_down%d" % l].rearrange("(fc p) n -> p fc n", p=128)

        def MM(out, lhsT, rhs, start, stop, R, Wb):
            P.op("pe", "matmul", reads=R, writes=[Wb], out=out, lhsT=lhsT, rhs=rhs, start=start, stop=stop)

        seqs = [("prm", t * TT, TT) for t in range(T // TT)] + [("smp", T, TS)]
        if "f_seqs" in DBG:
            seqs = [seqs[i] for i in DBG["f_seqs"]]
        ui = 0
        di = 0
        fi = 0

        def tile_geom(ti):
            kind, t0, n = seqs[ti]
            return kind, t0, n, blocks(n), min(128, n)

        def load_x(ti):
            kind, t0, n, blks, pp = tile_geom(ti)
            P.dma("sp", xts[ti % 2][0:pp, 0:len(blks), :], k.S["x"][t0:t0 + n, :].rearrange("(b p) d -> p b d", p=pp),
                  writes=[XTBs[ti % 2]])

        def prenorm_e(ti):
            kind, t0, n, blks, pp = tile_geom(ti)
            pre_norm_elem(P, xts[ti % 2], XTBs[ti % 2], g4, CW, junk2, JB2, ssv2, SSB2, epsc, h, HB, blks)

        def prenorm_t(ti):
            kind, t0, n, blks, pp = tile_geom(ti)
            pre_norm_T(P, bank, PB, [0, 1, 2, 3], h, HB, hT, HTB, identb, blks, n)
        def load_up(p):
            if p >= 4 * len(seqs):
                return
            q_ = p % 4
            sl_ = p % 2
            P.dma("sp", up[sl_][:, 0:4, :], wup[:, 0:4, q_ * 1024:(q_ + 1) * 1024], writes=[UPB[sl_]])
            P.dma("pool", up[sl_][:, 4:8, :], wup[:, 4:8, q_ * 1024:(q_ + 1) * 1024], writes=[UPB[sl_]])
        load_x(0)
        load_up(0)
        load_up(1)
        prenorm_e(0)
        prenorm_t(0)
        for ti, (kind, t0, n) in enumerate(seqs):
            blks = blocks(n)
            nb = len(blks)
            pp = min(128, n)
            xt, XTB = xts[ti % 2], XTBs[ti % 2]
            if ti + 1 < len(seqs):
                load_x(ti + 1)

            def tm(ap2d):
                return ap2d.rearrange("(b p) d -> p b d", p=pp)
            for q in range(4):
                sl = (ti * 4 + q) % 2
                for fcl in range(8):
                    fc = q * 8 + fcl
                    bk = 4 + fi % 4
                    sq_ = fi % 2
                    fi += 1
                    for kc in range(8):
                        MM(bank(bk)[:, 0:n], up[sl][:, kc, fcl * 128:(fcl + 1) * 128], hT[:, kc, 0:n], kc == 0, kc == 7,
                           [UPB[sl], HTB], PB[bk])
                    P.op("act", "activation", reads=[PB[bk]], writes=[SQB[sq_]], out=sqv[sq_][:, 0:n], in_=bank(bk)[:, 0:n],
                         func=AF.Square)
                    P.op("dve", "scalar_tensor_tensor", reads=[PB[bk], SQB[sq_]], writes=[ATB], out=aT[:, fc, 0:n],
                         in0=bank(bk)[:, 0:n], scalar=0.0, in1=sqv[sq_][:, 0:n], op0=ALU.is_gt, op1=ALU.mult)
                load_up(ti * 4 + q + 2)
            for q in range(4):
                sl = di % 2
                di += 1
                P.dma("sp", dn[sl][:, 0:4, :], wdn[:, q * 8:q * 8 + 4, :], writes=[DNB[sl]])
                P.dma("pool", dn[sl][:, 4:8, :], wdn[:, q * 8 + 4:q * 8 + 8, :], writes=[DNB[sl]])
                for ib, (io, is_) in enumerate(blks):
                    for half in range(2):
                        bk = 2 * ib + half
                        for fcl in range(8):
                            fc = q * 8 + fcl
                            MM(bank(bk)[0:is_, :], aT[:, fc, io:io + is_], dn[sl][:, fcl, half * 512:(half + 1) * 512],
                               q == 0 and fcl == 0, q == 3 and fcl == 7, [ATB, DNB[sl]], PB[bk])
            if ti + 1 < len(seqs):
                prenorm_e(ti + 1)
            for ib, (io, is_) in enumerate(blks):
                pso = ps[0:is_, 2 * ib * 512:2 * ib * 512 + 1024]
                post_norm_residual(P, pso, [PB[2 * ib], PB[2 * ib + 1]], xt[0:is_, ib, :], XTB, g5, CW, junk, JB, ssv, SSB,
                                   epsc, tmp, TMPB, is_)
            if ti + 1 < len(seqs):
                prenorm_t(ti + 1)
            if final:
                dst = k.O["y_sample"] if kind == "smp" else k.O["y_prompt"][t0:t0 + n, :]
            else:
                dst = k.S["x"][t0:t0 + n, :]
            P.dma("sp", tm(dst), xt[0:pp, 0:nb, :], reads=[XTB], writes=[Buf()])
            P.maybe_flush()
        P.flush()


def phase_S1(k, P, ps, PB, bank, WB, identf, identb, onesf, onesb):
    with contextlib.ExitStack() as st:
        def sb(name, shape, dt):
            return k.sbuf(st, "s1_" + name, shape, dt)
        winc = sb("winc", [128, 8, D], BF16)
        g0 = sb("g0", [128, D], F32)
        epsc = sb("epsc", [128, 1], F32)
        CW = Buf("cw")
        P.dma("sp", winc[:], WB["w_in_c"].rearrange("(kc p) n -> p kc n", p=128), writes=[CW])
        P.dma("sp", g0[:], k.I["norm_w"][1, 0:1, :].broadcast_to([128, D]), writes=[CW])
        P.op("pool", "memset", writes=[CW], ap=epsc[:], constant=EPS)
        xts = [sb("xt%d" % i, [128, 4, D], F32) for i in range(2)]
        XTBs = [Buf("xt%d" % i) for i in range(2)]
        h = sb("h", [128, 4, D], BF16)
        HB = Buf("h")
        hT = sb("hT", [128, 8, TT], BF16)
        HTB = Buf("hT")
        junk = sb("junk", [128, D], BF16)
        JB = Buf("junk")
        ssv = sb("ssv", [128, 8], F32)
        SSB = Buf("ssv")
        uTs = [sb("uT%d" % i, [128, 8, TT], BF16) for i in range(2)]
        UTBs = [Buf("uT%d" % i) for i in range(2)]
        seqs = [("prm", t * TT, TT) for t in range(T // TT)] + [("smp", T, TS)]
        def load_x(ti):
            kind, t0, n = seqs[ti]
            pp = min(128, n)
            P.dma("sp", xts[ti % 2][0:pp, 0:len(blocks(n)), :], k.S["x"][t0:t0 + n, :].rearrange("(b p) d -> p b d", p=pp),
                  writes=[XTBs[ti % 2]])
        load_x(0)
        for ti, (kind, t0, n) in enumerate(seqs):
            blks = blocks(n)
            nb = len(blks)
            pp = min(128, n)
            xt, XTB = xts[ti % 2], XTBs[ti % 2]
            uT, UTB = uTs[ti % 2], UTBs[ti % 2]
            if ti + 1 < len(seqs):
                load_x(ti + 1)
            pre_norm_hT(P, bank, PB, [0, 1, 2, 3], xt, XTB, g0, CW, junk, JB, ssv, SSB, epsc, h, HB, hT, HTB, identb, blks, n)
            for mc in range(8):
                bk = 4 + mc % 4
                for kc in range(8):
                    P.op("pe", "matmul", reads=[HTB, CW], writes=[PB[bk]], out=bank(bk)[:, 0:n],
                         lhsT=winc[:, kc, mc * 128:(mc + 1) * 128], rhs=hT[:, kc, 0:n], start=(kc == 0), stop=(kc == 7))
                if mc % 2 == 0:
                    P.op("act", "copy", reads=[PB[bk]], writes=[UTB], out=uT[:, mc, 0:n], in_=bank(bk)[:, 0:n])
                else:
                    P.op("dve", "tensor_copy", reads=[PB[bk]], writes=[UTB], out=uT[:, mc, 0:n], in_=bank(bk)[:, 0:n])
            P.dma("act", k.S["uT"][:, :, t0:t0 + n].rearrange("c p t -> p c t"), uT[:, :, 0:n], reads=[UTB], writes=[Buf()])
            P.maybe_flush()
        P.flush()


def phase_S3(k, P, ps, PB, bank, WB, identf, identb, onesf, onesb):
    with contextlib.ExitStack() as st:
        def sb(name, shape, dt):
            return k.sbuf(st, "s3_" + name, shape, dt)
        wg = sb("wg", [128, 8, 2 * D], BF16)
        g1 = sb("g1", [128, D], F32)
        epsc = sb("epsc", [128, 1], F32)
        CW = Buf("cw")
        P.dma("sp", wg[:], WB["w_glu"].rearrange("(kc p) n -> p kc n", p=128), writes=[CW])
        P.dma("sp", g1[:], k.I["norm_w"][1, 1:2, :].broadcast_to([128, D]), writes=[CW])
        P.op("pool", "memset", writes=[CW], ap=epsc[:], constant=EPS)
        xts = [sb("xt%d" % i, [128, 4, D], F32) for i in range(2)]
        XTBs = [Buf("xt%d" % i) for i in range(2)]
        yTs = [sb("yT%d" % i, [128, 8, TT], BF16) for i in range(2)]
        YTBs = [Buf("yT%d" % i) for i in range(2)]
        junk = sb("junk", [128, D], BF16)
        JB = Buf("junk")
        ssv = sb("ssv", [128, 8], F32)
        SSB = Buf("ssv")
        tmp = sb("tmp", [128, D], F32)
        TMPB = Buf("tmp")
        sig = sb("sig", [128, D], F32)
        SIGB = Buf("sig")
        gl = sb("gl", [128, D], F32)
        GLB = Buf("gl")
        seqs = [("prm", t * TT, TT) for t in range(T // TT)] + [("smp", T, TS)]
        def load_in(ti):
            kind, t0, n = seqs[ti]
            pp = min(128, n)
            P.dma("sp", xts[ti % 2][0:pp, 0:len(blocks(n)), :], k.S["x"][t0:t0 + n, :].rearrange("(b p) d -> p b d", p=pp),
                  writes=[XTBs[ti % 2]])
            P.dma("sp", yTs[ti % 2][:, :, 0:n], k.S["yT"][:, :, t0:t0 + n].rearrange("c p t -> p c t"), writes=[YTBs[ti % 2]])
        load_in(0)
        for ti, (kind, t0, n) in enumerate(seqs):
            blks = blocks(n)
            nb = len(blks)
            pp = min(128, n)
            xt, XTB = xts[ti % 2], XTBs[ti % 2]
            yT, YTB = yTs[ti % 2], YTBs[ti % 2]
            if ti + 1 < len(seqs):
                load_in(ti + 1)
            for ib, (io, is_) in enumerate(blks):
                b0 = 4 * (ib % 2)
                for q4 in range(4):
                    bk = b0 + q4
                    for kc in range(8):
                        P.op("pe", "matmul", reads=[YTB, CW], writes=[PB[bk]], out=bank(bk)[0:is_, :],
                             lhsT=yT[:, kc, io:io + is_], rhs=wg[:, kc, q4 * 512:(q4 + 1) * 512], start=(kc == 0), stop=(kc == 7))
                P.op("act", "activation", reads=[PB[b0 + 2], PB[b0 + 3]], writes=[SIGB], out=sig[0:is_, :],
                     in_=ps[0:is_, (b0 + 2) * 512:(b0 + 4) * 512], func=AF.Sigmoid)
                P.op("dve", "tensor_tensor", reads=[PB[b0], PB[b0 + 1], SIGB], writes=[GLB], out=gl[0:is_, :],
                     in0=ps[0:is_, b0 * 512:(b0 + 2) * 512], in1=sig[0:is_, :], op=ALU.mult)
                post_norm_residual(P, gl[0:is_, :], [GLB], xt[0:is_, ib, :], XTB, g1, CW, junk, JB, ssv, SSB, epsc, tmp, TMPB, is_)
            P.dma("sp", k.S["x"][t0:t0 + n, :].rearrange("(b p) d -> p b d", p=pp), xt[0:pp, 0:nb, :], reads=[XTB], writes=[Buf()])
            P.maybe_flush()
        P.flush()


PI = float(np.pi)


def phase_S2(k, P, ps, PB, bank, WB, identf, identb, onesf, onesb):
    with contextlib.ExitStack() as st:
        def sb(name, shape, dt):
            return k.sbuf(st, "s2_" + name, shape, dt)
        Tall = sb("Tall", [128, 64, 128], BF16)
        Vall = sb("Vall", [128, 64, 128], BF16)
        WR = sb("WR", [128, 32, 128], BF16)
        WI = sb("WI", [128, 32, 128], BF16)
        Ec = sb("Ec", [128, 32, 64], F32)
        Es = sb("Es", [128, 32, 64], F32)
        Rt = sb("Rt", [128, 32, 64], F32)
        r8s = sb("r8s", [128, 32], F32)
        dfm = sb("dfm", [128, 8], F32)
        XRc = sb("XRc", [128, 32], F32)
        XIc = sb("XIc", [128, 32], F32)
        TB = Buf("s2tab")
        CARB = Buf("s2carry")
        P.dma("sp", dfm[:], k.I["s5_d"].rearrange("o (c p) -> p (o c)", p=128), writes=[TB], allow_slow_non_contiguous=True)

        def MM(out, lhsT, rhs, start, stop, R, Wb):
            P.op("pe", "matmul", reads=R, writes=[Wb], out=out, lhsT=lhsT, rhs=rhs, start=start, stop=stop)

        with contextlib.ExitStack() as st2:
            def sb2(name, shape, dt):
                return k.sbuf(st2, "s2t_" + name, shape, dt)
            lamr = sb2("lamr", [128, 32], F32)
            lami = sb2("lami", [128, 32], F32)
            dtv = sb2("dtv", [128, 32], F32)
            lr = sb2("lr", [128, 32], F32)
            li = sb2("li", [128, 32], F32)
            kvec = sb2("kvec", [128, 16], F32)
            nvec = sb2("nvec", [128, 64], F32)
            tmask = sb2("tmask", [128, 512], F32)
            Bre = sb2("Bre", [128, 32, 16], F32)
            Bim = sb2("Bim", [128, 32, 16], F32)
            Cnat = sb2("Cnat", [128, 4, 128], F32)
            Cre = sb2("Cre", [128, 32, 16], F32)
            Cim = sb2("Cim", [128, 32, 16], F32)
            arg = sb2("arg", [128, 32, 16], F32)
            mag = sb2("mag", [128, 32, 16], F32)
            Are = sb2("Are", [128, 32, 16], F32)
            Aim = sb2("Aim", [128, 32, 16], F32)
            s1 = sb2("s1", [128, 32], F32)
            s2_ = sb2("s2", [128, 32], F32)
            s3 = sb2("s3", [128, 32], F32)
            fre = sb2("fre", [128, 32], F32)
            fim = sb2("fim", [128, 32], F32)
            bbr = sb2("bbr", [128, 32, 16], F32)
            bbi = sb2("bbi", [128, 32, 16], F32)
            t16 = sb2("t16", [128, 32, 16], F32)
            Gre = sb2("Gre", [128, 32, 128], F32)
            Gim = sb2("Gim", [128, 32, 128], F32)
            Hre = sb2("Hre", [128, 32, 128], F32)
            Him = sb2("Him", [128, 32, 128], F32)
            big = sb2("big", [128, 32, 128], F32)
            eb = sb2("eb", [128, 32, 64], F32)
            SB_ = Buf("setup")
            for gh in range(2):
                psl = slice(gh * 64, gh * 64 + 64)
                gs = slice(gh * 32, gh * 32 + 32)
                P.dma("sp", lamr[psl, :], k.I["s5_lambda_re"][gs, :].rearrange("g p -> p g"), writes=[SB_],
                      allow_slow_non_contiguous=True)
                P.dma("act", lami[psl, :], k.I["s5_lambda_im"][gs, :].rearrange("g p -> p g"), writes=[SB_],
                      allow_slow_non_contiguous=True)
                P.dma("sp", dtv[psl, :], k.I["s5_log_dt"][0:1, gs].broadcast_to([64, 32]), writes=[SB_])
                P.dma("sp", Bre[psl, :, :], k.I["s5_b_re"][gs].rearrange("g p c -> p g c"), writes=[SB_])
                P.dma("act", Bim[psl, :, :], k.I["s5_b_im"][gs].rearrange("g p c -> p g c"), writes=[SB_])
            P.dma("sp", kvec[:], k.C["kvec"], writes=[SB_])
            P.dma("sp", nvec[:], k.C["nvec"], writes=[SB_])
            P.dma("sp", tmask[:], k.C["tmask"], writes=[SB_])
            for (src, dstC) in ((k.I["s5_c_re"], Cre), (k.I["s5_c_im"], Cim)):
                for gh in range(2):
                    P.dma("sp", Cnat[:, :, gh * 64:gh * 64 + 64],
                          src[gh * 32:gh * 32 + 32].rearrange("(o g) c p -> (g c) o p", o=4), writes=[SB_])
                for o4 in range(4):
                    MM(bank(0)[:, 0:128], Cnat[:, o4, :], identf[:, :], True, True, [SB_], PB[0])
                    gl0 = o4 * 8
                    P.op("dve", "tensor_copy", reads=[PB[0]], writes=[SB_],
                         out=dstC[:, gl0:gl0 + 8, :], in_=bank(0)[:, 0:128].rearrange("p (g c) -> p g c", c=16))

            def V(eng, name, **kw):
                P.op(eng, name, reads=[SB_], writes=[SB_], **kw)
            V("act", "activation", out=dtv[:], in_=dtv[:], func=AF.Exp)
            V("dve", "tensor_tensor", out=lr[:], in0=lamr[:], in1=dtv[:], op=ALU.mult)
            V("dve", "tensor_tensor", out=li[:], in0=lami[:], in1=dtv[:], op=ALU.mult)
            kb = kvec[:].unsqueeze(1).broadcast_to([128, 32, 16])
            V("dve", "tensor_tensor", out=arg[:], in0=lr[:].unsqueeze(2).broadcast_to([128, 32, 16]), in1=kb, op=ALU.mult)
            V("act", "activation", out=mag[:], in_=arg[:], func=AF.Exp)
            V("dve", "tensor_tensor", out=arg[:], in0=li[:].unsqueeze(2).broadcast_to([128, 32, 16]), in1=kb, op=ALU.mult)

            isc = sb2("isc", [128, 32, 64], mybir.dt.int32)

            def sincos(out_s, out_c, ang, f1, f2, i1):
                for (dst, off) in ((out_s, 0.0), (out_c, 0.25)):
                    V("dve", "tensor_scalar", out=f1, in0=ang, scalar1=1.0 / (2 * PI), scalar2=off, op0=ALU.mult, op1=ALU.add)
                    V("dve", "tensor_copy", out=i1, in_=f1)
                    V("dve", "tensor_copy", out=f2, in_=i1)
                    V("dve", "tensor_tensor", out=f1, in0=f1, in1=f2, op=ALU.subtract)
                    V("act", "activation", out=dst, in_=f1, func=AF.Sin, scale=2 * PI)
            sincos(Aim[:], Are[:], arg[:], t16[:], bbr[:], isc[:, :, 0:16])
            V("dve", "tensor_tensor", out=Are[:], in0=Are[:], in1=mag[:], op=ALU.mult)
            V("dve", "tensor_tensor", out=Aim[:], in0=Aim[:], in1=mag[:], op=ALU.mult)
            a_re, a_im = Are[:, :, 8], Aim[:, :, 8]
            V("dve", "tensor_tensor", out=s1[:], in0=lamr[:], in1=lamr[:], op=ALU.mult)
            V("dve", "tensor_tensor", out=s2_[:], in0=lami[:], in1=lami[:], op=ALU.mult)
            V("dve", "tensor_tensor", out=s1[:], in0=s1[:], in1=s2_[:], op=ALU.add)
            V("dve", "reciprocal", out=s1[:], in_=s1[:])
            V("dve", "tensor_scalar", out=s2_[:], in0=a_re, scalar1=-1.0, scalar2=None, op0=ALU.add)
            V("dve", "tensor_tensor", out=fre[:], in0=s2_[:], in1=lamr[:], op=ALU.mult)
            V("dve", "tensor_tensor", out=s3[:], in0=a_im, in1=lami[:], op=ALU.mult)
            V("dve", "tensor_tensor", out=fre[:], in0=fre[:], in1=s3[:], op=ALU.add)
            V("dve", "tensor_tensor", out=fre[:], in0=fre[:], in1=s1[:], op=ALU.mult)
            V("dve", "tensor_tensor", out=fim[:], in0=a_im, in1=lamr[:], op=ALU.mult)
            V("dve", "tensor_tensor", out=s3[:], in0=s2_[:], in1=lami[:], op=ALU.mult)
            V("dve", "tensor_tensor", out=fim[:], in0=fim[:], in1=s3[:], op=ALU.subtract)
            V("dve", "tensor_tensor", out=fim[:], in0=fim[:], in1=s1[:], op=ALU.mult)
            frb = fre[:].unsqueeze(2).broadcast_to([128, 32, 16])
            fib = fim[:].unsqueeze(2).broadcast_to([128, 32, 16])
            V("dve", "tensor_tensor", out=bbr[:], in0=Bre[:], in1=frb, op=ALU.mult)
            V("dve", "tensor_tensor", out=t16[:], in0=Bim[:], in1=fib, op=ALU.mult)
            V("dve", "tensor_tensor", out=bbr[:], in0=bbr[:], in1=t16[:], op=ALU.subtract)
            V("dve", "tensor_tensor", out=bbi[:], in0=Bim[:], in1=frb, op=ALU.mult)
            V("dve", "tensor_tensor", out=t16[:], in0=Bre[:], in1=fib, op=ALU.mult)
            V("dve", "tensor_tensor", out=bbi[:], in0=bbi[:], in1=t16[:], op=ALU.add)

            def v4(t):
                return t[:].rearrange("p g (s c) -> p g s c", c=16)

            def outer(are, aim, xre, xim, ore, oim, neg_im=False):
                ab = are.unsqueeze(3).broadcast_to([128, 32, 8, 16])
                ai = aim.unsqueeze(3).broadcast_to([128, 32, 8, 16])
                xr = xre.unsqueeze(2).broadcast_to([128, 32, 8, 16])
                xi = xim.unsqueeze(2).broadcast_to([128, 32, 8, 16])
                V("dve", "tensor_tensor", out=v4(ore), in0=ab, in1=xr, op=ALU.mult)
                V("dve", "tensor_tensor", out=v4(big), in0=ai, in1=xi, op=ALU.mult)
                V("dve", "tensor_tensor", out=ore[:], in0=ore[:], in1=big[:], op=ALU.subtract)
                V("dve", "tensor_tensor", out=v4(oim), in0=ab, in1=xi, op=ALU.mult)
                V("dve", "tensor_tensor", out=v4(big), in0=ai, in1=xr, op=ALU.mult)
                V("dve", "tensor_tensor", out=oim[:], in0=oim[:], in1=big[:], op=ALU.add)
            outer(Are[:, :, 0:8], Aim[:, :, 0:8], bbr[:], bbi[:], Gre, Gim)
            outer(Are[:, :, 8:16], Aim[:, :, 8:16], Cre[:], Cim[:], Hre, Him)
            V("dve", "tensor_scalar", out=Him[:], in0=Him[:], scalar1=-1.0, scalar2=None, op0=ALU.mult)
            for g in range(64):
                gh, gl_ = g // 32, g % 32
                psl = slice(gh * 64, gh * 64 + 64)
                bk = (g // 4) % 2
                col = (g % 4) * 128
                MM(bank(bk)[:, col:col + 128], Gre[psl, gl_, :], Hre[psl, gl_, :], True, False, [SB_], PB[bk])
                MM(bank(bk)[:, col:col + 128], Gim[psl, gl_, :], Him[psl, gl_, :], False, True, [SB_], PB[bk])
                if g % 4 == 3:
                    g0 = g - 3
                    P.op("dve", "tensor_tensor", reads=[PB[bk], SB_], writes=[TB],
                         out=Tall[:, g0:g0 + 4, :], in0=bank(bk)[:, :].rearrange("p (g m) -> p g m", m=128),
                         in1=tmask[:].rearrange("p (g m) -> p g m", m=128), op=ALU.mult)
            V("act", "copy", out=WR[:], in_=Hre[:])
            V("act", "copy", out=WI[:], in_=Him[:])
            P.op("act", "copy", reads=[SB_], writes=[TB], out=WR[:], in_=Hre[:])
            a8r = Are[:, :, 15:16].broadcast_to([128, 32, 128])
            a8i = Aim[:, :, 15:16].broadcast_to([128, 32, 128])
            V("dve", "tensor_tensor", out=Hre[:], in0=Gre[:], in1=a8r, op=ALU.mult)
            V("dve", "tensor_tensor", out=big[:], in0=Gim[:], in1=a8i, op=ALU.mult)
            V("dve", "tensor_tensor", out=Hre[:], in0=Hre[:], in1=big[:], op=ALU.subtract)
            V("dve", "tensor_tensor", out=Him[:], in0=Gim[:], in1=a8r, op=ALU.mult)
            V("dve", "tensor_tensor", out=big[:], in0=Gre[:], in1=a8i, op=ALU.mult)
            V("dve", "tensor_tensor", out=Him[:], in0=Him[:], in1=big[:], op=ALU.add)
            for g in range(64):
                gh, gl_ = g // 32, g % 32
                psl = slice(gh * 64, gh * 64 + 64)
                bk = 2 + (g // 4) % 2
                col = (g % 4) * 128
                MM(bank(bk)[:, col:col + 64], Hre[psl, gl_, :], identf[psl, psl], True, True, [SB_], PB[bk])
                MM(bank(bk)[:, col + 64:col + 128], Him[psl, gl_, :], identf[psl, psl], True, True, [SB_], PB[bk])
                if g % 4 == 3:
                    g0 = g - 3
                    P.op("act", "copy", reads=[PB[bk]], writes=[TB], out=Vall[:, g0:g0 + 4, :],
                         in_=bank(bk)[:, :].rearrange("p (g m) -> p g m", m=128))
            nb_ = nvec[:].unsqueeze(1).broadcast_to([128, 32, 64])
            V("dve", "tensor_tensor", out=eb[:], in0=li[:].unsqueeze(2).broadcast_to([128, 32, 64]), in1=nb_, op=ALU.mult)
            sincos(Es[:], Ec[:], eb[:], Rt[:], big[:, :, 0:64], isc[:])
            V("act", "activation", out=r8s[:], in_=lr[:], func=AF.Exp, scale=8.0)
            P.op("dve", "tensor_copy", reads=[SB_], writes=[TB], out=Rt[:], in_=r8s[:].unsqueeze(2).broadcast_to([128, 32, 64]))
            P.op("pool", "memset", reads=[SB_], writes=[TB], ap=Rt[:, :, 0:1], constant=0.0)
            P.flush()

        sel = sb("sel", [128, 64, 128], BF16)
        selT = sb("selT", [128, 64, 128], BF16)
        P.dma("sp", sel[:], k.C["sel_b"], writes=[TB])
        P.dma("act", selT[:], k.C["selT_b"], writes=[TB])
        uTs = [sb("uT%d" % i, [128, 8, TT], BF16) for i in range(2)]
        UTBs = [Buf("uT%d" % i) for i in range(2)]
        Ugs = [sb("Ug%d" % i, [128, 64, 64], BF16) for i in range(2)]
        UGBs = [Buf("Ug%d" % i) for i in range(2)]
        XsR = sb("XsR", [128, 32, 64], BF16)
        XsI = sb("XsI", [128, 32, 64], BF16)
        XSB = Buf("Xs")
        Yg = sb("Yg", [128, 64, 64], BF16)
        YGB = Buf("Yg")
        yT = sb("yT", [128, 8, TT], BF16)
        YTB = Buf("yT")
        ta = sb("ta", [128, 16, 64], F32)
        tb = sb("tb", [128, 16, 64], F32)
        vr = sb("vr", [128, 16, 64], F32)
        vi = sb("vi", [128, 16, 64], F32)
        zr = sb("zr", [128, 16, 64], F32)
        zi = sb("zi", [128, 16, 64], F32)
        tc_ = sb("tc", [128, 16], F32)
        WKB = Buf("s2work")
        for t_ in (ta, tb, vr, vi, zr, zi):
            P.op("pool", "memset", writes=[WKB], ap=t_[:], constant=0.0)
        P.op("pool", "memset", writes=[CARB], ap=XRc[:], constant=0.0)
        P.op("pool", "memset", writes=[CARB], ap=XIc[:], constant=0.0)
        P.op("pool", "memset", writes=[XSB], ap=XsR[:], constant=0.0)
        P.op("pool", "memset", writes=[XSB], ap=XsI[:], constant=0.0)

        def f2(t):
            return t[:].rearrange("p g n -> p (g n)")
        seqs = [("prm", t * TT, TT) for t in range(T // TT)] + [("smp", T, TS)]
        if "s2_seqs" in DBG:
            seqs = [seqs[i] for i in DBG["s2_seqs"]]
        def load_u(ti):
            kind, t0, n = seqs[ti]
            P.dma("sp", uTs[ti % 2][:, :, 0:n], k.S["uT"][:, :, t0:t0 + n].rearrange("c p t -> p c t"), writes=[UTBs[ti % 2]])
        load_u(0)
        def tile_vars(ti):
            kind, t0, n = seqs[ti]
            return kind, t0, n, n // 8, uTs[ti % 2], UTBs[ti % 2], Ugs[ti % 2], UGBs[ti % 2]

        def stageA(ti):
            kind, t0, n, nch, uT, UTB, Ug, UGB = tile_vars(ti)
            for g in range(64):
                kc, g8 = g // 8, g % 8
                bk = (g // 8) % 2
                col = g8 * 64
                for s_ in range(8):
                    rhs = uT[:, kc, 0:n].rearrange("p (n s) -> p n s", s=8)[:, :, s_]
                    MM(bank(bk)[:, col:col + nch], sel[:, g8 * 8 + s_, :], rhs, s_ == 0, s_ == 7, [UTB, TB], PB[bk])
                if g8 == 7:
                    g0 = g - 7
                    src = bank(bk)[:, :].rearrange("p (g n) -> p g n", n=64)[:, :, 0:nch]
                    if (g // 8) % 2 == 0:
                        P.op("act", "copy", reads=[PB[bk]], writes=[UGB], out=Ug[:, g0:g0 + 8, 0:nch], in_=src)
                    else:
                        P.op("dve", "tensor_copy", reads=[PB[bk]], writes=[UGB], out=Ug[:, g0:g0 + 8, 0:nch], in_=src)

        def stageB(ti):
            kind, t0, n, nch, uT, UTB, Ug, UGB = tile_vars(ti)
            if kind == "smp":
                for gh in range(2):
                    psl = slice(gh * 64, gh * 64 + 64)
                    gs = slice(gh * 32, gh * 32 + 32)
                    P.dma("sp", XRc[psl, :], k.I["state_s5_re"][gs, :].rearrange("g p -> p g"), writes=[CARB],
                          allow_slow_non_contiguous=True)
                    P.dma("act", XIc[psl, :], k.I["state_s5_im"][gs, :].rearrange("g p -> p g"), writes=[CARB],
                          allow_slow_non_contiguous=True)
            for hf in range(2):
                glo = hf * 16
                for gh in range(2):
                    psl = slice(gh * 64, gh * 64 + 64)
                    for gq in range(16):
                        g = gh * 32 + glo + gq
                        bR = 2 + gq // 8
                        bI = 4 + gq // 8
                        col = (gq % 8) * 64
                        MM(bank(bR)[psl, col:col + nch], Vall[:, g, 0:64], Ug[:, g, 0:nch], True, True, [UGB, TB], PB[bR])
                        MM(bank(bI)[psl, col:col + nch], Vall[:, g, 64:128], Ug[:, g, 0:nch], True, True, [UGB, TB], PB[bI])
                pR = ps[:, 2 * 512:4 * 512].rearrange("p (g n) -> p g n", n=64)[:, :, 0:nch]
                pI = ps[:, 4 * 512:6 * 512].rearrange("p (g n) -> p g n", n=64)[:, :, 0:nch]
                ec = Ec[:, glo:glo + 16, 0:nch]
                es = Es[:, glo:glo + 16, 0:nch]
                RD = [PB[2], PB[3], PB[4], PB[5], TB, WKB]
                WW = [WKB]

                def W3(t):
                    return t[:, :, 0:nch]
                P.op("dve", "tensor_tensor", reads=RD, writes=WW, out=W3(ta), in0=pR, in1=ec, op=ALU.mult)
                P.op("dve", "tensor_tensor", reads=RD, writes=WW, out=W3(tb), in0=pI, in1=es, op=ALU.mult)
                P.op("pool", "tensor_tensor", reads=RD, writes=WW, out=W3(vr), in0=W3(ta), in1=W3(tb), op=ALU.add)
                P.op("dve", "tensor_tensor", reads=RD, writes=WW, out=W3(ta), in0=pI, in1=ec, op=ALU.mult)
                P.op("dve", "tensor_tensor", reads=RD, writes=WW, out=W3(tb), in0=pR, in1=es, op=ALU.mult)
                P.op("pool", "tensor_tensor", reads=RD, writes=WW, out=W3(vi), in0=W3(ta), in1=W3(tb), op=ALU.subtract)
                for (vv, cc) in ((vr, XRc), (vi, XIc)):
                    P.op("dve", "tensor_tensor", reads=[CARB, TB, WKB], writes=WW, out=tc_[:], in0=cc[:, glo:glo + 16],
                         in1=r8s[:, glo:glo + 16], op=ALU.mult)
                    P.op("dve", "tensor_tensor", reads=[WKB], writes=WW, out=vv[:, :, 0], in0=vv[:, :, 0], in1=tc_[:],
                         op=ALU.add)
                rt2 = Rt[:, glo:glo + 16, :].rearrange("p g n -> p (g n)")
                P.op("dve", "tensor_tensor_scan", reads=[WKB, TB], writes=WW, out=f2(zr), data0=rt2, data1=f2(vr), initial=0.0,
                     op0=ALU.mult, op1=ALU.add)
                P.op("dve", "tensor_tensor_scan", reads=[WKB, TB], writes=WW, out=f2(zi), data0=rt2, data1=f2(vi), initial=0.0,
                     op0=ALU.mult, op1=ALU.add)
                P.op("dve", "tensor_tensor", reads=[WKB, TB], writes=WW, out=W3(ta), in0=W3(zr), in1=ec, op=ALU.mult)
                P.op("pool", "tensor_tensor", reads=[WKB, TB], writes=WW, out=W3(tb), in0=W3(zi), in1=es, op=ALU.mult)
                P.op("dve", "tensor_tensor", reads=[WKB], writes=WW, out=W3(vr), in0=W3(ta), in1=W3(tb), op=ALU.subtract)
                P.op("dve", "tensor_tensor", reads=[WKB, TB], writes=WW, out=W3(ta), in0=W3(zi), in1=ec, op=ALU.mult)
                P.op("pool", "tensor_tensor", reads=[WKB, TB], writes=WW, out=W3(tb), in0=W3(zr), in1=es, op=ALU.mult)
                P.op("dve", "tensor_tensor", reads=[WKB], writes=WW, out=W3(vi), in0=W3(ta), in1=W3(tb), op=ALU.add)
                P.op("act", "copy", reads=[CARB], writes=[XSB], out=XsR[:, glo:glo + 16, 0], in_=XRc[:, glo:glo + 16])
                P.op("act", "copy", reads=[CARB], writes=[XSB], out=XsI[:, glo:glo + 16, 0], in_=XIc[:, glo:glo + 16])
                if nch > 1:
                    P.op("act", "copy", reads=[WKB], writes=[XSB], out=XsR[:, glo:glo + 16, 1:nch], in_=vr[:, :, 0:nch - 1])
                    P.op("act", "copy", reads=[WKB], writes=[XSB], out=XsI[:, glo:glo + 16, 1:nch], in_=vi[:, :, 0:nch - 1])
                P.op("dve", "tensor_copy", reads=[WKB, XSB], writes=[CARB], out=XRc[:, glo:glo + 16], in_=vr[:, :, nch - 1])
                P.op("dve", "tensor_copy", reads=[WKB, XSB], writes=[CARB], out=XIc[:, glo:glo + 16], in_=vi[:, :, nch - 1])
            last_of_kind = (kind == "smp") or (t0 + n == T)
            if last_of_kind:
                for gh in range(2):
                    psl = slice(gh * 64, gh * 64 + 64)
                    gs = slice(gh * 32, gh * 32 + 32)
                    dr = k.O["s5_re_sample" if kind == "smp" else "s5_re_prompt"]
                    di = k.O["s5_im_sample" if kind == "smp" else "s5_im_prompt"]
                    P.dma("sp", dr[gs, :].rearrange("g p -> p g"), XRc[psl, :], reads=[CARB], writes=[Buf()],
                          allow_slow_non_contiguous=True)
                    P.dma("act", di[gs, :].rearrange("g p -> p g"), XIc[psl, :], reads=[CARB], writes=[Buf()],
                          allow_slow_non_contiguous=True)

        def stageC(ti):
            kind, t0, n, nch, uT, UTB, Ug, UGB = tile_vars(ti)
            for g in range(64):
                gh, gl_ = g // 32, g % 32
                psl = slice(gh * 64, gh * 64 + 64)
                bk = 6 + (g // 8) % 2
                col = (g % 8) * 64
                MM(bank(bk)[:, col:col + nch], Tall[:, g, :], Ug[:, g, 0:nch], True, False, [UGB, TB], PB[bk])
                MM(bank(bk)[:, col:col + nch], WR[psl, gl_, :], XsR[psl, gl_, 0:nch], False, False, [XSB, TB], PB[bk])
                MM(bank(bk)[:, col:col + nch], WI[psl, gl_, :], XsI[psl, gl_, 0:nch], False, True, [XSB, TB], PB[bk])
                if g % 8 == 7:
                    g0 = g - 7
                    src = bank(bk)[:, :].rearrange("p (g n) -> p g n", n=64)[:, :, 0:nch]
                    if (g // 8) % 2 == 0:
                        P.op("act", "copy", reads=[PB[bk]], writes=[YGB], out=Yg[:, g0:g0 + 8, 0:nch], in_=src)
                    else:
                        P.op("dve", "tensor_copy", reads=[PB[bk]], writes=[YGB], out=Yg[:, g0:g0 + 8, 0:nch], in_=src)
            for kc in range(8):
                bk = kc % 2
                for tau in range(8):
                    for g8 in range(8):
                        MM(bank(bk)[:, tau * 64:tau * 64 + nch], selT[:, g8 * 8 + tau, :], Yg[:, kc * 8 + g8, 0:nch], g8 == 0,
                           g8 == 7, [YGB, TB], PB[bk])
                yv = yT[:, kc, 0:n].rearrange("p (n s) -> p s n", s=8)
                uv = uT[:, kc, 0:n].rearrange("p (n s) -> p s n", s=8)
                pv = bank(bk)[:, :].rearrange("p (s n) -> p s n", n=64)[:, :, 0:nch]
                P.op("dve", "scalar_tensor_tensor", reads=[PB[bk], UTB, TB], writes=[YTB], out=yv, in0=uv, scalar=dfm[:, kc:kc + 1],
                     in1=pv, op0=ALU.mult, op1=ALU.add)
            P.dma("act", k.S["yT"][:, :, t0:t0 + n].rearrange("c p t -> p c t"), yT[:, :, 0:n], reads=[YTB], writes=[Buf()])

        stageA(0)
        for ti in range(len(seqs)):
            if ti + 1 < len(seqs):
                load_u(ti + 1)
            stageB(ti)
            if ti + 1 < len(seqs):
                stageA(ti + 1)
            stageC(ti)
            P.maybe_flush()
        P.flush()
```

```python
import contextlib
import numpy as np
import ml_dtypes
import concourse.bass as bass
import concourse.mybir as mybir
from concourse.bass_utils import run_bass_kernel_spmd

F32 = mybir.dt.float32
BF16 = mybir.dt.bfloat16
AF = mybir.ActivationFunctionType
ALU = mybir.AluOpType
AX = mybir.AxisListType

D = 1024
T = 8192
TS = 32
PAST = 4096
TALL = T + 128
TT = 512
EPS = 1e-6
AB_IN = 3592
NCORES = 8


class Buf:
    __slots__ = ("name", "w", "r", "pw")

    def __init__(self, name=""):
        self.name = name
        self.w = {}
        self.r = {}
        self.pw = {}


class Op:
    __slots__ = ("eng", "fn", "deps", "marked", "sem", "val", "is_dma", "slotwait", "phase")

    def __init__(self, eng, fn, is_dma, phase):
        self.eng = eng
        self.fn = fn
        self.deps = []
        self.marked = False
        self.sem = None
        self.val = 0
        self.is_dma = is_dma
        self.slotwait = None
        self.phase = phase


class Prog:
    ENGS = ("pe", "act", "dve", "pool", "sp")
    NSLOT = 8

    def __init__(self, nc, stack):
        self.nc = nc
        self.ops = []
        self.phase = 0
        self.csem = {e: stack.enter_context(nc.semaphore("c_" + e)) for e in ("pe", "act", "dve", "pool")}
        self.dsem = {e: [stack.enter_context(nc.semaphore("d_%s%d" % (e, i))) for i in range(self.NSLOT)]
                     for e in ("sp", "act", "pool")}
        self.ccount = {e: 0 for e in self.csem}
        self.dcount = {e: 0 for e in self.dsem}
        self.waited = {e: {} for e in self.ENGS}
        self.dma_final = {}
        self.n_emitted = 0

    def _add(self, op, reads, writes):
        deps = {}
        for b in reads:
            for o in b.w.values():
                deps[id(o)] = o
        key = id(op) if op.is_dma else op.eng
        for b in writes:
            merge = op.is_dma and b.w and not b.r and all(o.is_dma and o.phase == self.phase for o in b.w.values())
            if merge:
                for o in getattr(b, "pw", {}).values():
                    deps[id(o)] = o
                b.w[key] = op
                continue
            prev = {}
            for o in b.w.values():
                deps[id(o)] = o
                prev[id(o)] = o
            for o in b.r.values():
                deps[id(o)] = o
                prev[id(o)] = o
            b.pw = prev
            b.w = {key: op}
            b.r = {}
        for b in reads:
            b.r[key] = op
        for o in deps.values():
            if o is op or o.phase != self.phase:
                continue
            if (not o.is_dma) and (not op.is_dma) and o.eng == "pe" and op.eng == "pe":
                continue
            op.deps.append(o)
            o.marked = True
        self.ops.append(op)
        return op

    def op(self, eng, name, reads=(), writes=(), **kw):
        def fn(e):
            return getattr(e, name)(**kw)
        return self._add(Op(eng, fn, False, self.phase), reads, writes)

    def dma(self, eng, out, in_, reads=(), writes=(), **kw):
        def fn(e):
            return e.dma_start(out=out, in_=in_, **kw)
        o = Op(eng, fn, True, self.phase)
        o.marked = True
        return self._add(o, reads, writes)

    def maybe_flush(self, limit=2500):
        if len(self.ops) >= limit:
            self.flush()

    def flush(self):
        if not self.ops:
            return
        nc = self.nc
        per = {e: [] for e in self.ENGS}
        last_c = {}
        for o in self.ops:
            per[o.eng].append(o)
            if not o.is_dma:
                last_c[o.eng] = o
        for o in last_c.values():
            o.marked = True
        for o in self.ops:
            if o.is_dma:
                i = self.dcount[o.eng]
                self.dcount[o.eng] += 1
                o.sem = self.dsem[o.eng][i % self.NSLOT]
                o.val = 16 * (i // self.NSLOT + 1)
                if i >= self.NSLOT:
                    o.slotwait = (o.sem, o.val - 16)
                self.dma_final[id(o.sem)] = (o.sem, o.val)
            elif o.marked:
                self.ccount[o.eng] += 1
                o.sem = self.csem[o.eng]
                o.val = self.ccount[o.eng]
        finals = list(self.dma_final.values()) + [(self.csem[e], self.ccount[e]) for e in self.csem]
        self.n_emitted += len(self.ops)

        def make(ename, ops):
            waited = self.waited[ename]

            def body(e):
                def w(sem, val):
                    if val > 0 and waited.get(id(sem), 0) < val:
                        e.wait_ge(sem, val)
                        waited[id(sem)] = val
                for o in ops:
                    if o.slotwait is not None:
                        w(*o.slotwait)
                    for d in o.deps:
                        w(d.sem, d.val)
                    ins = o.fn(e)
                    if o.marked:
                        ins.then_inc(o.sem, 16 if o.is_dma else 1)
                for sem, val in finals:
                    w(sem, val)
            return body

        with nc.allow_low_precision(reason="bf16 matmul operands by design"), nc.Block() as blk:
            blk.tensor(make("pe", per["pe"]))
            blk.scalar(make("act", per["act"]))
            blk.vector(make("dve", per["dve"]))
            blk.gpsimd(make("pool", per["pool"]))
            blk.sync(make("sp", per["sp"]))
        self.ops = []
        self.phase += 1


def _consts():
    c = {}
    c["ident_f"] = np.eye(128, dtype=np.float32)
    c["ident_b"] = np.eye(128, dtype=np.float32).astype(ml_dtypes.bfloat16)
    j = np.arange(128)
    c["tri_f"] = (j[:, None] <= j[None, :]).astype(np.float32)
    c["ones_f"] = np.ones((128, 128), np.float32)
    c["ones_b"] = np.ones((128, 128), np.float32).astype(ml_dtypes.bfloat16)
    perm = np.zeros((128, 128), np.float32)
    perm[(j + 64) % 128, j] = 1.0
    c["perm_b"] = perm.astype(ml_dtypes.bfloat16)
    half = 64
    inv = (10000.0 ** (-np.arange(half, dtype=np.float32) / half)).astype(np.float32)
    pos = np.concatenate([np.arange(T), PAST + np.arange(128)]).astype(np.float32)
    ang = pos[None, :] * inv[:, None]
    cos = np.cos(ang).astype(np.float32)
    sin = np.sin(ang).astype(np.float32)
    c["ropec"] = np.concatenate([cos, cos], 0)
    c["ropes"] = np.concatenate([-sin, sin], 0)
    lg = np.log(1.0 - 2.0 ** (-5.0 - np.arange(4, dtype=np.float64)))
    il = np.arange(512)[None, :]
    jl = np.arange(128)[:, None]
    rmask = np.zeros((4, 128, 512), np.float64)
    for h in range(4):
        m = np.exp(lg[h] * np.abs(il - jl))
        ok = (jl // 64) <= (il // 64)
        rmask[h] = np.where(ok, m, 0.0)
    c["rmask"] = rmask.transpose(1, 0, 2).astype(np.float32).copy()
    ii = np.arange(512).reshape(4, 128)
    wq = np.exp(lg[None, None, :] * (ii.T[:, :, None] + 1.0))
    c["wq"] = np.repeat(wq[:, :, :, None], 128, 3).astype(np.float32)
    wk = np.exp(lg[None, None, :] * (511.0 - ii.T[:, :, None]))
    c["wk"] = np.repeat(wk[:, :, :, None], 128, 3).astype(np.float32)
    wks = np.exp(lg[None, :] * (31.0 - np.arange(128)[:, None]))
    wks[32:] = 0.0
    c["wks"] = np.repeat(wks[:, :, None], 128, 2).astype(np.float32)
    c["rdec"] = np.stack([np.exp(lg * 512.0), np.exp(lg * 32.0)]).astype(np.float32)
    qs = np.ones((128, 1), np.float32)
    qs[:64] = 0.125
    c["qscale"] = qs
    selc = np.zeros((8, 8, 3, 3), np.float32)
    for h in range(8):
        for p_ in range(3):
            selc[h, h, p_, p_] = 1.0
    c["selc"] = selc.astype(ml_dtypes.bfloat16)
    c["caus_b"] = (j[:, None] <= j[None, :]).astype(np.float32).astype(ml_dtypes.bfloat16)
    kv = np.concatenate([-(np.arange(8) + 1.0), np.arange(8) + 1.0]).astype(np.float32)
    c["kvec"] = np.repeat(kv[None, :], 128, 0).copy()
    c["nvec"] = np.repeat((8.0 * (np.arange(64) + 1.0)).astype(np.float32)[None, :], 128, 0).copy()
    sel = np.zeros((128, 8, 8, 128), np.float32)
    selT = np.zeros((128, 8, 8, 128), np.float32)
    for g_ in range(8):
        for s_ in range(8):
            for c_ in range(16):
                sel[g_ * 16 + c_, g_, s_, s_ * 16 + c_] = 1.0
                selT[s_ * 16 + c_, g_, s_, g_ * 16 + c_] = 1.0
    c["sel_b"] = sel.reshape(128, 64, 128).astype(ml_dtypes.bfloat16)
    c["selT_b"] = selT.reshape(128, 64, 128).astype(ml_dtypes.bfloat16)
    sg = np.arange(128) // 16
    tm_ = (sg[None, :] >= sg[:, None]).astype(np.float32)
    c["tmask"] = np.tile(tm_, (1, 4)).copy()
    return c


CONST = None


def get_consts():
    global CONST
    if CONST is None:
        CONST = _consts()
    return CONST


IN_SHAPES = {
    "x_prompt": [T, D], "x_sample": [TS, D], "cache_ret_state": [4, 128, 128],
    "cache_fox_k": [PAST, 512], "cache_fox_v": [PAST, 512], "cache_fox_logf": [PAST, 8],
    "state_s5_re": [64, 64], "state_s5_im": [64, 64],
    "cache_mem_k": [2, 256, 1024], "cache_mem_v": [2, 256, 1024], "mem_prompt": [256, 1024],
    "norm_w": [2, 6, D], "w_in_ab": [D, AB_IN], "b_fox_f": [1, 8], "ret_gn_w": [1, 512],
    "w_out_ab": [D, D], "w_in_c": [D, D], "s5_lambda_re": [64, 64], "s5_lambda_im": [64, 64],
    "s5_log_dt": [1, 64], "s5_b_re": [64, 64, 16], "s5_b_im": [64, 64, 16],
    "s5_c_re": [64, 16, 64], "s5_c_im": [64, 16, 64], "s5_d": [1, D], "w_glu": [D, 2 * D],
    "mem_norm_w": [2, D], "w_xq": [2, D, D], "w_xk": [2, D, D], "w_xv": [2, D, D], "w_xo": [2, D, D],
    "w_up": [2, D, 4 * D], "w_down": [2, 4 * D, D],
}
OUT_SHAPES = {
    "y_prompt": [T, D], "y_sample": [TS, D], "ret_state_prompt": [4, 128, 128], "ret_state_sample": [4, 128, 128],
    "fox_k_prompt": [T, 512], "fox_v_prompt": [T, 512], "fox_logf_prompt": [T, 8],
    "fox_k_sample": [TS, 512], "fox_v_sample": [TS, 512], "fox_logf_sample": [TS, 8],
    "s5_re_prompt": [64, 64], "s5_im_prompt": [64, 64], "s5_re_sample": [64, 64], "s5_im_sample": [64, 64],
    "mem_k_prompt": [2, 256, 1024], "mem_v_prompt": [2, 256, 1024],
}
OUT_ORDER = list(OUT_SHAPES.keys())


class K:
    def __init__(self):
        self.nc = bass.Bass("TRN2", target_bir_lowering=False)
        nc = self.nc
        self.I = {k: nc.dram_tensor(k, s, F32, kind="ExternalInput").ap() for k, s in IN_SHAPES.items()}
        self.O = {k: nc.dram_tensor(k, s, F32, kind="ExternalOutput").ap() for k, s in OUT_SHAPES.items()}
        self.C = {}
        for k, v in get_consts().items():
            dt = BF16 if v.dtype == ml_dtypes.bfloat16 else F32
            self.C[k] = nc.dram_tensor("c_" + k, list(v.shape), dt, kind="ExternalInput").ap()
        self.S = {}
        self.SB = {}

    def scratch(self, name, shape, dt):
        self.S[name] = self.nc.dram_tensor("s_" + name, shape, dt, kind="Internal").ap()
        self.SB[name] = {}
        return self.S[name]

    def sbuf(self, st, name, shape, dt):
        return st.enter_context(self.nc.sbuf_tensor(name, shape, dt))


def tiles():
    out = [(t * TT, TT, False) for t in range(T // TT)]
    out.append((T, TS, True))
    return out


def xsrc(k, which, t0, n, sample):
    return k.I["x_sample"] if sample else k.I["x_prompt"][t0:t0 + n, :]


def blocks(n):
    return [(b * 128, min(128, n - b * 128)) for b in range((n + 127) // 128)]


def build(phases=("W", "A1")):
    k = K()
    nc = k.nc
    with contextlib.ExitStack() as top:
        P = Prog(nc, top)
        ps = top.enter_context(nc.psum_tensor("ps", [128, 4096], F32))
        PB = [Buf("ps%d" % i) for i in range(8)]

        def bank(i):
            return ps[:, i * 512:(i + 1) * 512]

        identf = k.sbuf(top, "identf", [128, 128], F32)
        identb = k.sbuf(top, "identb", [128, 128], BF16)
        onesf = k.sbuf(top, "onesf", [128, 128], F32)
        onesb = k.sbuf(top, "onesb", [128, 128], BF16)
        CB = Buf("consts")
        P.dma("sp", identf[:], k.C["ident_f"], writes=[CB])
        P.dma("sp", identb[:], k.C["ident_b"], writes=[CB])
        P.dma("sp", onesf[:], k.C["ones_f"], writes=[CB])
        P.dma("sp", onesb[:], k.C["ones_b"], writes=[CB])
        P.flush()

        WNAMES = [("w_in_ab", None, [D, AB_IN]), ("w_out_ab", None, [D, D]), ("w_in_c", None, [D, D]),
                  ("w_glu", None, [D, 2 * D])]
        for l in range(2):
            for nm in ("w_xq", "w_xk", "w_xv", "w_xo"):
                WNAMES.append((nm, l, [D, D]))
            WNAMES.append(("w_up", l, [D, 4 * D]))
            WNAMES.append(("w_down", l, [4 * D, D]))
        WB = {}
        for nm, l, shp in WNAMES:
            key = nm if l is None else "%s%d" % (nm, l)
            WB[key] = k.scratch("wb_" + key, shp, BF16)
        if "W" in phases:
            wbuf = Buf("wcast")
            for nm, l, shp in WNAMES:
                key = nm if l is None else "%s%d" % (nm, l)
                src = k.I[nm] if l is None else k.I[nm][l]
                rows = shp[0]
                step = 256
                for r0 in range(0, rows, step):
                    P.dma("pool", WB[key][r0:r0 + step, :], src[r0:r0 + step, :], writes=[Buf()])
            P.flush()

        k.scratch("x", [TALL, D], F32)
        k.scratch("rqT", [4, 128, TALL], BF16)
        k.scratch("rkT", [4, 128, TALL], BF16)
        k.scratch("rkk", [TALL, 512], BF16)
        k.scratch("rv", [TALL, 512], BF16)
        k.scratch("rg", [TALL, 512], BF16)
        k.scratch("fqa", [17, 4, 67, 2, TT], BF16)
        k.scratch("fka", [25, 4, 67, 2, TT], BF16)
        k.scratch("fva", [25, 4, 128, 4, 130], BF16)
        k.scratch("negc", [25, 128, 4, 8], F32)

        k.scratch("foT", [8, 64, TALL], BF16)
        k.scratch("uT", [8, 128, TALL], BF16)
        k.scratch("yT", [8, 128, TALL], BF16)
        env = dict(k=k, P=P, ps=ps, PB=PB, bank=bank, WB=WB, identf=identf, identb=identb, onesf=onesf, onesb=onesb)
        if "A1" in phases:
            phase_A1(k, P, ps, PB, bank, WB, identf, identb, onesf, onesb)
        if "A3" in phases:
            phase_A3(**env)
        if "A2" in phases:
            phase_A2(**env)
        for l in range(2):
            if "S" in phases and l == 1:
                phase_S1(**env)
                phase_S2(**env)
                phase_S3(**env)
            if ("C%d" % l) in phases:
                phase_C(l=l, **env)
            if ("F%d" % l) in phases:
                phase_F(l=l, final=(l == 1), **env)
        if "dumpx" in DBG:
            P.dma("sp", k.O["y_prompt"], k.S["x"][0:T, :], writes=[Buf()])
            P.dma("sp", k.O["y_sample"], k.S["x"][T:T + TS, :], writes=[Buf()])
            P.flush()
        if "dumpfo" in DBG:
            P.dma("pool", k.O["y_prompt"][0:512, :].rearrange("(h r) t -> h r t", h=8),
                  k.S["foT"][:, :, DBG["dumpfo"]:DBG["dumpfo"] + 1024], writes=[Buf()])
            P.flush()
    return k


CB_GLOBAL = Buf("cglobal")
DBG = {}


def phase_A1(k, P, ps, PB, bank, WB, identf, identb, onesf, onesb):
    with contextlib.ExitStack() as st:
        def sb(name, shape, dt):
            return k.sbuf(st, "a1_" + name, shape, dt)
        win = sb("win", [128, 8, AB_IN], BF16)
        g0 = sb("g0", [128, D], F32)
        bfox = sb("bfox", [128, 8], F32)
        perm = sb("perm", [128, 128], BF16)
        trif = sb("trif", [128, 128], F32)
        qscale = sb("qscale", [128, 1], F32)
        selc = sb("selc", [8, 72], BF16)
        wk = sb("wk", [128, 4, 512], F32)
        wks = sb("wks", [128, 512], F32)
        WBUF = Buf("a1w")
        wsrc = WB["w_in_ab"].rearrange("(kc p) n -> p kc n", p=128)
        for kc in range(8):
            P.dma("sp" if kc % 2 == 0 else "act", win[:, kc, :], wsrc[:, kc, :], writes=[WBUF])
        P.dma("sp", g0[:], k.I["norm_w"][0, 0:1, :].broadcast_to([128, D]), writes=[WBUF])
        P.dma("sp", bfox[:], k.I["b_fox_f"][0:1, :].broadcast_to([128, 8]), writes=[WBUF])
        P.dma("sp", perm[:], k.C["perm_b"], writes=[WBUF])
        P.dma("sp", trif[:], k.C["tri_f"], writes=[WBUF])
        P.dma("sp", qscale[:], k.C["qscale"], writes=[WBUF])
        P.dma("sp", selc[:], k.C["selc"].rearrange("k h p m -> k (h p m)"), writes=[WBUF])
        P.dma("sp", wk[:], k.C["wk"].rearrange("p b h e -> p b (h e)"), writes=[WBUF])
        P.dma("sp", wks[:], k.C["wks"].rearrange("p h e -> p (h e)"), writes=[WBUF])

        xt0 = sb("xt0", [128, 4, D], F32)
        xt = [xt0, xt0]
        XT0 = Buf("xt0")
        XT = [XT0, XT0]
        rc0 = sb("rc0", [128, TT], F32)
        rs0 = sb("rs0", [128, TT], F32)
        rc = [rc0, rc0]
        rs = [rs0, rs0]
        RT0 = Buf("rt0")
        RT = [RT0, RT0]
        junk = sb("junk", [128, D], BF16)
        JB = Buf("junk")
        ss = sb("ss", [128, 4], F32)
        rstd = sb("rstd", [128, 4], F32)
        SSB = Buf("ss")
        h = sb("h", [128, 4, D], BF16)
        HB = Buf("h")
        hT = sb("hT", [128, 8, TT], BF16)
        HTB = Buf("hT")
        qbs = [sb("qb%d" % i, [128, TT], BF16) for i in range(2)]
        QBs = [Buf("qb%d" % i) for i in range(2)]
        t1s = [sb("t1%d" % i, [128, TT], F32) for i in range(2)]
        t2s = [sb("t2%d" % i, [128, TT], F32) for i in range(2)]
        T1Bs = [Buf("t1%d" % i) for i in range(2)]
        T2Bs = [Buf("t2%d" % i) for i in range(2)]
        rqT = sb("rqT", [128, 8, TT], BF16)
        RQB = [Buf("rq%d" % i) for i in range(8)]
        rkk = sb("rkk", [128, 4, 512], BF16)
        RKKB = Buf("rkk")
        rv = sb("rv", [128, 4, 512], BF16)
        RVB = Buf("rv")
        rg = sb("rg", [128, 4, 512], BF16)
        RGB = Buf("rg")
        fkf = sb("fkf", [128, 4, 512], F32)
        FKFB = Buf("fkf")
        fkb = h
        FKBB = HB
        fvf = sb("fvf", [128, 4, 512], F32)
        FVFB = Buf("fvf")
        fva = sb("fva", [128, 4, 8 * 65], BF16)
        FVAB = Buf("fva")
        fka = sb("fka", [67, 8, TT], BF16)
        FKAB = Buf("fka")
        fqa = sb("fqa", [67, 8, TT], BF16)
        FQAB = Buf("fqa")
        lf = sb("lf", [128, 4, 8], F32)
        LFB = Buf("lf")
        xb = sb("xb", [128, 4, 8], F32)
        XBB = Buf("xb")
        cS = sb("cS", [128, 4, 8], F32)
        CSB = Buf("cS")
        ngc = sb("ngc", [128, 4, 8], F32)
        NGB = Buf("ngc")
        carry = sb("carry", [1, 8], F32)
        CARB = Buf("carry")
        cT = sb("cT", [8, TT], F32)
        c3 = sb("c3", [8, 3, TT], BF16)
        r1 = sb("r1", [8, TT], F32)
        r2 = cT
        CTB, C3B, R1B = Buf("cT"), Buf("c3"), Buf("r1")
        R2B = CTB

        epsc = sb("epsc", [128, 1], F32)
        P.op("pool", "memset", writes=[SSB], ap=epsc[:], constant=EPS)
        P.op("pool", "memset", writes=[FVAB], ap=fva[:], constant=1.0)
        P.op("pool", "memset", writes=[FKAB], ap=fka[64:67, :, :], constant=1.0)
        P.op("pool", "memset", writes=[CARB], ap=carry[:], constant=0.0)

        def MM(out, lhsT, rhs, start, stop, R, Wb):
            P.op("pe", "matmul", reads=R, writes=[Wb], out=out, lhsT=lhsT, rhs=rhs, start=start, stop=stop)

        seqs = [("prm", t * TT, TT) for t in range(T // TT)]
        seqs += [("past", pt * TT, TT) for pt in range(PAST // TT)]
        seqs += [("smp", T, TS)]
        if "a1_seqs" in DBG:
            seqs = [seqs[i] for i in DBG["a1_seqs"]]
        for ti, (kind, t0, n) in enumerate(seqs):
            s = ti % 2
            blks = blocks(n)
            nb = len(blks)
            pp = min(128, n)
            kcol0 = t0 if kind == "past" else PAST + t0

            def tm(ap2d):
                return ap2d.rearrange("(b p) d -> p b d", p=pp)
            if kind == "past" and t0 == 0:
                P.op("pool", "memset", writes=[CARB], ap=carry[:], constant=0.0)
            if kind == "past":
                P.dma("sp", fkf[:, :, :], tm(k.I["cache_fox_k"][t0:t0 + n, :]), writes=[FKFB])
                P.dma("act", fvf[:, :, :], tm(k.I["cache_fox_v"][t0:t0 + n, :]), writes=[FVFB])
                P.dma("sp", lf[:, :, :], tm(k.I["cache_fox_logf"][t0:t0 + n, :]), writes=[LFB])
            else:
                xs = k.I["x_sample"] if kind == "smp" else k.I["x_prompt"][t0:t0 + n, :]
                P.dma("sp", xt[s][0:pp, 0:nb, :], tm(xs), writes=[XT[s]])
                P.dma("act", rc[s][:, 0:n], k.C["ropec"][:, t0:t0 + n], writes=[RT[s]])
                P.dma("act", rs[s][:, 0:n], k.C["ropes"][:, t0:t0 + n], writes=[RT[s]])
                for b, (o, bs) in enumerate(blks):
                    P.op("act", "activation", reads=[XT[s]], writes=[JB, SSB], out=junk[0:bs, :], in_=xt[s][0:bs, b, :],
                         func=AF.Square, accum_out=ss[0:bs, b:b + 1])
                P.op("act", "activation", reads=[SSB], writes=[SSB], out=rstd[:, 0:nb], in_=ss[:, 0:nb], func=AF.Sqrt,
                     scale=1.0 / D, bias=epsc[:, 0:1])
                P.op("dve", "reciprocal", reads=[SSB], writes=[SSB], out=rstd[:, 0:nb], in_=rstd[:, 0:nb])
                for b, (o, bs) in enumerate(blks):
                    P.op("dve", "scalar_tensor_tensor", reads=[XT[s], SSB, WBUF], writes=[HB],
                         out=h[0:bs, b, :], in0=xt[s][0:bs, b, :], scalar=rstd[0:bs, b:b + 1], in1=g0[0:bs, :],
                         op0=ALU.mult, op1=ALU.mult)
                for kc in range(8):
                    bk = kc % 4
                    for b, (o, bs) in enumerate(blks):
                        MM(bank(bk)[:, o:o + bs], h[0:bs, b, kc * 128:(kc + 1) * 128], identb[0:bs, 0:bs], True, True,
                           [HB], PB[bk])
                    if kc % 2 == 0:
                        P.op("act", "copy", reads=[PB[bk]], writes=[HTB], out=hT[:, kc, 0:n], in_=bank(bk)[:, 0:n])
                    else:
                        P.op("dve", "tensor_copy", reads=[PB[bk]], writes=[HTB], out=hT[:, kc, 0:n], in_=bank(bk)[:, 0:n])

                def proj_fm(bk, c0, m):
                    for kc in range(8):
                        MM(bank(bk)[0:m, 0:n], win[:, kc, c0:c0 + m], hT[:, kc, 0:n], kc == 0, kc == 7,
                           [HTB, WBUF], PB[bk])

                def proj_tm(bk, o, bs, c0, ncols):
                    for kc in range(8):
                        MM(bank(bk)[0:bs, 0:ncols], hT[:, kc, o:o + bs], win[:, kc, c0:c0 + ncols], kc == 0, kc == 7,
                           [HTB, WBUF], PB[bk])

                for i8 in range(8):
                    bk = 4 + (i8 % 2)
                    qb, QB = qbs[i8 % 2], QBs[i8 % 2]
                    t1, t2, T1B, T2B = t1s[i8 % 2], t2s[i8 % 2], T1Bs[i8 % 2], T2Bs[i8 % 2]
                    pbk = 6 if i8 % 2 == 0 else 7
                    sc = 1.0 if i8 < 4 else 128.0 ** -0.5
                    proj_fm(bk, i8 * 128, 128)
                    P.op("act", "activation", reads=[PB[bk]], writes=[QB], out=qb[:, 0:n], in_=bank(bk)[:, 0:n],
                         func=AF.Copy, scale=sc)
                    MM(bank(pbk)[:, 0:n], perm[:, :], qb[:, 0:n], True, True, [QB, WBUF], PB[pbk])
                    P.op("pool", "tensor_tensor", reads=[QB, RT[s]], writes=[T1B], out=t1[:, 0:n], in0=qb[:, 0:n],
                         in1=rc[s][:, 0:n], op=ALU.mult)
                    P.op("dve", "tensor_tensor", reads=[PB[pbk], RT[s]], writes=[T2B], out=t2[:, 0:n], in0=bank(pbk)[:, 0:n],
                         in1=rs[s][:, 0:n], op=ALU.mult)
                    P.op("dve", "tensor_tensor", reads=[T1B, T2B], writes=[RQB[i8]], out=rqT[:, i8, 0:n], in0=t1[:, 0:n],
                         in1=t2[:, 0:n], op=ALU.add)
                    dst = k.S["rqT"][i8, :, t0:t0 + n] if i8 < 4 else k.S["rkT"][i8 - 4, :, t0:t0 + n]
                    P.dma("sp", dst, rqT[:, i8, 0:n], reads=[RQB[i8]], writes=[Buf()])
                for b, (o, bs) in enumerate(blks):
                    bk = b % 2
                    for hh in range(4):
                        MM(bank(bk)[0:bs, hh * 128:(hh + 1) * 128], rqT[:, 4 + hh, o:o + bs], identb[:, :], True, True,
                           [RQB[4 + hh]], PB[bk])
                    wtab = wks[0:bs, :] if kind == "smp" else wk[0:bs, b, :]
                    P.op("dve", "tensor_tensor", reads=[PB[bk], WBUF], writes=[RKKB], out=rkk[0:bs, b, :],
                         in0=bank(bk)[0:bs, :], in1=wtab, op=ALU.mult)
                P.dma("act", tm(k.S["rkk"][t0:t0 + n, :]), rkk[0:pp, 0:nb, :], reads=[RKKB], writes=[Buf()])
                for b, (o, bs) in enumerate(blks):
                    proj_tm(2, o, bs, 1024, 512)
                    P.op("act", "copy", reads=[PB[2]], writes=[RVB], out=rv[0:bs, b, :], in_=bank(2)[0:bs, :])
                    proj_tm(3, o, bs, 1536, 512)
                    P.op("act", "activation", reads=[PB[3]], writes=[RGB], out=rg[0:bs, b, :], in_=bank(3)[0:bs, :],
                         func=AF.Silu)
                    proj_tm(2, o, bs, 2560, 512)
                    P.op("act", "copy", reads=[PB[2]], writes=[FKFB], out=fkf[0:bs, b, :], in_=bank(2)[0:bs, :])
                    proj_tm(3, o, bs, 3072, 512)
                    P.op("dve", "tensor_copy", reads=[PB[3]], writes=[FVFB], out=fvf[0:bs, b, :], in_=bank(3)[0:bs, :])
                    proj_tm(7, o, bs, 3584, 8)
                    P.op("dve", "tensor_tensor", reads=[PB[7], WBUF], writes=[XBB], out=xb[0:bs, b, :],
                         in0=bank(7)[0:bs, 0:8], in1=bfox[0:bs, :], op=ALU.add)
                P.dma("act", tm(k.S["rv"][t0:t0 + n, :]), rv[0:pp, 0:nb, :], reads=[RVB], writes=[Buf()])
                P.dma("act", tm(k.S["rg"][t0:t0 + n, :]), rg[0:pp, 0:nb, :], reads=[RGB], writes=[Buf()])
                P.op("act", "activation", reads=[XBB], writes=[XBB], out=xb[0:pp, 0:nb, :], in_=xb[0:pp, 0:nb, :],
                     func=AF.Exp, scale=-1.0)
                P.op("act", "activation", reads=[XBB], writes=[XBB], out=xb[0:pp, 0:nb, :], in_=xb[0:pp, 0:nb, :],
                     func=AF.Ln, bias=1.0)
                P.op("dve", "tensor_scalar", reads=[XBB], writes=[LFB], out=lf[0:pp, 0:nb, :], in0=xb[0:pp, 0:nb, :],
                     scalar1=-1.0, scalar2=None, op0=ALU.mult)
                ko = k.O["fox_k_sample"] if kind == "smp" else k.O["fox_k_prompt"][t0:t0 + n, :]
                vo = k.O["fox_v_sample"] if kind == "smp" else k.O["fox_v_prompt"][t0:t0 + n, :]
                lo = k.O["fox_logf_sample"] if kind == "smp" else k.O["fox_logf_prompt"][t0:t0 + n, :]
                P.dma("sp", tm(ko), fkf[0:pp, 0:nb, :], reads=[FKFB], writes=[Buf()])
                P.dma("sp", tm(vo), fvf[0:pp, 0:nb, :], reads=[FVFB], writes=[Buf()])
                P.dma("sp", tm(lo), lf[0:pp, 0:nb, :], reads=[LFB], writes=[Buf()])
            P.op("pool", "tensor_copy", reads=[FKFB], writes=[FKBB], out=fkb[0:pp, 0:nb, 0:512], in_=fkf[0:pp, 0:nb, :])
            P.op("pool", "tensor_copy", reads=[FVFB], writes=[FVAB],
                 out=fva[0:pp, 0:nb, :].rearrange("p b (h e) -> p b h e", e=65)[:, :, :, 0:64],
                 in_=fvf[0:pp, 0:nb, :].rearrange("p b (h e) -> p b h e", e=64))
            ktile = (t0 // TT) if kind == "past" else (8 + t0 // TT)
            for hp in range(4):
                P.dma("act", k.S["fva"][ktile, hp, 0:pp, 0:nb, :], fva[0:pp, 0:nb, 130 * hp:130 * hp + 130], reads=[FVAB],
                      writes=[Buf()])
            for hh in range(8):
                bk = hh % 2
                for b, (o, bs) in enumerate(blks):
                    MM(bank(bk)[0:64, o:o + bs], fkb[0:bs, b, hh * 64:(hh + 1) * 64], identb[0:bs, 0:bs], True, True,
                       [FKBB], PB[bk])
                if hh % 2 == 0:
                    P.op("act", "copy", reads=[PB[bk]], writes=[FKAB], out=fka[0:64, hh, 0:n], in_=bank(bk)[0:64, 0:n])
                else:
                    P.op("dve", "tensor_copy", reads=[PB[bk]], writes=[FKAB], out=fka[0:64, hh, 0:n],
                         in_=bank(bk)[0:64, 0:n])
            for hp in range(4):
                P.dma("sp", k.S["fka"][ktile, hp, :, :, 0:n], fka[:, 2 * hp:2 * hp + 2, 0:n], reads=[FKAB], writes=[Buf()])
            for b, (o, bs) in enumerate(blks):
                mm = [(trif[0:bs, 0:bs], lf[0:bs, b, :])]
                for b2 in range(b):
                    mm.append((onesf[0:128, 0:bs], lf[0:128, b2, :]))
                mm.append((onesf[0:1, 0:bs], carry[0:1, :]))
                for i, (l_, r_) in enumerate(mm):
                    MM(bank(7)[0:bs, 8 + b * 8:16 + b * 8], l_, r_, i == 0, i == len(mm) - 1, [LFB, CARB, WBUF], PB[7])
            cview = bank(7)[0:pp, 8:8 + nb * 8].rearrange("p (b h) -> p b h", h=8)
            P.op("dve", "tensor_copy", reads=[PB[7]], writes=[CSB], out=cS[0:pp, 0:nb, :], in_=cview)
            P.op("act", "activation", reads=[PB[7]], writes=[NGB], out=ngc[0:pp, 0:nb, :], in_=cview, func=AF.Copy,
                 scale=-1.0)
            P.dma("act", k.S["negc"][ktile, 0:pp, 0:nb, :], ngc[0:pp, 0:nb, :], reads=[NGB], writes=[Buf()])
            for b, (o, bs) in enumerate(blks):
                MM(bank(7)[0:1, 48:56], onesf[0:bs, 0:1], lf[0:bs, b, :], b == 0, b == nb - 1, [LFB], PB[7])
            P.op("dve", "tensor_tensor", reads=[PB[7], CARB], writes=[CARB], out=carry[:], in0=carry[:],
                 in1=bank(7)[0:1, 48:56], op=ALU.add)
            if kind == "past":
                continue
            for b, (o, bs) in enumerate(blks):
                MM(bank(6)[0:8, o:o + bs], cS[0:bs, b, :], identf[0:bs, 0:bs], True, True, [CSB], PB[6])
            P.op("dve", "tensor_copy", reads=[PB[6]], writes=[CTB], out=cT[:, 0:n], in_=bank(6)[0:8, 0:n])
            P.op("dve", "tensor_copy", reads=[CTB], writes=[C3B], out=c3[:, 0, 0:n], in_=cT[:, 0:n])
            P.op("dve", "tensor_tensor", reads=[CTB, C3B], writes=[R1B], out=r1[:, 0:n], in0=cT[:, 0:n],
                 in1=c3[:, 0, 0:n], op=ALU.subtract)
            P.op("dve", "tensor_copy", reads=[R1B], writes=[C3B], out=c3[:, 1, 0:n], in_=r1[:, 0:n])
            P.op("dve", "tensor_tensor", reads=[R1B, C3B], writes=[R2B], out=r2[:, 0:n], in0=r1[:, 0:n],
                 in1=c3[:, 1, 0:n], op=ALU.subtract)
            P.op("dve", "tensor_copy", reads=[R2B], writes=[C3B], out=c3[:, 2, 0:n], in_=r2[:, 0:n])
            for hh in range(8):
                bk = 4 + (hh % 2)
                proj_fm(bk, 2048 + hh * 64, 64)
                for p_ in range(3):
                    i0 = (hh * 3 + p_) * 3
                    MM(bank(bk)[64:67, 0:n], selc[0:8, i0:i0 + 3], c3[0:8, p_, 0:n], p_ == 0, p_ == 2, [C3B, WBUF], PB[bk])
                P.op("act", "activation", reads=[PB[bk], WBUF], writes=[FQAB], out=fqa[0:67, hh, 0:n],
                     in_=bank(bk)[0:67, 0:n], func=AF.Identity, scale=qscale[0:67, 0:1])
            for hp in range(4):
                P.dma("sp", k.S["fqa"][t0 // TT, hp, :, :, 0:n], fqa[:, 2 * hp:2 * hp + 2, 0:n], reads=[FQAB], writes=[Buf()])
            P.maybe_flush()
        P.flush()


ALL_PHASES = ("W", "A1", "A3", "A2", "C0", "F0", "S", "C1", "F1")


def core_inputs(inp, c):
    f = np.ascontiguousarray
    m = {
        "x_prompt": inp["x_prompt"][c], "x_sample": inp["x_sample"][c],
        "cache_ret_state": inp["cache_ret_state"][0, c],
        "cache_fox_k": inp["cache_fox_k"][0, c].reshape(PAST, 512),
        "cache_fox_v": inp["cache_fox_v"][0, c].reshape(PAST, 512),
        "cache_fox_logf": inp["cache_fox_logf"][0, c],
        "state_s5_re": inp["state_s5_re"][0, c], "state_s5_im": inp["state_s5_im"][0, c],
        "cache_mem_k": inp["cache_mem_k"][:, c].reshape(2, 256, 1024),
        "cache_mem_v": inp["cache_mem_v"][:, c].reshape(2, 256, 1024),
        "mem_prompt": inp["mem_prompt"][c],
        "norm_w": inp["norm_w"], "w_in_ab": inp["w_in_ab"][0], "b_fox_f": inp["b_fox_f"].reshape(1, 8),
        "ret_gn_w": inp["ret_gn_w"].reshape(1, 512), "w_out_ab": inp["w_out_ab"][0], "w_in_c": inp["w_in_c"][0],
        "s5_lambda_re": inp["s5_lambda_re"][0], "s5_lambda_im": inp["s5_lambda_im"][0],
        "s5_log_dt": inp["s5_log_dt"].reshape(1, 64), "s5_b_re": inp["s5_b_re"][0], "s5_b_im": inp["s5_b_im"][0],
        "s5_c_re": inp["s5_c_re"][0], "s5_c_im": inp["s5_c_im"][0], "s5_d": inp["s5_d"].reshape(1, D),
        "w_glu": inp["w_glu"][0], "mem_norm_w": inp["mem_norm_w"],
        "w_xq": inp["w_xq"], "w_xk": inp["w_xk"], "w_xv": inp["w_xv"], "w_xo": inp["w_xo"],
        "w_up": inp["w_up"], "w_down": inp["w_down"],
    }
    m = {k_: f(np.asarray(v, dtype=np.float32)) for k_, v in m.items()}
    for k_, v in get_consts().items():
        m["c_" + k_] = v
    return m


def run(inputs, phases=ALL_PHASES, cores=NCORES):
    inp = {k_: np.asarray(v) for k_, v in inputs.items()}
    kk = build(phases)
    in_maps = [core_inputs(inp, c) for c in range(cores)]
    res = run_bass_kernel_spmd(kk.nc, in_maps, core_ids=list(range(cores)))
    return res.results


def kernel(**inputs):
    r = run(inputs)
    B = NCORES

    def st(name, shape):
        return np.stack([np.asarray(r[c][name], dtype=np.float32).reshape(shape) for c in range(B)])
    y_prompt = st("y_prompt", (T, D))
    y_sample = st("y_sample", (TS, D))
    ret_p = st("ret_state_prompt", (4, 128, 128))[None]
    ret_s = st("ret_state_sample", (4, 128, 128))[None]
    fkp = st("fox_k_prompt", (T, 8, 64))[None]
    fvp = st("fox_v_prompt", (T, 8, 64))[None]
    flp = st("fox_logf_prompt", (T, 8))[None]
    fks = st("fox_k_sample", (TS, 8, 64))[None]
    fvs = st("fox_v_sample", (TS, 8, 64))[None]
    fls = st("fox_logf_sample", (TS, 8))[None]
    s5rp = st("s5_re_prompt", (64, 64))[None]
    s5ip = st("s5_im_prompt", (64, 64))[None]
    s5rs = st("s5_re_sample", (64, 64))[None]
    s5is = st("s5_im_sample", (64, 64))[None]
    mkp = np.stack([np.asarray(r[c]["mem_k_prompt"], dtype=np.float32).reshape(2, 256, 4, 256) for c in range(B)], 1)
    mvp = np.stack([np.asarray(r[c]["mem_v_prompt"], dtype=np.float32).reshape(2, 256, 4, 256) for c in range(B)], 1)
    return (y_prompt, y_sample, ret_p, ret_s, fkp, fvp, flp, fks, fvs, fls, s5rp, s5ip, s5rs, s5is, mkp, mvp)


def phase_A3(k, P, ps, PB, bank, WB, identf, identb, onesf, onesb):
    with contextlib.ExitStack() as st:
        def sb(name, shape, dt):
            return k.sbuf(st, "a3_" + name, shape, dt)
        caus = sb("caus", [128, 128], BF16)
        CW = Buf("a3c")
        P.dma("sp", caus[:], k.C["caus_b"], writes=[CW])
        qas = [sb("qa%d" % i, [67, 2, 2 * TT], BF16) for i in range(2)]
        QABs = [Buf("qa%d" % i) for i in range(2)]
        NKV = 4
        ka = [sb("ka%d" % i, [67, 2, TT], BF16) for i in range(2 * NKV)]
        va = [sb("va%d" % i, [128, 4, 2 * 65], BF16) for i in range(2 * NKV)]
        ng = [sb("ng%d" % i, [128, 4, 8], F32) for i in range(2 * NKV)]
        KVB = [Buf("kk%d" % i) for i in range(2 * NKV)]
        VVB = [Buf("kvv%d" % i) for i in range(2 * NKV)]
        NNB = [Buf("kn%d" % i) for i in range(2 * NKV)]
        pT = [sb("pT%d" % i, [128, TT], BF16) for i in range(4)]
        PTB = [Buf("pT%d" % i) for i in range(4)]
        osbs = [sb("osb%d" % i, [65, TT], F32) for i in range(4)]
        OSBs = [Buf("osb%d" % i) for i in range(4)]
        rrows = [sb("rrow%d" % i, [65, TT], BF16) for i in range(4)]
        RRBs = [Buf("rrow%d" % i) for i in range(4)]
        foT = sb("foT", [64, 4, TT], BF16)
        FOB = Buf("foT")

        def MM(out, lhsT, rhs, start, stop, R, Wb):
            P.op("pe", "matmul", reads=R, writes=[Wb], out=out, lhsT=lhsT, rhs=rhs, start=start, stop=stop)

        supers = []
        for Q in range(T // (2 * TT)):
            subs = [(2 * Q + s_, (2 * Q + s_) * TT, TT) for s_ in range(2)]
            ktl = [(8 + kt, TT, kt) for kt in range(2 * Q + 2)]
            supers.append(("prm", subs, ktl))
        supers.append(("smp", [(16, T, TS)], [(pt, TT, -1) for pt in range(PAST // TT)] + [(24, TS, 16)]))
        if "a3_q" in DBG:
            supers = [supers[i] for i in DBG["a3_q"]]
        groups = [(kind, subs, ktl, hp) for (kind, subs, ktl) in supers for hp in range(4)]
        pti = 0

        def load_kv(gi, kti):
            kind, subs, ktl, hp = groups[gi]
            ktile, nk, kpos = ktl[kti]
            sl = (gi % 2) * NKV + kti % NKV
            pk = min(128, nk)
            nkb = (nk + 127) // 128
            P.dma("sp", ka[sl][:, :, 0:nk], k.S["fka"][ktile, hp, :, :, 0:nk], writes=[KVB[sl]])
            P.dma("pool", va[sl][0:pk, 0:nkb, :], k.S["fva"][ktile, hp, 0:pk, 0:nkb, :], writes=[VVB[sl]])
            P.dma("pool", ng[sl][0:pk, 0:nkb, :], k.S["negc"][ktile, 0:pk, 0:nkb, :], writes=[NNB[sl]])

        def load_q(gi):
            kind, subs, ktl, hp = groups[gi]
            for s_, (qidx, t0, n) in enumerate(subs):
                P.dma("sp", qas[gi % 2][:, :, s_ * TT:s_ * TT + n], k.S["fqa"][qidx, hp, :, :, 0:n], writes=[QABs[gi % 2]])

        def prefetch(gi):
            if gi < len(groups):
                load_q(gi)
                for kti in range(min(NKV, len(groups[gi][2]))):
                    load_kv(gi, kti)
        prefetch(0)
        for gi, (kind, subs, ktl, hp) in enumerate(groups):
            qa, QAB = qas[gi % 2], QABs[gi % 2]
            prefetch(gi + 1)
            steps = []
            for kti, (ktile, nk, kpos) in enumerate(ktl):
                kblks = blocks(nk)
                for hh in range(2):
                    for jb, (jo, js) in enumerate(kblks):
                        for s_, (qidx, t0, n) in enumerate(subs):
                            if kpos > qidx:
                                continue
                            steps.append((kti, hh, jb, jo, js, kpos == qidx, s_, n))
            firsts, lasts = {}, {}
            for i, stp in enumerate(steps):
                a_ = stp[1] * 2 + stp[6]
                firsts.setdefault(a_, i)
                lasts[a_] = i
            LOOK = 2
            info = {}
            for i in range(len(steps) + LOOK):
                if i < len(steps):
                    kti, hh, jb, jo, js, diag, s_, n = steps[i]
                    sl = (gi % 2) * NKV + kti % NKV
                    h8 = 2 * hp + hh
                    c0 = jo if diag else 0
                    sbk = 4 + (pti % 4)
                    pb = pti % 4
                    pti += 1
                    info[i] = (sl, pb, c0)
                    MM(bank(sbk)[0:js, c0:n], ka[sl][0:67, hh, jo:jo + js], qa[0:67, hh, s_ * TT + c0:s_ * TT + n], True, True,
                       [KVB[sl], QAB], PB[sbk])
                    P.op("act", "activation", reads=[PB[sbk], NNB[sl]], writes=[PTB[pb]], out=pT[pb][0:js, c0:n],
                         in_=bank(sbk)[0:js, c0:n], func=AF.Exp, bias=ng[sl][0:js, jb, h8:h8 + 1], scale=1.0)
                    if diag:
                        P.op("pool", "tensor_tensor", reads=[PTB[pb], CW], writes=[PTB[pb]],
                             out=pT[pb][0:js, c0:c0 + js], in0=pT[pb][0:js, c0:c0 + js], in1=caus[0:js, 0:js], op=ALU.mult)
                j = i - LOOK
                if j >= 0:
                    kti, hh, jb, jo, js, diag, s_, n = steps[j]
                    sl, pb, c0 = info[j]
                    a_ = hh * 2 + s_
                    MM(bank(a_)[0:65, c0:n], va[sl][0:js, jb, hh * 65:(hh + 1) * 65], pT[pb][0:js, c0:n], firsts[a_] == j,
                       lasts[a_] == j, [VVB[sl], PTB[pb]], PB[a_])
                    if (j + 1 == len(steps) or steps[j + 1][0] != kti) and kti + NKV < len(ktl):
                        load_kv(gi, kti + NKV)
            accs = [(hh, s_, subs[s_][1], subs[s_][2]) for hh in range(2) for s_ in range(len(subs))]
            for (hh, s_, t0, n) in accs:
                a_ = hh * 2 + s_
                P.op("act", "copy", reads=[PB[a_]], writes=[OSBs[a_]], out=osbs[a_][0:65, 0:n], in_=bank(a_)[0:65, 0:n])
            for (hh, s_, t0, n) in accs:
                a_ = hh * 2 + s_
                P.op("dve", "reciprocal", reads=[OSBs[a_]], writes=[RRBs[a_]], out=rrows[a_][64:65, 0:n],
                     in_=osbs[a_][64:65, 0:n])
            for (hh, s_, t0, n) in accs:
                a_ = hh * 2 + s_
                MM(bank(4 + a_)[0:64, 0:n], onesb[64:65, 0:64], rrows[a_][64:65, 0:n], True, True, [RRBs[a_]], PB[4 + a_])
            for (hh, s_, t0, n) in accs:
                a_ = hh * 2 + s_
                P.op("dve", "tensor_tensor", reads=[PB[4 + a_], OSBs[a_]], writes=[FOB], out=foT[0:64, a_, 0:n],
                     in0=osbs[a_][0:64, 0:n], in1=bank(4 + a_)[0:64, 0:n], op=ALU.mult)
            for (hh, s_, t0, n) in accs:
                a_ = hh * 2 + s_
                P.dma("sp", k.S["foT"][2 * hp + hh, :, t0:t0 + n], foT[:, a_, 0:n], reads=[FOB], writes=[Buf()])
            P.maybe_flush()
        P.flush()


def rstd_ops(P, out, in_, scale, epsc, R, W):
    P.op("act", "activation", reads=R, writes=W, out=out, in_=in_, func=AF.Sqrt, scale=scale, bias=epsc)
    P.op("dve", "reciprocal", reads=W, writes=W, out=out, in_=out)


def post_norm_residual(P, pso, PSB, xt_blk, XTB, g_bc, GB, junk, JB, ssv, SSB, epsc, tmp, TMPB, bs):
    P.op("act", "activation", reads=PSB, writes=[JB, SSB], out=junk[0:bs, :], in_=pso, func=AF.Square,
         accum_out=ssv[0:bs, 0:1])
    rstd_ops(P, ssv[0:bs, 1:2], ssv[0:bs, 0:1], 1.0 / D, epsc[0:bs, 0:1], [SSB], [SSB])
    P.op("dve", "scalar_tensor_tensor", reads=PSB + [SSB, GB], writes=[TMPB], out=tmp[0:bs, :], in0=pso,
         scalar=ssv[0:bs, 1:2], in1=g_bc[0:bs, :], op0=ALU.mult, op1=ALU.mult)
    P.op("pool", "tensor_tensor", reads=[TMPB, XTB], writes=[XTB], out=xt_blk, in0=xt_blk, in1=tmp[0:bs, :], op=ALU.add)


def pre_norm_elem(P, xt, XTB, g_bc, GB, junk, JB, ssv, SSB, epsc, h, HB, blks):
    for b, (o, bs) in enumerate(blks):
        P.op("act", "activation", reads=[XTB], writes=[JB, SSB], out=junk[0:bs, :], in_=xt[0:bs, b, :], func=AF.Square,
             accum_out=ssv[0:bs, 2 * b:2 * b + 1])
        rstd_ops(P, ssv[0:bs, 2 * b + 1:2 * b + 2], ssv[0:bs, 2 * b:2 * b + 1], 1.0 / D, epsc[0:bs, 0:1], [SSB], [SSB])
        P.op("dve", "scalar_tensor_tensor", reads=[XTB, SSB, GB], writes=[HB], out=h[0:bs, b, :], in0=xt[0:bs, b, :],
             scalar=ssv[0:bs, 2 * b + 1:2 * b + 2], in1=g_bc[0:bs, :], op0=ALU.mult, op1=ALU.mult)


def pre_norm_T(P, bank, PB, banks, h, HB, hT, HTB, identb, blks, n):
    for kc in range(8):
        bk = banks[kc % len(banks)]
        for b, (o, bs) in enumerate(blks):
            P.op("pe", "matmul", reads=[HB], writes=[PB[bk]], out=bank(bk)[:, o:o + bs],
                 lhsT=h[0:bs, b, kc * 128:(kc + 1) * 128], rhs=identb[0:bs, 0:bs], start=True, stop=True)
        if kc % 2 == 0:
            P.op("act", "copy", reads=[PB[bk]], writes=[HTB], out=hT[:, kc, 0:n], in_=bank(bk)[:, 0:n])
        else:
            P.op("dve", "tensor_copy", reads=[PB[bk]], writes=[HTB], out=hT[:, kc, 0:n], in_=bank(bk)[:, 0:n])


def pre_norm_hT(P, bank, PB, banks, xt, XTB, g_bc, GB, junk, JB, ssv, SSB, epsc, h, HB, hT, HTB, identb, blks, n):
    pre_norm_elem(P, xt, XTB, g_bc, GB, junk, JB, ssv, SSB, epsc, h, HB, blks)
    pre_norm_T(P, bank, PB, banks, h, HB, hT, HTB, identb, blks, n)


def x_src(k, first_layer_input, kind, t0, n):
    if first_layer_input:
        return k.I["x_sample"] if kind == "smp" else k.I["x_prompt"][t0:t0 + n, :]
    return k.S["x"][t0:t0 + n, :]


def phase_A2(k, P, ps, PB, bank, WB, identf, identb, onesf, onesb):
    lg = np.log(1.0 - 2.0 ** (-5.0 - np.arange(4, dtype=np.float64)))
    with contextlib.ExitStack() as st:
        def sb(name, shape, dt):
            return k.sbuf(st, "a2_" + name, shape, dt)
        rmask = sb("rmask", [128, 4, 512], F32)
        wq = sb("wq", [128, 4, 512], F32)
        gnw = sb("gnw", [128, 512], F32)
        g1 = sb("g1", [128, D], F32)
        wor = sb("wor", [128, 4, D], BF16)
        wof = sb("wof", [64, 8, D], BF16)
        epsc = sb("epsc", [128, 1], F32)
        CW = Buf("a2c")
        P.dma("sp", rmask[:], k.C["rmask"], writes=[CW])
        P.dma("sp", wq[:], k.C["wq"].rearrange("p b h e -> p b (h e)"), writes=[CW])
        P.dma("sp", gnw[:], k.I["ret_gn_w"][0:1, :].broadcast_to([128, 512]), writes=[CW])
        P.dma("sp", g1[:], k.I["norm_w"][0, 1:2, :].broadcast_to([128, D]), writes=[CW])
        P.dma("act", wor[:], WB["w_out_ab"][0:512, :].rearrange("(kc p) n -> p kc n", p=128), writes=[CW])
        P.dma("act", wof[:], WB["w_out_ab"][512:1024, :].rearrange("(h p) n -> p h n", p=64), writes=[CW])
        P.op("pool", "memset", writes=[CW], ap=epsc[:], constant=EPS)
        Sf = sb("Sf", [128, 4, 128], F32)
        Sb = sb("Sb", [128, 4, 128], BF16)
        SFB, SBB = Buf("Sf"), Buf("Sb")
        qTs = [sb("qT%d" % i, [128, 4, TT], BF16) for i in range(2)]
        kTs = [sb("kT%d" % i, [128, 4, TT], BF16) for i in range(2)]
        rkks = [sb("rkk%d" % i, [128, 4, 512], BF16) for i in range(2)]
        rvs = [sb("rv%d" % i, [128, 4, 512], BF16) for i in range(2)]
        rgs = [sb("rg%d" % i, [128, 4, 512], BF16) for i in range(2)]
        foTs = [sb("foT%d" % i, [64, 8, TT], BF16) for i in range(2)]
        xts = [sb("xt%d" % i, [128, 4, D], F32) for i in range(2)]
        INBs = [Buf("a2in%d" % i) for i in range(2)]
        XTBs = [Buf("a2x%d" % i) for i in range(2)]
        PT = sb("PT", [128, 4, 4, TT], BF16)
        PTB = [Buf("PT%d" % i) for i in range(4)]
        ro = sb("ro", [128, 512], F32)
        t1 = sb("t1", [128, 512], F32)
        sq = sb("sq", [128, 512], F32)
        ROB, T1B, SQB = Buf("ro"), Buf("t1"), Buf("sq")
        st8 = sb("st8", [128, 24], F32)
        STB = Buf("st8")
        mixr = sb("mixr", [128, 512], BF16)
        MXB = Buf("mixr")
        roT = sb("roT", [128, 4, TT], BF16)
        ROTB = Buf("roT")
        junk = sb("junk", [128, D], BF16)
        JB = Buf("junk")
        ssv = sb("ssv", [128, 8], F32)
        SSB = Buf("ssv")
        tmp = sb("tmp", [128, D], F32)
        TMPB = Buf("tmp")

        def MM(out, lhsT, rhs, start, stop, R, Wb):
            P.op("pe", "matmul", reads=R, writes=[Wb], out=out, lhsT=lhsT, rhs=rhs, start=start, stop=stop)

        seqs = [("prm", t * TT, TT) for t in range(T // TT)] + [("smp", T, TS)]
        if "a2_seqs" in DBG:
            seqs = [seqs[i] for i in DBG["a2_seqs"]]
        P.op("pool", "memset", writes=[SFB], ap=Sf[:], constant=0.0)
        P.op("pool", "memset", writes=[SBB], ap=Sb[:], constant=0.0)
        def load_in(ti):
            kind, t0, n = seqs[ti]
            pp = min(128, n)
            nb = len(blocks(n))
            j = ti % 2

            def tm_(ap2d):
                return ap2d.rearrange("(b p) d -> p b d", p=pp)
            P.dma("sp", qTs[j][:, :, 0:n], k.S["rqT"][:, :, t0:t0 + n].rearrange("h d t -> d h t"), writes=[INBs[j]])
            P.dma("act", kTs[j][:, :, 0:n], k.S["rkT"][:, :, t0:t0 + n].rearrange("h d t -> d h t"), writes=[INBs[j]])
            P.dma("sp", rkks[j][0:pp, 0:nb, :], tm_(k.S["rkk"][t0:t0 + n, :]), writes=[INBs[j]])
            P.dma("act", rvs[j][0:pp, 0:nb, :], tm_(k.S["rv"][t0:t0 + n, :]), writes=[INBs[j]])
            P.dma("sp", rgs[j][0:pp, 0:nb, :], tm_(k.S["rg"][t0:t0 + n, :]), writes=[INBs[j]])
            P.dma("act", foTs[j][:, :, 0:n], k.S["foT"][:, :, t0:t0 + n].rearrange("h r t -> r h t"), writes=[INBs[j]])
            P.dma("sp", xts[j][0:pp, 0:nb, :], tm_(x_src(k, True, kind, t0, n)), writes=[XTBs[j]])
        load_in(0)
        for ti, (kind, t0, n) in enumerate(seqs):
            blks = blocks(n)
            nb = len(blks)
            pp = min(128, n)
            dec = np.exp(lg * (512.0 if kind == "prm" else 32.0))
            j_ = ti % 2
            qT, kT, rkk, rv, rg, foT, xt = qTs[j_], kTs[j_], rkks[j_], rvs[j_], rgs[j_], foTs[j_], xts[j_]
            INB, XTB = INBs[j_], XTBs[j_]
            if ti + 1 < len(seqs):
                load_in(ti + 1)
            if kind == "smp":
                P.dma("sp", Sf[:], k.I["cache_ret_state"].rearrange("h d e -> d h e"), writes=[SFB])
                P.op("act", "copy", reads=[SFB], writes=[SBB], out=Sb[:], in_=Sf[:])

            def tm(ap2d):
                return ap2d.rearrange("(b p) d -> p b d", p=pp)
            for hh in range(4):
                for jb, (jo, js) in enumerate(blks):
                    bk = (hh * 4 + jb) % 2
                    MM(bank(bk)[0:js, jo:n], kT[:, hh, jo:jo + js], qT[:, hh, jo:n], True, True, [INB], PB[bk])
                    P.op("dve", "tensor_tensor", reads=[PB[bk], CW], writes=[PTB[hh]], out=PT[0:js, hh, jb, jo:n],
                         in0=bank(bk)[0:js, jo:n], in1=rmask[0:js, hh, 0:n - jo], op=ALU.mult)
            for ib, (io, is_) in enumerate(blks):
                for hh in range(4):
                    for jb in range(ib + 1):
                        jo, js = blks[jb]
                        MM(bank(2)[0:is_, hh * 128:(hh + 1) * 128], PT[0:js, hh, jb, io:io + is_],
                           rv[0:js, jb, hh * 128:(hh + 1) * 128], jb == 0, jb == ib, [PTB[hh], INB], PB[2])
                    MM(bank(3)[0:is_, hh * 128:(hh + 1) * 128], qT[:, hh, io:io + is_], Sb[:, hh, :], True, True,
                       [INB, SBB], PB[3])
                P.op("dve", "tensor_tensor", reads=[PB[3], CW], writes=[T1B], out=t1[0:is_, :], in0=bank(3)[0:is_, :],
                     in1=wq[0:is_, ib, :], op=ALU.mult)
                P.op("dve", "tensor_tensor", reads=[PB[2], T1B], writes=[ROB], out=ro[0:is_, :], in0=bank(2)[0:is_, :],
                     in1=t1[0:is_, :], op=ALU.add)
                P.op("act", "activation", reads=[ROB], writes=[SQB], out=sq[0:is_, :], in_=ro[0:is_, :], func=AF.Square)
                ro3 = ro[0:is_, :].rearrange("p (h e) -> p h e", e=128)
                sq3 = sq[0:is_, :].rearrange("p (h e) -> p h e", e=128)
                P.op("dve", "tensor_reduce", reads=[ROB], writes=[STB], out=st8[0:is_, 0:4], in_=ro3, axis=AX.X, op=ALU.add)
                P.op("dve", "tensor_reduce", reads=[SQB], writes=[STB], out=st8[0:is_, 4:8], in_=sq3, axis=AX.X, op=ALU.add)
                P.op("dve", "tensor_scalar", reads=[STB], writes=[STB], out=st8[0:is_, 8:12], in0=st8[0:is_, 0:4],
                     scalar1=1.0 / 128, scalar2=None, op0=ALU.mult)
                P.op("dve", "tensor_tensor", reads=[STB], writes=[STB], out=st8[0:is_, 12:16], in0=st8[0:is_, 8:12],
                     in1=st8[0:is_, 8:12], op=ALU.mult)
                P.op("dve", "scalar_tensor_tensor", reads=[STB], writes=[STB], out=st8[0:is_, 16:20], in0=st8[0:is_, 4:8],
                     scalar=1.0 / 128, in1=st8[0:is_, 12:16], op0=ALU.mult, op1=ALU.subtract)
                rstd_ops(P, st8[0:is_, 20:24], st8[0:is_, 16:20], 1.0, epsc[0:is_, 0:1], [STB], [STB])
                t13 = t1[0:is_, :].rearrange("p (h e) -> p h e", e=128)
                P.op("dve", "tensor_tensor", reads=[ROB, STB], writes=[T1B], out=t13, in0=ro3,
                     in1=st8[0:is_, 8:12].unsqueeze(2).broadcast_to([is_, 4, 128]), op=ALU.subtract)
                P.op("dve", "tensor_tensor", reads=[T1B, STB], writes=[T1B], out=t13, in0=t13,
                     in1=st8[0:is_, 20:24].unsqueeze(2).broadcast_to([is_, 4, 128]), op=ALU.mult)
                P.op("pool", "tensor_tensor", reads=[T1B, CW], writes=[T1B], out=t1[0:is_, :], in0=t1[0:is_, :],
                     in1=gnw[0:is_, :], op=ALU.mult)
                P.op("pool", "tensor_tensor", reads=[T1B, INB], writes=[MXB], out=mixr[0:is_, :], in0=t1[0:is_, :],
                     in1=rg[0:is_, ib, :], op=ALU.mult)
                for hh in range(4):
                    MM(bank(4 + hh)[:, io:io + is_], mixr[0:is_, hh * 128:(hh + 1) * 128], identb[0:is_, 0:is_], True, True,
                       [MXB], PB[4 + hh])
            for hh in range(4):
                if hh % 2 == 0:
                    P.op("act", "copy", reads=[PB[4 + hh]], writes=[ROTB], out=roT[:, hh, 0:n], in_=bank(4 + hh)[:, 0:n])
                else:
                    P.op("dve", "tensor_copy", reads=[PB[4 + hh]], writes=[ROTB], out=roT[:, hh, 0:n],
                         in_=bank(4 + hh)[:, 0:n])
            for hh in range(4):
                for jb, (jo, js) in enumerate(blks):
                    MM(bank(2)[:, hh * 128:(hh + 1) * 128], rkk[0:js, jb, hh * 128:(hh + 1) * 128],
                       rv[0:js, jb, hh * 128:(hh + 1) * 128], jb == 0, jb == nb - 1, [INB], PB[2])
            for hh in range(4):
                P.op("dve", "scalar_tensor_tensor", reads=[PB[2], SFB], writes=[SFB], out=Sf[:, hh, :], in0=Sf[:, hh, :],
                     scalar=float(dec[hh]), in1=bank(2)[:, hh * 128:(hh + 1) * 128], op0=ALU.mult, op1=ALU.add)
            P.op("act", "copy", reads=[SFB], writes=[SBB], out=Sb[:], in_=Sf[:])
            last_of_kind = (kind == "smp") or (t0 + n == T)
            if last_of_kind:
                dst = k.O["ret_state_sample"] if kind == "smp" else k.O["ret_state_prompt"]
                P.dma("sp", dst.rearrange("h d e -> d h e"), Sf[:], reads=[SFB], writes=[Buf()])
            for ib, (io, is_) in enumerate(blks):
                for half in range(2):
                    bk = 2 * (ib % 2) + half
                    for i in range(12):
                        if i < 4:
                            l_, r_ = roT[:, i, io:io + is_], wor[:, i, half * 512:(half + 1) * 512]
                        else:
                            l_, r_ = foT[0:64, i - 4, io:io + is_], wof[0:64, i - 4, half * 512:(half + 1) * 512]
                        MM(bank(bk)[0:is_, :], l_, r_, i == 0, i == 11, [ROTB, INB, CW], PB[bk])
                b0 = 2 * (ib % 2)
                pso = ps[0:is_, b0 * 512:b0 * 512 + 1024]
                post_norm_residual(P, pso, [PB[b0], PB[b0 + 1]], xt[0:is_, ib, :], XTB, g1, CW, junk, JB, ssv, SSB, epsc,
                                   tmp, TMPB, is_)
            P.dma("sp", tm(k.S["x"][t0:t0 + n, :]), xt[0:pp, 0:nb, :], reads=[XTB], writes=[Buf()])
            P.maybe_flush()
        P.flush()


def phase_C(k, P, ps, PB, bank, WB, identf, identb, onesf, onesb, l):
    with contextlib.ExitStack() as st:
        def sb(name, shape, dt):
            return k.sbuf(st, "c%d_" % l + name, shape, dt)
        wxq = sb("wxq", [128, 8, D], BF16)
        wxo = sb("wxo", [128, 8, D], BF16)
        wkv = sb("wkv", [128, 8, D], BF16)
        g2 = sb("g2", [128, D], F32)
        g3 = sb("g3", [128, D], F32)
        gm = sb("gm", [128, D], F32)
        epsc = sb("epsc", [128, 1], F32)
        CW = Buf("cw")
        WKVB = Buf("wkv")
        P.dma("sp", wxq[:], WB["w_xq%d" % l].rearrange("(kc p) n -> p kc n", p=128), writes=[CW])
        P.dma("act", wxo[:], WB["w_xo%d" % l].rearrange("(kc p) n -> p kc n", p=128), writes=[CW])
        P.dma("sp", g2[:], k.I["norm_w"][l, 2:3, :].broadcast_to([128, D]), writes=[CW])
        P.dma("sp", g3[:], k.I["norm_w"][l, 3:4, :].broadcast_to([128, D]), writes=[CW])
        P.dma("sp", gm[:], k.I["mem_norm_w"][l:l + 1, :].broadcast_to([128, D]), writes=[CW])
        P.op("pool", "memset", writes=[CW], ap=epsc[:], constant=EPS)
        xts = [sb("xt%d" % i, [128, 4, D], F32) for i in range(2)]
        XTBs = [Buf("xt%d" % i) for i in range(2)]
        xt, XTB = xts[0], XTBs[0]
        junk2 = sb("junk2", [128, D], BF16)
        JB2 = Buf("junk2")
        ssv2 = sb("ssv2", [128, 8], F32)
        SSB2 = Buf("ssv2")
        h = sb("h", [128, 4, D], BF16)
        HB = Buf("h")
        hT = sb("hT", [128, 8, TT], BF16)
        HTB = Buf("hT")
        junk = sb("junk", [128, D], BF16)
        JB = Buf("junk")
        ssv = sb("ssv", [128, 8], F32)
        SSB = Buf("ssv")
        tmp = sb("tmp", [128, D], F32)
        TMPB = Buf("tmp")
        mkf = sb("mkf", [128, 2, D], F32)
        MKFB = Buf("mkf")
        mkb = sb("mkb", [128, 2, D], BF16)
        MKBB = Buf("mkb")
        mkT = sb("mkT", [128, 8, 256], BF16)
        MKTB = Buf("mkT")
        mvb = sb("mvb", [128, 2, D], BF16)
        MVBB = Buf("mvb")
        qT = sb("qT", [128, 8, TT], BF16)
        QTB = Buf("qT")
        oT = sb("oT", [128, 8, TT], BF16)
        OTB = Buf("oT")
        pT = [sb("pT%d" % i, [128, TT], BF16) for i in range(2)]
        PTB = [Buf("pT%d" % i) for i in range(2)]
        rec = sb("rec", [128, TT], F32)
        RECB = Buf("rec")

        def MM(out, lhsT, rhs, start, stop, R, Wb):
            P.op("pe", "matmul", reads=R, writes=[Wb], out=out, lhsT=lhsT, rhs=rhs, start=start, stop=stop)

        mblks = [(0, 128), (128, 128)]

        def kv_finish():
            P.op("pool", "tensor_copy", reads=[MKFB], writes=[MKBB], out=mkb[:], in_=mkf[:])
            for c8 in range(8):
                bk = c8 % 2
                for mb in range(2):
                    MM(bank(bk)[:, mb * 128:(mb + 1) * 128], mkb[:, mb, c8 * 128:(c8 + 1) * 128], identb[:, :], True, True,
                       [MKBB], PB[bk])
                if c8 % 2 == 0:
                    P.op("act", "copy", reads=[PB[bk]], writes=[MKTB], out=mkT[:, c8, :], in_=bank(bk)[:, 0:256])
                else:
                    P.op("dve", "tensor_copy", reads=[PB[bk]], writes=[MKTB], out=mkT[:, c8, :], in_=bank(bk)[:, 0:256])

        def setup_prompt_mem():
            P.dma("sp", xt[:, 0:2, :], k.I["mem_prompt"].rearrange("(b p) d -> p b d", p=128), writes=[XTB])
            pre_norm_hT(P, bank, PB, [0, 1, 2, 3], xt, XTB, gm, CW, junk, JB, ssv, SSB, epsc, h, HB, hT, HTB, identb,
                        mblks, 256)
            for which in ("k", "v"):
                P.dma("sp", wkv[:], WB["w_x%s%d" % (which, l)].rearrange("(kc p) n -> p kc n", p=128), writes=[WKVB])
                for mb in range(2):
                    for half in range(2):
                        bk = 4 + half
                        for kc in range(8):
                            MM(bank(bk)[:, :], hT[:, kc, mb * 128:(mb + 1) * 128], wkv[:, kc, half * 512:(half + 1) * 512],
                               kc == 0, kc == 7, [HTB, WKVB], PB[bk])
                        P.op("act", "copy", reads=[PB[bk]], writes=[MKFB], out=mkf[:, mb, half * 512:(half + 1) * 512],
                             in_=bank(bk)[:, :])
                dst = k.O["mem_k_prompt" if which == "k" else "mem_v_prompt"][l]
                P.dma("sp", dst.rearrange("(b p) d -> p b d", p=128), mkf[:], reads=[MKFB], writes=[Buf()])
                if which == "k":
                    kv_finish()
                else:
                    P.op("pool", "tensor_copy", reads=[MKFB], writes=[MVBB], out=mvb[:], in_=mkf[:])

        def setup_sample_mem():
            P.dma("sp", mkf[:], k.I["cache_mem_k"][l].rearrange("(b p) d -> p b d", p=128), writes=[MKFB])
            kv_finish()
            P.dma("sp", mkf[:], k.I["cache_mem_v"][l].rearrange("(b p) d -> p b d", p=128), writes=[MKFB])
            P.op("pool", "tensor_copy", reads=[MKFB], writes=[MVBB], out=mvb[:], in_=mkf[:])

        seqs = [("prm", t * TT, TT) for t in range(T // TT)] + [("smp", T, TS)]
        if "c_seqs" in DBG:
            seqs = [seqs[i] for i in DBG["c_seqs"]]
        def tile_geom(ti):
            kind, t0, n = seqs[ti]
            return kind, t0, n, blocks(n), min(128, n)

        def load_x(ti):
            kind, t0, n, blks, pp = tile_geom(ti)
            P.dma("sp", xts[ti % 2][0:pp, 0:len(blks), :], k.S["x"][t0:t0 + n, :].rearrange("(b p) d -> p b d", p=pp),
                  writes=[XTBs[ti % 2]])

        def prenorm_e(ti):
            kind, t0, n, blks, pp = tile_geom(ti)
            pre_norm_elem(P, xts[ti % 2], XTBs[ti % 2], g2, CW, junk2, JB2, ssv2, SSB2, epsc, h, HB, blks)

        def prenorm_t(ti):
            kind, t0, n, blks, pp = tile_geom(ti)
            pre_norm_T(P, bank, PB, [0, 1, 2, 3], h, HB, hT, HTB, identb, blks, n)
        if seqs[0][0] == "prm":
            setup_prompt_mem()
        load_x(0)
        prenorm_e(0)
        prenorm_t(0)
        for ti, (kind, t0, n) in enumerate(seqs):
            blks = blocks(n)
            nb = len(blks)
            pp = min(128, n)
            xt, XTB = xts[ti % 2], XTBs[ti % 2]
            if ti + 1 < len(seqs):
                load_x(ti + 1)
            if kind == "smp":
                setup_sample_mem()

            def tm(ap2d):
                return ap2d.rearrange("(b p) d -> p b d", p=pp)
            for mc in range(8):
                bk = 4 + mc % 2
                for kc in range(8):
                    MM(bank(bk)[:, 0:n], wxq[:, kc, mc * 128:(mc + 1) * 128], hT[:, kc, 0:n], kc == 0, kc == 7, [HTB, CW], PB[bk])
                if mc % 2 == 0:
                    P.op("act", "copy", reads=[PB[bk]], writes=[QTB], out=qT[:, mc, 0:n], in_=bank(bk)[:, 0:n])
                else:
                    P.op("dve", "tensor_copy", reads=[PB[bk]], writes=[QTB], out=qT[:, mc, 0:n], in_=bank(bk)[:, 0:n])
            for hh in range(4):
                for mb in range(2):
                    for dc in range(2):
                        MM(bank(mb)[:, 0:n], mkT[:, hh * 2 + dc, mb * 128:(mb + 1) * 128], qT[:, hh * 2 + dc, 0:n],
                           dc == 0, dc == 1, [MKTB, QTB], PB[mb])
                    P.op("act", "activation", reads=[PB[mb]], writes=[PTB[mb]], out=pT[mb][:, 0:n], in_=bank(mb)[:, 0:n],
                         func=AF.Exp, scale=1.0 / 16.0)
                for mb in range(2):
                    MM(bank(2)[:, 0:n], onesb[:, :], pT[mb][:, 0:n], mb == 0, mb == 1, [PTB[mb]], PB[2])
                P.op("dve", "reciprocal", reads=[PB[2]], writes=[RECB], out=rec[:, 0:n], in_=bank(2)[:, 0:n])
                for dc in range(2):
                    bk = 3 if dc == 0 else 6
                    for mb in range(2):
                        MM(bank(bk)[:, 0:n], mvb[:, mb, hh * 256 + dc * 128:hh * 256 + dc * 128 + 128], pT[mb][:, 0:n],
                           mb == 0, mb == 1, [MVBB, PTB[mb]], PB[bk])
                    P.op("dve", "tensor_tensor", reads=[PB[bk], RECB], writes=[OTB], out=oT[:, hh * 2 + dc, 0:n],
                         in0=bank(bk)[:, 0:n], in1=rec[:, 0:n], op=ALU.mult)
            for ib, (io, is_) in enumerate(blks):
                b0 = 4 if ib % 2 == 0 else 6
                if b0 == 6:
                    b0 = 0
                for half in range(2):
                    bk = b0 + half
                    for kc in range(8):
                        MM(bank(bk)[0:is_, :], oT[:, kc, io:io + is_], wxo[:, kc, half * 512:(half + 1) * 512], kc == 0, kc == 7,
                           [OTB, CW], PB[bk])
                if ib == 0 and ti + 1 < len(seqs):
                    prenorm_e(ti + 1)
                pso = ps[0:is_, b0 * 512:b0 * 512 + 1024]
                post_norm_residual(P, pso, [PB[b0], PB[b0 + 1]], xt[0:is_, ib, :], XTB, g3, CW, junk, JB, ssv, SSB, epsc,
                                   tmp, TMPB, is_)
            if ti + 1 < len(seqs):
                prenorm_t(ti + 1)
            P.dma("sp", tm(k.S["x"][t0:t0 + n, :]), xt[0:pp, 0:nb, :], reads=[XTB], writes=[Buf()])
            P.maybe_flush()
        P.flush()


def phase_F(k, P, ps, PB, bank, WB, identf, identb, onesf, onesb, l, final):
    with contextlib.ExitStack() as st:
        def sb(name, shape, dt):
            return k.sbuf(st, "f%d_" % l + name, shape, dt)
        g4 = sb("g4", [128, D], F32)
        g5 = sb("g5", [128, D], F32)
        epsc = sb("epsc", [128, 1], F32)
        CW = Buf("cw")
        P.dma("sp", g4[:], k.I["norm_w"][l, 4:5, :].broadcast_to([128, D]), writes=[CW])
        P.dma("sp", g5[:], k.I["norm_w"][l, 5:6, :].broadcast_to([128, D]), writes=[CW])
        P.op("pool", "memset", writes=[CW], ap=epsc[:], constant=EPS)
        up = [sb("up%d" % i, [128, 8, 1024], BF16) for i in range(2)]
        dn = [sb("dn%d" % i, [128, 8, 1024], BF16) for i in range(2)]
        UPB = [Buf("up%d" % i) for i in range(2)]
        DNB = [Buf("dn%d" % i) for i in range(2)]
        xts = [sb("xt%d" % i, [128, 4, D], F32) for i in range(2)]
        XTBs = [Buf("xt%d" % i) for i in range(2)]
        junk2 = sb("junk2", [128, D], BF16)
        JB2 = Buf("junk2")
        ssv2 = sb("ssv2", [128, 8], F32)
        SSB2 = Buf("ssv2")
        h = sb("h", [128, 4, D], BF16)
        HB = Buf("h")
        hT = sb("hT", [128, 8, TT], BF16)
        HTB = Buf("hT")
        junk = sb("junk", [128, D], BF16)
        JB = Buf("junk")
        ssv = sb("ssv", [128, 8], F32)
        SSB = Buf("ssv")
        tmp = sb("tmp", [128, D], F32)
        TMPB = Buf("tmp")
        aT = sb("aT", [128, 32, TT], BF16)
        ATB = Buf("aT")
        sqv = [sb("sqv%d" % i, [128, TT], F32) for i in range(2)]
        SQB = [Buf("sqv%d" % i) for i in range(2)]
        wup = WB["w_up%d" % l].rearrange("(kc p) n -> p kc n", p=128)
        wdn = WB["w_down%d" % l].rearrange("(fc p) n -> p fc n", p=128)

        def MM(out, lhsT, rhs, start, stop, R, Wb):
            P.op("pe", "matmul", reads=R, writes=[Wb], out=out, lhsT=lhsT, rhs=rhs, start=start, stop=stop)

        seqs = [("prm", t * TT, TT) for t in range(T // TT)] + [("smp", T, TS)]
        if "f_seqs" in DBG:
            seqs = [seqs[i] for i in DBG["f_seqs"]]
        ui = 0
        di = 0
        fi = 0

        def tile_geom(ti):
            kind, t0, n = seqs[ti]
            return kind, t0, n, blocks(n), min(128, n)

        def load_x(ti):
            kind, t0, n, blks, pp = tile_geom(ti)
            P.dma("sp", xts[ti % 2][0:pp, 0:len(blks), :], k.S["x"][t0:t0 + n, :].rearrange("(b p) d -> p b d", p=pp),
                  writes=[XTBs[ti % 2]])

        def prenorm_e(ti):
            kind, t0, n, blks, pp = tile_geom(ti)
            pre_norm_elem(P, xts[ti % 2], XTBs[ti % 2], g4, CW, junk2, JB2, ssv2, SSB2, epsc, h, HB, blks)

        def prenorm_t(ti):
            kind, t0, n, blks, pp = tile_geom(ti)
            pre_norm_T(P, bank, PB, [0, 1, 2, 3], h, HB, hT, HTB, identb, blks, n)
        def load_up(p):
            if p >= 4 * len(seqs):
                return
            q_ = p % 4
            sl_ = p % 2
            P.dma("sp", up[sl_][:, 0:4, :], wup[:, 0:4, q_ * 1024:(q_ + 1) * 1024], writes=[UPB[sl_]])
            P.dma("pool", up[sl_][:, 4:8, :], wup[:, 4:8, q_ * 1024:(q_ + 1) * 1024], writes=[UPB[sl_]])
        def load_dn(q_):
            sl_ = q_ % 2
            P.dma("sp", dn[sl_][:, 0:4, :], wdn[:, q_ * 8:q_ * 8 + 4, :], writes=[DNB[sl_]])
            P.dma("pool", dn[sl_][:, 4:8, :], wdn[:, q_ * 8 + 4:q_ * 8 + 8, :], writes=[DNB[sl_]])
        load_x(0)
        load_up(0)
        load_up(1)
        prenorm_e(0)
        prenorm_t(0)
        for ti, (kind, t0, n) in enumerate(seqs):
            blks = blocks(n)
            nb = len(blks)
            pp = min(128, n)
            xt, XTB = xts[ti % 2], XTBs[ti % 2]
            if ti + 1 < len(seqs):
                load_x(ti + 1)
            load_dn(0)
            load_dn(1)

            def tm(ap2d):
                return ap2d.rearrange("(b p) d -> p b d", p=pp)
            for q in range(4):
                sl = (ti * 4 + q) % 2
                for fcl in range(8):
                    fc = q * 8 + fcl
                    bk = 4 + fi % 4
                    sq_ = fi % 2
                    fi += 1
                    for kc in range(8):
                        MM(bank(bk)[:, 0:n], up[sl][:, kc, fcl * 128:(fcl + 1) * 128], hT[:, kc, 0:n], kc == 0, kc == 7,
                           [UPB[sl], HTB], PB[bk])
                    P.op("act", "activation", reads=[PB[bk]], writes=[SQB[sq_]], out=sqv[sq_][:, 0:n], in_=bank(bk)[:, 0:n],
                         func=AF.Square)
                    P.op("dve", "scalar_tensor_tensor", reads=[PB[bk], SQB[sq_]], writes=[ATB], out=aT[:, fc, 0:n],
                         in0=bank(bk)[:, 0:n], scalar=0.0, in1=sqv[sq_][:, 0:n], op0=ALU.is_gt, op1=ALU.mult)
                load_up(ti * 4 + q + 2)
            for q in range(4):
                sl = q % 2
                if q >= 2:
                    load_dn(q)
                for ib, (io, is_) in enumerate(blks):
                    for half in range(2):
                        bk = 2 * ib + half
                        for fcl in range(8):
                            fc = q * 8 + fcl
                            MM(bank(bk)[0:is_, :], aT[:, fc, io:io + is_], dn[sl][:, fcl, half * 512:(half + 1) * 512],
                               q == 0 and fcl == 0, q == 3 and fcl == 7, [ATB, DNB[sl]], PB[bk])
            if ti + 1 < len(seqs):
                prenorm_e(ti + 1)
            for ib, (io, is_) in enumerate(blks):
                pso = ps[0:is_, 2 * ib * 512:2 * ib * 512 + 1024]
                post_norm_residual(P, pso, [PB[2 * ib], PB[2 * ib + 1]], xt[0:is_, ib, :], XTB, g5, CW, junk, JB, ssv, SSB,
                                   epsc, tmp, TMPB, is_)
            if ti + 1 < len(seqs):
                prenorm_t(ti + 1)
            if final:
                dst = k.O["y_sample"] if kind == "smp" else k.O["y_prompt"][t0:t0 + n, :]
            else:
                dst = k.S["x"][t0:t0 + n, :]
            P.dma("sp", tm(dst), xt[0:pp, 0:nb, :], reads=[XTB], writes=[Buf()])
            P.maybe_flush()
        P.flush()


def phase_S1(k, P, ps, PB, bank, WB, identf, identb, onesf, onesb):
    with contextlib.ExitStack() as st:
        def sb(name, shape, dt):
            return k.sbuf(st, "s1_" + name, shape, dt)
        winc = sb("winc", [128, 8, D], BF16)
        g0 = sb("g0", [128, D], F32)
        epsc = sb("epsc", [128, 1], F32)
        CW = Buf("cw")
        P.dma("sp", winc[:], WB["w_in_c"].rearrange("(kc p) n -> p kc n", p=128), writes=[CW])
        P.dma("sp", g0[:], k.I["norm_w"][1, 0:1, :].broadcast_to([128, D]), writes=[CW])
        P.op("pool", "memset", writes=[CW], ap=epsc[:], constant=EPS)
        xts = [sb("xt%d" % i, [128, 4, D], F32) for i in range(2)]
        XTBs = [Buf("xt%d" % i) for i in range(2)]
        h = sb("h", [128, 4, D], BF16)
        HB = Buf("h")
        hT = sb("hT", [128, 8, TT], BF16)
        HTB = Buf("hT")
        junk = sb("junk", [128, D], BF16)
        JB = Buf("junk")
        ssv = sb("ssv", [128, 8], F32)
        SSB = Buf("ssv")
        uTs = [sb("uT%d" % i, [128, 8, TT], BF16) for i in range(2)]
        UTBs = [Buf("uT%d" % i) for i in range(2)]
        seqs = [("prm", t * TT, TT) for t in range(T // TT)] + [("smp", T, TS)]
        def load_x(ti):
            kind, t0, n = seqs[ti]
            pp = min(128, n)
            P.dma("sp", xts[ti % 2][0:pp, 0:len(blocks(n)), :], k.S["x"][t0:t0 + n, :].rearrange("(b p) d -> p b d", p=pp),
                  writes=[XTBs[ti % 2]])
        load_x(0)
        for ti, (kind, t0, n) in enumerate(seqs):
            blks = blocks(n)
            nb = len(blks)
            pp = min(128, n)
            xt, XTB = xts[ti % 2], XTBs[ti % 2]
            uT, UTB = uTs[ti % 2], UTBs[ti % 2]
            if ti + 1 < len(seqs):
                load_x(ti + 1)
            pre_norm_hT(P, bank, PB, [0, 1, 2, 3], xt, XTB, g0, CW, junk, JB, ssv, SSB, epsc, h, HB, hT, HTB, identb, blks, n)
            for mc in range(8):
                bk = 4 + mc % 4
                for kc in range(8):
                    P.op("pe", "matmul", reads=[HTB, CW], writes=[PB[bk]], out=bank(bk)[:, 0:n],
                         lhsT=winc[:, kc, mc * 128:(mc + 1) * 128], rhs=hT[:, kc, 0:n], start=(kc == 0), stop=(kc == 7))
                if mc % 2 == 0:
                    P.op("act", "copy", reads=[PB[bk]], writes=[UTB], out=uT[:, mc, 0:n], in_=bank(bk)[:, 0:n])
                else:
                    P.op("dve", "tensor_copy", reads=[PB[bk]], writes=[UTB], out=uT[:, mc, 0:n], in_=bank(bk)[:, 0:n])
            P.dma("act", k.S["uT"][:, :, t0:t0 + n].rearrange("c p t -> p c t"), uT[:, :, 0:n], reads=[UTB], writes=[Buf()])
            P.maybe_flush()
        P.flush()


def phase_S3(k, P, ps, PB, bank, WB, identf, identb, onesf, onesb):
    with contextlib.ExitStack() as st:
        def sb(name, shape, dt):
            return k.sbuf(st, "s3_" + name, shape, dt)
        wg = sb("wg", [128, 8, 2 * D], BF16)
        g1 = sb("g1", [128, D], F32)
        epsc = sb("epsc", [128, 1], F32)
        CW = Buf("cw")
        P.dma("sp", wg[:], WB["w_glu"].rearrange("(kc p) n -> p kc n", p=128), writes=[CW])
        P.dma("sp", g1[:], k.I["norm_w"][1, 1:2, :].broadcast_to([128, D]), writes=[CW])
        P.op("pool", "memset", writes=[CW], ap=epsc[:], constant=EPS)
        xts = [sb("xt%d" % i, [128, 4, D], F32) for i in range(2)]
        XTBs = [Buf("xt%d" % i) for i in range(2)]
        yTs = [sb("yT%d" % i, [128, 8, TT], BF16) for i in range(2)]
        YTBs = [Buf("yT%d" % i) for i in range(2)]
        junk = sb("junk", [128, D], BF16)
        JB = Buf("junk")
        ssv = sb("ssv", [128, 8], F32)
        SSB = Buf("ssv")
        tmp = sb("tmp", [128, D], F32)
        TMPB = Buf("tmp")
        sig = sb("sig", [128, D], F32)
        SIGB = Buf("sig")
        gl = sb("gl", [128, D], F32)
        GLB = Buf("gl")
        seqs = [("prm", t * TT, TT) for t in range(T // TT)] + [("smp", T, TS)]
        def load_in(ti):
            kind, t0, n = seqs[ti]
            pp = min(128, n)
            P.dma("sp", xts[ti % 2][0:pp, 0:len(blocks(n)), :], k.S["x"][t0:t0 + n, :].rearrange("(b p) d -> p b d", p=pp),
                  writes=[XTBs[ti % 2]])
            P.dma("sp", yTs[ti % 2][:, :, 0:n], k.S["yT"][:, :, t0:t0 + n].rearrange("c p t -> p c t"), writes=[YTBs[ti % 2]])
        load_in(0)
        for ti, (kind, t0, n) in enumerate(seqs):
            blks = blocks(n)
            nb = len(blks)
            pp = min(128, n)
            xt, XTB = xts[ti % 2], XTBs[ti % 2]
            yT, YTB = yTs[ti % 2], YTBs[ti % 2]
            if ti + 1 < len(seqs):
                load_in(ti + 1)
            for ib, (io, is_) in enumerate(blks):
                b0 = 4 * (ib % 2)
                for q4 in range(4):
                    bk = b0 + q4
                    for kc in range(8):
                        P.op("pe", "matmul", reads=[YTB, CW], writes=[PB[bk]], out=bank(bk)[0:is_, :],
                             lhsT=yT[:, kc, io:io + is_], rhs=wg[:, kc, q4 * 512:(q4 + 1) * 512], start=(kc == 0), stop=(kc == 7))
                P.op("act", "activation", reads=[PB[b0 + 2], PB[b0 + 3]], writes=[SIGB], out=sig[0:is_, :],
                     in_=ps[0:is_, (b0 + 2) * 512:(b0 + 4) * 512], func=AF.Sigmoid)
                P.op("dve", "tensor_tensor", reads=[PB[b0], PB[b0 + 1], SIGB], writes=[GLB], out=gl[0:is_, :],
                     in0=ps[0:is_, b0 * 512:(b0 + 2) * 512], in1=sig[0:is_, :], op=ALU.mult)
                post_norm_residual(P, gl[0:is_, :], [GLB], xt[0:is_, ib, :], XTB, g1, CW, junk, JB, ssv, SSB, epsc, tmp, TMPB, is_)
            P.dma("sp", k.S["x"][t0:t0 + n, :].rearrange("(b p) d -> p b d", p=pp), xt[0:pp, 0:nb, :], reads=[XTB], writes=[Buf()])
            P.maybe_flush()
        P.flush()


PI = float(np.pi)


def phase_S2(k, P, ps, PB, bank, WB, identf, identb, onesf, onesb):
    with contextlib.ExitStack() as st:
        def sb(name, shape, dt):
            return k.sbuf(st, "s2_" + name, shape, dt)
        Tall = sb("Tall", [128, 64, 128], BF16)
        Vall = sb("Vall", [128, 64, 128], BF16)
        WR = sb("WR", [128, 32, 128], BF16)
        WI = sb("WI", [128, 32, 128], BF16)
        Ec = sb("Ec", [128, 32, 64], F32)
        Es = sb("Es", [128, 32, 64], F32)
        Rt = sb("Rt", [128, 32, 64], F32)
        r8s = sb("r8s", [128, 32], F32)
        dfm = sb("dfm", [128, 8], F32)
        XRc = sb("XRc", [128, 32], F32)
        XIc = sb("XIc", [128, 32], F32)
        TB = Buf("s2tab")
        CARB = Buf("s2carry")
        P.dma("sp", dfm[:], k.I["s5_d"].rearrange("o (c p) -> p (o c)", p=128), writes=[TB], allow_slow_non_contiguous=True)

        def MM(out, lhsT, rhs, start, stop, R, Wb):
            P.op("pe", "matmul", reads=R, writes=[Wb], out=out, lhsT=lhsT, rhs=rhs, start=start, stop=stop)

        with contextlib.ExitStack() as st2:
            def sb2(name, shape, dt):
                return k.sbuf(st2, "s2t_" + name, shape, dt)
            lamr = sb2("lamr", [128, 32], F32)
            lami = sb2("lami", [128, 32], F32)
            dtv = sb2("dtv", [128, 32], F32)
            lr = sb2("lr", [128, 32], F32)
            li = sb2("li", [128, 32], F32)
            kvec = sb2("kvec", [128, 16], F32)
            nvec = sb2("nvec", [128, 64], F32)
            tmask = sb2("tmask", [128, 512], F32)
            Bre = sb2("Bre", [128, 32, 16], F32)
            Bim = sb2("Bim", [128, 32, 16], F32)
            Cnat = sb2("Cnat", [128, 4, 128], F32)
            Cre = sb2("Cre", [128, 32, 16], F32)
            Cim = sb2("Cim", [128, 32, 16], F32)
            arg = sb2("arg", [128, 32, 16], F32)
            mag = sb2("mag", [128, 32, 16], F32)
            Are = sb2("Are", [128, 32, 16], F32)
            Aim = sb2("Aim", [128, 32, 16], F32)
            s1 = sb2("s1", [128, 32], F32)
            s2_ = sb2("s2", [128, 32], F32)
            s3 = sb2("s3", [128, 32], F32)
            fre = sb2("fre", [128, 32], F32)
            fim = sb2("fim", [128, 32], F32)
            bbr = sb2("bbr", [128, 32, 16], F32)
            bbi = sb2("bbi", [128, 32, 16], F32)
            t16 = sb2("t16", [128, 32, 16], F32)
            Gre = sb2("Gre", [128, 32, 128], F32)
            Gim = sb2("Gim", [128, 32, 128], F32)
            Hre = sb2("Hre", [128, 32, 128], F32)
            Him = sb2("Him", [128, 32, 128], F32)
            big = sb2("big", [128, 32, 128], F32)
            eb = sb2("eb", [128, 32, 64], F32)
            SB_ = Buf("setup")
            for gh in range(2):
                psl = slice(gh * 64, gh * 64 + 64)
                gs = slice(gh * 32, gh * 32 + 32)
                P.dma("sp", lamr[psl, :], k.I["s5_lambda_re"][gs, :].rearrange("g p -> p g"), writes=[SB_],
                      allow_slow_non_contiguous=True)
                P.dma("act", lami[psl, :], k.I["s5_lambda_im"][gs, :].rearrange("g p -> p g"), writes=[SB_],
                      allow_slow_non_contiguous=True)
                P.dma("sp", dtv[psl, :], k.I["s5_log_dt"][0:1, gs].broadcast_to([64, 32]), writes=[SB_])
                P.dma("sp", Bre[psl, :, :], k.I["s5_b_re"][gs].rearrange("g p c -> p g c"), writes=[SB_])
                P.dma("act", Bim[psl, :, :], k.I["s5_b_im"][gs].rearrange("g p c -> p g c"), writes=[SB_])
            P.dma("sp", kvec[:], k.C["kvec"], writes=[SB_])
            P.dma("sp", nvec[:], k.C["nvec"], writes=[SB_])
            P.dma("sp", tmask[:], k.C["tmask"], writes=[SB_])
            for (src, dstC) in ((k.I["s5_c_re"], Cre), (k.I["s5_c_im"], Cim)):
                for gh in range(2):
                    P.dma("sp", Cnat[:, :, gh * 64:gh * 64 + 64],
                          src[gh * 32:gh * 32 + 32].rearrange("(o g) c p -> (g c) o p", o=4), writes=[SB_])
                for o4 in range(4):
                    MM(bank(0)[:, 0:128], Cnat[:, o4, :], identf[:, :], True, True, [SB_], PB[0])
                    gl0 = o4 * 8
                    P.op("dve", "tensor_copy", reads=[PB[0]], writes=[SB_],
                         out=dstC[:, gl0:gl0 + 8, :], in_=bank(0)[:, 0:128].rearrange("p (g c) -> p g c", c=16))

            def V(eng, name, **kw):
                P.op(eng, name, reads=[SB_], writes=[SB_], **kw)
            V("act", "activation", out=dtv[:], in_=dtv[:], func=AF.Exp)
            V("dve", "tensor_tensor", out=lr[:], in0=lamr[:], in1=dtv[:], op=ALU.mult)
            V("dve", "tensor_tensor", out=li[:], in0=lami[:], in1=dtv[:], op=ALU.mult)
            kb = kvec[:].unsqueeze(1).broadcast_to([128, 32, 16])
            V("dve", "tensor_tensor", out=arg[:], in0=lr[:].unsqueeze(2).broadcast_to([128, 32, 16]), in1=kb, op=ALU.mult)
            V("act", "activation", out=mag[:], in_=arg[:], func=AF.Exp)
            V("dve", "tensor_tensor", out=arg[:], in0=li[:].unsqueeze(2).broadcast_to([128, 32, 16]), in1=kb, op=ALU.mult)

            isc = sb2("isc", [128, 32, 64], mybir.dt.int32)

            def sincos(out_s, out_c, ang, f1, f2, i1):
                for (dst, off) in ((out_s, 0.0), (out_c, 0.25)):
                    V("dve", "tensor_scalar", out=f1, in0=ang, scalar1=1.0 / (2 * PI), scalar2=off, op0=ALU.mult, op1=ALU.add)
                    V("dve", "tensor_copy", out=i1, in_=f1)
                    V("dve", "tensor_copy", out=f2, in_=i1)
                    V("dve", "tensor_tensor", out=f1, in0=f1, in1=f2, op=ALU.subtract)
                    V("act", "activation", out=dst, in_=f1, func=AF.Sin, scale=2 * PI)
            sincos(Aim[:], Are[:], arg[:], t16[:], bbr[:], isc[:, :, 0:16])
            V("dve", "tensor_tensor", out=Are[:], in0=Are[:], in1=mag[:], op=ALU.mult)
            V("dve", "tensor_tensor", out=Aim[:], in0=Aim[:], in1=mag[:], op=ALU.mult)
            a_re, a_im = Are[:, :, 8], Aim[:, :, 8]
            V("dve", "tensor_tensor", out=s1[:], in0=lamr[:], in1=lamr[:], op=ALU.mult)
            V("dve", "tensor_tensor", out=s2_[:], in0=lami[:], in1=lami[:], op=ALU.mult)
            V("dve", "tensor_tensor", out=s1[:], in0=s1[:], in1=s2_[:], op=ALU.add)
            V("dve", "reciprocal", out=s1[:], in_=s1[:])
            V("dve", "tensor_scalar", out=s2_[:], in0=a_re, scalar1=-1.0, scalar2=None, op0=ALU.add)
            V("dve", "tensor_tensor", out=fre[:], in0=s2_[:], in1=lamr[:], op=ALU.mult)
            V("dve", "tensor_tensor", out=s3[:], in0=a_im, in1=lami[:], op=ALU.mult)
            V("dve", "tensor_tensor", out=fre[:], in0=fre[:], in1=s3[:], op=ALU.add)
            V("dve", "tensor_tensor", out=fre[:], in0=fre[:], in1=s1[:], op=ALU.mult)
            V("dve", "tensor_tensor", out=fim[:], in0=a_im, in1=lamr[:], op=ALU.mult)
            V("dve", "tensor_tensor", out=s3[:], in0=s2_[:], in1=lami[:], op=ALU.mult)
            V("dve", "tensor_tensor", out=fim[:], in0=fim[:], in1=s3[:], op=ALU.subtract)
            V("dve", "tensor_tensor", out=fim[:], in0=fim[:], in1=s1[:], op=ALU.mult)
            frb = fre[:].unsqueeze(2).broadcast_to([128, 32, 16])
            fib = fim[:].unsqueeze(2).broadcast_to([128, 32, 16])
            V("dve", "tensor_tensor", out=bbr[:], in0=Bre[:], in1=frb, op=ALU.mult)
            V("dve", "tensor_tensor", out=t16[:], in0=Bim[:], in1=fib, op=ALU.mult)
            V("dve", "tensor_tensor", out=bbr[:], in0=bbr[:], in1=t16[:], op=ALU.subtract)
            V("dve", "tensor_tensor", out=bbi[:], in0=Bim[:], in1=frb, op=ALU.mult)
            V("dve", "tensor_tensor", out=t16[:], in0=Bre[:], in1=fib, op=ALU.mult)
            V("dve", "tensor_tensor", out=bbi[:], in0=bbi[:], in1=t16[:], op=ALU.add)

            def v4(t):
                return t[:].rearrange("p g (s c) -> p g s c", c=16)

            def outer(are, aim, xre, xim, ore, oim, neg_im=False):
                ab = are.unsqueeze(3).broadcast_to([128, 32, 8, 16])
                ai = aim.unsqueeze(3).broadcast_to([128, 32, 8, 16])
                xr = xre.unsqueeze(2).broadcast_to([128, 32, 8, 16])
                xi = xim.unsqueeze(2).broadcast_to([128, 32, 8, 16])
                V("dve", "tensor_tensor", out=v4(ore), in0=ab, in1=xr, op=ALU.mult)
                V("dve", "tensor_tensor", out=v4(big), in0=ai, in1=xi, op=ALU.mult)
                V("dve", "tensor_tensor", out=ore[:], in0=ore[:], in1=big[:], op=ALU.subtract)
                V("dve", "tensor_tensor", out=v4(oim), in0=ab, in1=xi, op=ALU.mult)
                V("dve", "tensor_tensor", out=v4(big), in0=ai, in1=xr, op=ALU.mult)
                V("dve", "tensor_tensor", out=oim[:], in0=oim[:], in1=big[:], op=ALU.add)
            outer(Are[:, :, 0:8], Aim[:, :, 0:8], bbr[:], bbi[:], Gre, Gim)
            outer(Are[:, :, 8:16], Aim[:, :, 8:16], Cre[:], Cim[:], Hre, Him)
            V("dve", "tensor_scalar", out=Him[:], in0=Him[:], scalar1=-1.0, scalar2=None, op0=ALU.mult)
            for g in range(64):
                gh, gl_ = g // 32, g % 32
                psl = slice(gh * 64, gh * 64 + 64)
                bk = (g // 4) % 2
                col = (g % 4) * 128
                MM(bank(bk)[:, col:col + 128], Gre[psl, gl_, :], Hre[psl, gl_, :], True, False, [SB_], PB[bk])
                MM(bank(bk)[:, col:col + 128], Gim[psl, gl_, :], Him[psl, gl_, :], False, True, [SB_], PB[bk])
                if g % 4 == 3:
                    g0 = g - 3
                    P.op("dve", "tensor_tensor", reads=[PB[bk], SB_], writes=[TB],
                         out=Tall[:, g0:g0 + 4, :], in0=bank(bk)[:, :].rearrange("p (g m) -> p g m", m=128),
                         in1=tmask[:].rearrange("p (g m) -> p g m", m=128), op=ALU.mult)
            V("act", "copy", out=WR[:], in_=Hre[:])
            V("act", "copy", out=WI[:], in_=Him[:])
            P.op("act", "copy", reads=[SB_], writes=[TB], out=WR[:], in_=Hre[:])
            a8r = Are[:, :, 15:16].broadcast_to([128, 32, 128])
            a8i = Aim[:, :, 15:16].broadcast_to([128, 32, 128])
            V("dve", "tensor_tensor", out=Hre[:], in0=Gre[:], in1=a8r, op=ALU.mult)
            V("dve", "tensor_tensor", out=big[:], in0=Gim[:], in1=a8i, op=ALU.mult)
            V("dve", "tensor_tensor", out=Hre[:], in0=Hre[:], in1=big[:], op=ALU.subtract)
            V("dve", "tensor_tensor", out=Him[:], in0=Gim[:], in1=a8r, op=ALU.mult)
            V("dve", "tensor_tensor", out=big[:], in0=Gre[:], in1=a8i, op=ALU.mult)
            V("dve", "tensor_tensor", out=Him[:], in0=Him[:], in1=big[:], op=ALU.add)
            for g in range(64):
                gh, gl_ = g // 32, g % 32
                psl = slice(gh * 64, gh * 64 + 64)
                bk = 2 + (g // 4) % 2
                col = (g % 4) * 128
                MM(bank(bk)[:, col:col + 64], Hre[psl, gl_, :], identf[psl, psl], True, True, [SB_], PB[bk])
                MM(bank(bk)[:, col + 64:col + 128], Him[psl, gl_, :], identf[psl, psl], True, True, [SB_], PB[bk])
                if g % 4 == 3:
                    g0 = g - 3
                    P.op("act", "copy", reads=[PB[bk]], writes=[TB], out=Vall[:, g0:g0 + 4, :],
                         in_=bank(bk)[:, :].rearrange("p (g m) -> p g m", m=128))
            nb_ = nvec[:].unsqueeze(1).broadcast_to([128, 32, 64])
            V("dve", "tensor_tensor", out=eb[:], in0=li[:].unsqueeze(2).broadcast_to([128, 32, 64]), in1=nb_, op=ALU.mult)
            sincos(Es[:], Ec[:], eb[:], Rt[:], big[:, :, 0:64], isc[:])
            V("act", "activation", out=r8s[:], in_=lr[:], func=AF.Exp, scale=8.0)
            P.op("dve", "tensor_copy", reads=[SB_], writes=[TB], out=Rt[:], in_=r8s[:].unsqueeze(2).broadcast_to([128, 32, 64]))
            P.op("pool", "memset", reads=[SB_], writes=[TB], ap=Rt[:, :, 0:1], constant=0.0)
            P.flush()

        sel = sb("sel", [128, 64, 128], BF16)
        selT = sb("selT", [128, 64, 128], BF16)
        P.dma("sp", sel[:], k.C["sel_b"], writes=[TB])
        P.dma("act", selT[:], k.C["selT_b"], writes=[TB])
        uTs = [sb("uT%d" % i, [128, 8, TT], BF16) for i in range(2)]
        UTBs = [Buf("uT%d" % i) for i in range(2)]
        Ugs = [sb("Ug%d" % i, [128, 64, 64], BF16) for i in range(2)]
        UGBs = [Buf("Ug%d" % i) for i in range(2)]
        XsR = sb("XsR", [128, 32, 64], BF16)
        XsI = sb("XsI", [128, 32, 64], BF16)
        XSB = Buf("Xs")
        Yg = sb("Yg", [128, 64, 64], BF16)
        YGB = Buf("Yg")
        yT = sb("yT", [128, 8, TT], BF16)
        YTB = Buf("yT")
        ta = sb("ta", [128, 16, 64], F32)
        tb = sb("tb", [128, 16, 64], F32)
        vr = sb("vr", [128, 16, 64], F32)
        vi = sb("vi", [128, 16, 64], F32)
        zr = sb("zr", [128, 16, 64], F32)
        zi = sb("zi", [128, 16, 64], F32)
        tc_ = sb("tc", [128, 16], F32)
        WKB = Buf("s2work")
        for t_ in (ta, tb, vr, vi, zr, zi):
            P.op("pool", "memset", writes=[WKB], ap=t_[:], constant=0.0)
        P.op("pool", "memset", writes=[CARB], ap=XRc[:], constant=0.0)
        P.op("pool", "memset", writes=[CARB], ap=XIc[:], constant=0.0)
        P.op("pool", "memset", writes=[XSB], ap=XsR[:], constant=0.0)
        P.op("pool", "memset", writes=[XSB], ap=XsI[:], constant=0.0)

        def f2(t):
            return t[:].rearrange("p g n -> p (g n)")
        seqs = [("prm", t * TT, TT) for t in range(T // TT)] + [("smp", T, TS)]
        if "s2_seqs" in DBG:
            seqs = [seqs[i] for i in DBG["s2_seqs"]]
        def load_u(ti):
            kind, t0, n = seqs[ti]
            P.dma("sp", uTs[ti % 2][:, :, 0:n], k.S["uT"][:, :, t0:t0 + n].rearrange("c p t -> p c t"), writes=[UTBs[ti % 2]])
        load_u(0)
        def tile_vars(ti):
            kind, t0, n = seqs[ti]
            return kind, t0, n, n // 8, uTs[ti % 2], UTBs[ti % 2], Ugs[ti % 2], UGBs[ti % 2]

        def stageA(ti):
            kind, t0, n, nch, uT, UTB, Ug, UGB = tile_vars(ti)
            for g in range(64):
                kc, g8 = g // 8, g % 8
                bk = (g // 8) % 2
                col = g8 * 64
                for s_ in range(8):
                    rhs = uT[:, kc, 0:n].rearrange("p (n s) -> p n s", s=8)[:, :, s_]
                    MM(bank(bk)[:, col:col + nch], sel[:, g8 * 8 + s_, :], rhs, s_ == 0, s_ == 7, [UTB, TB], PB[bk])
                if g8 == 7:
                    g0 = g - 7
                    src = bank(bk)[:, :].rearrange("p (g n) -> p g n", n=64)[:, :, 0:nch]
                    if (g // 8) % 2 == 0:
                        P.op("act", "copy", reads=[PB[bk]], writes=[UGB], out=Ug[:, g0:g0 + 8, 0:nch], in_=src)
                    else:
                        P.op("dve", "tensor_copy", reads=[PB[bk]], writes=[UGB], out=Ug[:, g0:g0 + 8, 0:nch], in_=src)

        def stageB(ti):
            kind, t0, n, nch, uT, UTB, Ug, UGB = tile_vars(ti)
            if kind == "smp":
                for gh in range(2):
                    psl = slice(gh * 64, gh * 64 + 64)
                    gs = slice(gh * 32, gh * 32 + 32)
                    P.dma("sp", XRc[psl, :], k.I["state_s5_re"][gs, :].rearrange("g p -> p g"), writes=[CARB],
                          allow_slow_non_contiguous=True)
                    P.dma("act", XIc[psl, :], k.I["state_s5_im"][gs, :].rearrange("g p -> p g"), writes=[CARB],
                          allow_slow_non_contiguous=True)
            for hf in range(2):
                glo = hf * 16
                for gh in range(2):
                    psl = slice(gh * 64, gh * 64 + 64)
                    for gq in range(16):
                        g = gh * 32 + glo + gq
                        bR = 2 + gq // 8
                        bI = 4 + gq // 8
                        col = (gq % 8) * 64
                        MM(bank(bR)[psl, col:col + nch], Vall[:, g, 0:64], Ug[:, g, 0:nch], True, True, [UGB, TB], PB[bR])
                        MM(bank(bI)[psl, col:col + nch], Vall[:, g, 64:128], Ug[:, g, 0:nch], True, True, [UGB, TB], PB[bI])
                pR = ps[:, 2 * 512:4 * 512].rearrange("p (g n) -> p g n", n=64)[:, :, 0:nch]
                pI = ps[:, 4 * 512:6 * 512].rearrange("p (g n) -> p g n", n=64)[:, :, 0:nch]
                ec = Ec[:, glo:glo + 16, 0:nch]
                es = Es[:, glo:glo + 16, 0:nch]
                RD = [PB[2], PB[3], PB[4], PB[5], TB, WKB]
                WW = [WKB]

                def W3(t):
                    return t[:, :, 0:nch]
                P.op("dve", "tensor_tensor", reads=RD, writes=WW, out=W3(ta), in0=pR, in1=ec, op=ALU.mult)
                P.op("dve", "tensor_tensor", reads=RD, writes=WW, out=W3(tb), in0=pI, in1=es, op=ALU.mult)
                P.op("pool", "tensor_tensor", reads=RD, writes=WW, out=W3(vr), in0=W3(ta), in1=W3(tb), op=ALU.add)
                P.op("dve", "tensor_tensor", reads=RD, writes=WW, out=W3(ta), in0=pI, in1=ec, op=ALU.mult)
                P.op("dve", "tensor_tensor", reads=RD, writes=WW, out=W3(tb), in0=pR, in1=es, op=ALU.mult)
                P.op("pool", "tensor_tensor", reads=RD, writes=WW, out=W3(vi), in0=W3(ta), in1=W3(tb), op=ALU.subtract)
                for (vv, cc) in ((vr, XRc), (vi, XIc)):
                    P.op("dve", "tensor_tensor", reads=[CARB, TB, WKB], writes=WW, out=tc_[:], in0=cc[:, glo:glo + 16],
                         in1=r8s[:, glo:glo + 16], op=ALU.mult)
                    P.op("dve", "tensor_tensor", reads=[WKB], writes=WW, out=vv[:, :, 0], in0=vv[:, :, 0], in1=tc_[:],
                         op=ALU.add)
                rt2 = Rt[:, glo:glo + 16, :].rearrange("p g n -> p (g n)")
                P.op("dve", "tensor_tensor_scan", reads=[WKB, TB], writes=WW, out=f2(zr), data0=rt2, data1=f2(vr), initial=0.0,
                     op0=ALU.mult, op1=ALU.add)
                P.op("dve", "tensor_tensor_scan", reads=[WKB, TB], writes=WW, out=f2(zi), data0=rt2, data1=f2(vi), initial=0.0,
                     op0=ALU.mult, op1=ALU.add)
                P.op("dve", "tensor_tensor", reads=[WKB, TB], writes=WW, out=W3(ta), in0=W3(zr), in1=ec, op=ALU.mult)
                P.op("pool", "tensor_tensor", reads=[WKB, TB], writes=WW, out=W3(tb), in0=W3(zi), in1=es, op=ALU.mult)
                P.op("dve", "tensor_tensor", reads=[WKB], writes=WW, out=W3(vr), in0=W3(ta), in1=W3(tb), op=ALU.subtract)
                P.op("dve", "tensor_tensor", reads=[WKB, TB], writes=WW, out=W3(ta), in0=W3(zi), in1=ec, op=ALU.mult)
                P.op("pool", "tensor_tensor", reads=[WKB, TB], writes=WW, out=W3(tb), in0=W3(zr), in1=es, op=ALU.mult)
                P.op("dve", "tensor_tensor", reads=[WKB], writes=WW, out=W3(vi), in0=W3(ta), in1=W3(tb), op=ALU.add)
                P.op("act", "copy", reads=[CARB], writes=[XSB], out=XsR[:, glo:glo + 16, 0], in_=XRc[:, glo:glo + 16])
                P.op("act", "copy", reads=[CARB], writes=[XSB], out=XsI[:, glo:glo + 16, 0], in_=XIc[:, glo:glo + 16])
                if nch > 1:
                    P.op("act", "copy", reads=[WKB], writes=[XSB], out=XsR[:, glo:glo + 16, 1:nch], in_=vr[:, :, 0:nch - 1])
                    P.op("act", "copy", reads=[WKB], writes=[XSB], out=XsI[:, glo:glo + 16, 1:nch], in_=vi[:, :, 0:nch - 1])
                P.op("dve", "tensor_copy", reads=[WKB, XSB], writes=[CARB], out=XRc[:, glo:glo + 16], in_=vr[:, :, nch - 1])
                P.op("dve", "tensor_copy", reads=[WKB, XSB], writes=[CARB], out=XIc[:, glo:glo + 16], in_=vi[:, :, nch - 1])
            last_of_kind = (kind == "smp") or (t0 + n == T)
            if last_of_kind:
                for gh in range(2):
                    psl = slice(gh * 64, gh * 64 + 64)
                    gs = slice(gh * 32, gh * 32 + 32)
                    dr = k.O["s5_re_sample" if kind == "smp" else "s5_re_prompt"]
                    di = k.O["s5_im_sample" if kind == "smp" else "s5_im_prompt"]
                    P.dma("sp", dr[gs, :].rearrange("g p -> p g"), XRc[psl, :], reads=[CARB], writes=[Buf()],
                          allow_slow_non_contiguous=True)
                    P.dma("act", di[gs, :].rearrange("g p -> p g"), XIc[psl, :], reads=[CARB], writes=[Buf()],
                          allow_slow_non_contiguous=True)

        def stageC(ti):
            kind, t0, n, nch, uT, UTB, Ug, UGB = tile_vars(ti)
            for g in range(64):
                gh, gl_ = g // 32, g % 32
                psl = slice(gh * 64, gh * 64 + 64)
                bk = 6 + (g // 8) % 2
                col = (g % 8) * 64
                MM(bank(bk)[:, col:col + nch], Tall[:, g, :], Ug[:, g, 0:nch], True, False, [UGB, TB], PB[bk])
                MM(bank(bk)[:, col:col + nch], WR[psl, gl_, :], XsR[psl, gl_, 0:nch], False, False, [XSB, TB], PB[bk])
                MM(bank(bk)[:, col:col + nch], WI[psl, gl_, :], XsI[psl, gl_, 0:nch], False, True, [XSB, TB], PB[bk])
                if g % 8 == 7:
                    g0 = g - 7
                    src = bank(bk)[:, :].rearrange("p (g n) -> p g n", n=64)[:, :, 0:nch]
                    if (g // 8) % 2 == 0:
                        P.op("act", "copy", reads=[PB[bk]], writes=[YGB], out=Yg[:, g0:g0 + 8, 0:nch], in_=src)
                    else:
                        P.op("dve", "tensor_copy", reads=[PB[bk]], writes=[YGB], out=Yg[:, g0:g0 + 8, 0:nch], in_=src)
            for kc in range(8):
                bk = kc % 2
                for tau in range(8):
                    for g8 in range(8):
                        MM(bank(bk)[:, tau * 64:tau * 64 + nch], selT[:, g8 * 8 + tau, :], Yg[:, kc * 8 + g8, 0:nch], g8 == 0,
                           g8 == 7, [YGB, TB], PB[bk])
                yv = yT[:, kc, 0:n].rearrange("p (n s) -> p s n", s=8)
                uv = uT[:, kc, 0:n].rearrange("p (n s) -> p s n", s=8)
                pv = bank(bk)[:, :].rearrange("p (s n) -> p s n", n=64)[:, :, 0:nch]
                P.op("dve", "scalar_tensor_tensor", reads=[PB[bk], UTB, TB], writes=[YTB], out=yv, in0=uv, scalar=dfm[:, kc:kc + 1],
                     in1=pv, op0=ALU.mult, op1=ALU.add)
            P.dma("act", k.S["yT"][:, :, t0:t0 + n].rearrange("c p t -> p c t"), yT[:, :, 0:n], reads=[YTB], writes=[Buf()])

        stageA(0)
        for ti in range(len(seqs)):
            if ti + 1 < len(seqs):
                load_u(ti + 1)
            stageB(ti)
            if ti + 1 < len(seqs):
                stageA(ti + 1)
            stageC(ti)
            P.maybe_flush()
        P.flush()
```

```python
import contextlib
import numpy as np
import ml_dtypes
import concourse.bass as bass
import concourse.mybir as mybir
from concourse.bass_utils import run_bass_kernel_spmd

F32 = mybir.dt.float32
BF16 = mybir.dt.bfloat16
AF = mybir.ActivationFunctionType
ALU = mybir.AluOpType
AX = mybir.AxisListType

D = 1024
T = 8192
TS = 32
PAST = 4096
TALL = T + 128
TT = 512
EPS = 1e-6
AB_IN = 3592
NCORES = 8


class Buf:
    __slots__ = ("name", "w", "r", "pw")

    def __init__(self, name=""):
        self.name = name
        self.w = {}
        self.r = {}
        self.pw = {}


class Op:
    __slots__ = ("eng", "fn", "deps", "marked", "sem", "val", "is_dma", "slotwait", "phase")

    def __init__(self, eng, fn, is_dma, phase):
        self.eng = eng
        self.fn = fn
        self.deps = []
        self.marked = False
        self.sem = None
        self.val = 0
        self.is_dma = is_dma
        self.slotwait = None
        self.phase = phase


class Prog:
    ENGS = ("pe", "act", "dve", "pool", "sp")
    NSLOT = 8

    def __init__(self, nc, stack):
        self.nc = nc
        self.ops = []
        self.phase = 0
        self.csem = {e: stack.enter_context(nc.semaphore("c_" + e)) for e in ("pe", "act", "dve", "pool")}
        self.dsem = {e: [stack.enter_context(nc.semaphore("d_%s%d" % (e, i))) for i in range(self.NSLOT)]
                     for e in ("sp", "act", "pool")}
        self.ccount = {e: 0 for e in self.csem}
        self.dcount = {e: 0 for e in self.dsem}
        self.waited = {e: {} for e in self.ENGS}
        self.dma_final = {}
        self.n_emitted = 0

    def _add(self, op, reads, writes):
        deps = {}
        for b in reads:
            for o in b.w.values():
                deps[id(o)] = o
        key = id(op) if op.is_dma else op.eng
        for b in writes:
            merge = op.is_dma and b.w and not b.r and all(o.is_dma and o.phase == self.phase for o in b.w.values())
            if merge:
                for o in getattr(b, "pw", {}).values():
                    deps[id(o)] = o
                b.w[key] = op
                continue
            prev = {}
            for o in b.w.values():
                deps[id(o)] = o
                prev[id(o)] = o
            for o in b.r.values():
                deps[id(o)] = o
                prev[id(o)] = o
            b.pw = prev
            b.w = {key: op}
            b.r = {}
        for b in reads:
            b.r[key] = op
        for o in deps.values():
            if o is op or o.phase != self.phase:
                continue
            if (not o.is_dma) and (not op.is_dma) and o.eng == "pe" and op.eng == "pe":
                continue
            op.deps.append(o)
            o.marked = True
        self.ops.append(op)
        return op

    def op(self, eng, name, reads=(), writes=(), **kw):
        def fn(e):
            return getattr(e, name)(**kw)
        return self._add(Op(eng, fn, False, self.phase), reads, writes)

    def dma(self, eng, out, in_, reads=(), writes=(), **kw):
        def fn(e):
            return e.dma_start(out=out, in_=in_, **kw)
        o = Op(eng, fn, True, self.phase)
        o.marked = True
        return self._add(o, reads, writes)

    def maybe_flush(self, limit=2500):
        if len(self.ops) >= limit:
            self.flush()

    def flush(self):
        if not self.ops:
            return
        nc = self.nc
        per = {e: [] for e in self.ENGS}
        last_c = {}
        for o in self.ops:
            per[o.eng].append(o)
            if not o.is_dma:
                last_c[o.eng] = o
        for o in last_c.values():
            o.marked = True
        for o in self.ops:
            if o.is_dma:
                i = self.dcount[o.eng]
                self.dcount[o.eng] += 1
                o.sem = self.dsem[o.eng][i % self.NSLOT]
                o.val = 16 * (i // self.NSLOT + 1)
                if i >= self.NSLOT:
                    o.slotwait = (o.sem, o.val - 16)
                self.dma_final[id(o.sem)] = (o.sem, o.val)
            elif o.marked:
                self.ccount[o.eng] += 1
                o.sem = self.csem[o.eng]
                o.val = self.ccount[o.eng]
        finals = list(self.dma_final.values()) + [(self.csem[e], self.ccount[e]) for e in self.csem]
        self.n_emitted += len(self.ops)

        def make(ename, ops):
            waited = self.waited[ename]

            def body(e):
                def w(sem, val):
                    if val > 0 and waited.get(id(sem), 0) < val:
                        e.wait_ge(sem, val)
                        waited[id(sem)] = val
                for o in ops:
                    if o.slotwait is not None:
                        w(*o.slotwait)
                    for d in o.deps:
                        w(d.sem, d.val)
                    ins = o.fn(e)
                    if o.marked:
                        ins.then_inc(o.sem, 16 if o.is_dma else 1)
                for sem, val in finals:
                    w(sem, val)
            return body

        with nc.allow_low_precision(reason="bf16 matmul operands by design"), nc.Block() as blk:
            blk.tensor(make("pe", per["pe"]))
            blk.scalar(make("act", per["act"]))
            blk.vector(make("dve", per["dve"]))
            blk.gpsimd(make("pool", per["pool"]))
            blk.sync(make("sp", per["sp"]))
        self.ops = []
        self.phase += 1


def _consts():
    c = {}
    c["ident_f"] = np.eye(128, dtype=np.float32)
    c["ident_b"] = np.eye(128, dtype=np.float32).astype(ml_dtypes.bfloat16)
    j = np.arange(128)
    c["tri_f"] = (j[:, None] <= j[None, :]).astype(np.float32)
    c["ones_f"] = np.ones((128, 128), np.float32)
    c["ones_b"] = np.ones((128, 128), np.float32).astype(ml_dtypes.bfloat16)
    perm = np.zeros((128, 128), np.float32)
    perm[(j + 64) % 128, j] = 1.0
    c["perm_b"] = perm.astype(ml_dtypes.bfloat16)
    half = 64
    inv = (10000.0 ** (-np.arange(half, dtype=np.float32) / half)).astype(np.float32)
    pos = np.concatenate([np.arange(T), PAST + np.arange(128)]).astype(np.float32)
    ang = pos[None, :] * inv[:, None]
    cos = np.cos(ang).astype(np.float32)
    sin = np.sin(ang).astype(np.float32)
    c["ropec"] = np.concatenate([cos, cos], 0)
    c["ropes"] = np.concatenate([-sin, sin], 0)
    lg = np.log(1.0 - 2.0 ** (-5.0 - np.arange(4, dtype=np.float64)))
    il = np.arange(512)[None, :]
    jl = np.arange(128)[:, None]
    rmask = np.zeros((4, 128, 512), np.float64)
    for h in range(4):
        m = np.exp(lg[h] * np.abs(il - jl))
        ok = (jl // 64) <= (il // 64)
        rmask[h] = np.where(ok, m, 0.0)
    c["rmask"] = rmask.transpose(1, 0, 2).astype(np.float32).copy()
    ii = np.arange(512).reshape(4, 128)
    wq = np.exp(lg[None, None, :] * (ii.T[:, :, None] + 1.0))
    c["wq"] = np.repeat(wq[:, :, :, None], 128, 3).astype(np.float32)
    wk = np.exp(lg[None, None, :] * (511.0 - ii.T[:, :, None]))
    c["wk"] = np.repeat(wk[:, :, :, None], 128, 3).astype(np.float32)
    wks = np.exp(lg[None, :] * (31.0 - np.arange(128)[:, None]))
    wks[32:] = 0.0
    c["wks"] = np.repeat(wks[:, :, None], 128, 2).astype(np.float32)
    c["rdec"] = np.stack([np.exp(lg * 512.0), np.exp(lg * 32.0)]).astype(np.float32)
    qs = np.ones((128, 1), np.float32)
    qs[:64] = 0.125
    c["qscale"] = qs
    selc = np.zeros((8, 8, 3, 3), np.float32)
    for h in range(8):
        for p_ in range(3):
            selc[h, h, p_, p_] = 1.0
    c["selc"] = selc.astype(ml_dtypes.bfloat16)
    c["caus_b"] = (j[:, None] <= j[None, :]).astype(np.float32).astype(ml_dtypes.bfloat16)
    kv = np.concatenate([-(np.arange(8) + 1.0), np.arange(8) + 1.0]).astype(np.float32)
    c["kvec"] = np.repeat(kv[None, :], 128, 0).copy()
    c["nvec"] = np.repeat((8.0 * (np.arange(64) + 1.0)).astype(np.float32)[None, :], 128, 0).copy()
    sel = np.zeros((128, 8, 8, 128), np.float32)
    selT = np.zeros((128, 8, 8, 128), np.float32)
    for g_ in range(8):
        for s_ in range(8):
            for c_ in range(16):
                sel[g_ * 16 + c_, g_, s_, s_ * 16 + c_] = 1.0
                selT[s_ * 16 + c_, g_, s_, g_ * 16 + c_] = 1.0
    c["sel_b"] = sel.reshape(128, 64, 128).astype(ml_dtypes.bfloat16)
    c["selT_b"] = selT.reshape(128, 64, 128).astype(ml_dtypes.bfloat16)
    sg = np.arange(128) // 16
    tm_ = (sg[None, :] >= sg[:, None]).astype(np.float32)
    c["tmask"] = np.tile(tm_, (1, 4)).copy()
    return c


CONST = None


def get_consts():
    global CONST
    if CONST is None:
        CONST = _consts()
    return CONST


IN_SHAPES = {
    "x_prompt": [T, D], "x_sample": [TS, D], "cache_ret_state": [4, 128, 128],
    "cache_fox_k": [PAST, 512], "cache_fox_v": [PAST, 512], "cache_fox_logf": [PAST, 8],
    "state_s5_re": [64, 64], "state_s5_im": [64, 64],
    "cache_mem_k": [2, 256, 1024], "cache_mem_v": [2, 256, 1024], "mem_prompt": [256, 1024],
    "norm_w": [2, 6, D], "w_in_ab": [D, AB_IN], "b_fox_f": [1, 8], "ret_gn_w": [1, 512],
    "w_out_ab": [D, D], "w_in_c": [D, D], "s5_lambda_re": [64, 64], "s5_lambda_im": [64, 64],
    "s5_log_dt": [1, 64], "s5_b_re": [64, 64, 16], "s5_b_im": [64, 64, 16],
    "s5_c_re": [64, 16, 64], "s5_c_im": [64, 16, 64], "s5_d": [1, D], "w_glu": [D, 2 * D],
    "mem_norm_w": [2, D], "w_xq": [2, D, D], "w_xk": [2, D, D], "w_xv": [2, D, D], "w_xo": [2, D, D],
    "w_up": [2, D, 4 * D], "w_down": [2, 4 * D, D],
}
OUT_SHAPES = {
    "y_prompt": [T, D], "y_sample": [TS, D], "ret_state_prompt": [4, 128, 128], "ret_state_sample": [4, 128, 128],
    "fox_k_prompt": [T, 512], "fox_v_prompt": [T, 512], "fox_logf_prompt": [T, 8],
    "fox_k_sample": [TS, 512], "fox_v_sample": [TS, 512], "fox_logf_sample": [TS, 8],
    "s5_re_prompt": [64, 64], "s5_im_prompt": [64, 64], "s5_re_sample": [64, 64], "s5_im_sample": [64, 64],
    "mem_k_prompt": [2, 256, 1024], "mem_v_prompt": [2, 256, 1024],
}
OUT_ORDER = list(OUT_SHAPES.keys())


class K:
    def __init__(self):
        self.nc = bass.Bass("TRN2", target_bir_lowering=False)
        nc = self.nc
        self.I = {k: nc.dram_tensor(k, s, F32, kind="ExternalInput").ap() for k, s in IN_SHAPES.items()}
        self.O = {k: nc.dram_tensor(k, s, F32, kind="ExternalOutput").ap() for k, s in OUT_SHAPES.items()}
        self.C = {}
        for k, v in get_consts().items():
            dt = BF16 if v.dtype == ml_dtypes.bfloat16 else F32
            self.C[k] = nc.dram_tensor("c_" + k, list(v.shape), dt, kind="ExternalInput").ap()
        self.S = {}
        self.SB = {}

    def scratch(self, name, shape, dt):
        self.S[name] = self.nc.dram_tensor("s_" + name, shape, dt, kind="Internal").ap()
        self.SB[name] = {}
        return self.S[name]

    def sbuf(self, st, name, shape, dt):
        return st.enter_context(self.nc.sbuf_tensor(name, shape, dt))


def tiles():
    out = [(t * TT, TT, False) for t in range(T // TT)]
    out.append((T, TS, True))
    return out


def xsrc(k, which, t0, n, sample):
    return k.I["x_sample"] if sample else k.I["x_prompt"][t0:t0 + n, :]


def blocks(n):
    return [(b * 128, min(128, n - b * 128)) for b in range((n + 127) // 128)]


def build(phases=("W", "A1")):
    k = K()
    nc = k.nc
    with contextlib.ExitStack() as top:
        P = Prog(nc, top)
        ps = top.enter_context(nc.psum_tensor("ps", [128, 4096], F32))
        PB = [Buf("ps%d" % i) for i in range(8)]

        def bank(i):
            return ps[:, i * 512:(i + 1) * 512]

        identf = k.sbuf(top, "identf", [128, 128], F32)
        identb = k.sbuf(top, "identb", [128, 128], BF16)
        onesf = k.sbuf(top, "onesf", [128, 128], F32)
        onesb = k.sbuf(top, "onesb", [128, 128], BF16)
        CB = Buf("consts")
        P.dma("sp", identf[:], k.C["ident_f"], writes=[CB])
        P.dma("sp", identb[:], k.C["ident_b"], writes=[CB])
        P.dma("sp", onesf[:], k.C["ones_f"], writes=[CB])
        P.dma("sp", onesb[:], k.C["ones_b"], writes=[CB])
        P.flush()

        WNAMES = [("w_in_ab", None, [D, AB_IN]), ("w_out_ab", None, [D, D]), ("w_in_c", None, [D, D]),
                  ("w_glu", None, [D, 2 * D])]
        for l in range(2):
            for nm in ("w_xq", "w_xk", "w_xv", "w_xo"):
                WNAMES.append((nm, l, [D, D]))
            WNAMES.append(("w_up", l, [D, 4 * D]))
            WNAMES.append(("w_down", l, [4 * D, D]))
        WB = {}
        for nm, l, shp in WNAMES:
            key = nm if l is None else "%s%d" % (nm, l)
            WB[key] = k.scratch("wb_" + key, shp, BF16)
        if "W" in phases:
            wbuf = Buf("wcast")
            for nm, l, shp in WNAMES:
                key = nm if l is None else "%s%d" % (nm, l)
                src = k.I[nm] if l is None else k.I[nm][l]
                rows = shp[0]
                step = 256
                for r0 in range(0, rows, step):
                    P.dma("pool", WB[key][r0:r0 + step, :], src[r0:r0 + step, :], writes=[Buf()])
            P.flush()

        k.scratch("x", [TALL, D], F32)
        k.scratch("rqT", [4, 128, TALL], BF16)
        k.scratch("rkT", [4, 128, TALL], BF16)
        k.scratch("rkk", [TALL, 512], BF16)
        k.scratch("rv", [TALL, 512], BF16)
        k.scratch("rg", [TALL, 512], BF16)
        k.scratch("fqa", [17, 4, 67, 2, TT], BF16)
        k.scratch("fka", [25, 4, 67, 2, TT], BF16)
        k.scratch("fva", [25, 4, 128, 4, 130], BF16)
        k.scratch("negc", [25, 128, 4, 8], F32)

        k.scratch("foT", [8, 64, TALL], BF16)
        k.scratch("uT", [8, 128, TALL], BF16)
        k.scratch("yT", [8, 128, TALL], BF16)
        env = dict(k=k, P=P, ps=ps, PB=PB, bank=bank, WB=WB, identf=identf, identb=identb, onesf=onesf, onesb=onesb)
        if "A1" in phases:
            phase_A1(k, P, ps, PB, bank, WB, identf, identb, onesf, onesb)
        if "A3" in phases:
            phase_A3(**env)
        if "A2" in phases:
            phase_A2(**env)
        for l in range(2):
            if "S" in phases and l == 1:
                phase_S1(**env)
                phase_S2(**env)
                phase_S3(**env)
            if ("C%d" % l) in phases:
                phase_C(l=l, **env)
            if ("F%d" % l) in phases:
                phase_F(l=l, final=(l == 1), **env)
        if "dumpx" in DBG:
            P.dma("sp", k.O["y_prompt"], k.S["x"][0:T, :], writes=[Buf()])
            P.dma("sp", k.O["y_sample"], k.S["x"][T:T + TS, :], writes=[Buf()])
            P.flush()
        if "dumpfo" in DBG:
            P.dma("pool", k.O["y_prompt"][0:512, :].rearrange("(h r) t -> h r t", h=8),
                  k.S["foT"][:, :, DBG["dumpfo"]:DBG["dumpfo"] + 1024], writes=[Buf()])
            P.flush()
    return k


CB_GLOBAL = Buf("cglobal")
DBG = {}


def phase_A1(k, P, ps, PB, bank, WB, identf, identb, onesf, onesb):
    with contextlib.ExitStack() as st:
        def sb(name, shape, dt):
            return k.sbuf(st, "a1_" + name, shape, dt)
        win = sb("win", [128, 8, AB_IN], BF16)
        g0 = sb("g0", [128, D], F32)
        bfox = sb("bfox", [128, 8], F32)
        perm = sb("perm", [128, 128], BF16)
        trif = sb("trif", [128, 128], F32)
        qscale = sb("qscale", [128, 1], F32)
        selc = sb("selc", [8, 72], BF16)
        wk = sb("wk", [128, 4, 512], F32)
        wks = sb("wks", [128, 512], F32)
        WBUF = Buf("a1w")
        wsrc = WB["w_in_ab"].rearrange("(kc p) n -> p kc n", p=128)
        for kc in range(8):
            P.dma("sp" if kc % 2 == 0 else "act", win[:, kc, :], wsrc[:, kc, :], writes=[WBUF])
        P.dma("sp", g0[:], k.I["norm_w"][0, 0:1, :].broadcast_to([128, D]), writes=[WBUF])
        P.dma("sp", bfox[:], k.I["b_fox_f"][0:1, :].broadcast_to([128, 8]), writes=[WBUF])
        P.dma("sp", perm[:], k.C["perm_b"], writes=[WBUF])
        P.dma("sp", trif[:], k.C["tri_f"], writes=[WBUF])
        P.dma("sp", qscale[:], k.C["qscale"], writes=[WBUF])
        P.dma("sp", selc[:], k.C["selc"].rearrange("k h p m -> k (h p m)"), writes=[WBUF])
        P.dma("sp", wk[:], k.C["wk"].rearrange("p b h e -> p b (h e)"), writes=[WBUF])
        P.dma("sp", wks[:], k.C["wks"].rearrange("p h e -> p (h e)"), writes=[WBUF])

        xt0 = sb("xt0", [128, 4, D], F32)
        xt = [xt0, xt0]
        XT0 = Buf("xt0")
        XT = [XT0, XT0]
        XTb = [Buf("xt0b%d" % i) for i in range(4)]
        rc0 = sb("rc0", [128, TT], F32)
        rs0 = sb("rs0", [128, TT], F32)
        rc = [rc0, rc0]
        rs = [rs0, rs0]
        RT0 = Buf("rt0")
        RT = [RT0, RT0]
        junk = sb("junk", [128, D], BF16)
        JB = Buf("junk")
        ss = sb("ss", [128, 4], F32)
        rstd = sb("rstd", [128, 4], F32)
        SSB = Buf("ss")
        h = sb("h", [128, 4, D], BF16)
        HB = Buf("h")
        hT = sb("hT", [128, 8, TT], BF16)
        HTB = Buf("hT")
        qbs = [sb("qb%d" % i, [128, TT], BF16) for i in range(2)]
        QBs = [Buf("qb%d" % i) for i in range(2)]
        t1s = [sb("t1%d" % i, [128, TT], F32) for i in range(2)]
        t2s = [sb("t2%d" % i, [128, TT], F32) for i in range(2)]
        T1Bs = [Buf("t1%d" % i) for i in range(2)]
        T2Bs = [Buf("t2%d" % i) for i in range(2)]
        rqT = sb("rqT", [128, 8, TT], BF16)
        RQB = [Buf("rq%d" % i) for i in range(8)]
        rkk = sb("rkk", [128, 4, 512], BF16)
        RKKB = Buf("rkk")
        rv = sb("rv", [128, 4, 512], BF16)
        RVB = Buf("rv")
        rg = sb("rg", [128, 4, 512], BF16)
        RGB = Buf("rg")
        fkf = sb("fkf", [128, 4, 512], F32)
        FKFB = Buf("fkf")
        fkb = h
        FKBB = HB
        fvf = sb("fvf", [128, 4, 512], F32)
        FVFB = Buf("fvf")
        fva = sb("fva", [128, 4, 8 * 65], BF16)
        FVAB = Buf("fva")
        fka = sb("fka", [67, 8, TT], BF16)
        FKAB = Buf("fka")
        fqa = sb("fqa", [67, 8, TT], BF16)
        FQAB = Buf("fqa")
        lf = sb("lf", [128, 4, 8], F32)
        LFB = Buf("lf")
        xb = sb("xb", [128, 4, 8], F32)
        XBB = Buf("xb")
        cS = sb("cS", [128, 4, 8], F32)
        CSB = Buf("cS")
        ngc = sb("ngc", [128, 4, 8], F32)
        NGB = Buf("ngc")
        carry = sb("carry", [1, 8], F32)
        CARB = Buf("carry")
        cT = sb("cT", [8, TT], F32)
        c3 = sb("c3", [8, 3, TT], BF16)
        r1 = sb("r1", [8, TT], F32)
        r2 = cT
        CTB, C3B, R1B = Buf("cT"), Buf("c3"), Buf("r1")
        R2B = CTB

        epsc = sb("epsc", [128, 1], F32)
        P.op("pool", "memset", writes=[SSB], ap=epsc[:], constant=EPS)
        P.op("pool", "memset", writes=[FVAB], ap=fva[:], constant=1.0)
        P.op("pool", "memset", writes=[FKAB], ap=fka[64:67, :, :], constant=1.0)
        P.op("pool", "memset", writes=[CARB], ap=carry[:], constant=0.0)

        def MM(out, lhsT, rhs, start, stop, R, Wb):
            P.op("pe", "matmul", reads=R, writes=[Wb], out=out, lhsT=lhsT, rhs=rhs, start=start, stop=stop)

        seqs = [("prm", t * TT, TT) for t in range(T // TT)]
        seqs += [("past", pt * TT, TT) for pt in range(PAST // TT)]
        seqs += [("smp", T, TS)]
        if "a1_seqs" in DBG:
            seqs = [seqs[i] for i in DBG["a1_seqs"]]
        for ti, (kind, t0, n) in enumerate(seqs):
            s = ti % 2
            blks = blocks(n)
            nb = len(blks)
            pp = min(128, n)
            kcol0 = t0 if kind == "past" else PAST + t0

            def tm(ap2d):
                return ap2d.rearrange("(b p) d -> p b d", p=pp)
            if kind == "past" and t0 == 0:
                P.op("pool", "memset", writes=[CARB], ap=carry[:], constant=0.0)
            if kind == "past":
                P.dma("sp", fkf[:, :, :], tm(k.I["cache_fox_k"][t0:t0 + n, :]), writes=[FKFB])
                P.dma("act", fvf[:, :, :], tm(k.I["cache_fox_v"][t0:t0 + n, :]), writes=[FVFB])
                P.dma("sp", lf[:, :, :], tm(k.I["cache_fox_logf"][t0:t0 + n, :]), writes=[LFB])
            else:
                xs = k.I["x_sample"] if kind == "smp" else k.I["x_prompt"][t0:t0 + n, :]
                for b, (o, bs) in enumerate(blks):
                    P.dma("sp" if b % 2 == 0 else "pool", xt[s][0:bs, b, :], xs[o:o + bs, :], writes=[XTb[b]])
                P.dma("act", rc[s][:, 0:n], k.C["ropec"][:, t0:t0 + n], writes=[RT[s]])
                P.dma("act", rs[s][:, 0:n], k.C["ropes"][:, t0:t0 + n], writes=[RT[s]])
                for b, (o, bs) in enumerate(blks):
                    P.op("act", "activation", reads=[XTb[b]], writes=[JB, SSB], out=junk[0:bs, :], in_=xt[s][0:bs, b, :],
                         func=AF.Square, accum_out=ss[0:bs, b:b + 1])
                P.op("act", "activation", reads=[SSB], writes=[SSB], out=rstd[:, 0:nb], in_=ss[:, 0:nb], func=AF.Sqrt,
                     scale=1.0 / D, bias=epsc[:, 0:1])
                P.op("dve", "reciprocal", reads=[SSB], writes=[SSB], out=rstd[:, 0:nb], in_=rstd[:, 0:nb])
                for b, (o, bs) in enumerate(blks):
                    P.op("dve", "scalar_tensor_tensor", reads=[XTb[b], SSB, WBUF], writes=[HB],
                         out=h[0:bs, b, :], in0=xt[s][0:bs, b, :], scalar=rstd[0:bs, b:b + 1], in1=g0[0:bs, :],
                         op0=ALU.mult, op1=ALU.mult)
                for kc in range(8):
                    bk = kc % 4
                    for b, (o, bs) in enumerate(blks):
                        MM(bank(bk)[:, o:o + bs], h[0:bs, b, kc * 128:(kc + 1) * 128], identb[0:bs, 0:bs], True, True,
                           [HB], PB[bk])
                    if kc % 2 == 0:
                        P.op("act", "copy", reads=[PB[bk]], writes=[HTB], out=hT[:, kc, 0:n], in_=bank(bk)[:, 0:n])
                    else:
                        P.op("dve", "tensor_copy", reads=[PB[bk]], writes=[HTB], out=hT[:, kc, 0:n], in_=bank(bk)[:, 0:n])

                def proj_fm(bk, c0, m):
                    for kc in range(8):
                        MM(bank(bk)[0:m, 0:n], win[:, kc, c0:c0 + m], hT[:, kc, 0:n], kc == 0, kc == 7,
                           [HTB, WBUF], PB[bk])

                def proj_tm(bk, o, bs, c0, ncols):
                    for kc in range(8):
                        MM(bank(bk)[0:bs, 0:ncols], hT[:, kc, o:o + bs], win[:, kc, c0:c0 + ncols], kc == 0, kc == 7,
                           [HTB, WBUF], PB[bk])

                for i8 in range(8):
                    bk = 4 + (i8 % 2)
                    qb, QB = qbs[i8 % 2], QBs[i8 % 2]
                    t1, t2, T1B, T2B = t1s[i8 % 2], t2s[i8 % 2], T1Bs[i8 % 2], T2Bs[i8 % 2]
                    pbk = 6 if i8 % 2 == 0 else 7
                    sc = 1.0 if i8 < 4 else 128.0 ** -0.5
                    proj_fm(bk, i8 * 128, 128)
                    P.op("act", "activation", reads=[PB[bk]], writes=[QB], out=qb[:, 0:n], in_=bank(bk)[:, 0:n],
                         func=AF.Copy, scale=sc)
                    MM(bank(pbk)[:, 0:n], perm[:, :], qb[:, 0:n], True, True, [QB, WBUF], PB[pbk])
                    P.op("pool", "tensor_tensor", reads=[QB, RT[s]], writes=[T1B], out=t1[:, 0:n], in0=qb[:, 0:n],
                         in1=rc[s][:, 0:n], op=ALU.mult)
                    P.op("dve", "tensor_tensor", reads=[PB[pbk], RT[s]], writes=[T2B], out=t2[:, 0:n], in0=bank(pbk)[:, 0:n],
                         in1=rs[s][:, 0:n], op=ALU.mult)
                    P.op("dve", "tensor_tensor", reads=[T1B, T2B], writes=[RQB[i8]], out=rqT[:, i8, 0:n], in0=t1[:, 0:n],
                         in1=t2[:, 0:n], op=ALU.add)
                    dst = k.S["rqT"][i8, :, t0:t0 + n] if i8 < 4 else k.S["rkT"][i8 - 4, :, t0:t0 + n]
                    P.dma("sp", dst, rqT[:, i8, 0:n], reads=[RQB[i8]], writes=[Buf()])
                for b, (o, bs) in enumerate(blks):
                    bk = b % 2
                    for hh in range(4):
                        MM(bank(bk)[0:bs, hh * 128:(hh + 1) * 128], rqT[:, 4 + hh, o:o + bs], identb[:, :], True, True,
                           [RQB[4 + hh]], PB[bk])
                    wtab = wks[0:bs, :] if kind == "smp" else wk[0:bs, b, :]
                    P.op("dve", "tensor_tensor", reads=[PB[bk], WBUF], writes=[RKKB], out=rkk[0:bs, b, :],
                         in0=bank(bk)[0:bs, :], in1=wtab, op=ALU.mult)
                P.dma("act", tm(k.S["rkk"][t0:t0 + n, :]), rkk[0:pp, 0:nb, :], reads=[RKKB], writes=[Buf()])
                for b, (o, bs) in enumerate(blks):
                    proj_tm(2, o, bs, 1024, 512)
                    P.op("act", "copy", reads=[PB[2]], writes=[RVB], out=rv[0:bs, b, :], in_=bank(2)[0:bs, :])
                    proj_tm(3, o, bs, 1536, 512)
                    P.op("act", "activation", reads=[PB[3]], writes=[RGB], out=rg[0:bs, b, :], in_=bank(3)[0:bs, :],
                         func=AF.Silu)
                    proj_tm(2, o, bs, 2560, 512)
                    P.op("act", "copy", reads=[PB[2]], writes=[FKFB], out=fkf[0:bs, b, :], in_=bank(2)[0:bs, :])
                    proj_tm(3, o, bs, 3072, 512)
                    P.op("dve", "tensor_copy", reads=[PB[3]], writes=[FVFB], out=fvf[0:bs, b, :], in_=bank(3)[0:bs, :])
                    proj_tm(7, o, bs, 3584, 8)
                    P.op("dve", "tensor_tensor", reads=[PB[7], WBUF], writes=[XBB], out=xb[0:bs, b, :],
                         in0=bank(7)[0:bs, 0:8], in1=bfox[0:bs, :], op=ALU.add)
                P.dma("act", tm(k.S["rv"][t0:t0 + n, :]), rv[0:pp, 0:nb, :], reads=[RVB], writes=[Buf()])
                P.dma("act", tm(k.S["rg"][t0:t0 + n, :]), rg[0:pp, 0:nb, :], reads=[RGB], writes=[Buf()])
                P.op("act", "activation", reads=[XBB], writes=[XBB], out=xb[0:pp, 0:nb, :], in_=xb[0:pp, 0:nb, :],
                     func=AF.Exp, scale=-1.0)
                P.op("act", "activation", reads=[XBB], writes=[XBB], out=xb[0:pp, 0:nb, :], in_=xb[0:pp, 0:nb, :],
                     func=AF.Ln, bias=1.0)
                P.op("dve", "tensor_scalar", reads=[XBB], writes=[LFB], out=lf[0:pp, 0:nb, :], in0=xb[0:pp, 0:nb, :],
                     scalar1=-1.0, scalar2=None, op0=ALU.mult)
                ko = k.O["fox_k_sample"] if kind == "smp" else k.O["fox_k_prompt"][t0:t0 + n, :]
                vo = k.O["fox_v_sample"] if kind == "smp" else k.O["fox_v_prompt"][t0:t0 + n, :]
                lo = k.O["fox_logf_sample"] if kind == "smp" else k.O["fox_logf_prompt"][t0:t0 + n, :]
                P.dma("sp", tm(ko), fkf[0:pp, 0:nb, :], reads=[FKFB], writes=[Buf()])
                P.dma("sp", tm(vo), fvf[0:pp, 0:nb, :], reads=[FVFB], writes=[Buf()])
                P.dma("sp", tm(lo), lf[0:pp, 0:nb, :], reads=[LFB], writes=[Buf()])
            P.op("pool", "tensor_copy", reads=[FKFB], writes=[FKBB], out=fkb[0:pp, 0:nb, 0:512], in_=fkf[0:pp, 0:nb, :])
            P.op("pool", "tensor_copy", reads=[FVFB], writes=[FVAB],
                 out=fva[0:pp, 0:nb, :].rearrange("p b (h e) -> p b h e", e=65)[:, :, :, 0:64],
                 in_=fvf[0:pp, 0:nb, :].rearrange("p b (h e) -> p b h e", e=64))
            ktile = (t0 // TT) if kind == "past" else (8 + t0 // TT)
            for hp in range(4):
                P.dma("act", k.S["fva"][ktile, hp, 0:pp, 0:nb, :], fva[0:pp, 0:nb, 130 * hp:130 * hp + 130], reads=[FVAB],
                      writes=[Buf()])
            for hh in range(8):
                bk = hh % 2
                for b, (o, bs) in enumerate(blks):
                    MM(bank(bk)[0:64, o:o + bs], fkb[0:bs, b, hh * 64:(hh + 1) * 64], identb[0:bs, 0:bs], True, True,
                       [FKBB], PB[bk])
                if hh % 2 == 0:
                    P.op("act", "copy", reads=[PB[bk]], writes=[FKAB], out=fka[0:64, hh, 0:n], in_=bank(bk)[0:64, 0:n])
                else:
                    P.op("dve", "tensor_copy", reads=[PB[bk]], writes=[FKAB], out=fka[0:64, hh, 0:n],
                         in_=bank(bk)[0:64, 0:n])
            for hp in range(4):
                P.dma("sp", k.S["fka"][ktile, hp, :, :, 0:n], fka[:, 2 * hp:2 * hp + 2, 0:n], reads=[FKAB], writes=[Buf()])
            for b, (o, bs) in enumerate(blks):
                mm = [(trif[0:bs, 0:bs], lf[0:bs, b, :])]
                for b2 in range(b):
                    mm.append((onesf[0:128, 0:bs], lf[0:128, b2, :]))
                mm.append((onesf[0:1, 0:bs], carry[0:1, :]))
                for i, (l_, r_) in enumerate(mm):
                    MM(bank(7)[0:bs, 8 + b * 8:16 + b * 8], l_, r_, i == 0, i == len(mm) - 1, [LFB, CARB, WBUF], PB[7])
            cview = bank(7)[0:pp, 8:8 + nb * 8].rearrange("p (b h) -> p b h", h=8)
            P.op("dve", "tensor_copy", reads=[PB[7]], writes=[CSB], out=cS[0:pp, 0:nb, :], in_=cview)
            P.op("act", "activation", reads=[PB[7]], writes=[NGB], out=ngc[0:pp, 0:nb, :], in_=cview, func=AF.Copy,
                 scale=-1.0)
            P.dma("act", k.S["negc"][ktile, 0:pp, 0:nb, :], ngc[0:pp, 0:nb, :], reads=[NGB], writes=[Buf()])
            for b, (o, bs) in enumerate(blks):
                MM(bank(7)[0:1, 48:56], onesf[0:bs, 0:1], lf[0:bs, b, :], b == 0, b == nb - 1, [LFB], PB[7])
            P.op("dve", "tensor_tensor", reads=[PB[7], CARB], writes=[CARB], out=carry[:], in0=carry[:],
                 in1=bank(7)[0:1, 48:56], op=ALU.add)
            if kind == "past":
                continue
            for b, (o, bs) in enumerate(blks):
                MM(bank(6)[0:8, o:o + bs], cS[0:bs, b, :], identf[0:bs, 0:bs], True, True, [CSB], PB[6])
            P.op("dve", "tensor_copy", reads=[PB[6]], writes=[CTB], out=cT[:, 0:n], in_=bank(6)[0:8, 0:n])
            P.op("dve", "tensor_copy", reads=[CTB], writes=[C3B], out=c3[:, 0, 0:n], in_=cT[:, 0:n])
            P.op("dve", "tensor_tensor", reads=[CTB, C3B], writes=[R1B], out=r1[:, 0:n], in0=cT[:, 0:n],
                 in1=c3[:, 0, 0:n], op=ALU.subtract)
            P.op("dve", "tensor_copy", reads=[R1B], writes=[C3B], out=c3[:, 1, 0:n], in_=r1[:, 0:n])
            P.op("dve", "tensor_tensor", reads=[R1B, C3B], writes=[R2B], out=r2[:, 0:n], in0=r1[:, 0:n],
                 in1=c3[:, 1, 0:n], op=ALU.subtract)
            P.op("dve", "tensor_copy", reads=[R2B], writes=[C3B], out=c3[:, 2, 0:n], in_=r2[:, 0:n])
            for hh in range(8):
                bk = 4 + (hh % 2)
                proj_fm(bk, 2048 + hh * 64, 64)
                for p_ in range(3):
                    i0 = (hh * 3 + p_) * 3
                    MM(bank(bk)[64:67, 0:n], selc[0:8, i0:i0 + 3], c3[0:8, p_, 0:n], p_ == 0, p_ == 2, [C3B, WBUF], PB[bk])
                P.op("act", "activation", reads=[PB[bk], WBUF], writes=[FQAB], out=fqa[0:67, hh, 0:n],
                     in_=bank(bk)[0:67, 0:n], func=AF.Identity, scale=qscale[0:67, 0:1])
            for hp in range(4):
                P.dma("sp", k.S["fqa"][t0 // TT, hp, :, :, 0:n], fqa[:, 2 * hp:2 * hp + 2, 0:n], reads=[FQAB], writes=[Buf()])
            P.maybe_flush()
        P.flush()


ALL_PHASES = ("W", "A1", "A3", "A2", "C0", "F0", "S", "C1", "F1")


def core_inputs(inp, c):
    f = np.ascontiguousarray
    m = {
        "x_prompt": inp["x_prompt"][c], "x_sample": inp["x_sample"][c],
        "cache_ret_state": inp["cache_ret_state"][0, c],
        "cache_fox_k": inp["cache_fox_k"][0, c].reshape(PAST, 512),
        "cache_fox_v": inp["cache_fox_v"][0, c].reshape(PAST, 512),
        "cache_fox_logf": inp["cache_fox_logf"][0, c],
        "state_s5_re": inp["state_s5_re"][0, c], "state_s5_im": inp["state_s5_im"][0, c],
        "cache_mem_k": inp["cache_mem_k"][:, c].reshape(2, 256, 1024),
        "cache_mem_v": inp["cache_mem_v"][:, c].reshape(2, 256, 1024),
        "mem_prompt": inp["mem_prompt"][c],
        "norm_w": inp["norm_w"], "w_in_ab": inp["w_in_ab"][0], "b_fox_f": inp["b_fox_f"].reshape(1, 8),
        "ret_gn_w": inp["ret_gn_w"].reshape(1, 512), "w_out_ab": inp["w_out_ab"][0], "w_in_c": inp["w_in_c"][0],
        "s5_lambda_re": inp["s5_lambda_re"][0], "s5_lambda_im": inp["s5_lambda_im"][0],
        "s5_log_dt": inp["s5_log_dt"].reshape(1, 64), "s5_b_re": inp["s5_b_re"][0], "s5_b_im": inp["s5_b_im"][0],
        "s5_c_re": inp["s5_c_re"][0], "s5_c_im": inp["s5_c_im"][0], "s5_d": inp["s5_d"].reshape(1, D),
        "w_glu": inp["w_glu"][0], "mem_norm_w": inp["mem_norm_w"],
        "w_xq": inp["w_xq"], "w_xk": inp["w_xk"], "w_xv": inp["w_xv"], "w_xo": inp["w_xo"],
        "w_up": inp["w_up"], "w_down": inp["w_down"],
    }
    m = {k_: f(np.asarray(v, dtype=np.float32)) for k_, v in m.items()}
    for k_, v in get_consts().items():
        m["c_" + k_] = v
    return m


def run(inputs, phases=ALL_PHASES, cores=NCORES):
    inp = {k_: np.asarray(v) for k_, v in inputs.items()}
    kk = build(phases)
    in_maps = [core_inputs(inp, c) for c in range(cores)]
    res = run_bass_kernel_spmd(kk.nc, in_maps, core_ids=list(range(cores)))
    return res.results


def kernel(**inputs):
    r = run(inputs)
    B = NCORES

    def st(name, shape):
        return np.stack([np.asarray(r[c][name], dtype=np.float32).reshape(shape) for c in range(B)])
    y_prompt = st("y_prompt", (T, D))
    y_sample = st("y_sample", (TS, D))
    ret_p = st("ret_state_prompt", (4, 128, 128))[None]
    ret_s = st("ret_state_sample", (4, 128, 128))[None]
    fkp = st("fox_k_prompt", (T, 8, 64))[None]
    fvp = st("fox_v_prompt", (T, 8, 64))[None]
    flp = st("fox_logf_prompt", (T, 8))[None]
    fks = st("fox_k_sample", (TS, 8, 64))[None]
    fvs = st("fox_v_sample", (TS, 8, 64))[None]
    fls = st("fox_logf_sample", (TS, 8))[None]
    s5rp = st("s5_re_prompt", (64, 64))[None]
    s5ip = st("s5_im_prompt", (64, 64))[None]
    s5rs = st("s5_re_sample", (64, 64))[None]
    s5is = st("s5_im_sample", (64, 64))[None]
    mkp = np.stack([np.asarray(r[c]["mem_k_prompt"], dtype=np.float32).reshape(2, 256, 4, 256) for c in range(B)], 1)
    mvp = np.stack([np.asarray(r[c]["mem_v_prompt"], dtype=np.float32).reshape(2, 256, 4, 256) for c in range(B)], 1)
    return (y_prompt, y_sample, ret_p, ret_s, fkp, fvp, flp, fks, fvs, fls, s5rp, s5ip, s5rs, s5is, mkp, mvp)


def phase_A3(k, P, ps, PB, bank, WB, identf, identb, onesf, onesb):
    with contextlib.ExitStack() as st:
        def sb(name, shape, dt):
            return k.sbuf(st, "a3_" + name, shape, dt)
        caus = sb("caus", [128, 128], BF16)
        CW = Buf("a3c")
        P.dma("sp", caus[:], k.C["caus_b"], writes=[CW])
        qas = [sb("qa%d" % i, [67, 2, 2 * TT], BF16) for i in range(2)]
        QABs = [Buf("qa%d" % i) for i in range(2)]
        NKV = 4
        ka = [sb("ka%d" % i, [67, 2, TT], BF16) for i in range(2 * NKV)]
        va = [sb("va%d" % i, [128, 4, 2 * 65], BF16) for i in range(2 * NKV)]
        ng = [sb("ng%d" % i, [128, 4, 8], F32) for i in range(2 * NKV)]
        KVB = [Buf("kk%d" % i) for i in range(2 * NKV)]
        VVB = [Buf("kvv%d" % i) for i in range(2 * NKV)]
        NNB = [Buf("kn%d" % i) for i in range(2 * NKV)]
        pT = [sb("pT%d" % i, [128, TT], BF16) for i in range(4)]
        PTB = [Buf("pT%d" % i) for i in range(4)]
        osbs = [sb("osb%d" % i, [65, TT], F32) for i in range(4)]
        OSBs = [Buf("osb%d" % i) for i in range(4)]
        rrows = [sb("rrow%d" % i, [65, TT], BF16) for i in range(4)]
        RRBs = [Buf("rrow%d" % i) for i in range(4)]
        foT = sb("foT", [64, 4, TT], BF16)
        FOB = Buf("foT")

        def MM(out, lhsT, rhs, start, stop, R, Wb):
            P.op("pe", "matmul", reads=R, writes=[Wb], out=out, lhsT=lhsT, rhs=rhs, start=start, stop=stop)

        supers = []
        for Q in range(T // (2 * TT)):
            subs = [(2 * Q + s_, (2 * Q + s_) * TT, TT) for s_ in range(2)]
            ktl = [(8 + kt, TT, kt) for kt in range(2 * Q + 2)]
            supers.append(("prm", subs, ktl))
        supers.append(("smp", [(16, T, TS)], [(pt, TT, -1) for pt in range(PAST // TT)] + [(24, TS, 16)]))
        if "a3_q" in DBG:
            supers = [supers[i] for i in DBG["a3_q"]]
        groups = [(kind, subs, ktl, hp) for (kind, subs, ktl) in supers for hp in range(4)]
        pti = 0

        def load_kv(gi, kti):
            kind, subs, ktl, hp = groups[gi]
            ktile, nk, kpos = ktl[kti]
            sl = (gi % 2) * NKV + kti % NKV
            pk = min(128, nk)
            nkb = (nk + 127) // 128
            P.dma("sp", ka[sl][:, :, 0:nk], k.S["fka"][ktile, hp, :, :, 0:nk], writes=[KVB[sl]])
            P.dma("pool", va[sl][0:pk, 0:nkb, :], k.S["fva"][ktile, hp, 0:pk, 0:nkb, :], writes=[VVB[sl]])
            P.dma("pool", ng[sl][0:pk, 0:nkb, :], k.S["negc"][ktile, 0:pk, 0:nkb, :], writes=[NNB[sl]])

        def load_q(gi):
            kind, subs, ktl, hp = groups[gi]
            for s_, (qidx, t0, n) in enumerate(subs):
                P.dma("sp", qas[gi % 2][:, :, s_ * TT:s_ * TT + n], k.S["fqa"][qidx, hp, :, :, 0:n], writes=[QABs[gi % 2]])

        def prefetch(gi):
            if gi < len(groups):
                load_q(gi)
                for kti in range(min(NKV, len(groups[gi][2]))):
                    load_kv(gi, kti)
        prefetch(0)
        for gi, (kind, subs, ktl, hp) in enumerate(groups):
            qa, QAB = qas[gi % 2], QABs[gi % 2]
            prefetch(gi + 1)
            steps = []
            for kti, (ktile, nk, kpos) in enumerate(ktl):
                kblks = blocks(nk)
                for hh in range(2):
                    for jb, (jo, js) in enumerate(kblks):
                        for s_, (qidx, t0, n) in enumerate(subs):
                            if kpos > qidx:
                                continue
                            steps.append((kti, hh, jb, jo, js, kpos == qidx, s_, n))
            firsts, lasts = {}, {}
            for i, stp in enumerate(steps):
                a_ = stp[1] * 2 + stp[6]
                firsts.setdefault(a_, i)
                lasts[a_] = i
            LOOK = 2
            info = {}
            for i in range(len(steps) + LOOK):
                if i < len(steps):
                    kti, hh, jb, jo, js, diag, s_, n = steps[i]
                    sl = (gi % 2) * NKV + kti % NKV
                    h8 = 2 * hp + hh
                    c0 = jo if diag else 0
                    sbk = 4 + (pti % 4)
                    pb = pti % 4
                    pti += 1
                    info[i] = (sl, pb, c0)
                    MM(bank(sbk)[0:js, c0:n], ka[sl][0:67, hh, jo:jo + js], qa[0:67, hh, s_ * TT + c0:s_ * TT + n], True, True,
                       [KVB[sl], QAB], PB[sbk])
                    P.op("act", "activation", reads=[PB[sbk], NNB[sl]], writes=[PTB[pb]], out=pT[pb][0:js, c0:n],
                         in_=bank(sbk)[0:js, c0:n], func=AF.Exp, bias=ng[sl][0:js, jb, h8:h8 + 1], scale=1.0)
                    if diag:
                        P.op("pool", "tensor_tensor", reads=[PTB[pb], CW], writes=[PTB[pb]],
                             out=pT[pb][0:js, c0:c0 + js], in0=pT[pb][0:js, c0:c0 + js], in1=caus[0:js, 0:js], op=ALU.mult)
                j = i - LOOK
                if j >= 0:
                    kti, hh, jb, jo, js, diag, s_, n = steps[j]
                    sl, pb, c0 = info[j]
                    a_ = hh * 2 + s_
                    MM(bank(a_)[0:65, c0:n], va[sl][0:js, jb, hh * 65:(hh + 1) * 65], pT[pb][0:js, c0:n], firsts[a_] == j,
                       lasts[a_] == j, [VVB[sl], PTB[pb]], PB[a_])
                    if (j + 1 == len(steps) or steps[j + 1][0] != kti) and kti + NKV < len(ktl):
                        load_kv(gi, kti + NKV)
            accs = [(hh, s_, subs[s_][1], subs[s_][2]) for hh in range(2) for s_ in range(len(subs))]
            for (hh, s_, t0, n) in accs:
                a_ = hh * 2 + s_
                P.op("act", "copy", reads=[PB[a_]], writes=[OSBs[a_]], out=osbs[a_][0:65, 0:n], in_=bank(a_)[0:65, 0:n])
            for (hh, s_, t0, n) in accs:
                a_ = hh * 2 + s_
                P.op("dve", "reciprocal", reads=[OSBs[a_]], writes=[RRBs[a_]], out=rrows[a_][64:65, 0:n],
                     in_=osbs[a_][64:65, 0:n])
            for (hh, s_, t0, n) in accs:
                a_ = hh * 2 + s_
                MM(bank(4 + a_)[0:64, 0:n], onesb[64:65, 0:64], rrows[a_][64:65, 0:n], True, True, [RRBs[a_]], PB[4 + a_])
            for (hh, s_, t0, n) in accs:
                a_ = hh * 2 + s_
                P.op("dve", "tensor_tensor", reads=[PB[4 + a_], OSBs[a_]], writes=[FOB], out=foT[0:64, a_, 0:n],
                     in0=osbs[a_][0:64, 0:n], in1=bank(4 + a_)[0:64, 0:n], op=ALU.mult)
            for (hh, s_, t0, n) in accs:
                a_ = hh * 2 + s_
                P.dma("sp", k.S["foT"][2 * hp + hh, :, t0:t0 + n], foT[:, a_, 0:n], reads=[FOB], writes=[Buf()])
            P.maybe_flush()
        P.flush()


def rstd_ops(P, out, in_, scale, epsc, R, W):
    P.op("act", "activation", reads=R, writes=W, out=out, in_=in_, func=AF.Sqrt, scale=scale, bias=epsc)
    P.op("dve", "reciprocal", reads=W, writes=W, out=out, in_=out)


def post_norm_residual(P, pso, PSB, xt_blk, XTB, g_bc, GB, junk, JB, ssv, SSB, epsc, tmp, TMPB, bs):
    P.op("act", "activation", reads=PSB, writes=[JB, SSB], out=junk[0:bs, :], in_=pso, func=AF.Square,
         accum_out=ssv[0:bs, 0:1])
    rstd_ops(P, ssv[0:bs, 1:2], ssv[0:bs, 0:1], 1.0 / D, epsc[0:bs, 0:1], [SSB], [SSB])
    P.op("dve", "scalar_tensor_tensor", reads=PSB + [SSB, GB], writes=[TMPB], out=tmp[0:bs, :], in0=pso,
         scalar=ssv[0:bs, 1:2], in1=g_bc[0:bs, :], op0=ALU.mult, op1=ALU.mult)
    P.op("pool", "tensor_tensor", reads=[TMPB, XTB], writes=[XTB], out=xt_blk, in0=xt_blk, in1=tmp[0:bs, :], op=ALU.add)


def pre_norm_elem(P, xt, XTB, g_bc, GB, junk, JB, ssv, SSB, epsc, h, HB, blks):
    for b, (o, bs) in enumerate(blks):
        P.op("act", "activation", reads=[XTB], writes=[JB, SSB], out=junk[0:bs, :], in_=xt[0:bs, b, :], func=AF.Square,
             accum_out=ssv[0:bs, 2 * b:2 * b + 1])
        rstd_ops(P, ssv[0:bs, 2 * b + 1:2 * b + 2], ssv[0:bs, 2 * b:2 * b + 1], 1.0 / D, epsc[0:bs, 0:1], [SSB], [SSB])
        P.op("dve", "scalar_tensor_tensor", reads=[XTB, SSB, GB], writes=[HB], out=h[0:bs, b, :], in0=xt[0:bs, b, :],
             scalar=ssv[0:bs, 2 * b + 1:2 * b + 2], in1=g_bc[0:bs, :], op0=ALU.mult, op1=ALU.mult)


def pre_norm_T(P, bank, PB, banks, h, HB, hT, HTB, identb, blks, n):
    for kc in range(8):
        bk = banks[kc % len(banks)]
        for b, (o, bs) in enumerate(blks):
            P.op("pe", "matmul", reads=[HB], writes=[PB[bk]], out=bank(bk)[:, o:o + bs],
                 lhsT=h[0:bs, b, kc * 128:(kc + 1) * 128], rhs=identb[0:bs, 0:bs], start=True, stop=True)
        if kc % 2 == 0:
            P.op("act", "copy", reads=[PB[bk]], writes=[HTB], out=hT[:, kc, 0:n], in_=bank(bk)[:, 0:n])
        else:
            P.op("dve", "tensor_copy", reads=[PB[bk]], writes=[HTB], out=hT[:, kc, 0:n], in_=bank(bk)[:, 0:n])


def pre_norm_hT(P, bank, PB, banks, xt, XTB, g_bc, GB, junk, JB, ssv, SSB, epsc, h, HB, hT, HTB, identb, blks, n):
    pre_norm_elem(P, xt, XTB, g_bc, GB, junk, JB, ssv, SSB, epsc, h, HB, blks)
    pre_norm_T(P, bank, PB, banks, h, HB, hT, HTB, identb, blks, n)


def x_src(k, first_layer_input, kind, t0, n):
    if first_layer_input:
        return k.I["x_sample"] if kind == "smp" else k.I["x_prompt"][t0:t0 + n, :]
    return k.S["x"][t0:t0 + n, :]


def phase_A2(k, P, ps, PB, bank, WB, identf, identb, onesf, onesb):
    lg = np.log(1.0 - 2.0 ** (-5.0 - np.arange(4, dtype=np.float64)))
    with contextlib.ExitStack() as st:
        def sb(name, shape, dt):
            return k.sbuf(st, "a2_" + name, shape, dt)
        rmask = sb("rmask", [128, 4, 512], F32)
        wq = sb("wq", [128, 4, 512], F32)
        gnw = sb("gnw", [128, 512], F32)
        g1 = sb("g1", [128, D], F32)
        wor = sb("wor", [128, 4, D], BF16)
        wof = sb("wof", [64, 8, D], BF16)
        epsc = sb("epsc", [128, 1], F32)
        CW = Buf("a2c")
        P.dma("sp", rmask[:], k.C["rmask"], writes=[CW])
        P.dma("sp", wq[:], k.C["wq"].rearrange("p b h e -> p b (h e)"), writes=[CW])
        P.dma("sp", gnw[:], k.I["ret_gn_w"][0:1, :].broadcast_to([128, 512]), writes=[CW])
        P.dma("sp", g1[:], k.I["norm_w"][0, 1:2, :].broadcast_to([128, D]), writes=[CW])
        P.dma("act", wor[:], WB["w_out_ab"][0:512, :].rearrange("(kc p) n -> p kc n", p=128), writes=[CW])
        P.dma("act", wof[:], WB["w_out_ab"][512:1024, :].rearrange("(h p) n -> p h n", p=64), writes=[CW])
        P.op("pool", "memset", writes=[CW], ap=epsc[:], constant=EPS)
        Sf = sb("Sf", [128, 4, 128], F32)
        Sb = sb("Sb", [128, 4, 128], BF16)
        SFB, SBB = Buf("Sf"), Buf("Sb")
        qTs = [sb("qT%d" % i, [128, 4, TT], BF16) for i in range(2)]
        kTs = [sb("kT%d" % i, [128, 4, TT], BF16) for i in range(2)]
        rkks = [sb("rkk%d" % i, [128, 4, 512], BF16) for i in range(2)]
        rvs = [sb("rv%d" % i, [128, 4, 512], BF16) for i in range(2)]
        rgs = [sb("rg%d" % i, [128, 4, 512], BF16) for i in range(2)]
        foTs = [sb("foT%d" % i, [64, 8, TT], BF16) for i in range(2)]
        xts = [sb("xt%d" % i, [128, 4, D], F32) for i in range(2)]
        INBs = [Buf("a2in%d" % i) for i in range(2)]
        XTBs = [Buf("a2x%d" % i) for i in range(2)]
        PT = sb("PT", [128, 4, 4, TT], BF16)
        PTB = [Buf("PT%d" % i) for i in range(4)]
        ro = sb("ro", [128, 512], F32)
        t1 = sb("t1", [128, 512], F32)
        sq = sb("sq", [128, 512], F32)
        ROB, T1B, SQB = Buf("ro"), Buf("t1"), Buf("sq")
        st8 = sb("st8", [128, 24], F32)
        STB = Buf("st8")
        mixr = sb("mixr", [128, 512], BF16)
        MXB = Buf("mixr")
        roT = sb("roT", [128, 4, TT], BF16)
        ROTB = Buf("roT")
        junk = sb("junk", [128, D], BF16)
        JB = Buf("junk")
        ssv = sb("ssv", [128, 8], F32)
        SSB = Buf("ssv")
        tmp = sb("tmp", [128, D], F32)
        TMPB = Buf("tmp")

        def MM(out, lhsT, rhs, start, stop, R, Wb):
            P.op("pe", "matmul", reads=R, writes=[Wb], out=out, lhsT=lhsT, rhs=rhs, start=start, stop=stop)

        seqs = [("prm", t * TT, TT) for t in range(T // TT)] + [("smp", T, TS)]
        if "a2_seqs" in DBG:
            seqs = [seqs[i] for i in DBG["a2_seqs"]]
        P.op("pool", "memset", writes=[SFB], ap=Sf[:], constant=0.0)
        P.op("pool", "memset", writes=[SBB], ap=Sb[:], constant=0.0)
        def load_in(ti):
            kind, t0, n = seqs[ti]
            pp = min(128, n)
            nb = len(blocks(n))
            j = ti % 2

            def tm_(ap2d):
                return ap2d.rearrange("(b p) d -> p b d", p=pp)
            P.dma("sp", qTs[j][:, :, 0:n], k.S["rqT"][:, :, t0:t0 + n].rearrange("h d t -> d h t"), writes=[INBs[j]])
            P.dma("act", kTs[j][:, :, 0:n], k.S["rkT"][:, :, t0:t0 + n].rearrange("h d t -> d h t"), writes=[INBs[j]])
            P.dma("sp", rkks[j][0:pp, 0:nb, :], tm_(k.S["rkk"][t0:t0 + n, :]), writes=[INBs[j]])
            P.dma("act", rvs[j][0:pp, 0:nb, :], tm_(k.S["rv"][t0:t0 + n, :]), writes=[INBs[j]])
            P.dma("sp", rgs[j][0:pp, 0:nb, :], tm_(k.S["rg"][t0:t0 + n, :]), writes=[INBs[j]])
            P.dma("act", foTs[j][:, :, 0:n], k.S["foT"][:, :, t0:t0 + n].rearrange("h r t -> r h t"), writes=[INBs[j]])
            P.dma("sp", xts[j][0:pp, 0:nb, :], tm_(x_src(k, True, kind, t0, n)), writes=[XTBs[j]])
        load_in(0)
        for ti, (kind, t0, n) in enumerate(seqs):
            blks = blocks(n)
            nb = len(blks)
            pp = min(128, n)
            dec = np.exp(lg * (512.0 if kind == "prm" else 32.0))
            j_ = ti % 2
            qT, kT, rkk, rv, rg, foT, xt = qTs[j_], kTs[j_], rkks[j_], rvs[j_], rgs[j_], foTs[j_], xts[j_]
            INB, XTB = INBs[j_], XTBs[j_]
            if ti + 1 < len(seqs):
                load_in(ti + 1)
            if kind == "smp":
                P.dma("sp", Sf[:], k.I["cache_ret_state"].rearrange("h d e -> d h e"), writes=[SFB])
                P.op("act", "copy", reads=[SFB], writes=[SBB], out=Sb[:], in_=Sf[:])

            def tm(ap2d):
                return ap2d.rearrange("(b p) d -> p b d", p=pp)
            for hh in range(4):
                for jb, (jo, js) in enumerate(blks):
                    bk = (hh * 4 + jb) % 2
                    MM(bank(bk)[0:js, jo:n], kT[:, hh, jo:jo + js], qT[:, hh, jo:n], True, True, [INB], PB[bk])
                    P.op("dve", "tensor_tensor", reads=[PB[bk], CW], writes=[PTB[hh]], out=PT[0:js, hh, jb, jo:n],
                         in0=bank(bk)[0:js, jo:n], in1=rmask[0:js, hh, 0:n - jo], op=ALU.mult)
            for ib, (io, is_) in enumerate(blks):
                for hh in range(4):
                    for jb in range(ib + 1):
                        jo, js = blks[jb]
                        MM(bank(2)[0:is_, hh * 128:(hh + 1) * 128], PT[0:js, hh, jb, io:io + is_],
                           rv[0:js, jb, hh * 128:(hh + 1) * 128], jb == 0, jb == ib, [PTB[hh], INB], PB[2])
                    MM(bank(3)[0:is_, hh * 128:(hh + 1) * 128], qT[:, hh, io:io + is_], Sb[:, hh, :], True, True,
                       [INB, SBB], PB[3])
                P.op("dve", "tensor_tensor", reads=[PB[3], CW], writes=[T1B], out=t1[0:is_, :], in0=bank(3)[0:is_, :],
                     in1=wq[0:is_, ib, :], op=ALU.mult)
                P.op("dve", "tensor_tensor", reads=[PB[2], T1B], writes=[ROB], out=ro[0:is_, :], in0=bank(2)[0:is_, :],
                     in1=t1[0:is_, :], op=ALU.add)
                P.op("act", "activation", reads=[ROB], writes=[SQB], out=sq[0:is_, :], in_=ro[0:is_, :], func=AF.Square)
                ro3 = ro[0:is_, :].rearrange("p (h e) -> p h e", e=128)
                sq3 = sq[0:is_, :].rearrange("p (h e) -> p h e", e=128)
                P.op("dve", "tensor_reduce", reads=[ROB], writes=[STB], out=st8[0:is_, 0:4], in_=ro3, axis=AX.X, op=ALU.add)
                P.op("dve", "tensor_reduce", reads=[SQB], writes=[STB], out=st8[0:is_, 4:8], in_=sq3, axis=AX.X, op=ALU.add)
                P.op("dve", "tensor_scalar", reads=[STB], writes=[STB], out=st8[0:is_, 8:12], in0=st8[0:is_, 0:4],
                     scalar1=1.0 / 128, scalar2=None, op0=ALU.mult)
                P.op("dve", "tensor_tensor", reads=[STB], writes=[STB], out=st8[0:is_, 12:16], in0=st8[0:is_, 8:12],
                     in1=st8[0:is_, 8:12], op=ALU.mult)
                P.op("dve", "scalar_tensor_tensor", reads=[STB], writes=[STB], out=st8[0:is_, 16:20], in0=st8[0:is_, 4:8],
                     scalar=1.0 / 128, in1=st8[0:is_, 12:16], op0=ALU.mult, op1=ALU.subtract)
                rstd_ops(P, st8[0:is_, 20:24], st8[0:is_, 16:20], 1.0, epsc[0:is_, 0:1], [STB], [STB])
                t13 = t1[0:is_, :].rearrange("p (h e) -> p h e", e=128)
                P.op("dve", "tensor_tensor", reads=[ROB, STB], writes=[T1B], out=t13, in0=ro3,
                     in1=st8[0:is_, 8:12].unsqueeze(2).broadcast_to([is_, 4, 128]), op=ALU.subtract)
                P.op("dve", "tensor_tensor", reads=[T1B, STB], writes=[T1B], out=t13, in0=t13,
                     in1=st8[0:is_, 20:24].unsqueeze(2).broadcast_to([is_, 4, 128]), op=ALU.mult)
                P.op("pool", "tensor_tensor", reads=[T1B, CW], writes=[T1B], out=t1[0:is_, :], in0=t1[0:is_, :],
                     in1=gnw[0:is_, :], op=ALU.mult)
                P.op("pool", "tensor_tensor", reads=[T1B, INB], writes=[MXB], out=mixr[0:is_, :], in0=t1[0:is_, :],
                     in1=rg[0:is_, ib, :], op=ALU.mult)
                for hh in range(4):
                    MM(bank(4 + hh)[:, io:io + is_], mixr[0:is_, hh * 128:(hh + 1) * 128], identb[0:is_, 0:is_], True, True,
                       [MXB], PB[4 + hh])
            for hh in range(4):
                if hh % 2 == 0:
                    P.op("act", "copy", reads=[PB[4 + hh]], writes=[ROTB], out=roT[:, hh, 0:n], in_=bank(4 + hh)[:, 0:n])
                else:
                    P.op("dve", "tensor_copy", reads=[PB[4 + hh]], writes=[ROTB], out=roT[:, hh, 0:n],
                         in_=bank(4 + hh)[:, 0:n])
            for hh in range(4):
                for jb, (jo, js) in enumerate(blks):
                    MM(bank(2)[:, hh * 128:(hh + 1) * 128], rkk[0:js, jb, hh * 128:(hh + 1) * 128],
                       rv[0:js, jb, hh * 128:(hh + 1) * 128], jb == 0, jb == nb - 1, [INB], PB[2])
            for hh in range(4):
                P.op("dve", "scalar_tensor_tensor", reads=[PB[2], SFB], writes=[SFB], out=Sf[:, hh, :], in0=Sf[:, hh, :],
                     scalar=float(dec[hh]), in1=bank(2)[:, hh * 128:(hh + 1) * 128], op0=ALU.mult, op1=ALU.add)
            P.op("act", "copy", reads=[SFB], writes=[SBB], out=Sb[:], in_=Sf[:])
            last_of_kind = (kind == "smp") or (t0 + n == T)
            if last_of_kind:
                dst = k.O["ret_state_sample"] if kind == "smp" else k.O["ret_state_prompt"]
                P.dma("sp", dst.rearrange("h d e -> d h e"), Sf[:], reads=[SFB], writes=[Buf()])
            for ib, (io, is_) in enumerate(blks):
                for half in range(2):
                    bk = 2 * (ib % 2) + half
                    for i in range(12):
                        if i < 4:
                            l_, r_ = roT[:, i, io:io + is_], wor[:, i, half * 512:(half + 1) * 512]
                        else:
                            l_, r_ = foT[0:64, i - 4, io:io + is_], wof[0:64, i - 4, half * 512:(half + 1) * 512]
                        MM(bank(bk)[0:is_, :], l_, r_, i == 0, i == 11, [ROTB, INB, CW], PB[bk])
                b0 = 2 * (ib % 2)
                pso = ps[0:is_, b0 * 512:b0 * 512 + 1024]
                post_norm_residual(P, pso, [PB[b0], PB[b0 + 1]], xt[0:is_, ib, :], XTB, g1, CW, junk, JB, ssv, SSB, epsc,
                                   tmp, TMPB, is_)
            P.dma("sp", tm(k.S["x"][t0:t0 + n, :]), xt[0:pp, 0:nb, :], reads=[XTB], writes=[Buf()])
            P.maybe_flush()
        P.flush()


def phase_C(k, P, ps, PB, bank, WB, identf, identb, onesf, onesb, l):
    with contextlib.ExitStack() as st:
        def sb(name, shape, dt):
            return k.sbuf(st, "c%d_" % l + name, shape, dt)
        wxq = sb("wxq", [128, 8, D], BF16)
        wxo = sb("wxo", [128, 8, D], BF16)
        wkv = sb("wkv", [128, 8, D], BF16)
        g2 = sb("g2", [128, D], F32)
        g3 = sb("g3", [128, D], F32)
        gm = sb("gm", [128, D], F32)
        epsc = sb("epsc", [128, 1], F32)
        CW = Buf("cw")
        WKVB = Buf("wkv")
        P.dma("sp", wxq[:], WB["w_xq%d" % l].rearrange("(kc p) n -> p kc n", p=128), writes=[CW])
        P.dma("act", wxo[:], WB["w_xo%d" % l].rearrange("(kc p) n -> p kc n", p=128), writes=[CW])
        P.dma("sp", g2[:], k.I["norm_w"][l, 2:3, :].broadcast_to([128, D]), writes=[CW])
        P.dma("sp", g3[:], k.I["norm_w"][l, 3:4, :].broadcast_to([128, D]), writes=[CW])
        P.dma("sp", gm[:], k.I["mem_norm_w"][l:l + 1, :].broadcast_to([128, D]), writes=[CW])
        P.op("pool", "memset", writes=[CW], ap=epsc[:], constant=EPS)
        xts = [sb("xt%d" % i, [128, 4, D], F32) for i in range(2)]
        XTBs = [Buf("xt%d" % i) for i in range(2)]
        xt, XTB = xts[0], XTBs[0]
        junk2 = sb("junk2", [128, D], BF16)
        JB2 = Buf("junk2")
        ssv2 = sb("ssv2", [128, 8], F32)
        SSB2 = Buf("ssv2")
        h = sb("h", [128, 4, D], BF16)
        HB = Buf("h")
        hT = sb("hT", [128, 8, TT], BF16)
        HTB = Buf("hT")
        junk = sb("junk", [128, D], BF16)
        JB = Buf("junk")
        ssv = sb("ssv", [128, 8], F32)
        SSB = Buf("ssv")
        tmp = sb("tmp", [128, D], F32)
        TMPB = Buf("tmp")
        mkf = sb("mkf", [128, 2, D], F32)
        MKFB = Buf("mkf")
        mkb = sb("mkb", [128, 2, D], BF16)
        MKBB = Buf("mkb")
        mkT = sb("mkT", [128, 8, 256], BF16)
        MKTB = Buf("mkT")
        mvb = sb("mvb", [128, 2, D], BF16)
        MVBB = Buf("mvb")
        qT = sb("qT", [128, 8, TT], BF16)
        QTB = Buf("qT")
        oT = sb("oT", [128, 8, TT], BF16)
        OTB = Buf("oT")
        pT = [sb("pT%d" % i, [128, TT], BF16) for i in range(2)]
        PTB = [Buf("pT%d" % i) for i in range(2)]
        rec = sb("rec", [128, TT], F32)
        RECB = Buf("rec")

        def MM(out, lhsT, rhs, start, stop, R, Wb):
            P.op("pe", "matmul", reads=R, writes=[Wb], out=out, lhsT=lhsT, rhs=rhs, start=start, stop=stop)

        mblks = [(0, 128), (128, 128)]

        def kv_finish():
            P.op("pool", "tensor_copy", reads=[MKFB], writes=[MKBB], out=mkb[:], in_=mkf[:])
            for c8 in range(8):
                bk = c8 % 2
                for mb in range(2):
                    MM(bank(bk)[:, mb * 128:(mb + 1) * 128], mkb[:, mb, c8 * 128:(c8 + 1) * 128], identb[:, :], True, True,
                       [MKBB], PB[bk])
                if c8 % 2 == 0:
                    P.op("act", "copy", reads=[PB[bk]], writes=[MKTB], out=mkT[:, c8, :], in_=bank(bk)[:, 0:256])
                else:
                    P.op("dve", "tensor_copy", reads=[PB[bk]], writes=[MKTB], out=mkT[:, c8, :], in_=bank(bk)[:, 0:256])

        def setup_prompt_mem():
            P.dma("sp", xt[:, 0:2, :], k.I["mem_prompt"].rearrange("(b p) d -> p b d", p=128), writes=[XTB])
            pre_norm_hT(P, bank, PB, [0, 1, 2, 3], xt, XTB, gm, CW, junk, JB, ssv, SSB, epsc, h, HB, hT, HTB, identb,
                        mblks, 256)
            for which in ("k", "v"):
                P.dma("sp", wkv[:], WB["w_x%s%d" % (which, l)].rearrange("(kc p) n -> p kc n", p=128), writes=[WKVB])
                for mb in range(2):
                    for half in range(2):
                        bk = 4 + half
                        for kc in range(8):
                            MM(bank(bk)[:, :], hT[:, kc, mb * 128:(mb + 1) * 128], wkv[:, kc, half * 512:(half + 1) * 512],
                               kc == 0, kc == 7, [HTB, WKVB], PB[bk])
                        P.op("act", "copy", reads=[PB[bk]], writes=[MKFB], out=mkf[:, mb, half * 512:(half + 1) * 512],
                             in_=bank(bk)[:, :])
                dst = k.O["mem_k_prompt" if which == "k" else "mem_v_prompt"][l]
                P.dma("sp", dst.rearrange("(b p) d -> p b d", p=128), mkf[:], reads=[MKFB], writes=[Buf()])
                if which == "k":
                    kv_finish()
                else:
                    P.op("pool", "tensor_copy", reads=[MKFB], writes=[MVBB], out=mvb[:], in_=mkf[:])

        def setup_sample_mem():
            P.dma("sp", mkf[:], k.I["cache_mem_k"][l].rearrange("(b p) d -> p b d", p=128), writes=[MKFB])
            kv_finish()
            P.dma("sp", mkf[:], k.I["cache_mem_v"][l].rearrange("(b p) d -> p b d", p=128), writes=[MKFB])
            P.op("pool", "tensor_copy", reads=[MKFB], writes=[MVBB], out=mvb[:], in_=mkf[:])

        seqs = [("prm", t * TT, TT) for t in range(T // TT)] + [("smp", T, TS)]
        if "c_seqs" in DBG:
            seqs = [seqs[i] for i in DBG["c_seqs"]]
        def tile_geom(ti):
            kind, t0, n = seqs[ti]
            return kind, t0, n, blocks(n), min(128, n)

        def load_x(ti):
            kind, t0, n, blks, pp = tile_geom(ti)
            P.dma("sp", xts[ti % 2][0:pp, 0:len(blks), :], k.S["x"][t0:t0 + n, :].rearrange("(b p) d -> p b d", p=pp),
                  writes=[XTBs[ti % 2]])

        def prenorm_e(ti):
            kind, t0, n, blks, pp = tile_geom(ti)
            pre_norm_elem(P, xts[ti % 2], XTBs[ti % 2], g2, CW, junk2, JB2, ssv2, SSB2, epsc, h, HB, blks)

        def prenorm_t(ti):
            kind, t0, n, blks, pp = tile_geom(ti)
            pre_norm_T(P, bank, PB, [0, 1, 2, 3], h, HB, hT, HTB, identb, blks, n)
        if seqs[0][0] == "prm":
            setup_prompt_mem()
        load_x(0)
        prenorm_e(0)
        prenorm_t(0)
        for ti, (kind, t0, n) in enumerate(seqs):
            blks = blocks(n)
            nb = len(blks)
            pp = min(128, n)
            xt, XTB = xts[ti % 2], XTBs[ti % 2]
            if ti + 1 < len(seqs):
                load_x(ti + 1)
            if kind == "smp":
                setup_sample_mem()

            def tm(ap2d):
                return ap2d.rearrange("(b p) d -> p b d", p=pp)
            for mc in range(8):
                bk = 4 + mc % 2
                for kc in range(8):
                    MM(bank(bk)[:, 0:n], wxq[:, kc, mc * 128:(mc + 1) * 128], hT[:, kc, 0:n], kc == 0, kc == 7, [HTB, CW], PB[bk])
                if mc % 2 == 0:
                    P.op("act", "copy", reads=[PB[bk]], writes=[QTB], out=qT[:, mc, 0:n], in_=bank(bk)[:, 0:n])
                else:
                    P.op("dve", "tensor_copy", reads=[PB[bk]], writes=[QTB], out=qT[:, mc, 0:n], in_=bank(bk)[:, 0:n])
            for hh in range(4):
                for mb in range(2):
                    for dc in range(2):
                        MM(bank(mb)[:, 0:n], mkT[:, hh * 2 + dc, mb * 128:(mb + 1) * 128], qT[:, hh * 2 + dc, 0:n],
                           dc == 0, dc == 1, [MKTB, QTB], PB[mb])
                    P.op("act", "activation", reads=[PB[mb]], writes=[PTB[mb]], out=pT[mb][:, 0:n], in_=bank(mb)[:, 0:n],
                         func=AF.Exp, scale=1.0 / 16.0)
                for mb in range(2):
                    MM(bank(2)[:, 0:n], onesb[:, :], pT[mb][:, 0:n], mb == 0, mb == 1, [PTB[mb]], PB[2])
                P.op("dve", "reciprocal", reads=[PB[2]], writes=[RECB], out=rec[:, 0:n], in_=bank(2)[:, 0:n])
                for dc in range(2):
                    bk = 3 if dc == 0 else 6
                    for mb in range(2):
                        MM(bank(bk)[:, 0:n], mvb[:, mb, hh * 256 + dc * 128:hh * 256 + dc * 128 + 128], pT[mb][:, 0:n],
                           mb == 0, mb == 1, [MVBB, PTB[mb]], PB[bk])
                    P.op("dve", "tensor_tensor", reads=[PB[bk], RECB], writes=[OTB], out=oT[:, hh * 2 + dc, 0:n],
                         in0=bank(bk)[:, 0:n], in1=rec[:, 0:n], op=ALU.mult)
            for ib, (io, is_) in enumerate(blks):
                b0 = 4 if ib % 2 == 0 else 6
                if b0 == 6:
                    b0 = 0
                for half in range(2):
                    bk = b0 + half
                    for kc in range(8):
                        MM(bank(bk)[0:is_, :], oT[:, kc, io:io + is_], wxo[:, kc, half * 512:(half + 1) * 512], kc == 0, kc == 7,
                           [OTB, CW], PB[bk])
                if ib == 0 and ti + 1 < len(seqs):
                    prenorm_e(ti + 1)
                pso = ps[0:is_, b0 * 512:b0 * 512 + 1024]
                post_norm_residual(P, pso, [PB[b0], PB[b0 + 1]], xt[0:is_, ib, :], XTB, g3, CW, junk, JB, ssv, SSB, epsc,
                                   tmp, TMPB, is_)
            if ti + 1 < len(seqs):
                prenorm_t(ti + 1)
            P.dma("sp", tm(k.S["x"][t0:t0 + n, :]), xt[0:pp, 0:nb, :], reads=[XTB], writes=[Buf()])
            P.maybe_flush()
        P.flush()


def phase_F(k, P, ps, PB, bank, WB, identf, identb, onesf, onesb, l, final):
    with contextlib.ExitStack() as st:
        def sb(name, shape, dt):
            return k.sbuf(st, "f%d_" % l + name, shape, dt)
        g4 = sb("g4", [128, D], F32)
        g5 = sb("g5", [128, D], F32)
        epsc = sb("epsc", [128, 1], F32)
        CW = Buf("cw")
        P.dma("sp", g4[:], k.I["norm_w"][l, 4:5, :].broadcast_to([128, D]), writes=[CW])
        P.dma("sp", g5[:], k.I["norm_w"][l, 5:6, :].broadcast_to([128, D]), writes=[CW])
        P.op("pool", "memset", writes=[CW], ap=epsc[:], constant=EPS)
        up = [sb("up%d" % i, [128, 8, 1024], BF16) for i in range(2)]
        dn = [sb("dn%d" % i, [128, 8, 1024], BF16) for i in range(2)]
        UPB = [Buf("up%d" % i) for i in range(2)]
        DNB = [Buf("dn%d" % i) for i in range(2)]
        xts = [sb("xt%d" % i, [128, 4, D], F32) for i in range(2)]
        XTBs = [Buf("xt%d" % i) for i in range(2)]
        junk2 = sb("junk2", [128, D], BF16)
        JB2 = Buf("junk2")
        ssv2 = sb("ssv2", [128, 8], F32)
        SSB2 = Buf("ssv2")
        h = sb("h", [128, 4, D], BF16)
        HB = Buf("h")
        hT = sb("hT", [128, 8, TT], BF16)
        HTB = Buf("hT")
        junk = sb("junk", [128, D], BF16)
        JB = Buf("junk")
        ssv = sb("ssv", [128, 8], F32)
        SSB = Buf("ssv")
        tmp = sb("tmp", [128, D], F32)
        TMPB = Buf("tmp")
        aT = sb("aT", [128, 32, TT], BF16)
        ATB = Buf("aT")
        sqv = [sb("sqv%d" % i, [128, TT], F32) for i in range(2)]
        SQB = [Buf("sqv%d" % i) for i in range(2)]
        wup = WB["w_up%d" % l].rearrange("(kc p) n -> p kc n", p=128)
        wdn = WB["w_down%d" % l].rearrange("(fc p) n -> p fc n", p=128)

        def MM(out, lhsT, rhs, start, stop, R, Wb):
            P.op("pe", "matmul", reads=R, writes=[Wb], out=out, lhsT=lhsT, rhs=rhs, start=start, stop=stop)

        seqs = [("prm", t * TT, TT) for t in range(T // TT)] + [("smp", T, TS)]
        if "f_seqs" in DBG:
            seqs = [seqs[i] for i in DBG["f_seqs"]]
        ui = 0
        di = 0
        fi = 0

        def tile_geom(ti):
            kind, t0, n = seqs[ti]
            return kind, t0, n, blocks(n), min(128, n)

        def load_x(ti):
            kind, t0, n, blks, pp = tile_geom(ti)
            P.dma("sp", xts[ti % 2][0:pp, 0:len(blks), :], k.S["x"][t0:t0 + n, :].rearrange("(b p) d -> p b d", p=pp),
                  writes=[XTBs[ti % 2]])

        def prenorm_e(ti):
            kind, t0, n, blks, pp = tile_geom(ti)
            pre_norm_elem(P, xts[ti % 2], XTBs[ti % 2], g4, CW, junk2, JB2, ssv2, SSB2, epsc, h, HB, blks)

        def prenorm_t(ti):
            kind, t0, n, blks, pp = tile_geom(ti)
            pre_norm_T(P, bank, PB, [0, 1, 2, 3], h, HB, hT, HTB, identb, blks, n)
        def load_up(p):
            if p >= 4 * len(seqs):
                return
            q_ = p % 4
            sl_ = p % 2
            P.dma("sp", up[sl_][:, 0:4, :], wup[:, 0:4, q_ * 1024:(q_ + 1) * 1024], writes=[UPB[sl_]])
            P.dma("pool", up[sl_][:, 4:8, :], wup[:, 4:8, q_ * 1024:(q_ + 1) * 1024], writes=[UPB[sl_]])
        def load_dn(q_):
            sl_ = q_ % 2
            P.dma("sp", dn[sl_][:, 0:4, :], wdn[:, q_ * 8:q_ * 8 + 4, :], writes=[DNB[sl_]])
            P.dma("pool", dn[sl_][:, 4:8, :], wdn[:, q_ * 8 + 4:q_ * 8 + 8, :], writes=[DNB[sl_]])
        load_x(0)
        load_up(0)
        load_up(1)
        prenorm_e(0)
        prenorm_t(0)
        for ti, (kind, t0, n) in enumerate(seqs):
            blks = blocks(n)
            nb = len(blks)
            pp = min(128, n)
            xt, XTB = xts[ti % 2], XTBs[ti % 2]
            if ti + 1 < len(seqs):
                load_x(ti + 1)
            load_dn(0)
            load_dn(1)

            def tm(ap2d):
                return ap2d.rearrange("(b p) d -> p b d", p=pp)
            for q in range(4):
                sl = (ti * 4 + q) % 2
                for fcl in range(8):
                    fc = q * 8 + fcl
                    bk = 4 + fi % 4
                    sq_ = fi % 2
                    fi += 1
                    for kc in range(8):
                        MM(bank(bk)[:, 0:n], up[sl][:, kc, fcl * 128:(fcl + 1) * 128], hT[:, kc, 0:n], kc == 0, kc == 7,
                           [UPB[sl], HTB], PB[bk])
                    P.op("act", "activation", reads=[PB[bk]], writes=[SQB[sq_]], out=sqv[sq_][:, 0:n], in_=bank(bk)[:, 0:n],
                         func=AF.Square)
                    P.op("dve", "scalar_tensor_tensor", reads=[PB[bk], SQB[sq_]], writes=[ATB], out=aT[:, fc, 0:n],
                         in0=bank(bk)[:, 0:n], scalar=0.0, in1=sqv[sq_][:, 0:n], op0=ALU.is_gt, op1=ALU.mult)
                load_up(ti * 4 + q + 2)
            for q in range(4):
                sl = q % 2
                if q >= 2:
                    load_dn(q)
                for ib, (io, is_) in enumerate(blks):
                    for half in range(2):
                        bk = 2 * ib + half
                        for fcl in range(8):
                            fc = q * 8 + fcl
                            MM(bank(bk)[0:is_, :], aT[:, fc, io:io + is_], dn[sl][:, fcl, half * 512:(half + 1) * 512],
                               q == 0 and fcl == 0, q == 3 and fcl == 7, [ATB, DNB[sl]], PB[bk])
            if ti + 1 < len(seqs):
                prenorm_e(ti + 1)
            for ib, (io, is_) in enumerate(blks):
                pso = ps[0:is_, 2 * ib * 512:2 * ib * 512 + 1024]
                post_norm_residual(P, pso, [PB[2 * ib], PB[2 * ib + 1]], xt[0:is_, ib, :], XTB, g5, CW, junk, JB, ssv, SSB,
                                   epsc, tmp, TMPB, is_)
            if ti + 1 < len(seqs):
                prenorm_t(ti + 1)
            if final:
                dst = k.O["y_sample"] if kind == "smp" else k.O["y_prompt"][t0:t0 + n, :]
            else:
                dst = k.S["x"][t0:t0 + n, :]
            P.dma("sp", tm(dst), xt[0:pp, 0:nb, :], reads=[XTB], writes=[Buf()])
            P.maybe_flush()
        P.flush()


def phase_S1(k, P, ps, PB, bank, WB, identf, identb, onesf, onesb):
    with contextlib.ExitStack() as st:
        def sb(name, shape, dt):
            return k.sbuf(st, "s1_" + name, shape, dt)
        winc = sb("winc", [128, 8, D], BF16)
        g0 = sb("g0", [128, D], F32)
        epsc = sb("epsc", [128, 1], F32)
        CW = Buf("cw")
        P.dma("sp", winc[:], WB["w_in_c"].rearrange("(kc p) n -> p kc n", p=128), writes=[CW])
        P.dma("sp", g0[:], k.I["norm_w"][1, 0:1, :].broadcast_to([128, D]), writes=[CW])
        P.op("pool", "memset", writes=[CW], ap=epsc[:], constant=EPS)
        xts = [sb("xt%d" % i, [128, 4, D], F32) for i in range(2)]
        XTBs = [Buf("xt%d" % i) for i in range(2)]
        h = sb("h", [128, 4, D], BF16)
        HB = Buf("h")
        hT = sb("hT", [128, 8, TT], BF16)
        HTB = Buf("hT")
        junk = sb("junk", [128, D], BF16)
        JB = Buf("junk")
        ssv = sb("ssv", [128, 8], F32)
        SSB = Buf("ssv")
        uTs = [sb("uT%d" % i, [128, 8, TT], BF16) for i in range(2)]
        UTBs = [Buf("uT%d" % i) for i in range(2)]
        seqs = [("prm", t * TT, TT) for t in range(T // TT)] + [("smp", T, TS)]
        def load_x(ti):
            kind, t0, n = seqs[ti]
            pp = min(128, n)
            P.dma("sp", xts[ti % 2][0:pp, 0:len(blocks(n)), :], k.S["x"][t0:t0 + n, :].rearrange("(b p) d -> p b d", p=pp),
                  writes=[XTBs[ti % 2]])
        load_x(0)
        for ti, (kind, t0, n) in enumerate(seqs):
            blks = blocks(n)
            nb = len(blks)
            pp = min(128, n)
            xt, XTB = xts[ti % 2], XTBs[ti % 2]
            uT, UTB = uTs[ti % 2], UTBs[ti % 2]
            if ti + 1 < len(seqs):
                load_x(ti + 1)
            pre_norm_hT(P, bank, PB, [0, 1, 2, 3], xt, XTB, g0, CW, junk, JB, ssv, SSB, epsc, h, HB, hT, HTB, identb, blks, n)
            for mc in range(8):
                bk = 4 + mc % 4
                for kc in range(8):
                    P.op("pe", "matmul", reads=[HTB, CW], writes=[PB[bk]], out=bank(bk)[:, 0:n],
                         lhsT=winc[:, kc, mc * 128:(mc + 1) * 128], rhs=hT[:, kc, 0:n], start=(kc == 0), stop=(kc == 7))
                if mc % 2 == 0:
                    P.op("act", "copy", reads=[PB[bk]], writes=[UTB], out=uT[:, mc, 0:n], in_=bank(bk)[:, 0:n])
                else:
                    P.op("dve", "tensor_copy", reads=[PB[bk]], writes=[UTB], out=uT[:, mc, 0:n], in_=bank(bk)[:, 0:n])
            P.dma("act", k.S["uT"][:, :, t0:t0 + n].rearrange("c p t -> p c t"), uT[:, :, 0:n], reads=[UTB], writes=[Buf()])
            P.maybe_flush()
        P.flush()


def phase_S3(k, P, ps, PB, bank, WB, identf, identb, onesf, onesb):
    with contextlib.ExitStack() as st:
        def sb(name, shape, dt):
            return k.sbuf(st, "s3_" + name, shape, dt)
        wg = sb("wg", [128, 8, 2 * D], BF16)
        g1 = sb("g1", [128, D], F32)
        epsc = sb("epsc", [128, 1], F32)
        CW = Buf("cw")
        P.dma("sp", wg[:], WB["w_glu"].rearrange("(kc p) n -> p kc n", p=128), writes=[CW])
        P.dma("sp", g1[:], k.I["norm_w"][1, 1:2, :].broadcast_to([128, D]), writes=[CW])
        P.op("pool", "memset", writes=[CW], ap=epsc[:], constant=EPS)
        xts = [sb("xt%d" % i, [128, 4, D], F32) for i in range(2)]
        XTBs = [Buf("xt%d" % i) for i in range(2)]
        yTs = [sb("yT%d" % i, [128, 8, TT], BF16) for i in range(2)]
        YTBs = [Buf("yT%d" % i) for i in range(2)]
        junk = sb("junk", [128, D], BF16)
        JB = Buf("junk")
        ssv = sb("ssv", [128, 8], F32)
        SSB = Buf("ssv")
        tmp = sb("tmp", [128, D], F32)
        TMPB = Buf("tmp")
        sig = sb("sig", [128, D], F32)
        SIGB = Buf("sig")
        gl = sb("gl", [128, D], F32)
        GLB = Buf("gl")
        seqs = [("prm", t * TT, TT) for t in range(T // TT)] + [("smp", T, TS)]
        def load_in(ti):
            kind, t0, n = seqs[ti]
            pp = min(128, n)
            P.dma("sp", xts[ti % 2][0:pp, 0:len(blocks(n)), :], k.S["x"][t0:t0 + n, :].rearrange("(b p) d -> p b d", p=pp),
                  writes=[XTBs[ti % 2]])
            P.dma("sp", yTs[ti % 2][:, :, 0:n], k.S["yT"][:, :, t0:t0 + n].rearrange("c p t -> p c t"), writes=[YTBs[ti % 2]])
        load_in(0)
        for ti, (kind, t0, n) in enumerate(seqs):
            blks = blocks(n)
            nb = len(blks)
            pp = min(128, n)
            xt, XTB = xts[ti % 2], XTBs[ti % 2]
            yT, YTB = yTs[ti % 2], YTBs[ti % 2]
            if ti + 1 < len(seqs):
                load_in(ti + 1)
            for ib, (io, is_) in enumerate(blks):
                b0 = 4 * (ib % 2)
                for q4 in range(4):
                    bk = b0 + q4
                    for kc in range(8):
                        P.op("pe", "matmul", reads=[YTB, CW], writes=[PB[bk]], out=bank(bk)[0:is_, :],
                             lhsT=yT[:, kc, io:io + is_], rhs=wg[:, kc, q4 * 512:(q4 + 1) * 512], start=(kc == 0), stop=(kc == 7))
                P.op("act", "activation", reads=[PB[b0 + 2], PB[b0 + 3]], writes=[SIGB], out=sig[0:is_, :],
                     in_=ps[0:is_, (b0 + 2) * 512:(b0 + 4) * 512], func=AF.Sigmoid)
                P.op("dve", "tensor_tensor", reads=[PB[b0], PB[b0 + 1], SIGB], writes=[GLB], out=gl[0:is_, :],
                     in0=ps[0:is_, b0 * 512:(b0 + 2) * 512], in1=sig[0:is_, :], op=ALU.mult)
                post_norm_residual(P, gl[0:is_, :], [GLB], xt[0:is_, ib, :], XTB, g1, CW, junk, JB, ssv, SSB, epsc, tmp, TMPB, is_)
            P.dma("sp", k.S["x"][t0:t0 + n, :].rearrange("(b p) d -> p b d", p=pp), xt[0:pp, 0:nb, :], reads=[XTB], writes=[Buf()])
            P.maybe_flush()
        P.flush()


PI = float(np.pi)


def phase_S2(k, P, ps, PB, bank, WB, identf, identb, onesf, onesb):
    with contextlib.ExitStack() as st:
        def sb(name, shape, dt):
            return k.sbuf(st, "s2_" + name, shape, dt)
        Tall = sb("Tall", [128, 64, 128], BF16)
        Vall = sb("Vall", [128, 64, 128], BF16)
        WR = sb("WR", [128, 32, 128], BF16)
        WI = sb("WI", [128, 32, 128], BF16)
        Ec = sb("Ec", [128, 32, 64], F32)
        Es = sb("Es", [128, 32, 64], F32)
        Rt = sb("Rt", [128, 32, 64], F32)
        r8s = sb("r8s", [128, 32], F32)
        dfm = sb("dfm", [128, 8], F32)
        XRc = sb("XRc", [128, 32], F32)
        XIc = sb("XIc", [128, 32], F32)
        TB = Buf("s2tab")
        CARB = Buf("s2carry")
        P.dma("sp", dfm[:], k.I["s5_d"].rearrange("o (c p) -> p (o c)", p=128), writes=[TB], allow_slow_non_contiguous=True)

        def MM(out, lhsT, rhs, start, stop, R, Wb):
            P.op("pe", "matmul", reads=R, writes=[Wb], out=out, lhsT=lhsT, rhs=rhs, start=start, stop=stop)

        with contextlib.ExitStack() as st2:
            def sb2(name, shape, dt):
                return k.sbuf(st2, "s2t_" + name, shape, dt)
            lamr = sb2("lamr", [128, 32], F32)
            lami = sb2("lami", [128, 32], F32)
            dtv = sb2("dtv", [128, 32], F32)
            lr = sb2("lr", [128, 32], F32)
            li = sb2("li", [128, 32], F32)
            kvec = sb2("kvec", [128, 16], F32)
            nvec = sb2("nvec", [128, 64], F32)
            tmask = sb2("tmask", [128, 512], F32)
            Bre = sb2("Bre", [128, 32, 16], F32)
            Bim = sb2("Bim", [128, 32, 16], F32)
            Cnat = sb2("Cnat", [128, 4, 128], F32)
            Cre = sb2("Cre", [128, 32, 16], F32)
            Cim = sb2("Cim", [128, 32, 16], F32)
            arg = sb2("arg", [128, 32, 16], F32)
            mag = sb2("mag", [128, 32, 16], F32)
            Are = sb2("Are", [128, 32, 16], F32)
            Aim = sb2("Aim", [128, 32, 16], F32)
            s1 = sb2("s1", [128, 32], F32)
            s2_ = sb2("s2", [128, 32], F32)
            s3 = sb2("s3", [128, 32], F32)
            fre = sb2("fre", [128, 32], F32)
            fim = sb2("fim", [128, 32], F32)
            bbr = sb2("bbr", [128, 32, 16], F32)
            bbi = sb2("bbi", [128, 32, 16], F32)
            t16 = sb2("t16", [128, 32, 16], F32)
            Gre = sb2("Gre", [128, 32, 128], F32)
            Gim = sb2("Gim", [128, 32, 128], F32)
            Hre = sb2("Hre", [128, 32, 128], F32)
            Him = sb2("Him", [128, 32, 128], F32)
            big = sb2("big", [128, 32, 128], F32)
            eb = sb2("eb", [128, 32, 64], F32)
            SB_ = Buf("setup")
            for gh in range(2):
                psl = slice(gh * 64, gh * 64 + 64)
                gs = slice(gh * 32, gh * 32 + 32)
                P.dma("sp", lamr[psl, :], k.I["s5_lambda_re"][gs, :].rearrange("g p -> p g"), writes=[SB_],
                      allow_slow_non_contiguous=True)
                P.dma("act", lami[psl, :], k.I["s5_lambda_im"][gs, :].rearrange("g p -> p g"), writes=[SB_],
                      allow_slow_non_contiguous=True)
                P.dma("sp", dtv[psl, :], k.I["s5_log_dt"][0:1, gs].broadcast_to([64, 32]), writes=[SB_])
                P.dma("sp", Bre[psl, :, :], k.I["s5_b_re"][gs].rearrange("g p c -> p g c"), writes=[SB_])
                P.dma("act", Bim[psl, :, :], k.I["s5_b_im"][gs].rearrange("g p c -> p g c"), writes=[SB_])
            P.dma("sp", kvec[:], k.C["kvec"], writes=[SB_])
            P.dma("sp", nvec[:], k.C["nvec"], writes=[SB_])
            P.dma("sp", tmask[:], k.C["tmask"], writes=[SB_])
            for (src, dstC) in ((k.I["s5_c_re"], Cre), (k.I["s5_c_im"], Cim)):
                for gh in range(2):
                    P.dma("sp", Cnat[:, :, gh * 64:gh * 64 + 64],
                          src[gh * 32:gh * 32 + 32].rearrange("(o g) c p -> (g c) o p", o=4), writes=[SB_])
                for o4 in range(4):
                    MM(bank(0)[:, 0:128], Cnat[:, o4, :], identf[:, :], True, True, [SB_], PB[0])
                    gl0 = o4 * 8
                    P.op("dve", "tensor_copy", reads=[PB[0]], writes=[SB_],
                         out=dstC[:, gl0:gl0 + 8, :], in_=bank(0)[:, 0:128].rearrange("p (g c) -> p g c", c=16))

            def V(eng, name, **kw):
                P.op(eng, name, reads=[SB_], writes=[SB_], **kw)
            V("act", "activation", out=dtv[:], in_=dtv[:], func=AF.Exp)
            V("dve", "tensor_tensor", out=lr[:], in0=lamr[:], in1=dtv[:], op=ALU.mult)
            V("dve", "tensor_tensor", out=li[:], in0=lami[:], in1=dtv[:], op=ALU.mult)
            kb = kvec[:].unsqueeze(1).broadcast_to([128, 32, 16])
            V("dve", "tensor_tensor", out=arg[:], in0=lr[:].unsqueeze(2).broadcast_to([128, 32, 16]), in1=kb, op=ALU.mult)
            V("act", "activation", out=mag[:], in_=arg[:], func=AF.Exp)
            V("dve", "tensor_tensor", out=arg[:], in0=li[:].unsqueeze(2).broadcast_to([128, 32, 16]), in1=kb, op=ALU.mult)

            isc = sb2("isc", [128, 32, 64], mybir.dt.int32)

            def sincos(out_s, out_c, ang, f1, f2, i1):
                for (dst, off) in ((out_s, 0.0), (out_c, 0.25)):
                    V("dve", "tensor_scalar", out=f1, in0=ang, scalar1=1.0 / (2 * PI), scalar2=off, op0=ALU.mult, op1=ALU.add)
                    V("dve", "tensor_copy", out=i1, in_=f1)
                    V("dve", "tensor_copy", out=f2, in_=i1)
                    V("dve", "tensor_tensor", out=f1, in0=f1, in1=f2, op=ALU.subtract)
                    V("act", "activation", out=dst, in_=f1, func=AF.Sin, scale=2 * PI)
            sincos(Aim[:], Are[:], arg[:], t16[:], bbr[:], isc[:, :, 0:16])
            V("dve", "tensor_tensor", out=Are[:], in0=Are[:], in1=mag[:], op=ALU.mult)
            V("dve", "tensor_tensor", out=Aim[:], in0=Aim[:], in1=mag[:], op=ALU.mult)
            a_re, a_im = Are[:, :, 8], Aim[:, :, 8]
            V("dve", "tensor_tensor", out=s1[:], in0=lamr[:], in1=lamr[:], op=ALU.mult)
            V("dve", "tensor_tensor", out=s2_[:], in0=lami[:], in1=lami[:], op=ALU.mult)
            V("dve", "tensor_tensor", out=s1[:], in0=s1[:], in1=s2_[:], op=ALU.add)
            V("dve", "reciprocal", out=s1[:], in_=s1[:])
            V("dve", "tensor_scalar", out=s2_[:], in0=a_re, scalar1=-1.0, scalar2=None, op0=ALU.add)
            V("dve", "tensor_tensor", out=fre[:], in0=s2_[:], in1=lamr[:], op=ALU.mult)
            V("dve", "tensor_tensor", out=s3[:], in0=a_im, in1=lami[:], op=ALU.mult)
            V("dve", "tensor_tensor", out=fre[:], in0=fre[:], in1=s3[:], op=ALU.add)
            V("dve", "tensor_tensor", out=fre[:], in0=fre[:], in1=s1[:], op=ALU.mult)
            V("dve", "tensor_tensor", out=fim[:], in0=a_im, in1=lamr[:], op=ALU.mult)
            V("dve", "tensor_tensor", out=s3[:], in0=s2_[:], in1=lami[:], op=ALU.mult)
            V("dve", "tensor_tensor", out=fim[:], in0=fim[:], in1=s3[:], op=ALU.subtract)
            V("dve", "tensor_tensor", out=fim[:], in0=fim[:], in1=s1[:], op=ALU.mult)
            frb = fre[:].unsqueeze(2).broadcast_to([128, 32, 16])
            fib = fim[:].unsqueeze(2).broadcast_to([128, 32, 16])
            V("dve", "tensor_tensor", out=bbr[:], in0=Bre[:], in1=frb, op=ALU.mult)
            V("dve", "tensor_tensor", out=t16[:], in0=Bim[:], in1=fib, op=ALU.mult)
            V("dve", "tensor_tensor", out=bbr[:], in0=bbr[:], in1=t16[:], op=ALU.subtract)
            V("dve", "tensor_tensor", out=bbi[:], in0=Bim[:], in1=frb, op=ALU.mult)
            V("dve", "tensor_tensor", out=t16[:], in0=Bre[:], in1=fib, op=ALU.mult)
            V("dve", "tensor_tensor", out=bbi[:], in0=bbi[:], in1=t16[:], op=ALU.add)

            def v4(t):
                return t[:].rearrange("p g (s c) -> p g s c", c=16)

            def outer(are, aim, xre, xim, ore, oim, neg_im=False):
                ab = are.unsqueeze(3).broadcast_to([128, 32, 8, 16])
                ai = aim.unsqueeze(3).broadcast_to([128, 32, 8, 16])
                xr = xre.unsqueeze(2).broadcast_to([128, 32, 8, 16])
                xi = xim.unsqueeze(2).broadcast_to([128, 32, 8, 16])
                V("dve", "tensor_tensor", out=v4(ore), in0=ab, in1=xr, op=ALU.mult)
                V("dve", "tensor_tensor", out=v4(big), in0=ai, in1=xi, op=ALU.mult)
                V("dve", "tensor_tensor", out=ore[:], in0=ore[:], in1=big[:], op=ALU.subtract)
                V("dve", "tensor_tensor", out=v4(oim), in0=ab, in1=xi, op=ALU.mult)
                V("dve", "tensor_tensor", out=v4(big), in0=ai, in1=xr, op=ALU.mult)
                V("dve", "tensor_tensor", out=oim[:], in0=oim[:], in1=big[:], op=ALU.add)
            outer(Are[:, :, 0:8], Aim[:, :, 0:8], bbr[:], bbi[:], Gre, Gim)
            outer(Are[:, :, 8:16], Aim[:, :, 8:16], Cre[:], Cim[:], Hre, Him)
            V("dve", "tensor_scalar", out=Him[:], in0=Him[:], scalar1=-1.0, scalar2=None, op0=ALU.mult)
            for g in range(64):
                gh, gl_ = g // 32, g % 32
                psl = slice(gh * 64, gh * 64 + 64)
                bk = (g // 4) % 2
                col = (g % 4) * 128
                MM(bank(bk)[:, col:col + 128], Gre[psl, gl_, :], Hre[psl, gl_, :], True, False, [SB_], PB[bk])
                MM(bank(bk)[:, col:col + 128], Gim[psl, gl_, :], Him[psl, gl_, :], False, True, [SB_], PB[bk])
                if g % 4 == 3:
                    g0 = g - 3
                    P.op("dve", "tensor_tensor", reads=[PB[bk], SB_], writes=[TB],
                         out=Tall[:, g0:g0 + 4, :], in0=bank(bk)[:, :].rearrange("p (g m) -> p g m", m=128),
                         in1=tmask[:].rearrange("p (g m) -> p g m", m=128), op=ALU.mult)
            V("act", "copy", out=WR[:], in_=Hre[:])
            V("act", "copy", out=WI[:], in_=Him[:])
            P.op("act", "copy", reads=[SB_], writes=[TB], out=WR[:], in_=Hre[:])
            a8r = Are[:, :, 15:16].broadcast_to([128, 32, 128])
            a8i = Aim[:, :, 15:16].broadcast_to([128, 32, 128])
            V("dve", "tensor_tensor", out=Hre[:], in0=Gre[:], in1=a8r, op=ALU.mult)
            V("dve", "tensor_tensor", out=big[:], in0=Gim[:], in1=a8i, op=ALU.mult)
            V("dve", "tensor_tensor", out=Hre[:], in0=Hre[:], in1=big[:], op=ALU.subtract)
            V("dve", "tensor_tensor", out=Him[:], in0=Gim[:], in1=a8r, op=ALU.mult)
            V("dve", "tensor_tensor", out=big[:], in0=Gre[:], in1=a8i, op=ALU.mult)
            V("dve", "tensor_tensor", out=Him[:], in0=Him[:], in1=big[:], op=ALU.add)
            for g in range(64):
                gh, gl_ = g // 32, g % 32
                psl = slice(gh * 64, gh * 64 + 64)
                bk = 2 + (g // 4) % 2
                col = (g % 4) * 128
                MM(bank(bk)[:, col:col + 64], Hre[psl, gl_, :], identf[psl, psl], True, True, [SB_], PB[bk])
                MM(bank(bk)[:, col + 64:col + 128], Him[psl, gl_, :], identf[psl, psl], True, True, [SB_], PB[bk])
                if g % 4 == 3:
                    g0 = g - 3
                    P.op("act", "copy", reads=[PB[bk]], writes=[TB], out=Vall[:, g0:g0 + 4, :],
                         in_=bank(bk)[:, :].rearrange("p (g m) -> p g m", m=128))
            nb_ = nvec[:].unsqueeze(1).broadcast_to([128, 32, 64])
            V("dve", "tensor_tensor", out=eb[:], in0=li[:].unsqueeze(2).broadcast_to([128, 32, 64]), in1=nb_, op=ALU.mult)
            sincos(Es[:], Ec[:], eb[:], Rt[:], big[:, :, 0:64], isc[:])
            V("act", "activation", out=r8s[:], in_=lr[:], func=AF.Exp, scale=8.0)
            P.op("dve", "tensor_copy", reads=[SB_], writes=[TB], out=Rt[:], in_=r8s[:].unsqueeze(2).broadcast_to([128, 32, 64]))
            P.op("pool", "memset", reads=[SB_], writes=[TB], ap=Rt[:, :, 0:1], constant=0.0)
            P.flush()

        sel = sb("sel", [128, 64, 128], BF16)
        selT = sb("selT", [128, 64, 128], BF16)
        P.dma("sp", sel[:], k.C["sel_b"], writes=[TB])
        P.dma("act", selT[:], k.C["selT_b"], writes=[TB])
        uTs = [sb("uT%d" % i, [128, 8, TT], BF16) for i in range(2)]
        UTBs = [Buf("uT%d" % i) for i in range(2)]
        Ugs = [sb("Ug%d" % i, [128, 64, 64], BF16) for i in range(2)]
        UGBs = [Buf("Ug%d" % i) for i in range(2)]
        XsR = sb("XsR", [128, 32, 64], BF16)
        XsI = sb("XsI", [128, 32, 64], BF16)
        XSB = Buf("Xs")
        Yg = sb("Yg", [128, 64, 64], BF16)
        YGB = Buf("Yg")
        yT = sb("yT", [128, 8, TT], BF16)
        YTB = Buf("yT")
        ta = sb("ta", [128, 16, 64], F32)
        tb = sb("tb", [128, 16, 64], F32)
        vr = sb("vr", [128, 16, 64], F32)
        vi = sb("vi", [128, 16, 64], F32)
        zr = sb("zr", [128, 16, 64], F32)
        zi = sb("zi", [128, 16, 64], F32)
        tc_ = sb("tc", [128, 16], F32)
        WKB = Buf("s2work")
        for t_ in (ta, tb, vr, vi, zr, zi):
            P.op("pool", "memset", writes=[WKB], ap=t_[:], constant=0.0)
        P.op("pool", "memset", writes=[CARB], ap=XRc[:], constant=0.0)
        P.op("pool", "memset", writes=[CARB], ap=XIc[:], constant=0.0)
        P.op("pool", "memset", writes=[XSB], ap=XsR[:], constant=0.0)
        P.op("pool", "memset", writes=[XSB], ap=XsI[:], constant=0.0)

        def f2(t):
            return t[:].rearrange("p g n -> p (g n)")
        seqs = [("prm", t * TT, TT) for t in range(T // TT)] + [("smp", T, TS)]
        if "s2_seqs" in DBG:
            seqs = [seqs[i] for i in DBG["s2_seqs"]]
        def load_u(ti):
            kind, t0, n = seqs[ti]
            P.dma("sp", uTs[ti % 2][:, :, 0:n], k.S["uT"][:, :, t0:t0 + n].rearrange("c p t -> p c t"), writes=[UTBs[ti % 2]])
        load_u(0)
        def tile_vars(ti):
            kind, t0, n = seqs[ti]
            return kind, t0, n, n // 8, uTs[ti % 2], UTBs[ti % 2], Ugs[ti % 2], UGBs[ti % 2]

        def stageA(ti):
            kind, t0, n, nch, uT, UTB, Ug, UGB = tile_vars(ti)
            for g in range(64):
                kc, g8 = g // 8, g % 8
                bk = (g // 8) % 2
                col = g8 * 64
                for s_ in range(8):
                    rhs = uT[:, kc, 0:n].rearrange("p (n s) -> p n s", s=8)[:, :, s_]
                    MM(bank(bk)[:, col:col + nch], sel[:, g8 * 8 + s_, :], rhs, s_ == 0, s_ == 7, [UTB, TB], PB[bk])
                if g8 == 7:
                    g0 = g - 7
                    src = bank(bk)[:, :].rearrange("p (g n) -> p g n", n=64)[:, :, 0:nch]
                    if (g // 8) % 2 == 0:
                        P.op("act", "copy", reads=[PB[bk]], writes=[UGB], out=Ug[:, g0:g0 + 8, 0:nch], in_=src)
                    else:
                        P.op("dve", "tensor_copy", reads=[PB[bk]], writes=[UGB], out=Ug[:, g0:g0 + 8, 0:nch], in_=src)

        def stageB(ti):
            kind, t0, n, nch, uT, UTB, Ug, UGB = tile_vars(ti)
            if kind == "smp":
                for gh in range(2):
                    psl = slice(gh * 64, gh * 64 + 64)
                    gs = slice(gh * 32, gh * 32 + 32)
                    P.dma("sp", XRc[psl, :], k.I["state_s5_re"][gs, :].rearrange("g p -> p g"), writes=[CARB],
                          allow_slow_non_contiguous=True)
                    P.dma("act", XIc[psl, :], k.I["state_s5_im"][gs, :].rearrange("g p -> p g"), writes=[CARB],
                          allow_slow_non_contiguous=True)
            for hf in range(2):
                glo = hf * 16
                for gh in range(2):
                    psl = slice(gh * 64, gh * 64 + 64)
                    for gq in range(16):
                        g = gh * 32 + glo + gq
                        bR = 2 + gq // 8
                        bI = 4 + gq // 8
                        col = (gq % 8) * 64
                        MM(bank(bR)[psl, col:col + nch], Vall[:, g, 0:64], Ug[:, g, 0:nch], True, True, [UGB, TB], PB[bR])
                        MM(bank(bI)[psl, col:col + nch], Vall[:, g, 64:128], Ug[:, g, 0:nch], True, True, [UGB, TB], PB[bI])
                pR = ps[:, 2 * 512:4 * 512].rearrange("p (g n) -> p g n", n=64)[:, :, 0:nch]
                pI = ps[:, 4 * 512:6 * 512].rearrange("p (g n) -> p g n", n=64)[:, :, 0:nch]
                ec = Ec[:, glo:glo + 16, 0:nch]
                es = Es[:, glo:glo + 16, 0:nch]
                RD = [PB[2], PB[3], PB[4], PB[5], TB, WKB]
                WW = [WKB]

                def W3(t):
                    return t[:, :, 0:nch]
                P.op("dve", "tensor_tensor", reads=RD, writes=WW, out=W3(ta), in0=pR, in1=ec, op=ALU.mult)
                P.op("dve", "tensor_tensor", reads=RD, writes=WW, out=W3(tb), in0=pI, in1=es, op=ALU.mult)
                P.op("pool", "tensor_tensor", reads=RD, writes=WW, out=W3(vr), in0=W3(ta), in1=W3(tb), op=ALU.add)
                P.op("dve", "tensor_tensor", reads=RD, writes=WW, out=W3(ta), in0=pI, in1=ec, op=ALU.mult)
                P.op("dve", "tensor_tensor", reads=RD, writes=WW, out=W3(tb), in0=pR, in1=es, op=ALU.mult)
                P.op("pool", "tensor_tensor", reads=RD, writes=WW, out=W3(vi), in0=W3(ta), in1=W3(tb), op=ALU.subtract)
                for (vv, cc) in ((vr, XRc), (vi, XIc)):
                    P.op("dve", "tensor_tensor", reads=[CARB, TB, WKB], writes=WW, out=tc_[:], in0=cc[:, glo:glo + 16],
                         in1=r8s[:, glo:glo + 16], op=ALU.mult)
                    P.op("dve", "tensor_tensor", reads=[WKB], writes=WW, out=vv[:, :, 0], in0=vv[:, :, 0], in1=tc_[:],
                         op=ALU.add)
                rt2 = Rt[:, glo:glo + 16, :].rearrange("p g n -> p (g n)")
                P.op("dve", "tensor_tensor_scan", reads=[WKB, TB], writes=WW, out=f2(zr), data0=rt2, data1=f2(vr), initial=0.0,
                     op0=ALU.mult, op1=ALU.add)
                P.op("dve", "tensor_tensor_scan", reads=[WKB, TB], writes=WW, out=f2(zi), data0=rt2, data1=f2(vi), initial=0.0,
                     op0=ALU.mult, op1=ALU.add)
                P.op("dve", "tensor_tensor", reads=[WKB, TB], writes=WW, out=W3(ta), in0=W3(zr), in1=ec, op=ALU.mult)
                P.op("pool", "tensor_tensor", reads=[WKB, TB], writes=WW, out=W3(tb), in0=W3(zi), in1=es, op=ALU.mult)
                P.op("dve", "tensor_tensor", reads=[WKB], writes=WW, out=W3(vr), in0=W3(ta), in1=W3(tb), op=ALU.subtract)
                P.op("dve", "tensor_tensor", reads=[WKB, TB], writes=WW, out=W3(ta), in0=W3(zi), in1=ec, op=ALU.mult)
                P.op("pool", "tensor_tensor", reads=[WKB, TB], writes=WW, out=W3(tb), in0=W3(zr), in1=es, op=ALU.mult)
                P.op("dve", "tensor_tensor", reads=[WKB], writes=WW, out=W3(vi), in0=W3(ta), in1=W3(tb), op=ALU.add)
                P.op("act", "copy", reads=[CARB], writes=[XSB], out=XsR[:, glo:glo + 16, 0], in_=XRc[:, glo:glo + 16])
                P.op("act", "copy", reads=[CARB], writes=[XSB], out=XsI[:, glo:glo + 16, 0], in_=XIc[:, glo:glo + 16])
                if nch > 1:
                    P.op("act", "copy", reads=[WKB], writes=[XSB], out=XsR[:, glo:glo + 16, 1:nch], in_=vr[:, :, 0:nch - 1])
                    P.op("act", "copy", reads=[WKB], writes=[XSB], out=XsI[:, glo:glo + 16, 1:nch], in_=vi[:, :, 0:nch - 1])
                P.op("dve", "tensor_copy", reads=[WKB, XSB], writes=[CARB], out=XRc[:, glo:glo + 16], in_=vr[:, :, nch - 1])
                P.op("dve", "tensor_copy", reads=[WKB, XSB], writes=[CARB], out=XIc[:, glo:glo + 16], in_=vi[:, :, nch - 1])
            last_of_kind = (kind == "smp") or (t0 + n == T)
            if last_of_kind:
                for gh in range(2):
                    psl = slice(gh * 64, gh * 64 + 64)
                    gs = slice(gh * 32, gh * 32 + 32)
                    dr = k.O["s5_re_sample" if kind == "smp" else "s5_re_prompt"]
                    di = k.O["s5_im_sample" if kind == "smp" else "s5_im_prompt"]
                    P.dma("sp", dr[gs, :].rearrange("g p -> p g"), XRc[psl, :], reads=[CARB], writes=[Buf()],
                          allow_slow_non_contiguous=True)
                    P.dma("act", di[gs, :].rearrange("g p -> p g"), XIc[psl, :], reads=[CARB], writes=[Buf()],
                          allow_slow_non_contiguous=True)

        def stageC(ti):
            kind, t0, n, nch, uT, UTB, Ug, UGB = tile_vars(ti)
            for g in range(64):
                gh, gl_ = g // 32, g % 32
                psl = slice(gh * 64, gh * 64 + 64)
                bk = 6 + (g // 8) % 2
                col = (g % 8) * 64
                MM(bank(bk)[:, col:col + nch], Tall[:, g, :], Ug[:, g, 0:nch], True, False, [UGB, TB], PB[bk])
                MM(bank(bk)[:, col:col + nch], WR[psl, gl_, :], XsR[psl, gl_, 0:nch], False, False, [XSB, TB], PB[bk])
                MM(bank(bk)[:, col:col + nch], WI[psl, gl_, :], XsI[psl, gl_, 0:nch], False, True, [XSB, TB], PB[bk])
                if g % 8 == 7:
                    g0 = g - 7
                    src = bank(bk)[:, :].rearrange("p (g n) -> p g n", n=64)[:, :, 0:nch]
                    if (g // 8) % 2 == 0:
                        P.op("act", "copy", reads=[PB[bk]], writes=[YGB], out=Yg[:, g0:g0 + 8, 0:nch], in_=src)
                    else:
                        P.op("dve", "tensor_copy", reads=[PB[bk]], writes=[YGB], out=Yg[:, g0:g0 + 8, 0:nch], in_=src)
            for kc in range(8):
                bk = kc % 2
                for tau in range(8):
                    for g8 in range(8):
                        MM(bank(bk)[:, tau * 64:tau * 64 + nch], selT[:, g8 * 8 + tau, :], Yg[:, kc * 8 + g8, 0:nch], g8 == 0,
                           g8 == 7, [YGB, TB], PB[bk])
                yv = yT[:, kc, 0:n].rearrange("p (n s) -> p s n", s=8)
                uv = uT[:, kc, 0:n].rearrange("p (n s) -> p s n", s=8)
                pv = bank(bk)[:, :].rearrange("p (s n) -> p s n", n=64)[:, :, 0:nch]
                P.op("dve", "scalar_tensor_tensor", reads=[PB[bk], UTB, TB], writes=[YTB], out=yv, in0=uv, scalar=dfm[:, kc:kc + 1],
                     in1=pv, op0=ALU.mult, op1=ALU.add)
            P.dma("act", k.S["yT"][:, :, t0:t0 + n].rearrange("c p t -> p c t"), yT[:, :, 0:n], reads=[YTB], writes=[Buf()])

        stageA(0)
        for ti in range(len(seqs)):
            if ti + 1 < len(seqs):
                load_u(ti + 1)
            stageB(ti)
            if ti + 1 < len(seqs):
                stageA(ti + 1)
            stageC(ti)
            P.maybe_flush()
        P.flush()
```
